# Optimizing a Trainium2 kernel written in Bass

```python
import math
import jax, jax.numpy as jnp
from jax import lax
import numpy as np

D_MODEL = 1024
BATCH = 8
SEQ = 2048
DEPTH = 2
DEC_BATCH = 128
DEC_SEQ = 8
PAST_LEN = 16384
PAGE_SIZE = 128

N_MIXERS = 4
NH = 4
GROUP_WIDTH = D_MODEL // N_MIXERS
HEAD_DIM = GROUP_WIDTH // NH
MIX_WIDTH = N_MIXERS * GROUP_WIDTH
D_FF = 4 * D_MODEL
EPS = 1e-6

RWKV_R_W = 32
RWKV_R_A = 32
RWKV_R_G = 64
RWKV_GN_EPS = 64e-5
GLA_R = 16
GLA_CHUNK = 16
GLA_GATE_NORM = 16.0
DN_CONV = 4
DN_CHUNK = 64
SSM_STATE = 128
SSM_GROUPS = 2
SSM_CONV = 4
SSM_CHUNK = 64
SSM_XBC = GROUP_WIDTH + 2 * SSM_GROUPS * SSM_STATE

RWKV_SIZES = (GROUP_WIDTH, GROUP_WIDTH, GROUP_WIDTH, RWKV_R_W, RWKV_R_A, RWKV_R_G)
GLA_SIZES = (GROUP_WIDTH, GROUP_WIDTH, GROUP_WIDTH, GROUP_WIDTH, GLA_R)
DN_SIZES = (3 * GROUP_WIDTH, GROUP_WIDTH, NH, NH)
SSM_SIZES = (GROUP_WIDTH, SSM_XBC, NH)
RWKV_COLS = sum(RWKV_SIZES)
GLA_COLS = sum(GLA_SIZES)
DN_COLS = sum(DN_SIZES)
SSM_COLS = sum(SSM_SIZES)
IN_SIZES = (RWKV_COLS, GLA_COLS, DN_COLS, SSM_COLS)
P_TOTAL = sum(IN_SIZES)

kernel_name = "hybrid_rwkv7_gla_gdn_ssd_step"

F32 = jnp.float32


def split_last(x, sizes):
    offs, acc = [], 0
    for s in sizes[:-1]:
        acc += s
        offs.append(acc)
    return jnp.split(x, offs, axis=-1)


def heads(t):
    return t.reshape(t.shape[:-1] + (NH, HEAD_DIM))


def rmsnorm(x, w):
    xf = x.astype(F32)
    return (xf * lax.rsqrt(jnp.mean(xf * xf, -1, keepdims=True) + EPS)).astype(x.dtype) * w


def group_rmsnorm(y, w, n_groups):
    shp = y.shape
    yg = y.reshape(shp[:-1] + (n_groups, shp[-1] // n_groups)).astype(F32)
    yg = yg * lax.rsqrt(jnp.mean(yg * yg, -1, keepdims=True) + EPS)
    return yg.reshape(shp) * w


def l2norm(t):
    return t * lax.rsqrt(jnp.sum(t * t, -1, keepdims=True) + EPS)


def causal_conv(x, prev, w):
    K, T = w.shape[0], x.shape[1]
    xp = jnp.concatenate([prev, x], axis=1)
    y = sum(xp[:, i:i + T] * w[i] for i in range(K))
    return y, xp[:, T:]


def to_chunks(t, C):
    B, T, H = t.shape[:3]
    t = t.reshape((B, T // C, C, H) + t.shape[3:])
    return jnp.moveaxis(t, (1, 3), (0, 2))


def from_chunks(t):
    t = jnp.moveaxis(t, (0, 2), (1, 3))
    return t.reshape((t.shape[0], t.shape[1] * t.shape[2]) + t.shape[3:])


def seg_decay(cum):
    C = cum.shape[-1]
    mask = jnp.tril(jnp.ones((C, C), bool))
    diff = cum[..., :, None] - cum[..., None, :]
    return jnp.where(mask, jnp.exp(jnp.where(mask, diff, 0.0)), 0.0)


def rwkv7_mix(u, shift0, S0, mu, w0, w2, a0, a2, g2, k_k, k_a, r_k, ln_w, ln_b):
    odt = u.dtype
    u = u.astype(F32)
    B, T, _ = u.shape
    u_prev = jnp.concatenate([shift0.astype(F32)[:, None], u[:, :-1]], axis=1)
    xs = u + (u_prev - u) * mu
    r, k, v, xw, xa, xg = split_last(xs, RWKV_SIZES)
    log_w = -jax.nn.softplus(-(w0 + jnp.tanh(xw) @ w2)) - 0.5
    decay = jnp.exp(-jnp.exp(log_w))
    a = jax.nn.sigmoid(a0 + xa @ a2)
    g = jax.nn.sigmoid(xg) @ g2
    kk = l2norm(heads(k * k_k))
    k = k * (1 + (a - 1) * k_a)
    r, k, v, decay, a = (heads(t) for t in (r, k, v, decay, a))

    def step(S, inp):
        r_t, w_t, k_t, v_t, kk_t, a_t = inp
        Skk = jnp.einsum('bhvk,bhk->bhv', S, kk_t)
        S = (S * w_t[:, :, None, :] - Skk[..., None] * (kk_t * a_t)[:, :, None, :]
             + v_t[..., None] * k_t[:, :, None, :])
        return S, jnp.einsum('bhvk,bhk->bhv', S, r_t)

    seq = tuple(jnp.swapaxes(t, 0, 1) for t in (r, decay, k, v, kk, a))
    S, y = lax.scan(step, S0.astype(F32), seq)
    y = jnp.swapaxes(y, 0, 1)
    m = jnp.mean(y, -1, keepdims=True)
    var = jnp.mean(jnp.square(y - m), -1, keepdims=True)
    y = (y - m) * lax.rsqrt(var + RWKV_GN_EPS) * ln_w.reshape(NH, HEAD_DIM) + ln_b.reshape(NH, HEAD_DIM)
    y = y + jnp.sum(r * k * r_k.reshape(NH, HEAD_DIM), -1, keepdims=True) * v
    y = y.reshape(B, T, GROUP_WIDTH) * g
    return y.astype(odt), u[:, -1].astype(odt), S.astype(odt)


def gla_chunked(q, k, v, la, S0):
    C = math.gcd(q.shape[1], GLA_CHUNK)
    q, k, v, la = (to_chunks(t, C) for t in (q, k, v, la))
    b = jnp.cumsum(la, -2)
    last = b[..., -1:, :]
    q_in, k_in, k_out = q * jnp.exp(b), k * jnp.exp(-b), k * jnp.exp(last - b)
    causal = jnp.tril(jnp.ones((C, C), bool))
    A = jnp.where(causal, jnp.einsum('nbhid,nbhjd->nbhij', q_in, k_in), 0.0)
    o_intra = A @ v

    def step(S, inp):
        q_c, k_c, v_c, last_c, o_c = inp
        o = o_c + q_c @ S
        S = S * jnp.exp(last_c)[..., 0, :, None] + jnp.einsum('bhjd,bhjv->bhdv', k_c, v_c)
        return S, o

    S, o = lax.scan(step, S0, (q_in, k_out, v, last, o_intra))
    return from_chunks(o), S


def gla_mix(u, S0, gk_w2, gk_b, norm_w):
    odt = u.dtype
    u = u.astype(F32)
    B, T, _ = u.shape
    q, k, v, g, gl = split_last(u, GLA_SIZES)
    la = jax.nn.log_sigmoid(gl @ gk_w2 + gk_b) / GLA_GATE_NORM
    o, S = gla_chunked(heads(q) * HEAD_DIM ** -0.5, heads(k), heads(v), heads(la), S0.astype(F32))
    o = group_rmsnorm(o.reshape(B, T, GROUP_WIDTH), norm_w, NH) * jax.nn.silu(g)
    return o.astype(odt), S.astype(odt)


def gated_delta_chunked(q, k, v, g, beta, S0):
    C = math.gcd(q.shape[1], DN_CHUNK)
    q, k, v = (to_chunks(t, C) for t in (q, k, v))
    g, beta = to_chunks(g, C), to_chunks(beta, C)
    cum = jnp.cumsum(g, -1)
    L = seg_decay(cum)
    kb = k * beta[..., None]
    eye = jnp.eye(C, dtype=q.dtype)
    strict = jnp.tril(jnp.ones((C, C), bool), -1)
    M = jnp.where(strict, jnp.einsum('nbhid,nbhjd->nbhij', kb, k) * L, 0.0)
    Tm = lax.linalg.triangular_solve(eye + M, jnp.broadcast_to(eye, M.shape),
                                     left_side=True, lower=True, unit_diagonal=True)
    u = Tm @ (v * beta[..., None])
    w = Tm @ (kb * jnp.exp(cum)[..., None])
    A = jnp.einsum('nbhid,nbhjd->nbhij', q, k) * L

    def step(S, inp):
        q_c, k_c, u_c, w_c, A_c, cum_c = inp
        v_new = u_c - w_c @ S
        o = (q_c * jnp.exp(cum_c)[..., None]) @ S + A_c @ v_new
        last = cum_c[..., -1]
        S = (S * jnp.exp(last)[..., None, None]
             + jnp.einsum('bhjd,bhjv->bhdv', k_c * jnp.exp(last[..., None] - cum_c)[..., None], v_new))
        return S, o

    S, o = lax.scan(step, S0, (q, k, u, w, A, cum))
    return from_chunks(o), S


def deltanet_mix(u, conv0, S0, conv_w, A_log, dt_bias, norm_w):
    odt = u.dtype
    u = u.astype(F32)
    B, T, _ = u.shape
    qkv, z, a_in, b_in = split_last(u, DN_SIZES)
    qkv, conv1 = causal_conv(qkv, conv0.astype(F32), conv_w)
    q, k, v = split_last(jax.nn.silu(qkv), (GROUP_WIDTH,) * 3)
    q = l2norm(heads(q)) * HEAD_DIM ** -0.5
    k = l2norm(heads(k))
    beta = jax.nn.sigmoid(b_in)
    g = -jnp.exp(A_log) * jax.nn.softplus(a_in + dt_bias)
    o, S = gated_delta_chunked(q, k, heads(v), g, beta, S0.astype(F32))
    o = group_rmsnorm(o.reshape(B, T, GROUP_WIDTH), norm_w, NH) * jax.nn.silu(z)
    return o.astype(odt), conv1.astype(odt), S.astype(odt)


def ssd_chunked(xdt, la, Bh, Ch, S0):
    C = math.gcd(xdt.shape[1], SSM_CHUNK)
    xdt, Bh, Ch = (to_chunks(t, C) for t in (xdt, Bh, Ch))
    cum = jnp.cumsum(to_chunks(la, C), -1)
    L = seg_decay(cum)
    y_intra = (jnp.einsum('nbhis,nbhjs->nbhij', Ch, Bh) * L) @ xdt

    def step(S, inp):
        x_c, B_c, C_c, cum_c, y_c = inp
        last = cum_c[..., -1]
        y = y_c + jnp.exp(cum_c)[..., None] * jnp.einsum('bhis,bhps->bhip', C_c, S)
        S = (S * jnp.exp(last)[..., None, None]
             + jnp.einsum('bhjp,bhjs->bhps', x_c * jnp.exp(last[..., None] - cum_c)[..., None], B_c))
        return S, y

    S, y = lax.scan(step, S0, (xdt, Bh, Ch, cum, y_intra))
    return from_chunks(y), S


def ssd_mix(u, conv0, S0, conv_w, conv_b, dt_bias, A_log, D_skip, norm_w):
    odt = u.dtype
    u = u.astype(F32)
    B, T, _ = u.shape
    z, xbc, dt_raw = split_last(u, SSM_SIZES)
    xbc, conv1 = causal_conv(xbc, conv0.astype(F32), conv_w)
    xs, Bm, Cm = split_last(jax.nn.silu(xbc + conv_b),
                            (GROUP_WIDTH, SSM_GROUPS * SSM_STATE, SSM_GROUPS * SSM_STATE))
    rep = NH // SSM_GROUPS
    Bh = jnp.repeat(Bm.reshape(B, T, SSM_GROUPS, SSM_STATE), rep, axis=2)
    Ch = jnp.repeat(Cm.reshape(B, T, SSM_GROUPS, SSM_STATE), rep, axis=2)
    dt = jax.nn.softplus(dt_raw + dt_bias)
    la = dt * -jnp.exp(A_log)
    xh = heads(xs)
    y, S = ssd_chunked(xh * dt[..., None], la, Bh, Ch, S0.astype(F32))
    y = y + D_skip[:, None] * xh
    y = group_rmsnorm(y.reshape(B, T, GROUP_WIDTH) * jax.nn.silu(z), norm_w, SSM_GROUPS)
    return y.astype(odt), conv1.astype(odt), S.astype(odt)


def layer(x, c, states, p):
    shift0, wkv0, gla0, dnconv0, dn0, ssmconv0, ssm0 = states
    mod = jnp.einsum('bd,de->be', jax.nn.silu(c), p['ada_w']) + p['ada_b']
    sh1, sc1, gt1, sh2, sc2, gt2 = (m[:, None, :] for m in jnp.split(mod, 6, axis=-1))
    h = rmsnorm(x, p['norm1_w']) * (1 + sc1) + sh1
    u = jnp.einsum('btd,de->bte', h, p['w_in'])
    u_a, u_b, u_c, u_d = split_last(u, IN_SIZES)
    y_a, shift1, wkv1 = rwkv7_mix(u_a, shift0, wkv0, p['rwkv_mu'], p['rwkv_w0'], p['rwkv_w2'],
                                  p['rwkv_a0'], p['rwkv_a2'], p['rwkv_g2'], p['rwkv_k_k'],
                                  p['rwkv_k_a'], p['rwkv_r_k'], p['rwkv_ln_w'], p['rwkv_ln_b'])
    y_b, gla1 = gla_mix(u_b, gla0, p['gla_gk_w2'], p['gla_gk_b'], p['gla_norm_w'])
    y_c, dnconv1, dn1 = deltanet_mix(u_c, dnconv0, dn0, p['dn_conv_w'], p['dn_A_log'],
                                     p['dn_dt_bias'], p['dn_norm_w'])
    y_d, ssmconv1, ssm1 = ssd_mix(u_d, ssmconv0, ssm0, p['ssm_conv_w'], p['ssm_conv_b'],
                                  p['ssm_dt_bias'], p['ssm_A_log'], p['ssm_D'], p['ssm_norm_w'])
    y = jnp.concatenate([y_a, y_b, y_c, y_d], -1) @ p['w_out']
    x = x + gt1 * y
    h = rmsnorm(x, p['norm2_w']) * (1 + sc2) + sh2
    f = jnp.square(jax.nn.relu(h @ p['w_up'])) @ p['w_down']
    x = x + gt2 * f
    return x, (shift1, wkv1, gla1, dnconv1, dn1, ssmconv1, ssm1)


def trunk(x, c, states, P, final_norm_w):
    new = []
    for l in range(DEPTH):
        x, s = layer(x, c, tuple(st[l] for st in states), {n: w[l] for n, w in P.items()})
        new.append(s)
    y = rmsnorm(x, final_norm_w)
    return y, tuple(jnp.stack([s[i] for s in new]) for i in range(len(states)))


def setup_inputs(seed: int = 0) -> dict:
    key = jax.random.key(seed)
    ks = iter(jax.random.split(key, 64))
    def nrm(shape, s=1.0):
        return s * jax.random.normal(next(ks), shape, F32)
    def unif(shape, lo, hi):
        return jax.random.uniform(next(ks), shape, F32, lo, hi)
    L, GW = DEPTH, GROUP_WIDTH
    dn_dt = jnp.exp(unif((L, NH), math.log(1e-3), math.log(1e-1)))
    ssm_dt = jnp.exp(unif((L, NH), math.log(1e-3), math.log(1e-1)))
    inp = {}
    inp['x_prompt'] = nrm((BATCH, SEQ, D_MODEL))
    inp['x_sample'] = nrm((DEC_BATCH, DEC_SEQ, D_MODEL))
    inp['state_rwkv_shift'] = nrm((L, DEC_BATCH, RWKV_COLS))
    inp['state_rwkv_wkv'] = nrm((L, DEC_BATCH, NH, HEAD_DIM, HEAD_DIM), 0.1)
    inp['state_gla'] = nrm((L, DEC_BATCH, NH, HEAD_DIM, HEAD_DIM), 0.1)
    inp['state_dn_conv'] = nrm((L, DEC_BATCH, DN_CONV - 1, 3 * GW))
    inp['state_dn'] = nrm((L, DEC_BATCH, NH, HEAD_DIM, HEAD_DIM), 0.1)
    inp['state_ssm_conv'] = nrm((L, DEC_BATCH, SSM_CONV - 1, SSM_XBC))
    inp['state_ssm'] = nrm((L, DEC_BATCH, NH, HEAD_DIM, SSM_STATE), 0.1)
    inp['c_prompt'] = nrm((BATCH, D_MODEL))
    inp['c_sample'] = nrm((DEC_BATCH, D_MODEL))
    inp['ada_w'] = nrm((L, D_MODEL, 6 * D_MODEL), 0.3 * D_MODEL ** -0.5)
    inp['ada_b'] = nrm((L, 6 * D_MODEL), 0.02)
    inp['norm1_w'] = 1.0 + nrm((L, D_MODEL), 0.02)
    inp['norm2_w'] = 1.0 + nrm((L, D_MODEL), 0.02)
    inp['w_in'] = nrm((L, D_MODEL, P_TOTAL), D_MODEL ** -0.5)
    inp['w_out'] = nrm((L, MIX_WIDTH, D_MODEL), MIX_WIDTH ** -0.5)
    inp['w_up'] = nrm((L, D_MODEL, D_FF), D_MODEL ** -0.5)
    inp['w_down'] = nrm((L, D_FF, D_MODEL), D_FF ** -0.5)
    inp['rwkv_mu'] = unif((L, RWKV_COLS), 0.0, 1.0)
    inp['rwkv_w0'] = nrm((L, GW), 0.5)
    inp['rwkv_w2'] = nrm((L, RWKV_R_W, GW), 0.5 * RWKV_R_W ** -0.5)
    inp['rwkv_a0'] = nrm((L, GW), 0.1)
    inp['rwkv_a2'] = nrm((L, RWKV_R_A, GW), RWKV_R_A ** -0.5)
    inp['rwkv_g2'] = nrm((L, RWKV_R_G, GW), RWKV_R_G ** -0.5)
    inp['rwkv_k_k'] = 0.85 + nrm((L, GW), 0.05)
    inp['rwkv_k_a'] = 1.0 + nrm((L, GW), 0.05)
    inp['rwkv_r_k'] = nrm((L, GW), 0.1)
    inp['rwkv_ln_w'] = 1.0 + nrm((L, GW), 0.02)
    inp['rwkv_ln_b'] = nrm((L, GW), 0.01)
    inp['gla_gk_w2'] = nrm((L, GLA_R, GW), GLA_R ** -0.5)
    inp['gla_gk_b'] = nrm((L, GW), 0.5)
    inp['gla_norm_w'] = 1.0 + nrm((L, GW), 0.02)
    inp['dn_conv_w'] = nrm((L, DN_CONV, 3 * GW), 0.5)
    inp['dn_A_log'] = jnp.log(unif((L, NH), 1.0, 16.0))
    inp['dn_dt_bias'] = dn_dt + jnp.log(-jnp.expm1(-dn_dt))
    inp['dn_norm_w'] = 1.0 + nrm((L, GW), 0.02)
    inp['ssm_conv_w'] = nrm((L, SSM_CONV, SSM_XBC), 0.5)
    inp['ssm_conv_b'] = nrm((L, SSM_XBC), 0.01)
    inp['ssm_dt_bias'] = ssm_dt + jnp.log(-jnp.expm1(-ssm_dt))
    inp['ssm_A_log'] = jnp.log(unif((L, NH), 1.0, 16.0))
    inp['ssm_D'] = 1.0 + nrm((L, NH), 0.1)
    inp['ssm_norm_w'] = 1.0 + nrm((L, GW), 0.02)
    inp['final_norm_w'] = 1.0 + nrm((D_MODEL,), 0.02)
    return inp


def reference(x_prompt, x_sample, state_rwkv_shift, state_rwkv_wkv, state_gla, state_dn_conv,
              state_dn, state_ssm_conv, state_ssm, c_prompt, c_sample,
              ada_w, ada_b, norm1_w, norm2_w, w_in, w_out, w_up, w_down,
              rwkv_mu, rwkv_w0, rwkv_w2, rwkv_a0, rwkv_a2, rwkv_g2, rwkv_k_k, rwkv_k_a, rwkv_r_k,
              rwkv_ln_w, rwkv_ln_b, gla_gk_w2, gla_gk_b, gla_norm_w,
              dn_conv_w, dn_A_log, dn_dt_bias, dn_norm_w,
              ssm_conv_w, ssm_conv_b, ssm_dt_bias, ssm_A_log, ssm_D, ssm_norm_w, final_norm_w):
    P = dict(ada_w=ada_w, ada_b=ada_b, norm1_w=norm1_w, norm2_w=norm2_w, w_in=w_in, w_out=w_out,
             w_up=w_up, w_down=w_down, rwkv_mu=rwkv_mu, rwkv_w0=rwkv_w0, rwkv_w2=rwkv_w2,
             rwkv_a0=rwkv_a0, rwkv_a2=rwkv_a2, rwkv_g2=rwkv_g2, rwkv_k_k=rwkv_k_k, rwkv_k_a=rwkv_k_a,
             rwkv_r_k=rwkv_r_k, rwkv_ln_w=rwkv_ln_w, rwkv_ln_b=rwkv_ln_b, gla_gk_w2=gla_gk_w2,
             gla_gk_b=gla_gk_b, gla_norm_w=gla_norm_w, dn_conv_w=dn_conv_w, dn_A_log=dn_A_log,
             dn_dt_bias=dn_dt_bias, dn_norm_w=dn_norm_w, ssm_conv_w=ssm_conv_w, ssm_conv_b=ssm_conv_b,
             ssm_dt_bias=ssm_dt_bias, ssm_A_log=ssm_A_log, ssm_D=ssm_D, ssm_norm_w=ssm_norm_w)
    sample_states = (state_rwkv_shift, state_rwkv_wkv, state_gla, state_dn_conv,
                     state_dn, state_ssm_conv, state_ssm)
    prompt_states = tuple(jnp.zeros((DEPTH, x_prompt.shape[0]) + s.shape[2:], x_prompt.dtype)
                          for s in sample_states)
    y_prompt, ps = trunk(x_prompt, c_prompt, prompt_states, P, final_norm_w)
    y_sample, ss = trunk(x_sample, c_sample, sample_states, P, final_norm_w)
    p_shift, p_wkv, p_gla, p_dn_conv, p_dn, p_ssm_conv, p_ssm = ps
    s_shift, s_wkv, s_gla, s_dn_conv, s_dn, s_ssm_conv, s_ssm = ss
    return (y_prompt, y_sample, p_shift, p_wkv, p_gla, p_dn_conv, p_dn, p_ssm_conv, p_ssm,
            s_shift, s_wkv, s_gla, s_dn_conv, s_dn, s_ssm_conv, s_ssm)
```

```python
import numpy as np
import concourse.bass as bass
import concourse.mybir as mybir
from concourse.bass_utils import run_bass_kernel_spmd

F32 = mybir.dt.float32
BF16 = mybir.dt.bfloat16
AF = mybir.ActivationFunctionType
ALU = mybir.AluOpType
AX = mybir.AxisListType


class V:
    __slots__ = ("tile", "ap", "sub")

    def __init__(self, tile, ap, sub):
        self.tile, self.ap, self.sub = tile, ap, sub

    def __getitem__(self, idx):
        return V(self.tile, self.ap[idx], self.sub)

    def f(self, fn):
        return V(self.tile, fn(self.ap), self.sub)


class Tile:
    def __init__(self, h, name):
        self.h, self.name = h, name
        self.ww = {}
        self.wr = {}
        self.subs = {}

    def __getitem__(self, idx):
        return V(self, self.h[idx], None)

    def s(self, k, idx=None):
        ap = self.h[idx] if idx is not None else self.h[:]
        return V(self, ap, k)


def _mx(d, s, v):
    if d.get(s, 0) < v:
        d[s] = v


class Sched:
    ENG = ("pe", "act", "dve", "pool", "sp")
    NDMA = 12

    def __init__(self, nc, stack):
        self.nc = nc
        self.ops = {e: [] for e in self.ENG}
        self.cnt = {e: 0 for e in self.ENG}
        self.waited = {e: {} for e in self.ENG}
        self.sem = {}
        for e in ("pe", "act", "dve", "pool"):
            self.sem[e] = stack.enter_context(nc.semaphore("s_" + e))
        self.dq = {}
        for q in ("sp", "pool", "act"):
            n = self.NDMA if q == "sp" else 6
            sems = [stack.enter_context(nc.semaphore("d_%s%d" % (q, j))) for j in range(n)]
            for j, s_ in enumerate(sems):
                self.sem[("d", q, j)] = s_
            self.dq[q] = [n, 0, [0] * n]
        self.out_tokens = {}
        self.stack = stack
        self.ntile = 0

    def sb(self, shape, name=None, dt=F32):
        self.ntile += 1
        name = name or ("t%d" % self.ntile)
        h = self.stack.enter_context(self.nc.sbuf_tensor(name, list(shape), dt))
        return Tile(h, name)

    def ps(self, shape, name=None, dt=F32):
        self.ntile += 1
        name = name or ("p%d" % self.ntile)
        h = self.stack.enter_context(self.nc.psum_tensor(name, list(shape), dt))
        return Tile(h, name)

    def _deps(self, reads, writes):
        tok = {}
        for v in reads:
            t = v.tile
            for s, x in t.ww.items():
                _mx(tok, s, x)
            if v.sub is None:
                for k, (w, r) in t.subs.items():
                    for s, x in w.items():
                        _mx(tok, s, x)
            elif v.sub in t.subs:
                for s, x in t.subs[v.sub][0].items():
                    _mx(tok, s, x)
        for v in writes:
            t = v.tile
            for d in (t.ww, t.wr):
                for s, x in d.items():
                    _mx(tok, s, x)
            if v.sub is None:
                for k, (w, r) in t.subs.items():
                    for d in (w, r):
                        for s, x in d.items():
                            _mx(tok, s, x)
            elif v.sub in t.subs:
                for d in t.subs[v.sub]:
                    for s, x in d.items():
                        _mx(tok, s, x)
        return tok

    def _commit(self, reads, writes, s, x):
        for v in reads:
            t = v.tile
            if v.sub is None:
                _mx(t.wr, s, x)
            else:
                if v.sub not in t.subs:
                    t.subs[v.sub] = [{}, {}]
                _mx(t.subs[v.sub][1], s, x)
        for v in writes:
            t = v.tile
            if v.sub is None:
                t.ww = {s: x}
                t.wr = {}
                t.subs = {}
            else:
                t.subs[v.sub] = [{s: x}, {}]

    def _add(self, eng, fn, reads, writes, tokens_extra=None, dma_q=None, is_out=False):
        tok = self._deps(reads, writes)
        if tokens_extra:
            for s, x in tokens_extra.items():
                _mx(tok, s, x)
        if dma_q is not None:
            n, nxt, cnts = self.dq[dma_q]
            j = nxt
            self.dq[dma_q][1] = (nxt + 1) % n
            if cnts[j] > 0:
                _mx(tok, ("d", dma_q, j), 16 * cnts[j])
            cnts[j] += 1
            mysem, myval, inc = ("d", dma_q, j), 16 * cnts[j], 16
        else:
            self.cnt[eng] += 1
            mysem, myval, inc = eng, self.cnt[eng], 1
        waits = []
        wd = self.waited[eng]
        for s, x in tok.items():
            if eng == "pe" and s == "pe":
                continue
            if wd.get(s, 0) >= x:
                continue
            wd[s] = x
            waits.append((s, x))
        self.ops[eng].append((fn, waits, mysem, inc))
        self._commit(reads, writes, mysem, myval)
        if is_out:
            _mx(self.out_tokens, mysem, myval)

    def I(self, eng, meth, *args, **kw):
        reads, writes = [], []
        a2 = []
        for a in args:
            if isinstance(a, V):
                reads.append(a)
                a2.append(a.ap)
            else:
                a2.append(a)
        k2 = {}
        rw = kw.pop("_rw", False)
        for k, a in kw.items():
            if isinstance(a, V):
                if k in ("out", "accum_out"):
                    writes.append(a)
                    if rw:
                        reads.append(a)
                else:
                    reads.append(a)
                k2[k] = a.ap
            else:
                k2[k] = a
        fn = lambda e: getattr(e, meth)(*a2, **k2)
        self._add(eng, fn, reads, writes)

    def dma(self, q, out, in_, is_out=False, xr=(), xw=(), **kw):
        reads, writes = list(xr), list(xw)
        o = out.ap if isinstance(out, V) else out
        i = in_.ap if isinstance(in_, V) else in_
        if isinstance(out, V):
            writes.append(out)
        if isinstance(in_, V):
            reads.append(in_)
        fn = lambda e: e.dma_start(out=o, in_=i, **kw)
        self._add(q, fn, reads, writes, dma_q=q, is_out=is_out)

    def emit(self):
        nc = self.nc
        fin = [(s, x) for s, x in self.out_tokens.items()]
        eobj = {"pe": "tensor", "act": "scalar", "dve": "vector", "pool": "gpsimd", "sp": "sync"}
        with nc.Block() as block:
            for ename in self.ENG:
                ops = self.ops[ename]
                extra = fin if ename == "sp" else []

                def body(e, ops=ops, extra=extra):
                    for fn, waits, mysem, inc in ops:
                        for s, x in waits:
                            e.wait_ge(self.sem[s], x)
                        fn(e).then_inc(self.sem[mysem], inc)
                    for s, x in extra:
                        e.wait_ge(self.sem[s], x)
                getattr(block, eobj[ename])(body)

NL = 2
NTOK = 2176
NBLK = 17
C_DEC = 0.6065306597126334
CNAMES = ["IDENT", "ONES", "BLK", "TRS_P", "TRI_P", "SU_P", "NEGM_P", "TRS_S", "TRI_S", "SU_S", "NEGM_S"]
CI = {n: i * 128 for i, n in enumerate(CNAMES)}
C_SEG = 11 * 128
C_EPS = C_SEG + 16
C_GNEPS = C_EPS + 1
C_ONE = C_EPS + 2
NCONST = C_EPS + 4
NPV = 160
PVO = dict(ada_b=0, n1w=48, n2w=56, mu=64, a0=71, k_k=73, k_a=75, r_k=77, ln_w=79, ln_b=81, gla_nw=83,
           dn_cw=85, dn_nw=109, ss_cw=111, ss_cb=135, ss_nw=141, ss_D=143, fnw=145)
NR = 528
RO = dict(w0=0, gkb=256, dnA=512, dndt=516, ssdt=520, ssA=524)
ST_SHAPES = dict(shift=[2, 16, 896], wkv=[2, 16, 4, 64, 64], gla=[2, 16, 4, 64, 64], dnc=[2, 16, 3, 768],
                 dn=[2, 16, 4, 64, 64], ssc=[2, 16, 3, 768], ssm=[2, 16, 4, 64, 128])


def make_consts():
    c = np.zeros((128, NCONST), np.float32)
    s = np.arange(128)[:, None]
    i = np.arange(128)[None, :]
    same = (s // 8) == (i // 8)
    m = {}
    m["IDENT"] = (s == i)
    m["ONES"] = np.ones((128, 128))
    m["BLK"] = (s // 64) == (i // 64)
    m["TRI_P"] = s <= i
    m["TRS_P"] = s < i
    m["SU_P"] = s > i
    m["NEGM_P"] = (m["TRI_P"].astype(np.float32) - 1.0) * 30000.0
    m["TRI_S"] = (s <= i) & same
    m["TRS_S"] = (s < i) & same
    m["SU_S"] = (s > i) & same
    m["NEGM_S"] = (m["TRI_S"].astype(np.float32) - 1.0) * 30000.0
    for n in CNAMES:
        c[:, CI[n]:CI[n] + 128] = m[n].astype(np.float32)
    c[:, C_SEG:C_SEG + 16] = ((np.arange(128)[:, None] // 8) == np.arange(16)[None, :]).astype(np.float32)
    c[:, C_EPS] = 1e-6
    c[:, C_GNEPS] = 64e-5
    c[:, C_ONE] = 1.0
    return c


class BK:
    pass


def build():
    from contextlib import ExitStack
    nc = bass.Bass("TRN2", target_bir_lowering=False)

    def din(name, shape):
        return nc.dram_tensor(name, list(shape), F32, kind="ExternalInput").ap()

    def dout(name, shape):
        return nc.dram_tensor(name, list(shape), F32, kind="ExternalOutput").ap()

    xin = din("xin", [NTOK, 1024])
    ccd = din("cc", [17, 1024])
    std = {k: din("st_" + k, v) for k, v in ST_SHAPES.items()}
    wall = din("wall", [2, 27, 128, 8, 512])
    adad = din("ada", [2, 12, 128, 8, 512])
    pvd = din("pv", [128, 2 * NPV])
    rowsd = din("rows", [128, 2 * NR])
    smd = din("sm", [128, 2 * 4 * 256])
    constd = din("consts", [128, NCONST])
    yout = dout("y", [NTOK, 1024])
    nc.allow_low_precision("bf16 operands (fp32 PSUM accumulation) for the dense projections")
    wbf = nc.dram_tensor("wbf", [2, 27, 128, 8, 512], BF16).ap()
    pod = {k: dout("p_" + k, [v[0]] + v[2:]) for k, v in ST_SHAPES.items()}
    sod = {k: dout("s_" + k, v) for k, v in ST_SHAPES.items()}
    import os as _os
    KDBG = int(_os.environ.get("KDBG", "0"))
    dbgd = dout("dbg", [12, 128, 1024]) if KDBG else None

    with ExitStack() as es:
        S = Sched(nc, es)
        I = S.I
        CONST = S.sb([128, NCONST], "CONST")
        PV = S.sb([128, 2 * NPV], "PV")
        ROWS = S.sb([128, 2 * NR], "ROWS")
        SM = S.sb([128, 2 * 1024], "SM")
        S.dma("pool", CONST[:], constd)
        S.dma("pool", PV[:], pvd)
        S.dma("pool", ROWS[:], rowsd)
        S.dma("pool", SM[:], smd)

        def K(n):
            return CONST[:, CI[n]:CI[n] + 128]
        IDENT, ONES, BLK = K("IDENT"), K("ONES"), K("BLK")
        EPSc = CONST[:, C_EPS:C_EPS + 1]
        GNEPSc = CONST[:, C_GNEPS:C_GNEPS + 1]
        ONEc = CONST[:, C_ONE:C_ONE + 1]

        def pv(l, name, n=1, off=0):
            o = l * NPV + PVO[name] + off
            return PV[:, o:o + n]

        def row(l, name, n):
            o = l * NR + RO[name]
            return ROWS[:, o:o + n]

        def smat(l, j):
            o = l * 1024 + j * 256
            return SM[:, o:o + 256]

        PB = [S.ps([128, 512], "pb%d" % i) for i in range(8)]
        pbi = [0]

        def pb():
            t = PB[pbi[0] % 8]
            pbi[0] += 1
            return t

        WR = [S.sb([128, 4096], "wr%d" % i) for i in range(2)]
        wri = [0]
        WBF = Tile(None, "wbf_dep")

        def wr32(i):
            return V(WR[i], WR[i].h[:].rearrange("p (k c) -> p k c", c=512), None)

        def wr16(i, hf):
            return V(WR[i], WR[i].h[:].bitcast(BF16)[:, hf * 4096:(hf + 1) * 4096].rearrange("p (k c) -> p k c", c=512), hf)

        def wload(ap):
            i = wri[0] % 2
            wri[0] += 1
            t = wr32(i)
            S.dma("sp", t, ap)
            return t

        def wload16(l, sl):
            r = wri[0] % 4
            wri[0] += 1
            t = wr16(r // 2, r % 2)
            S.dma("sp", t, wbf[l, sl], xr=[V(WBF, None, (l, sl))])
            return t

        def mm(out, lhsT, rhs, start=True, stop=True):
            I("pe", "matmul", out=out, lhsT=lhsT, rhs=rhs, start=start, stop=stop, skip_group_check=True)

        def tr(out, in_, npart=128):
            I("pe", "transpose", out=out, in_=in_, identity=CONST[0:npart, 0:npart])

        def act(out, in_, func, bias=None, scale=1.0):
            if bias is None:
                I("act", "activation", out=out, in_=in_, func=func, scale=scale)
            else:
                I("act", "activation", out=out, in_=in_, func=func, bias=bias, scale=scale)

        def tt(out, a, b, op=ALU.mult, eng="dve"):
            I(eng, "tensor_tensor", out=out, in0=a, in1=b, op=op)

        def ts(out, a, s1, op0, s2=None, op1=None, eng="dve"):
            if op1 is None:
                I(eng, "tensor_scalar", out=out, in0=a, scalar1=s1, scalar2=None, op0=op0)
            else:
                I(eng, "tensor_scalar", out=out, in0=a, scalar1=s1, scalar2=s2, op0=op0, op1=op1)

        def stt(out, a, sc, b, op0, op1, eng="dve"):
            I(eng, "scalar_tensor_tensor", out=out, in0=a, scalar=sc, in1=b, op0=op0, op1=op1)

        def cp(out, in_, eng="dve"):
            if eng == "act":
                I("act", "activation", out=out, in_=in_, func=AF.Identity, scale=1.0)
            else:
                I("dve", "tensor_copy", out=out, in_=in_)

        G = [S.sb([128, 256], "g%d" % i) for i in range(28)]

        class GP:
            def __init__(self, pool=None):
                self.i = 0
                self.pool = G if pool is None else pool

            def t(self):
                t = self.pool[self.i]
                self.i += 1
                return t

        class Alias:
            def __init__(self, tile, base, sub):
                self.tile, self.base, self.sub = tile, base, sub

            def __getitem__(self, idx):
                return V(self.tile, self.base[idx], self.sub)

        MODT = S.sb([128, 2, 48, 17], "MODT")
        SC = S.sb([128, 2, 2, 8, 17], "SC")
        CT = S.sb([128, 8, 17], "CT")
        class Cx:
            pass
        CXS = []
        for i_ in range(2):
            c_ = Cx()
            c_.XTM = S.sb([128, 1024], "XTM%d" % i_)
            c_.XT = S.sb([128, 8, 128], "XT%d" % i_)
            c_.HT = S.sb([128, 8, 128], "HT%d" % i_, BF16)
            c_.YT = S.sb([128, 8, 128], "YT%d" % i_, BF16)
            CXS.append(c_)
        CUR = [CXS[0]]

        class Proxy:
            def __init__(self, name):
                self.name = name

            def __getitem__(self, idx):
                return getattr(CUR[0], self.name)[idx]
        XTM, XT, HT, YT = Proxy("XTM"), Proxy("XT"), Proxy("HT"), Proxy("YT")
        HTF = S.sb([128, 8, 128], "HTF")
        HID = S.sb([128, 32, 128], "HID", BF16)
        TMP5 = S.sb([128, 4, 128], "TMP5")
        UA = S.sb([128, 7, 144], "UA")
        XS = S.sb([128, 7, 128], "XS")
        XB = S.sb([128, 6, 176], "XB")
        CV = S.sb([128, 6, 128], "CV")
        PST = {}
        for l in range(NL):
            for mname in ("wkv", "gla", "dn", "ssm"):
                PST[(l, mname)] = S.sb([128, 2, 128], "pst_%s%d" % (mname, l))
                I("dve", "memset", PST[(l, mname)][:], 0.0)
        CARRY = {}
        for l in range(NL):
            CARRY[(l, "rw")] = S.sb([128, 7], "c_rw%d" % l)
            CARRY[(l, "dn")] = S.sb([128, 6, 3], "c_dn%d" % l)
            CARRY[(l, "ss")] = S.sb([128, 6, 3], "c_ss%d" % l)
            for k in ("rw", "dn", "ss"):
                I("dve", "memset", CARRY[(l, k)][:], 0.0)
        SST = [S.sb([128, 16, 2, 128], "sst%d" % i) for i in range(2)]
        for t in SST:
            I("dve", "memset", t[:], 0.0)
        G2 = [Alias(SST[j], SST[j].h[:, s_].rearrange("p r v -> p (r v)"), ("a", s_)) for j in range(2) for s_ in range(16)]
        PADS = [S.sb([128, 2, 128], "pad%d" % i) for i in range(3)]
        for t in PADS:
            I("dve", "memset", t[:], 0.0)

        Pk, Sk = BK(), BK()
        Pk.nseg, Pk.L, Pk.sfx, Pk.nsolve = 1, 128, "_P", 7
        Sk.nseg, Sk.L, Sk.sfx, Sk.nsolve = 16, 8, "_S", 3
        for bk in (Pk, Sk):
            bk.TRI, bk.TRS, bk.SU, bk.NEGM = K("TRI" + bk.sfx), K("TRS" + bk.sfx), K("SU" + bk.sfx), K("NEGM" + bk.sfx)

        def bc(v, shape, axis):
            return v.f(lambda a: a.unsqueeze(axis).to_broadcast(shape))

        def modv(v3, bk, nch=8):
            if bk.nseg == 1:
                return bc(v3[:, :, 0:1], [128, nch, 1, 128], 3)
            return bc(v3[:, :, 1:17], [128, nch, 16, 8], 3)

        def dv(v, bk):
            return v.f(lambda a: a.rearrange("p c (s t) -> p c s t", t=bk.L))

        ctm = XTM
        S.dma("pool", ctm[0:17, :], ccd)
        act(ctm[0:17, :], ctm[0:17, :], AF.Silu)
        for kc in range(8):
            p = pb()
            tr(p[:, 0:17], ctm[0:17, kc * 128:(kc + 1) * 128], 17)
            cp(CT[:, kc, :], p[:, 0:17])
        for l in range(NL):
            for s in range(12):
                w = wload(adad[l, s])
                p = pb()
                for j in range(4):
                    for kc in range(8):
                        mm(p[:, j * 17:(j + 1) * 17], w[:, kc, j * 128:(j + 1) * 128], CT[:, kc, :], kc == 0, kc == 7)
                tt(MODT[:, l, 4 * s:4 * s + 4, :], p[:, 0:68].f(lambda a: a.rearrange("p (j n) -> p j n", n=17)),
                   bc(pv(l, "ada_b", 4, 4 * s), [128, 4, 17], 2), ALU.add)
            stt(SC[:, l, 0], MODT[:, l, 8:16, :], 1.0, bc(pv(l, "n1w", 8), [128, 8, 17], 2), ALU.add, ALU.mult)
            stt(SC[:, l, 1], MODT[:, l, 32:40, :], 1.0, bc(pv(l, "n2w", 8), [128, 8, 17], 2), ALU.add, ALU.mult)

        STG = [wr32(0)] + [V(SST[j], SST[j].h[:].rearrange("p s r v -> p (s r v)").rearrange("p (k c) -> p k c", c=512), None)
                           for j in range(2)]
        for l in range(NL):
            for sl in range(27):
                i = (l * 27 + sl)
                src = STG[i % 3]
                S.dma("sp", src, wall[l, sl])
                dst = wr16(1, i % 2)
                I("dve", "tensor_copy", out=dst[:, 0:4, :], in_=src[:, 0:4, :])
                I("act", "activation", out=dst[:, 4:8, :], in_=src[:, 4:8, :], func=AF.Identity, scale=1.0)
                S.dma("sp", wbf[l, sl], dst, xw=[V(WBF, None, (l, sl))])

        RS = S.sb([128, 128], "RS")

        def norm_mod(bk, scale_bv, shift_bv, outT=None):
            outT = HT if outT is None else outT
            HIDF = V(HID, HID.h[:].rearrange("p c t -> p (c t)").bitcast(F32), None)
            p = pb()
            for kc in range(8):
                sq = HIDF[:, kc * 128:(kc + 1) * 128]
                act(sq, XT[:, kc, :], AF.Square)
                mm(p[:, 0:128], ONES, sq, kc == 0, kc == 7)
            act(RS[:], p[:, 0:128], AF.Sqrt, bias=EPSc, scale=1.0 / 1024.0)
            I("dve", "reciprocal", out=RS[:], in_=RS[:])
            tt(HTF[:], XT[:], bc(RS[:], [128, 8, 128], 1))
            if shift_bv is not None:
                tt(dv(HTF[:], bk), dv(HTF[:], bk), scale_bv)
                tt(dv(outT[:], bk), dv(HTF[:], bk), shift_bv, ALU.add)
            else:
                tt(dv(outT[:], bk), dv(HTF[:], bk), scale_bv)

        def proj_fm(out, w, c0, M):
            for kc in range(8):
                mm(out, w[:, kc, c0:c0 + M], HT[:, kc, :], kc == 0, kc == 7)

        def proj_tm(out, w, c0, N):
            for kc in range(8):
                mm(out, HT[:, kc, :], w[:, kc, c0:c0 + N], kc == 0, kc == 7)

        def sv(v, bk):
            return v.f(lambda a: a.rearrange("p (s t) -> p s t", t=bk.L))

        def lastcol(v, bk, s):
            c = s * bk.L + bk.L - 1
            return v[:, c:c + 1]

        def tri_solve(bk, NTs, Ns, X, g):
            tmpN = [[g.t(), g.t()] for _ in range(2)]
            for k in range(bk.nsolve):
                p = pb()
                for h in range(2):
                    mm(p[:, h * 64:h * 64 + 64], NTs[h][:, 0:128], X[:, h * 64:h * 64 + 64])
                tt(X[:, 0:128], X[:, 0:128], p[:, 0:128], ALU.add)
                yield
                if k == bk.nsolve - 1:
                    break
                for h in range(2):
                    p2 = pb()
                    mm(p2[:, 0:128], Ns[h][:, 0:128], NTs[h][:, 0:128])
                    if k < bk.nsolve - 2:
                        mm(p2[:, 128:256], NTs[h][:, 0:128], Ns[h][:, 0:128])
                    nn = tmpN[h][k % 2]
                    if k < bk.nsolve - 2:
                        cp(nn[:, 0:256], p2[:, 0:256], "act")
                    else:
                        cp(nn[:, 0:128], p2[:, 0:128], "act")
                    NTs[h] = nn
                    Ns[h] = _Shift(nn)
                    yield
            return

        class _Shift:
            def __init__(self, base):
                self.base = base

            def __getitem__(self, idx):
                assert idx == (slice(None), slice(0, 128))
                return self.base[:, 128:256]

        def post_rms(Y, ones, gsize, nw_col, gateT, out, g):
            sq = g.t()
            act(sq[:, 0:128], Y, AF.Square)
            p = pb()
            mm(p[:, 0:128], ones, sq[:, 0:128])
            r = g.t()
            act(r[:, 0:128], p[:, 0:128], AF.Sqrt, bias=EPSc, scale=1.0 / gsize)
            I("dve", "reciprocal", out=r[:, 0:128], in_=r[:, 0:128])
            stt(r[:, 128:256], Y, nw_col, r[:, 0:128], ALU.mult, ALU.mult)
            if gateT is not None:
                tt(out, r[:, 128:256], gateT)
            else:
                cp(out, r[:, 128:256])

        def load_bd(sst, src):
            for h in range(4):
                b = 64 * (h % 2)
                S.dma("pool", sst[b:b + 64, :, h // 2, b:b + 64], src[:, h].rearrange("s d v -> d s v"))

        def store_bd(dst, tile_bd, pr):
            for hh in range(2):
                b = 64 * hh
                S.dma("pool", dst[2 * pr + hh], tile_bd[b:b + 64, b:b + 64], is_out=True)

        OST = [S.sb([128, 128], "ost%d" % i) for i in range(4)]
        osti = [0]

        def ost():
            t = OST[osti[0] % 4]
            osti[0] += 1
            return t

        def state_update(bk, l, mname, pr, lhs, rhs, pcv, sst, outd, last, g, transpose_out=False):
            for s in range(bk.nseg):
                p = pb()
                for k in range(len(lhs)):
                    lv = lhs[k]
                    if bk.nseg > 1:
                        m = g_rot()
                        I("act", "mul", out=m[:, 0:128], in_=lv, mul=CONST[:, C_SEG + s:C_SEG + s + 1])
                        lv = m[:, 0:128]
                    mm(p[:, 0:128], lv, rhs[k], k == 0, k == len(lhs) - 1)
                tmp = g_rot()
                tt(tmp[:, 0:128], p[:, 0:128], BLK)
                if bk.nseg == 1:
                    hp = PST[(l, mname)][:, pr, :]
                    stt(hp, hp, lastcol(pcv, bk, 0), tmp[:, 0:128], ALU.mult, ALU.add)
                    if last:
                        if transpose_out:
                            p2 = pb()
                            tr(p2[:, 0:128], hp)
                            o = ost()
                            cp(o[:], p2[:, 0:128])
                            store_bd(outd[l], o, pr)
                        else:
                            store_bd(outd[l], PST[(l, mname)][:, pr, :], pr)
                else:
                    hp = sst[:, s, pr, :]
                    o = ost()
                    stt(o[:], hp, lastcol(pcv, bk, s), tmp[:, 0:128], ALU.mult, ALU.add)
                    if transpose_out:
                        p2 = pb()
                        tr(p2[:, 0:128], o[:])
                        o2 = ost()
                        cp(o2[:], p2[:, 0:128])
                        o = o2
                    store_bd(outd[l, s], o, pr)

        GR = [S.sb([128, 128], "gr%d" % i) for i in range(6)]
        gri = [0]

        def g_rot():
            t = GR[gri[0] % 6]
            gri[0] += 1
            return t

        def inter(bk, l, mname, pr, sst, out_ps, opT, stop):
            for s in range(bk.nseg):
                hp = PST[(l, mname)][:, pr, :] if bk.nseg == 1 else sst[:, s, pr, :]
                mm(out_ps[:, s * bk.L:(s + 1) * bk.L], hp, opT[:, s * bk.L:(s + 1) * bk.L], s == 0, stop)

        def decay_mask(bk, la_col, out, g):
            t = g.t()
            ts(t[:, 0:128], bk.SU, la_col, ALU.mult)
            p = pb()
            mm(p[:, 0:128], t[:, 0:128], bk.TRI, True, False)
            mm(p[:, 0:128], IDENT, bk.NEGM, False, True)
            act(out, p[:, 0:128], AF.Exp)

        def bc2(v, shape):
            return v.f(lambda a: a.unsqueeze(2).unsqueeze(3).to_broadcast(shape))

        GLT = S.sb([16, 128], "GLT")
        SHT = S.sb([48, 896], "SHT")
        CST = SHT
        CSO = S.sb([48, 768], "CSO")
        CS3 = S.sb([128, 48], "CS3")
        SMT = S.sb([128, 64], "SMT")
        ZT = S.sb([128, 512], "ZT")
        TMP4 = S.sb([128, 4, 128], "TMP4")
        YTM = XTM

        def dump(k, tile3):
            if not KDBG:
                return
            for b2 in range(2):
                p = pb()
                for j in range(4):
                    tr(p[:, j * 128:(j + 1) * 128], tile3[:, 4 * b2 + j, :])
                cp(TMP4[:].f(lambda a: a.rearrange("p c t -> p (c t)")), p[:, 0:512])
                S.dma("pool", dbgd[k, :, b2 * 512:(b2 + 1) * 512], TMP4[:].f(lambda a: a.rearrange("p c t -> p (c t)")), is_out=True)
        SSL = [S.sb([64, 4, 128], "ssl%d" % i) for i in range(1)]

        def rwkv(bk, l, last, wget, sst, pool=None):
            g = GP(pool)
            w0 = wget(0)
            w1 = wget(1)
            nseg, L = bk.nseg, bk.L
            W = L + 1
            sfx = bk.sfx
            UAv = UA[:, :, 0:nseg * W].f(lambda a: a.rearrange("p c (s t) -> p c s t", t=W))
            if nseg == 1:
                cp(UA[:, :, 0:1], CARRY[(l, "rw")][:].f(lambda a: a.unsqueeze(2)))
            else:
                S.dma("pool", SHT[0:16, :], std["shift"][l])
                for c in range(7):
                    p = pb()
                    tr(p[:, 0:16], SHT[0:16, c * 128:(c + 1) * 128], 16)
                    cp(UAv[:, c, :, 0], p[:, 0:16])
            for c in range(7):
                w = w0 if c < 4 else w1
                p = pb()
                proj_fm(p[:, 0:128], w, (c % 4) * 128, 128)
                cp(UAv[:, c, :, 1:W], sv(p[:, 0:128], bk), "act" if c % 2 else "dve")
            p = pb()
            proj_fm(p[0:16, 0:128], w1, 384, 16)
            cp(GLT[0:16, :], p[0:16, 0:128])
            if nseg == 1:
                cp(CARRY[(l, "rw")][:].f(lambda a: a.unsqueeze(2)), UA[:, :, 128:129])
                if last:
                    S.dma("pool", pod["shift"][l].rearrange("(c p) -> p c", p=128), CARRY[(l, "rw")][:], is_out=True,
                          allow_slow_non_contiguous=True)
            else:
                for c in range(7):
                    S.dma("pool", sod["shift"][l][:, c * 128:(c + 1) * 128].rearrange("s p -> p s"), UAv[:, c, :, L],
                          is_out=True, allow_slow_non_contiguous=True)
            ck('rw1')
            XSv = dv(XS[:], bk)
            tt(XSv, UAv[:, :, :, 0:L], UAv[:, :, :, 1:W], ALU.subtract)
            tt(XSv, XSv, bc2(pv(l, "mu", 7), [128, 7, nseg, L]))
            tt(XSv, XSv, UAv[:, :, :, 1:W], ALU.add)
            ck('rw2')
            X6 = XS[:, 6, :]
            T6 = g.t()
            act(T6[:, 0:128], X6, AF.Tanh)
            act(T6[:, 128:256], X6, AF.Sigmoid)
            p = pb()
            mm(p[:, 0:256], T6[:, 0:128], smat(l, 0))
            LAM = g.t()
            tt(LAM[:], p[:, 0:256], row(l, "w0", 256), ALU.add)
            act(LAM[:], LAM[:], AF.Sigmoid)
            AT, GTt = g.t(), g.t()
            for pr in range(2):
                p = pb()
                mm(p[:, 0:128], smat(l, 1)[:, pr * 128:(pr + 1) * 128], X6)
                act(AT[:, pr * 128:(pr + 1) * 128], p[:, 0:128], AF.Sigmoid, bias=pv(l, "a0", 1, pr))
                mm(p[:, 128:256], smat(l, 2)[:, pr * 128:(pr + 1) * 128], T6[:, 128:256])
                cp(GTt[:, pr * 128:(pr + 1) * 128], p[:, 128:256])
            ck('rw3')
            yield
            KK, KP, E, E2, EQT, KC, K2C2, VT, KC2, RT, X, YS, Mt, Bt = [g.t() for _ in range(14)]
            AE = [g.t(), g.t()]
            AC = [g.t(), g.t()]
            Nn = [g.t(), g.t()]
            gsave = g.i
            MASK2 = CONST[:, CI["TRS" + sfx]:CI["TRS" + sfx] + 256]
            TRI3 = CONST[:, CI["TRS" + sfx]:CI["TRS" + sfx] + 384]
            Vp, Up = PADS[0], PADS[1]
            for pr in range(2):
                g.i = gsave
                rT, kT, vT = XS[:, pr, :], XS[:, 2 + pr, :], XS[:, 4 + pr, :]
                aT = AT[:, pr * 128:(pr + 1) * 128]
                ts(KK[:, 0:128], kT, pv(l, "k_k", 1, pr), ALU.mult)
                act(KK[:, 128:256], KK[:, 0:128], AF.Square)
                p = pb()
                mm(p[:, 0:128], BLK, KK[:, 128:256])
                act(KK[:, 128:256], p[:, 0:128], AF.Sqrt, bias=EPSc)
                I("dve", "reciprocal", out=KK[:, 128:256], in_=KK[:, 128:256])
                tt(KK[:, 0:128], KK[:, 0:128], KK[:, 128:256])
                ts(KP[:, 0:128], aT, -1.0, ALU.add, pv(l, "k_a", 1, pr), ALU.mult)
                stt(KP[:, 0:128], KP[:, 0:128], 1.0, kT, ALU.add, ALU.mult)
                tt(KP[:, 128:256], KK[:, 0:128], aT)
                p = pb()
                mm(p[:, 0:384], LAM[:, pr * 128:(pr + 1) * 128], TRI3)
                act(E[:, 0:256], p[:, 0:256], AF.Exp, scale=-C_DEC)
                act(E2[:, 0:128], p[:, 256:384], AF.Exp, scale=-C_DEC)
                act(E2[:, 128:256], p[:, 128:256], AF.Exp, scale=C_DEC)
                tt(EQT[:, 0:128], KK[:, 0:128], E[:, 0:128])
                tt(EQT[:, 128:256], rT, E[:, 128:256])
                tt(KC[:, 0:128], KP[:, 0:128], E2[:, 128:256])
                tt(KC[:, 128:256], KP[:, 128:256], E2[:, 128:256])
                tt(K2C2[:, 0:128], KP[:, 0:128], E2[:, 0:128])
                stt(K2C2[:, 128:256], KP[:, 128:256], -1.0, E2[:, 0:128], ALU.mult, ALU.mult)
                ck('rw4')
                yield
                p = pb()
                tr(p[:, 0:128], vT)
                tr(p[:, 128:256], K2C2[:, 0:128])
                tr(p[:, 256:384], K2C2[:, 128:256])
                cp(VT[:, 0:128], p[:, 0:128])
                ck('rw4a1')
                cp(Vp[:, 0, 0:64], p[:, 0:64])
                cp(Vp[:, 1, 64:128], p[:, 64:128])
                ck('rw4a2')
                cp(KC2[:, 0:256], p[:, 128:384])
                ck('rw4b')
                yield
                for hh in range(2):
                    b = 64 * hh
                    if hh == 1:
                        ck('rw4c')
                    p = pb()
                    mm(p[:, 0:256], KC[b:b + 64, 0:128], EQT[b:b + 64, 0:256])
                    mm(p[:, 256:512], KC[b:b + 64, 128:256], EQT[b:b + 64, 0:256])
                    tt(AE[hh][:, 0:256], p[:, 0:256], MASK2)
                    stt(AC[hh][:, 0:256], p[:, 256:512], -1.0, MASK2, ALU.mult, ALU.mult)
                    p2 = pb()
                    tr(p2[:, 0:128], AC[hh][:, 0:128])
                    cp(Nn[hh][:, 0:128], p2[:, 0:128], "act")
                    yield
                ck('rw5')
                p = pb()
                inter(bk, l, "wkv", pr, sst, p, EQT[:, 0:128], False)
                for hh in range(2):
                    mm(p[:, 0:128], Vp[:, hh, :], AE[hh][:, 0:128], False, hh == 1)
                cp(RT[:, 0:128], p[:, 0:128])
                yield
                p = pb()
                tr(p[:, 0:128], RT[:, 0:128])
                cp(X[:, 0:128], p[:, 0:128])
                yield
                ck('rw6')
                yield from tri_solve(bk, [AC[0], AC[1]], [Nn[0], Nn[1]], X, g)
                ck('rw7')
                cp(Up[:, 0, 0:64], X[:, 0:64])
                cp(Up[:, 1, 64:128], X[:, 64:128])
                p = pb()
                inter(bk, l, "wkv", pr, sst, p, EQT[:, 128:256], False)
                for hh in range(2):
                    mm(p[:, 0:128], Vp[:, hh, :], AE[hh][:, 128:256], False, False)
                    mm(p[:, 0:128], Up[:, hh, :], AC[hh][:, 128:256], False, hh == 1)
                ck('rw8')
                cp(YS[:, 0:128], p[:, 0:128])
                act(YS[:, 128:256], p[:, 0:128], AF.Square)
                yield
                p2 = pb()
                mm(p2[:, 0:256], BLK, YS[:, 0:256])
                ts(Mt[:, 0:256], p2[:, 0:256], 1.0 / 64.0, ALU.mult)
                stt(Bt[:, 0:128], Mt[:, 0:128], -1.0, Mt[:, 0:128], ALU.mult, ALU.mult)
                tt(Mt[:, 128:256], Mt[:, 128:256], Bt[:, 0:128], ALU.add)
                act(Mt[:, 128:256], Mt[:, 128:256], AF.Sqrt, bias=GNEPSc)
                I("dve", "reciprocal", out=Mt[:, 128:256], in_=Mt[:, 128:256])
                tt(YS[:, 0:128], YS[:, 0:128], Mt[:, 0:128], ALU.subtract)
                tt(YS[:, 0:128], YS[:, 0:128], Mt[:, 128:256])
                ts(YS[:, 0:128], YS[:, 0:128], pv(l, "ln_w", 1, pr), ALU.mult, pv(l, "ln_b", 1, pr), ALU.add)
                stt(Bt[:, 0:128], rT, pv(l, "r_k", 1, pr), KP[:, 0:128], ALU.mult, ALU.mult)
                p3 = pb()
                mm(p3[:, 0:128], BLK, Bt[:, 0:128])
                tt(Bt[:, 128:256], p3[:, 0:128], vT)
                tt(YS[:, 0:128], YS[:, 0:128], Bt[:, 128:256], ALU.add)
                tt(YT[:, pr, :], YS[:, 0:128], GTt[:, pr * 128:(pr + 1) * 128])
                yield
                ck('rw9')
                state_update(bk, l, "wkv", pr, [KC2[:, 0:128], KC2[:, 128:256]], [VT[:, 0:128], X[:, 0:128]],
                             E[:, 128:256], sst, (pod if nseg == 1 else sod)["wkv"], last, g, transpose_out=True)

        def gla(bk, l, last, wget, sst, pool=None):
            g = GP(pool)
            w2 = wget(2)
            QK = [g.t(), g.t()]
            for c in range(4):
                p = pb()
                proj_fm(p[:, 0:128], w2, c * 128, 128)
                cp(QK[c % 2][:, (c // 2) * 128:(c // 2) * 128 + 128], p[:, 0:128], "act" if c % 2 else "dve")
            KTM = g.t()
            p = pb()
            proj_tm(p[:, 0:256], w2, 256, 256)
            cp(KTM[:], p[:, 0:256], "act")
            w3 = wget(3)
            VTM = g.t()
            p = pb()
            proj_tm(p[:, 0:256], w3, 0, 256)
            cp(VTM[:], p[:, 0:256])
            Vp = [PADS[0], PADS[2]]
            for pr in range(2):
                cp(Vp[pr][:, 0, 0:64], p[:, pr * 128:pr * 128 + 64])
                cp(Vp[pr][:, 1, 64:128], p[:, pr * 128 + 64:pr * 128 + 128])
            GT = g.t()
            for pr in range(2):
                p = pb()
                proj_fm(p[:, 0:128], w3, 256 + pr * 128, 128)
                act(GT[:, pr * 128:(pr + 1) * 128], p[:, 0:128], AF.Silu)
            p = pb()
            mm(p[:, 0:256], GLT[0:16, :], smat(l, 3)[0:16, :])
            LA = g.t()
            tt(LA[:], p[:, 0:256], row(l, "gkb", 256), ALU.add)
            act(LA[:], LA[:], AF.Sigmoid)
            act(LA[:], LA[:], AF.Ln)
            p = pb()
            mm(p[:, 0:256], bk.SU, LA[:])
            K2 = g.t()
            act(K2[:], p[:, 0:256], AF.Exp, scale=1.0 / 16.0)
            tt(K2[:], K2[:], KTM[:])
            yield
            E, QKh, YS = g.t(), g.t(), g.t()
            A = [g.t(), g.t()]
            gsave = g.i
            for pr in range(2):
                g.i = gsave
                p = pb()
                mm(p[:, 0:128], LA[:, pr * 128:(pr + 1) * 128], bk.TRI)
                act(E[:, 0:128], p[:, 0:128], AF.Exp, scale=1.0 / 16.0)
                act(E[:, 128:256], p[:, 0:128], AF.Exp, scale=-1.0 / 16.0)
                stt(QKh[:, 0:128], QK[pr][:, 0:128], 0.125, E[:, 0:128], ALU.mult, ALU.mult)
                tt(QKh[:, 128:256], QK[pr][:, 128:256], E[:, 128:256])
                for hh in range(2):
                    b = 64 * hh
                    p = pb()
                    mm(p[:, 0:128], QKh[b:b + 64, 128:256], QKh[b:b + 64, 0:128])
                    tt(A[hh][:, 0:128], p[:, 0:128], bk.TRI)
                    yield
                p = pb()
                inter(bk, l, "gla", pr, sst, p, QKh[:, 0:128], False)
                for hh in range(2):
                    mm(p[:, 0:128], Vp[pr][:, hh, :], A[hh][:, 0:128], False, hh == 1)
                cp(YS[:, 0:128], p[:, 0:128])
                yield
                post_rms(YS[:, 0:128], BLK, 64.0, pv(l, "gla_nw", 1, pr), GT[:, pr * 128:(pr + 1) * 128], YT[:, 2 + pr, :], g)
                yield
                state_update(bk, l, "gla", pr, [K2[:, pr * 128:(pr + 1) * 128]], [VTM[:, pr * 128:(pr + 1) * 128]],
                             E[:, 0:128], sst, (pod if bk.nseg == 1 else sod)["gla"], last, g)
                yield

        def conv_in(bk, l, ckey, skey):
            nseg, L = bk.nseg, bk.L
            W = L + 3
            XBv = XB[:, :, 0:nseg * W].f(lambda a: a.rearrange("p c (s t) -> p c s t", t=W))
            if nseg == 1:
                cp(XBv[:, :, 0, 0:3], CARRY[(l, ckey)][:])
            else:
                S.dma("pool", CST[0:48, 0:768], std[skey][l].rearrange("s i c -> (s i) c"))
                for c in range(6):
                    p = pb()
                    tr(p[:, 0:48], CST[0:48, c * 128:(c + 1) * 128], 48)
                    cp(XBv[:, c, :, 0:3], p[:, 0:48].f(lambda a: a.rearrange("p (s i) -> p s i", i=3)))
            return XBv

        def conv_run(bk, l, ckey, skey, cwname, XBv, last):
            nseg, L = bk.nseg, bk.L
            W = L + 3
            if nseg == 1:
                cp(CARRY[(l, ckey)][:], XBv[:, :, 0, L:L + 3])
                if last:
                    for c in range(6):
                        S.dma("pool", pod[skey][l][:, c * 128:(c + 1) * 128].rearrange("i p -> p i"), CARRY[(l, ckey)][:, c, :],
                              is_out=True, allow_slow_non_contiguous=True)
            else:
                for c in range(6):
                    cp(CS3[:].f(lambda a: a.rearrange("p (s i) -> p s i", i=3)), XBv[:, c, :, L:L + 3])
                    p = pb()
                    tr(p[0:48, 0:128], CS3[:])
                    cp(CSO[0:48, c * 128:(c + 1) * 128], p[0:48, 0:128], "act")
                S.dma("pool", sod[skey][l].rearrange("s i c -> (s i) c"), CSO[:], is_out=True)
            CVv = dv(CV[:], bk)
            TMv = dv(HTF[:, 0:6, :], bk)
            for i in range(4):
                wv = bc2(pv(l, cwname, 6, i * 6), [128, 6, nseg, L])
                if i == 0:
                    tt(CVv, XBv[:, :, :, 0:L], wv)
                else:
                    tt(TMv, XBv[:, :, :, i:i + L], wv)
                    tt(CVv, CVv, TMv, ALU.add)

        def softplus_cols(dst, src, biasrow):
            tt(dst, src, biasrow, ALU.add)
            act(dst, dst, AF.Exp)
            act(dst, dst, AF.Ln, bias=ONEc)

        def dnet(bk, l, last, wget, sst, pool=None):
            g = GP(pool)
            nseg, L = bk.nseg, bk.L
            W = L + 3
            XBv = conv_in(bk, l, "dn", "dnc")
            w4 = wget(4)
            for c in range(4):
                p = pb()
                proj_fm(p[:, 0:128], w4, c * 128, 128)
                cp(XBv[:, c, :, 3:W], sv(p[:, 0:128], bk), "act" if c % 2 else "dve")
            w5 = wget(5)
            for c in range(2):
                p = pb()
                proj_fm(p[:, 0:128], w5, c * 128, 128)
                cp(XBv[:, 4 + c, :, 3:W], sv(p[:, 0:128], bk), "act" if c % 2 else "dve")
            for c in range(2):
                p = pb()
                proj_fm(p[:, 0:128], w5, 256 + c * 128, 128)
                act(ZT[:, c * 128:(c + 1) * 128], p[:, 0:128], AF.Silu)
            w6 = wget(6)
            p = pb()
            proj_tm(p[:, 0:12], w6, 0, 12)
            cp(SMT[:, 0:12], p[:, 0:12])
            for c in range(2):
                p = pb()
                proj_fm(p[:, 0:128], w6, 12 + c * 128, 128)
                act(ZT[:, 256 + c * 128:256 + (c + 1) * 128], p[:, 0:128], AF.Silu)
            conv_run(bk, l, "dn", "dnc", "dn_cw", XBv, last)
            act(CV[:], CV[:], AF.Silu)
            act(SMT[:, 16:20], SMT[:, 4:8], AF.Sigmoid)
            softplus_cols(SMT[:, 20:24], SMT[:, 0:4], row(l, "dndt", 4))
            act(SMT[:, 24:28], row(l, "dnA", 4), AF.Exp)
            stt(SMT[:, 28:32], SMT[:, 20:24], -1.0, SMT[:, 24:28], ALU.mult, ALU.mult)
            p = pb()
            mm(p[:, 0:4], bk.SU, SMT[:, 28:32])
            act(SMT[:, 32:36], p[:, 0:4], AF.Exp)
            yield
            SQ, LB, ELb, R, X, QE, YS, K2 = [g.t() for _ in range(8)]
            KQ = [g.t(), g.t()]
            ET = [g.t(), g.t()]
            A = [g.t(), g.t()]
            T0 = [g.t(), g.t()]
            Nn = [g.t(), g.t()]
            NT = [g.t(), g.t()]
            gsave = g.i
            Wp = PADS[2]
            for pr in range(2):
                g.i = gsave
                for which, (src, dst, scl) in enumerate(((CV[:, 2 + pr, :], KQ[pr][:, 0:128], 1.0),
                                                         (CV[:, pr, :], KQ[pr][:, 128:256], 0.125))):
                    act(SQ[:, 0:128], src, AF.Square)
                    p = pb()
                    mm(p[:, 0:128], BLK, SQ[:, 0:128])
                    act(SQ[:, 128:256], p[:, 0:128], AF.Sqrt, bias=EPSc)
                    I("dve", "reciprocal", out=SQ[:, 128:256], in_=SQ[:, 128:256])
                    stt(dst, src, scl, SQ[:, 128:256], ALU.mult, ALU.mult)
                tt(LB[:, 0:128].f(lambda a: a.rearrange("p (h d) -> p h d", d=64)),
                   ONES.f(lambda a: a.rearrange("p (h d) -> p h d", d=64)),
                   bc(SMT[:, 28 + 2 * pr:30 + 2 * pr], [128, 2, 64], 2))
                p = pb()
                mm(p[:, 0:128], LB[:, 0:128], bk.TRI)
                act(ELb[:, 0:128], p[:, 0:128], AF.Exp)
                yield
                for hh in range(2):
                    h = 2 * pr + hh
                    b = 64 * hh
                    decay_mask(bk, SMT[:, 28 + h:29 + h], ET[hh][:, 0:128], g)
                    tt(ET[hh][:, 128:256], ET[hh][:, 0:128], bk.TRS)
                    p = pb()
                    mm(p[:, 0:256], KQ[pr][b:b + 64, 0:128], KQ[pr][b:b + 64, 0:256])
                    tt(A[hh][:, 0:128], p[:, 128:256], ET[hh][:, 0:128])
                    tt(T0[hh][:, 0:128], p[:, 0:128], ET[hh][:, 128:256])
                    p2 = pb()
                    tr(p2[:, 0:128], T0[hh][:, 0:128])
                    ts(Nn[hh][:, 0:128], p2[:, 0:128], SMT[:, 16 + h:17 + h], ALU.mult, -1.0, ALU.mult)
                    p3 = pb()
                    tr(p3[:, 0:128], Nn[hh][:, 0:128])
                    cp(NT[hh][:, 0:128], p3[:, 0:128], "act")
                    g.i -= 1
                    yield
                p = pb()
                inter(bk, l, "dn", pr, sst, p, KQ[pr][:, 0:128], True)
                tt(R[:, 0:128], p[:, 0:128], ELb[:, 0:128])
                tt(R[:, 0:128], CV[:, 4 + pr, :], R[:, 0:128], ALU.subtract)
                yield
                p = pb()
                tr(p[:, 0:128], R[:, 0:128])
                for hh in range(2):
                    h = 2 * pr + hh
                    ts(X[:, hh * 64:hh * 64 + 64], p[:, hh * 64:hh * 64 + 64], SMT[:, 16 + h:17 + h], ALU.mult)
                yield
                yield from tri_solve(bk, [NT[0], NT[1]], [Nn[0], Nn[1]], X, g)
                cp(Wp[:, 0, 0:64], X[:, 0:64])
                cp(Wp[:, 1, 64:128], X[:, 64:128])
                tt(QE[:, 0:128], KQ[pr][:, 128:256], ELb[:, 0:128])
                p = pb()
                inter(bk, l, "dn", pr, sst, p, QE[:, 0:128], False)
                for hh in range(2):
                    mm(p[:, 0:128], Wp[:, hh, :], A[hh][:, 0:128], False, hh == 1)
                cp(YS[:, 0:128], p[:, 0:128])
                yield
                post_rms(YS[:, 0:128], BLK, 64.0, pv(l, "dn_nw", 1, pr), ZT[:, pr * 128:(pr + 1) * 128], YT[:, 4 + pr, :], g)
                yield
                p = pb()
                tr(p[:, 0:128], KQ[pr][:, 0:128])
                for hh in range(2):
                    h = 2 * pr + hh
                    ts(K2[:, hh * 64:hh * 64 + 64], p[:, hh * 64:hh * 64 + 64], SMT[:, 32 + h:33 + h], ALU.mult)
                state_update(bk, l, "dn", pr, [K2[:, 0:128]], [X[:, 0:128]], ELb[:, 0:128], sst,
                             (pod if nseg == 1 else sod)["dn"], last, g)
                yield

        def ssd(bk, l, last, wget, sst, pool=None):
            g = GP(pool)
            nseg, L = bk.nseg, bk.L
            W = L + 3
            XBv = conv_in(bk, l, "ss", "ssc")
            w7 = wget(7)
            for c in range(4):
                p = pb()
                proj_fm(p[:, 0:128], w7, c * 128, 128)
                cp(XBv[:, c, :, 3:W], sv(p[:, 0:128], bk), "act" if c % 2 else "dve")
            w8 = wget(8)
            for c in range(2):
                p = pb()
                proj_fm(p[:, 0:128], w8, c * 128, 128)
                cp(XBv[:, 4 + c, :, 3:W], sv(p[:, 0:128], bk), "act" if c % 2 else "dve")
            conv_run(bk, l, "ss", "ssc", "ss_cw", XBv, last)
            for c in range(6):
                act(CV[:, c, :], CV[:, c, :], AF.Silu, bias=pv(l, "ss_cb", 1, c))
            softplus_cols(SMT[:, 40:44], SMT[:, 8:12], row(l, "ssdt", 4))
            act(SMT[:, 44:48], row(l, "ssA", 4), AF.Exp)
            stt(SMT[:, 48:52], SMT[:, 40:44], -1.0, SMT[:, 44:48], ALU.mult, ALU.mult)
            p = pb()
            mm(p[:, 0:4], bk.SU, SMT[:, 48:52])
            act(SMT[:, 52:56], p[:, 0:4], AF.Exp)
            yield
            BTM, X2, YS, LB = [g.t() for _ in range(4)]
            ET = [g.t(), g.t()]
            A = [g.t(), g.t()]
            EL = [g.t(), g.t()]
            CH = [g.t(), g.t()]
            gsave = g.i
            Xp = PADS[1]
            outd = (pod if nseg == 1 else sod)["ssm"]
            for pr in range(2):
                g.i = gsave
                p = pb()
                tr(p[:, 0:128], CV[:, 2 + pr, :])
                tr(p[:, 128:256], CV[:, pr, :])
                cp(BTM[:, 0:128], p[:, 0:128])
                for hh in range(2):
                    h = 2 * pr + hh
                    ts(Xp[:, hh, hh * 64:hh * 64 + 64], p[:, 128 + hh * 64:128 + hh * 64 + 64], SMT[:, 40 + h:41 + h], ALU.mult)
                    ts(X2[:, hh * 64:hh * 64 + 64], Xp[:, hh, hh * 64:hh * 64 + 64], SMT[:, 52 + h:53 + h], ALU.mult)
                pG = pb()
                mm(pG[:, 0:128], CV[:, 2 + pr, :], CV[:, 4 + pr, :])
                for hh in range(2):
                    h = 2 * pr + hh
                    decay_mask(bk, SMT[:, 48 + h:49 + h], ET[hh][:, 0:128], g)
                    g.i -= 1
                    tt(A[hh][:, 0:128], pG[:, 0:128], ET[hh][:, 0:128])
                    ts(LB[:, 0:128], ONES, SMT[:, 48 + h:49 + h], ALU.mult)
                    p = pb()
                    mm(p[:, 0:128], LB[:, 0:128], bk.TRI)
                    act(EL[hh][:, 0:128], p[:, 0:128], AF.Exp)
                    tt(CH[hh][:, 0:128], CV[:, 4 + pr, :], EL[hh][:, 0:128])
                yield
                pY = pb()
                for hh in range(2):
                    reg = pY[:, hh * 128:(hh + 1) * 128]
                    for s in range(nseg):
                        hg = PST[(l, "ssm")][:, pr, :] if nseg == 1 else sst[:, s, pr, :]
                        mm(reg[:, s * L:(s + 1) * L], hg, CH[hh][:, s * L:(s + 1) * L], s == 0, False)
                    mm(reg, Xp[:, hh, :], A[hh][:, 0:128], False, True)
                cp(YS[0:64, 0:128], pY[0:64, 0:128])
                cp(YS[64:128, 0:128], pY[64:128, 128:256])
                yield
                stt(YS[:, 0:128], CV[:, pr, :], pv(l, "ss_D", 1, pr), YS[:, 0:128], ALU.mult, ALU.add)
                tt(YS[:, 0:128], YS[:, 0:128], ZT[:, 256 + pr * 128:256 + (pr + 1) * 128])
                post_rms(YS[:, 0:128], ONES, 128.0, pv(l, "ss_nw", 1, pr), None, YT[:, 6 + pr, :], g)
                yield
                for s in range(nseg):
                    lv = BTM[:, 0:128]
                    if nseg > 1:
                        m = g_rot()
                        I("act", "mul", out=m[:, 0:128], in_=lv, mul=CONST[:, C_SEG + s:C_SEG + s + 1])
                        lv = m[:, 0:128]
                    p = pb()
                    mm(p[:, 0:128], lv, X2[:, 0:128])
                    if nseg == 1:
                        hg = PST[(l, "ssm")][:, pr, :]
                        dest = hg
                    else:
                        hg = sst[:, s, pr, :]
                        dest = ost()[:]
                    for hh in range(2):
                        stt(dest[:, hh * 64:hh * 64 + 64], hg[:, hh * 64:hh * 64 + 64], lastcol(EL[hh][:, 0:128], bk, s),
                            p[:, hh * 64:hh * 64 + 64], ALU.mult, ALU.add)
                    if nseg > 1 or last:
                        for hh in range(2):
                            p2 = pb()
                            tr(p2[0:64, 0:128], dest[:, hh * 64:hh * 64 + 64])
                            o = ost()
                            cp(o[0:64, :], p2[0:64, 0:128], "act")
                            dd = outd[l][2 * pr + hh] if nseg == 1 else outd[l, s, 2 * pr + hh]
                            S.dma("pool", dd, o[0:64, :], is_out=True)

        import os
        class _Stop(Exception):
            pass
        kstop = os.environ.get('KSTOP', '')
        def ck(name):
            if name == kstop:
                raise _Stop()
        blks = [int(x) for x in os.environ.get('KBLKS', ','.join(str(i) for i in range(NBLK))).split(',')]
        try:
          ck('setup')
          def with_cx(cx, gen):
              while True:
                  CUR[0] = cx
                  try:
                      next(gen)
                  except StopIteration:
                      return
                  yield

          def rr_gen(gens):
              gens = list(gens)
              while gens:
                  for g_ in list(gens):
                      try:
                          next(g_)
                      except StopIteration:
                          gens.remove(g_)
                      yield

          def run_seq(gen):
              for _ in gen:
                  pass

          def mix_gen(blk, bk, l, last, samp):
              def wget(idx, l=l):
                  return wload16(l, idx)
              if not samp:
                  yield from rr_gen([rwkv(bk, l, last, wget, None, G), dnet(bk, l, last, wget, None, G2)])
                  yield from rr_gen([gla(bk, l, last, wget, None, G), ssd(bk, l, last, wget, None, G2)])
              else:
                  sst = SST[0]
                  I("dve", "memset", sst[:], 0.0)
                  load_bd(sst, std["wkv"][l])
                  for j in range(8):
                      p = pb()
                      for q in range(4):
                          tr(p[:, q * 128:(q + 1) * 128], sst[:, 2 * j + q // 2, q % 2, :])
                      cp(sst[:, 2 * j:2 * j + 2, :, :].f(lambda a: a.rearrange("p s r v -> p (s r v)")), p[:, 0:512])
                  yield
                  yield from rwkv(bk, l, last, wget, sst)
                  sst = SST[1]
                  I("dve", "memset", sst[:], 0.0)
                  load_bd(sst, std["gla"][l])
                  yield from gla(bk, l, last, wget, sst)
                  sst = SST[0]
                  I("dve", "memset", sst[:], 0.0)
                  load_bd(sst, std["dn"][l])
                  yield from dnet(bk, l, last, wget, sst)
                  sst = SST[1]
                  for s in range(16):
                      sl = SSL[0]
                      S.dma("pool", sl[:], std["ssm"][l, s].rearrange("h p n -> p h n"))
                      p = pb()
                      for h in range(4):
                          tr(p[:, h * 64:(h + 1) * 64], sl[0:64, h, :], 64)
                      cp(sst[:, s, :, :].f(lambda a: a.rearrange("p r v -> p (r v)")), p[:, 0:256])
                  yield
                  yield from ssd(bk, l, last, wget, sst)

          def dense_gen(blk, bk, l, samp):
              def wget(idx, l=l):
                  return wload16(l, idx)
              for half in range(2):
                  w = wget(9 + half)
                  p = pb()
                  for j in range(4):
                      for c8 in range(8):
                          mm(p[:, j * 128:(j + 1) * 128], w[:, c8, j * 128:(j + 1) * 128], YT[:, c8, :], c8 == 0, c8 == 7)
                  pv4 = p[:, 0:512].f(lambda a: a.rearrange("p (c s t) -> p c s t", c=4, t=bk.L))
                  tt(dv(TMP4[:], bk), pv4, modv(MODT[:, l, 16 + 4 * half:20 + 4 * half, :], bk, 4))
                  tt(XT[:, 4 * half:4 * half + 4, :], XT[:, 4 * half:4 * half + 4, :], TMP4[:], ALU.add)
                  yield
              if samp:
                  dump(6 * l + 2, XT)
              norm_mod(bk, modv(SC[:, l, 1], bk), modv(MODT[:, l, 24:32, :], bk))
              yield
              for s8 in range(8):
                  w = wget(11 + s8)
                  p = pb()
                  for j in range(4):
                      for kc in range(8):
                          mm(p[:, j * 128:(j + 1) * 128], w[:, kc, j * 128:(j + 1) * 128], HT[:, kc, :], kc == 0, kc == 7)
                  hv = HID[:, 4 * s8:4 * s8 + 4, :]
                  tmpr = (TMP4 if s8 % 2 else TMP5)
                  act(tmpr[:], p[:, 0:512].f(lambda a: a.rearrange("p (c t) -> p c t", t=128)), AF.Relu)
                  tt(hv, tmpr[:], tmpr[:], ALU.mult)
                  yield
              for half in range(2):
                  p = pb()
                  for fcg in range(4):
                      w = wget(19 + half * 4 + fcg)
                      for j in range(4):
                          for f8 in range(8):
                              mm(p[:, j * 128:(j + 1) * 128], w[:, f8, j * 128:(j + 1) * 128], HID[:, fcg * 8 + f8, :],
                                 fcg == 0 and f8 == 0 and j == 0, fcg == 3 and f8 == 7)
                  pv4 = p[:, 0:512].f(lambda a: a.rearrange("p (c s t) -> p c s t", c=4, t=bk.L))
                  tt(dv(TMP4[:], bk), pv4, modv(MODT[:, l, 40 + 4 * half:44 + 4 * half, :], bk, 4))
                  tt(XT[:, 4 * half:4 * half + 4, :], XT[:, 4 * half:4 * half + 4, :], TMP4[:], ALU.add)
                  yield
              if samp:
                  dump(6 * l + 4, XT)
              if l == NL - 1:
                  norm_mod(bk, bc2(pv(0, "fnw", 8), [128, 8, bk.nseg, bk.L]), None, HTF)
                  for b2 in range(2):
                      p = pb()
                      for j in range(4):
                          tr(p[:, j * 128:(j + 1) * 128], HTF[:, 4 * b2 + j, :])
                      cp(YTM[:, b2 * 512:(b2 + 1) * 512], p[:, 0:512])
                  S.dma("pool", yout[blk * 128:(blk + 1) * 128, :], YTM[:], is_out=True)

          pending = None
          for bi, blk in enumerate(blks):
              cx = CXS[bi % 2]
              CUR[0] = cx
              bk = Pk if blk < 16 else Sk
              last = blk == max(b for b in blks if b < 16) if blk < 16 else False
              samp = blk == 16
              S.dma("pool", XTM[:], xin[blk * 128:(blk + 1) * 128, :])
              for b2 in range(2):
                  p = pb()
                  for j in range(4):
                      tr(p[:, j * 128:(j + 1) * 128], XTM[:, (4 * b2 + j) * 128:(4 * b2 + j + 1) * 128])
                  cp(XT[:, 4 * b2:4 * b2 + 4, :], p[:, 0:512].f(lambda a: a.rearrange("p (c t) -> p c t", t=128)))
              for l in range(NL):
                  CUR[0] = cx
                  norm_mod(bk, modv(SC[:, l, 0], bk), modv(MODT[:, l, 0:8, :], bk))
                  M = with_cx(cx, mix_gen(blk, bk, l, last, samp))
                  if l == 0 and pending is not None:
                      run_seq(rr_gen([M, pending]))
                      pending = None
                  else:
                      run_seq(M)
                  D = with_cx(cx, dense_gen(blk, bk, l, samp))
                  if l == NL - 1 and not KDBG:
                      pending = D
                  else:
                      run_seq(D)
          if pending is not None:
              run_seq(pending)
        except _Stop:
            pass
        print('opcounts', {e: len(v) for e, v in S.ops.items()}, flush=True)
        S.emit()
    return nc


W_IN_COLS = None


def _win_colmap():
    def rng(a, b):
        return list(range(a, b))
    slots = []
    slots.append(rng(0, 512))
    slots.append(rng(512, 896) + rng(1920, 1936))
    slots.append(rng(896, 1408))
    slots.append(rng(1408, 1920))
    slots.append(rng(1936, 2448))
    slots.append(rng(2448, 2960))
    slots.append(rng(2960, 2968) + rng(3992, 3996) + rng(2968, 3224))
    slots.append(rng(3224, 3736))
    slots.append(rng(3736, 3992))
    return slots


def _tile_rows(w, cols):
    out = np.zeros((128, 8, 512), np.float32)
    sub = w[:, cols]
    out[:, :, :len(cols)] = sub.reshape(8, 128, len(cols)).transpose(1, 0, 2)
    return out


_NC_CACHE = {}


def kernel(**inp):
    f = lambda k: np.ascontiguousarray(np.asarray(inp[k], dtype=np.float32))
    wall = np.zeros((2, 27, 128, 8, 512), np.float32)
    adaw = np.zeros((2, 12, 128, 8, 512), np.float32)
    cm = _win_colmap()
    w_in, w_out, w_up, w_down, ada_w = f("w_in"), f("w_out"), f("w_up"), f("w_down"), f("ada_w")
    for l in range(2):
        for s in range(9):
            wall[l, s] = _tile_rows(w_in[l], cm[s])
        for s in range(2):
            wall[l, 9 + s] = _tile_rows(w_out[l], list(range(s * 512, (s + 1) * 512)))
        for s in range(8):
            wall[l, 11 + s] = _tile_rows(w_up[l], list(range(s * 512, (s + 1) * 512)))
        for half in range(2):
            for fcg in range(4):
                blk = w_down[l][fcg * 1024:(fcg + 1) * 1024, half * 512:(half + 1) * 512]
                wall[l, 19 + half * 4 + fcg] = blk.reshape(8, 128, 512).transpose(1, 0, 2)
        for s in range(12):
            adaw[l, s] = _tile_rows(ada_w[l], list(range(s * 512, (s + 1) * 512)))
    pvv = np.zeros((128, 2 * NPV), np.float32)
    rows = np.zeros((128, 2 * NR), np.float32)
    sm = np.zeros((128, 2 * 1024), np.float32)

    def putv(l, name, vec, off=0):
        n = vec.shape[0] // 128
        pvv[:, l * NPV + PVO[name] + off:l * NPV + PVO[name] + off + n] = vec.reshape(n, 128).T
    for l in range(2):
        putv(l, "ada_b", f("ada_b")[l])
        putv(l, "n1w", f("norm1_w")[l])
        putv(l, "n2w", f("norm2_w")[l])
        putv(l, "mu", f("rwkv_mu")[l])
        putv(l, "a0", f("rwkv_a0")[l])
        putv(l, "k_k", f("rwkv_k_k")[l])
        putv(l, "k_a", f("rwkv_k_a")[l])
        putv(l, "r_k", f("rwkv_r_k")[l])
        putv(l, "ln_w", f("rwkv_ln_w")[l])
        putv(l, "ln_b", f("rwkv_ln_b")[l])
        putv(l, "gla_nw", f("gla_norm_w")[l])
        for i in range(4):
            putv(l, "dn_cw", f("dn_conv_w")[l, i], i * 6)
            putv(l, "ss_cw", f("ssm_conv_w")[l, i], i * 6)
        putv(l, "dn_nw", f("dn_norm_w")[l])
        putv(l, "ss_cb", f("ssm_conv_b")[l])
        putv(l, "ss_nw", f("ssm_norm_w")[l])
        putv(l, "ss_D", np.repeat(f("ssm_D")[l], 64))
        putv(l, "fnw", f("final_norm_w"))
        o = l * NR
        rows[:, o + RO["w0"]:o + RO["w0"] + 256] = f("rwkv_w0")[l][None, :]
        rows[:, o + RO["gkb"]:o + RO["gkb"] + 256] = f("gla_gk_b")[l][None, :]
        rows[:, o + RO["dnA"]:o + RO["dnA"] + 4] = f("dn_A_log")[l][None, :]
        rows[:, o + RO["dndt"]:o + RO["dndt"] + 4] = f("dn_dt_bias")[l][None, :]
        rows[:, o + RO["ssdt"]:o + RO["ssdt"] + 4] = f("ssm_dt_bias")[l][None, :]
        rows[:, o + RO["ssA"]:o + RO["ssA"] + 4] = f("ssm_A_log")[l][None, :]
        o = l * 1024
        sm[0:32, o:o + 256] = f("rwkv_w2")[l]
        sm[32:64, o + 256:o + 512] = f("rwkv_a2")[l]
        sm[64:128, o + 512:o + 768] = f("rwkv_g2")[l]
        sm[0:16, o + 768:o + 1024] = f("gla_gk_w2")[l]
    consts = make_consts()
    xp, xs = f("x_prompt"), f("x_sample")
    cpr, csm = f("c_prompt"), f("c_sample")
    stn = dict(shift="state_rwkv_shift", wkv="state_rwkv_wkv", gla="state_gla", dnc="state_dn_conv", dn="state_dn",
               ssc="state_ssm_conv", ssm="state_ssm")
    stf = {k: f(v) for k, v in stn.items()}
    in_maps = []
    for c in range(8):
        m = dict(wall=wall, ada=adaw, pv=pvv, rows=rows, sm=sm, consts=consts)
        m["xin"] = np.ascontiguousarray(np.concatenate([xp[c], xs[16 * c:16 * c + 16].reshape(128, 1024)], 0))
        m["cc"] = np.ascontiguousarray(np.concatenate([cpr[c:c + 1], csm[16 * c:16 * c + 16]], 0))
        for k in ST_SHAPES:
            m["st_" + k] = np.ascontiguousarray(stf[k][:, 16 * c:16 * c + 16])
        in_maps.append(m)
    if "nc" not in _NC_CACHE:
        _NC_CACHE["nc"] = build()
    res = run_bass_kernel_spmd(_NC_CACHE["nc"], in_maps, core_ids=list(range(8)))
    R = res.results
    global _LAST_R
    _LAST_R = R
    y_prompt = np.stack([R[c]["y"][:2048] for c in range(8)], 0)
    y_sample = np.concatenate([R[c]["y"][2048:].reshape(16, 8, 1024) for c in range(8)], 0)
    outs = [y_prompt, y_sample]
    for k in ("shift", "wkv", "gla", "dnc", "dn", "ssc", "ssm"):
        outs.append(np.stack([R[c]["p_" + k] for c in range(8)], 1))
    for k in ("shift", "wkv", "gla", "dnc", "dn", "ssc", "ssm"):
        outs.append(np.concatenate([R[c]["s_" + k] for c in range(8)], 1))
    return tuple(np.ascontiguousarray(o.astype(np.float32)) for o in outs)
```

```python
import numpy as np
import concourse.bass as bass
import concourse.mybir as mybir
from concourse.bass_utils import run_bass_kernel_spmd

F32 = mybir.dt.float32
BF16 = mybir.dt.bfloat16
AF = mybir.ActivationFunctionType
ALU = mybir.AluOpType
AX = mybir.AxisListType


class V:
    __slots__ = ("tile", "ap", "sub")

    def __init__(self, tile, ap, sub):
        self.tile, self.ap, self.sub = tile, ap, sub

    def __getitem__(self, idx):
        return V(self.tile, self.ap[idx], self.sub)

    def f(self, fn):
        return V(self.tile, fn(self.ap), self.sub)


class Tile:
    def __init__(self, h, name):
        self.h, self.name = h, name
        self.ww = {}
        self.wr = {}
        self.subs = {}

    def __getitem__(self, idx):
        return V(self, self.h[idx], None)

    def s(self, k, idx=None):
        ap = self.h[idx] if idx is not None else self.h[:]
        return V(self, ap, k)


def _mx(d, s, v):
    if d.get(s, 0) < v:
        d[s] = v


class Sched:
    ENG = ("pe", "act", "dve", "pool", "sp")
    NDMA = 12

    def __init__(self, nc, stack):
        self.nc = nc
        self.ops = {e: [] for e in self.ENG}
        self.cnt = {e: 0 for e in self.ENG}
        self.waited = {e: {} for e in self.ENG}
        self.sem = {}
        for e in ("pe", "act", "dve", "pool"):
            self.sem[e] = stack.enter_context(nc.semaphore("s_" + e))
        self.dq = {}
        for q in ("sp", "pool", "act"):
            n = self.NDMA if q == "sp" else 6
            sems = [stack.enter_context(nc.semaphore("d_%s%d" % (q, j))) for j in range(n)]
            for j, s_ in enumerate(sems):
                self.sem[("d", q, j)] = s_
            self.dq[q] = [n, 0, [0] * n]
        self.out_tokens = {}
        self.stack = stack
        self.ntile = 0

    def sb(self, shape, name=None, dt=F32):
        self.ntile += 1
        name = name or ("t%d" % self.ntile)
        h = self.stack.enter_context(self.nc.sbuf_tensor(name, list(shape), dt))
        return Tile(h, name)

    def ps(self, shape, name=None, dt=F32):
        self.ntile += 1
        name = name or ("p%d" % self.ntile)
        h = self.stack.enter_context(self.nc.psum_tensor(name, list(shape), dt))
        return Tile(h, name)

    def _deps(self, reads, writes):
        tok = {}
        for v in reads:
            t = v.tile
            for s, x in t.ww.items():
                _mx(tok, s, x)
            if v.sub is None:
                for k, (w, r) in t.subs.items():
                    for s, x in w.items():
                        _mx(tok, s, x)
            elif v.sub in t.subs:
                for s, x in t.subs[v.sub][0].items():
                    _mx(tok, s, x)
        for v in writes:
            t = v.tile
            for d in (t.ww, t.wr):
                for s, x in d.items():
                    _mx(tok, s, x)
            if v.sub is None:
                for k, (w, r) in t.subs.items():
                    for d in (w, r):
                        for s, x in d.items():
                            _mx(tok, s, x)
            elif v.sub in t.subs:
                for d in t.subs[v.sub]:
                    for s, x in d.items():
                        _mx(tok, s, x)
        return tok

    def _commit(self, reads, writes, s, x):
        for v in reads:
            t = v.tile
            if v.sub is None:
                _mx(t.wr, s, x)
            else:
                if v.sub not in t.subs:
                    t.subs[v.sub] = [{}, {}]
                _mx(t.subs[v.sub][1], s, x)
        for v in writes:
            t = v.tile
            if v.sub is None:
                t.ww = {s: x}
                t.wr = {}
                t.subs = {}
            else:
                t.subs[v.sub] = [{s: x}, {}]

    def _add(self, eng, fn, reads, writes, tokens_extra=None, dma_q=None, is_out=False):
        tok = self._deps(reads, writes)
        if tokens_extra:
            for s, x in tokens_extra.items():
                _mx(tok, s, x)
        if dma_q is not None:
            n, nxt, cnts = self.dq[dma_q]
            j = nxt
            self.dq[dma_q][1] = (nxt + 1) % n
            if cnts[j] > 0:
                _mx(tok, ("d", dma_q, j), 16 * cnts[j])
            cnts[j] += 1
            mysem, myval, inc = ("d", dma_q, j), 16 * cnts[j], 16
        else:
            self.cnt[eng] += 1
            mysem, myval, inc = eng, self.cnt[eng], 1
        waits = []
        wd = self.waited[eng]
        for s, x in tok.items():
            if eng == "pe" and s == "pe":
                continue
            if wd.get(s, 0) >= x:
                continue
            wd[s] = x
            waits.append((s, x))
        self.ops[eng].append((fn, waits, mysem, inc))
        self._commit(reads, writes, mysem, myval)
        if is_out:
            _mx(self.out_tokens, mysem, myval)

    def I(self, eng, meth, *args, **kw):
        reads, writes = [], []
        a2 = []
        for a in args:
            if isinstance(a, V):
                reads.append(a)
                a2.append(a.ap)
            else:
                a2.append(a)
        k2 = {}
        rw = kw.pop("_rw", False)
        for k, a in kw.items():
            if isinstance(a, V):
                if k in ("out", "accum_out"):
                    writes.append(a)
                    if rw:
                        reads.append(a)
                else:
                    reads.append(a)
                k2[k] = a.ap
            else:
                k2[k] = a
        fn = lambda e: getattr(e, meth)(*a2, **k2)
        self._add(eng, fn, reads, writes)

    def dma(self, q, out, in_, is_out=False, xr=(), xw=(), **kw):
        reads, writes = list(xr), list(xw)
        o = out.ap if isinstance(out, V) else out
        i = in_.ap if isinstance(in_, V) else in_
        if isinstance(out, V):
            writes.append(out)
        if isinstance(in_, V):
            reads.append(in_)
        fn = lambda e: e.dma_start(out=o, in_=i, **kw)
        self._add(q, fn, reads, writes, dma_q=q, is_out=is_out)

    def emit(self):
        nc = self.nc
        fin = [(s, x) for s, x in self.out_tokens.items()]
        eobj = {"pe": "tensor", "act": "scalar", "dve": "vector", "pool": "gpsimd", "sp": "sync"}
        with nc.Block() as block:
            for ename in self.ENG:
                ops = self.ops[ename]
                extra = fin if ename == "sp" else []

                def body(e, ops=ops, extra=extra):
                    for fn, waits, mysem, inc in ops:
                        for s, x in waits:
                            e.wait_ge(self.sem[s], x)
                        fn(e).then_inc(self.sem[mysem], inc)
                    for s, x in extra:
                        e.wait_ge(self.sem[s], x)
                getattr(block, eobj[ename])(body)

NL = 2
NTOK = 2176
NBLK = 17
C_DEC = 0.6065306597126334
CNAMES = ["IDENT", "ONES", "BLK", "TRS_P", "TRI_P", "SU_P", "NEGM_P", "TRS_S", "TRI_S", "SU_S", "NEGM_S"]
CI = {n: i * 128 for i, n in enumerate(CNAMES)}
C_SEG = 11 * 128
C_EPS = C_SEG + 16
C_GNEPS = C_EPS + 1
C_ONE = C_EPS + 2
NCONST = C_EPS + 4
NPV = 160
PVO = dict(ada_b=0, n1w=48, n2w=56, mu=64, a0=71, k_k=73, k_a=75, r_k=77, ln_w=79, ln_b=81, gla_nw=83,
           dn_cw=85, dn_nw=109, ss_cw=111, ss_cb=135, ss_nw=141, ss_D=143, fnw=145)
NR = 528
RO = dict(w0=0, gkb=256, dnA=512, dndt=516, ssdt=520, ssA=524)
ST_SHAPES = dict(shift=[2, 16, 896], wkv=[2, 16, 4, 64, 64], gla=[2, 16, 4, 64, 64], dnc=[2, 16, 3, 768],
                 dn=[2, 16, 4, 64, 64], ssc=[2, 16, 3, 768], ssm=[2, 16, 4, 64, 128])


def make_consts():
    c = np.zeros((128, NCONST), np.float32)
    s = np.arange(128)[:, None]
    i = np.arange(128)[None, :]
    same = (s // 8) == (i // 8)
    m = {}
    m["IDENT"] = (s == i)
    m["ONES"] = np.ones((128, 128))
    m["BLK"] = (s // 64) == (i // 64)
    m["TRI_P"] = s <= i
    m["TRS_P"] = s < i
    m["SU_P"] = s > i
    m["NEGM_P"] = (m["TRI_P"].astype(np.float32) - 1.0) * 30000.0
    m["TRI_S"] = (s <= i) & same
    m["TRS_S"] = (s < i) & same
    m["SU_S"] = (s > i) & same
    m["NEGM_S"] = (m["TRI_S"].astype(np.float32) - 1.0) * 30000.0
    for n in CNAMES:
        c[:, CI[n]:CI[n] + 128] = m[n].astype(np.float32)
    c[:, C_SEG:C_SEG + 16] = ((np.arange(128)[:, None] // 8) == np.arange(16)[None, :]).astype(np.float32)
    c[:, C_EPS] = 1e-6
    c[:, C_GNEPS] = 64e-5
    c[:, C_ONE] = 1.0
    return c


class BK:
    pass


def build():
    from contextlib import ExitStack
    nc = bass.Bass("TRN2", target_bir_lowering=False)

    def din(name, shape):
        return nc.dram_tensor(name, list(shape), F32, kind="ExternalInput").ap()

    def dout(name, shape):
        return nc.dram_tensor(name, list(shape), F32, kind="ExternalOutput").ap()

    xin = din("xin", [NTOK, 1024])
    ccd = din("cc", [17, 1024])
    std = {k: din("st_" + k, v) for k, v in ST_SHAPES.items()}
    wall = din("wall", [2, 27, 128, 8, 512])
    adad = din("ada", [2, 12, 128, 8, 512])
    pvd = din("pv", [128, 2 * NPV])
    rowsd = din("rows", [128, 2 * NR])
    smd = din("sm", [128, 2 * 4 * 256])
    constd = din("consts", [128, NCONST])
    yout = dout("y", [NTOK, 1024])
    nc.allow_low_precision("bf16 operands (fp32 PSUM accumulation) for the dense projections")
    wbf = nc.dram_tensor("wbf", [2, 27, 128, 8, 512], BF16).ap()
    pod = {k: dout("p_" + k, [v[0]] + v[2:]) for k, v in ST_SHAPES.items()}
    sod = {k: dout("s_" + k, v) for k, v in ST_SHAPES.items()}
    import os as _os
    KDBG = int(_os.environ.get("KDBG", "0"))
    dbgd = dout("dbg", [12, 128, 1024]) if KDBG else None

    with ExitStack() as es:
        S = Sched(nc, es)
        I = S.I
        CONST = S.sb([128, NCONST], "CONST")
        PV = S.sb([128, 2 * NPV], "PV")
        ROWS = S.sb([128, 2 * NR], "ROWS")
        SM = S.sb([128, 2 * 1024], "SM")
        S.dma("pool", CONST[:], constd)
        S.dma("pool", PV[:], pvd)
        S.dma("pool", ROWS[:], rowsd)
        S.dma("pool", SM[:], smd)

        def K(n):
            return CONST[:, CI[n]:CI[n] + 128]
        IDENT, ONES, BLK = K("IDENT"), K("ONES"), K("BLK")
        EPSc = CONST[:, C_EPS:C_EPS + 1]
        GNEPSc = CONST[:, C_GNEPS:C_GNEPS + 1]
        ONEc = CONST[:, C_ONE:C_ONE + 1]

        def pv(l, name, n=1, off=0):
            o = l * NPV + PVO[name] + off
            return PV[:, o:o + n]

        def row(l, name, n):
            o = l * NR + RO[name]
            return ROWS[:, o:o + n]

        def smat(l, j):
            o = l * 1024 + j * 256
            return SM[:, o:o + 256]

        PB = [S.ps([128, 512], "pb%d" % i) for i in range(8)]
        pbi = [0]

        def pb():
            t = PB[pbi[0] % 8]
            pbi[0] += 1
            return t

        WR = [S.sb([128, 4096], "wr%d" % i) for i in range(2)]
        wri = [0]
        WBF = Tile(None, "wbf_dep")

        def wr32(i):
            return V(WR[i], WR[i].h[:].rearrange("p (k c) -> p k c", c=512), None)

        def wr16(i, hf):
            return V(WR[i], WR[i].h[:].bitcast(BF16)[:, hf * 4096:(hf + 1) * 4096].rearrange("p (k c) -> p k c", c=512), hf)

        def wload(ap):
            i = wri[0] % 2
            wri[0] += 1
            t = wr32(i)
            S.dma("sp", t, ap)
            return t

        def wload16(l, sl):
            r = wri[0] % 4
            wri[0] += 1
            t = wr16(r // 2, r % 2)
            S.dma("sp", t, wbf[l, sl], xr=[V(WBF, None, (l, sl))])
            return t

        def mm(out, lhsT, rhs, start=True, stop=True):
            I("pe", "matmul", out=out, lhsT=lhsT, rhs=rhs, start=start, stop=stop, skip_group_check=True)

        def tr(out, in_, npart=128):
            I("pe", "transpose", out=out, in_=in_, identity=CONST[0:npart, 0:npart])

        def act(out, in_, func, bias=None, scale=1.0):
            if bias is None:
                I("act", "activation", out=out, in_=in_, func=func, scale=scale)
            else:
                I("act", "activation", out=out, in_=in_, func=func, bias=bias, scale=scale)

        def tt(out, a, b, op=ALU.mult, eng="dve"):
            I(eng, "tensor_tensor", out=out, in0=a, in1=b, op=op)

        def ts(out, a, s1, op0, s2=None, op1=None, eng="dve"):
            if op1 is None:
                I(eng, "tensor_scalar", out=out, in0=a, scalar1=s1, scalar2=None, op0=op0)
            else:
                I(eng, "tensor_scalar", out=out, in0=a, scalar1=s1, scalar2=s2, op0=op0, op1=op1)

        def stt(out, a, sc, b, op0, op1, eng="dve"):
            I(eng, "scalar_tensor_tensor", out=out, in0=a, scalar=sc, in1=b, op0=op0, op1=op1)

        def cp(out, in_, eng="dve"):
            if eng == "act":
                I("act", "activation", out=out, in_=in_, func=AF.Identity, scale=1.0)
            else:
                I("dve", "tensor_copy", out=out, in_=in_)

        G = [S.sb([128, 256], "g%d" % i) for i in range(28)]

        class GP:
            def __init__(self, pool=None):
                self.i = 0
                self.pool = G if pool is None else pool

            def t(self):
                t = self.pool[self.i]
                self.i += 1
                return t

        class Alias:
            def __init__(self, tile, base, sub):
                self.tile, self.base, self.sub = tile, base, sub

            def __getitem__(self, idx):
                return V(self.tile, self.base[idx], self.sub)

        MODT = S.sb([128, 2, 48, 17], "MODT")
        SC = S.sb([128, 2, 2, 8, 17], "SC")
        CT = S.sb([128, 8, 17], "CT")
        class Cx:
            pass
        CXS = []
        for i_ in range(2):
            c_ = Cx()
            c_.XTM = S.sb([128, 1024], "XTM%d" % i_)
            c_.XT = S.sb([128, 8, 128], "XT%d" % i_)
            c_.HT = S.sb([128, 8, 128], "HT%d" % i_, BF16)
            c_.YT = S.sb([128, 8, 128], "YT%d" % i_, BF16)
            CXS.append(c_)
        CUR = [CXS[0]]

        class Proxy:
            def __init__(self, name):
                self.name = name

            def __getitem__(self, idx):
                return getattr(CUR[0], self.name)[idx]
        XTM, XT, HT, YT = Proxy("XTM"), Proxy("XT"), Proxy("HT"), Proxy("YT")
        HTF = S.sb([128, 8, 128], "HTF")
        HID = S.sb([128, 32, 128], "HID", BF16)
        TMP5 = S.sb([128, 4, 128], "TMP5")
        UA = S.sb([128, 7, 144], "UA")
        XS = S.sb([128, 7, 128], "XS")
        XB = S.sb([128, 6, 176], "XB")
        CV = S.sb([128, 6, 128], "CV")
        PST = {}
        for l in range(NL):
            for mname in ("wkv", "gla", "dn", "ssm"):
                PST[(l, mname)] = S.sb([128, 2, 128], "pst_%s%d" % (mname, l))
                I("dve", "memset", PST[(l, mname)][:], 0.0)
        CARRY = {}
        for l in range(NL):
            CARRY[(l, "rw")] = S.sb([128, 7], "c_rw%d" % l)
            CARRY[(l, "dn")] = S.sb([128, 6, 3], "c_dn%d" % l)
            CARRY[(l, "ss")] = S.sb([128, 6, 3], "c_ss%d" % l)
            for k in ("rw", "dn", "ss"):
                I("dve", "memset", CARRY[(l, k)][:], 0.0)
        SST = [S.sb([128, 16, 2, 128], "sst%d" % i) for i in range(2)]
        for t in SST:
            I("dve", "memset", t[:], 0.0)
        G2 = [Alias(SST[j], SST[j].h[:, s_].rearrange("p r v -> p (r v)"), ("a", s_)) for j in range(2) for s_ in range(16)]
        PADS = [S.sb([128, 2, 128], "pad%d" % i) for i in range(3)]
        for t in PADS:
            I("dve", "memset", t[:], 0.0)

        Pk, Sk = BK(), BK()
        Pk.nseg, Pk.L, Pk.sfx, Pk.nsolve = 1, 128, "_P", 7
        Sk.nseg, Sk.L, Sk.sfx, Sk.nsolve = 16, 8, "_S", 3
        for bk in (Pk, Sk):
            bk.TRI, bk.TRS, bk.SU, bk.NEGM = K("TRI" + bk.sfx), K("TRS" + bk.sfx), K("SU" + bk.sfx), K("NEGM" + bk.sfx)

        def bc(v, shape, axis):
            return v.f(lambda a: a.unsqueeze(axis).to_broadcast(shape))

        def modv(v3, bk, nch=8):
            if bk.nseg == 1:
                return bc(v3[:, :, 0:1], [128, nch, 1, 128], 3)
            return bc(v3[:, :, 1:17], [128, nch, 16, 8], 3)

        def dv(v, bk):
            return v.f(lambda a: a.rearrange("p c (s t) -> p c s t", t=bk.L))

        ctm = XTM
        S.dma("pool", ctm[0:17, :], ccd)
        act(ctm[0:17, :], ctm[0:17, :], AF.Silu)
        for kc in range(8):
            p = pb()
            tr(p[:, 0:17], ctm[0:17, kc * 128:(kc + 1) * 128], 17)
            cp(CT[:, kc, :], p[:, 0:17])
        for l in range(NL):
            for s in range(12):
                w = wload(adad[l, s])
                p = pb()
                for j in range(4):
                    for kc in range(8):
                        mm(p[:, j * 17:(j + 1) * 17], w[:, kc, j * 128:(j + 1) * 128], CT[:, kc, :], kc == 0, kc == 7)
                tt(MODT[:, l, 4 * s:4 * s + 4, :], p[:, 0:68].f(lambda a: a.rearrange("p (j n) -> p j n", n=17)),
                   bc(pv(l, "ada_b", 4, 4 * s), [128, 4, 17], 2), ALU.add)
            stt(SC[:, l, 0], MODT[:, l, 8:16, :], 1.0, bc(pv(l, "n1w", 8), [128, 8, 17], 2), ALU.add, ALU.mult)
            stt(SC[:, l, 1], MODT[:, l, 32:40, :], 1.0, bc(pv(l, "n2w", 8), [128, 8, 17], 2), ALU.add, ALU.mult)

        STG = [wr32(0)] + [V(SST[j], SST[j].h[:].rearrange("p s r v -> p (s r v)").rearrange("p (k c) -> p k c", c=512), None)
                           for j in range(2)]
        for l in range(NL):
            for sl in range(27):
                i = (l * 27 + sl)
                src = STG[i % 3]
                S.dma("sp", src, wall[l, sl])
                dst = wr16(1, i % 2)
                I("dve", "tensor_copy", out=dst[:, 0:4, :], in_=src[:, 0:4, :])
                I("act", "activation", out=dst[:, 4:8, :], in_=src[:, 4:8, :], func=AF.Identity, scale=1.0)
                S.dma("sp", wbf[l, sl], dst, xw=[V(WBF, None, (l, sl))])

        RS = S.sb([128, 128], "RS")

        def norm_mod(bk, scale_bv, shift_bv, outT=None):
            outT = HT if outT is None else outT
            HIDF = V(HID, HID.h[:].rearrange("p c t -> p (c t)").bitcast(F32), None)
            p = pb()
            for kc in range(8):
                sq = HIDF[:, kc * 128:(kc + 1) * 128]
                act(sq, XT[:, kc, :], AF.Square)
                mm(p[:, 0:128], ONES, sq, kc == 0, kc == 7)
            act(RS[:], p[:, 0:128], AF.Sqrt, bias=EPSc, scale=1.0 / 1024.0)
            I("dve", "reciprocal", out=RS[:], in_=RS[:])
            tt(HTF[:], XT[:], bc(RS[:], [128, 8, 128], 1))
            if shift_bv is not None:
                tt(dv(HTF[:], bk), dv(HTF[:], bk), scale_bv)
                tt(dv(outT[:], bk), dv(HTF[:], bk), shift_bv, ALU.add)
            else:
                tt(dv(outT[:], bk), dv(HTF[:], bk), scale_bv)

        def proj_fm(out, w, c0, M):
            for kc in range(8):
                mm(out, w[:, kc, c0:c0 + M], HT[:, kc, :], kc == 0, kc == 7)

        def proj_tm(out, w, c0, N):
            for kc in range(8):
                mm(out, HT[:, kc, :], w[:, kc, c0:c0 + N], kc == 0, kc == 7)

        def sv(v, bk):
            return v.f(lambda a: a.rearrange("p (s t) -> p s t", t=bk.L))

        def lastcol(v, bk, s):
            c = s * bk.L + bk.L - 1
            return v[:, c:c + 1]

        def tri_solve(bk, NTs, Ns, X, g):
            tmpN = [[g.t(), g.t()] for _ in range(2)]
            for k in range(bk.nsolve):
                p = pb()
                for h in range(2):
                    mm(p[:, h * 64:h * 64 + 64], NTs[h][:, 0:128], X[:, h * 64:h * 64 + 64])
                tt(X[:, 0:128], X[:, 0:128], p[:, 0:128], ALU.add)
                yield
                if k == bk.nsolve - 1:
                    break
                for h in range(2):
                    p2 = pb()
                    mm(p2[:, 0:128], Ns[h][:, 0:128], NTs[h][:, 0:128])
                    if k < bk.nsolve - 2:
                        mm(p2[:, 128:256], NTs[h][:, 0:128], Ns[h][:, 0:128])
                    nn = tmpN[h][k % 2]
                    if k < bk.nsolve - 2:
                        cp(nn[:, 0:256], p2[:, 0:256], "act")
                    else:
                        cp(nn[:, 0:128], p2[:, 0:128], "act")
                    NTs[h] = nn
                    Ns[h] = _Shift(nn)
                    yield
            return

        class _Shift:
            def __init__(self, base):
                self.base = base

            def __getitem__(self, idx):
                assert idx == (slice(None), slice(0, 128))
                return self.base[:, 128:256]

        def post_rms(Y, ones, gsize, nw_col, gateT, out, g):
            sq = g.t()
            act(sq[:, 0:128], Y, AF.Square)
            p = pb()
            mm(p[:, 0:128], ones, sq[:, 0:128])
            r = g.t()
            act(r[:, 0:128], p[:, 0:128], AF.Sqrt, bias=EPSc, scale=1.0 / gsize)
            I("dve", "reciprocal", out=r[:, 0:128], in_=r[:, 0:128])
            stt(r[:, 128:256], Y, nw_col, r[:, 0:128], ALU.mult, ALU.mult)
            if gateT is not None:
                tt(out, r[:, 128:256], gateT)
            else:
                cp(out, r[:, 128:256])

        def load_bd(sst, src):
            for h in range(4):
                b = 64 * (h % 2)
                S.dma("pool", sst[b:b + 64, :, h // 2, b:b + 64], src[:, h].rearrange("s d v -> d s v"))

        def store_bd(dst, tile_bd, pr):
            for hh in range(2):
                b = 64 * hh
                S.dma("pool", dst[2 * pr + hh], tile_bd[b:b + 64, b:b + 64], is_out=True)

        OST = [S.sb([128, 128], "ost%d" % i) for i in range(4)]
        osti = [0]

        def ost():
            t = OST[osti[0] % 4]
            osti[0] += 1
            return t

        def state_update(bk, l, mname, pr, lhs, rhs, pcv, sst, outd, last, g, transpose_out=False):
            for s in range(bk.nseg):
                p = pb()
                for k in range(len(lhs)):
                    lv = lhs[k]
                    if bk.nseg > 1:
                        m = g_rot()
                        I("act", "mul", out=m[:, 0:128], in_=lv, mul=CONST[:, C_SEG + s:C_SEG + s + 1])
                        lv = m[:, 0:128]
                    mm(p[:, 0:128], lv, rhs[k], k == 0, k == len(lhs) - 1)
                tmp = g_rot()
                tt(tmp[:, 0:128], p[:, 0:128], BLK)
                if bk.nseg == 1:
                    hp = PST[(l, mname)][:, pr, :]
                    stt(hp, hp, lastcol(pcv, bk, 0), tmp[:, 0:128], ALU.mult, ALU.add)
                    if last:
                        if transpose_out:
                            p2 = pb()
                            tr(p2[:, 0:128], hp)
                            o = ost()
                            cp(o[:], p2[:, 0:128])
                            store_bd(outd[l], o, pr)
                        else:
                            store_bd(outd[l], PST[(l, mname)][:, pr, :], pr)
                else:
                    hp = sst[:, s, pr, :]
                    o = ost()
                    stt(o[:], hp, lastcol(pcv, bk, s), tmp[:, 0:128], ALU.mult, ALU.add)
                    if transpose_out:
                        p2 = pb()
                        tr(p2[:, 0:128], o[:])
                        o2 = ost()
                        cp(o2[:], p2[:, 0:128])
                        o = o2
                    store_bd(outd[l, s], o, pr)

        GR = [S.sb([128, 128], "gr%d" % i) for i in range(6)]
        gri = [0]

        def g_rot():
            t = GR[gri[0] % 6]
            gri[0] += 1
            return t

        def inter(bk, l, mname, pr, sst, out_ps, opT, stop):
            for s in range(bk.nseg):
                hp = PST[(l, mname)][:, pr, :] if bk.nseg == 1 else sst[:, s, pr, :]
                mm(out_ps[:, s * bk.L:(s + 1) * bk.L], hp, opT[:, s * bk.L:(s + 1) * bk.L], s == 0, stop)

        def decay_mask(bk, la_col, out, g):
            t = g.t()
            ts(t[:, 0:128], bk.SU, la_col, ALU.mult)
            p = pb()
            mm(p[:, 0:128], t[:, 0:128], bk.TRI, True, False)
            mm(p[:, 0:128], IDENT, bk.NEGM, False, True)
            act(out, p[:, 0:128], AF.Exp)

        def bc2(v, shape):
            return v.f(lambda a: a.unsqueeze(2).unsqueeze(3).to_broadcast(shape))

        GLT = S.sb([16, 128], "GLT")
        SHT = S.sb([48, 896], "SHT")
        CST = SHT
        CSO = S.sb([48, 768], "CSO")
        CS3 = S.sb([128, 48], "CS3")
        SMT = S.sb([128, 64], "SMT")
        ZT = S.sb([128, 512], "ZT")
        TMP4 = S.sb([128, 4, 128], "TMP4")
        YTM = XTM

        def dump(k, tile3):
            if not KDBG:
                return
            for b2 in range(2):
                p = pb()
                for j in range(4):
                    tr(p[:, j * 128:(j + 1) * 128], tile3[:, 4 * b2 + j, :])
                cp(TMP4[:].f(lambda a: a.rearrange("p c t -> p (c t)")), p[:, 0:512])
                S.dma("pool", dbgd[k, :, b2 * 512:(b2 + 1) * 512], TMP4[:].f(lambda a: a.rearrange("p c t -> p (c t)")), is_out=True)
        SSL = [S.sb([64, 4, 128], "ssl%d" % i) for i in range(1)]

        def rwkv(bk, l, last, wget, sst, pool=None):
            g = GP(pool)
            w0 = wget(0)
            w1 = wget(1)
            nseg, L = bk.nseg, bk.L
            W = L + 1
            sfx = bk.sfx
            UAv = UA[:, :, 0:nseg * W].f(lambda a: a.rearrange("p c (s t) -> p c s t", t=W))
            if nseg == 1:
                cp(UA[:, :, 0:1], CARRY[(l, "rw")][:].f(lambda a: a.unsqueeze(2)))
            else:
                S.dma("pool", SHT[0:16, :], std["shift"][l])
                for c in range(7):
                    p = pb()
                    tr(p[:, 0:16], SHT[0:16, c * 128:(c + 1) * 128], 16)
                    cp(UAv[:, c, :, 0], p[:, 0:16])
            for c in range(7):
                w = w0 if c < 4 else w1
                p = pb()
                proj_fm(p[:, 0:128], w, (c % 4) * 128, 128)
                cp(UAv[:, c, :, 1:W], sv(p[:, 0:128], bk), "act" if c % 2 else "dve")
            p = pb()
            proj_fm(p[0:16, 0:128], w1, 384, 16)
            cp(GLT[0:16, :], p[0:16, 0:128])
            if nseg == 1:
                cp(CARRY[(l, "rw")][:].f(lambda a: a.unsqueeze(2)), UA[:, :, 128:129])
                if last:
                    S.dma("pool", pod["shift"][l].rearrange("(c p) -> p c", p=128), CARRY[(l, "rw")][:], is_out=True,
                          allow_slow_non_contiguous=True)
            else:
                for c in range(7):
                    S.dma("pool", sod["shift"][l][:, c * 128:(c + 1) * 128].rearrange("s p -> p s"), UAv[:, c, :, L],
                          is_out=True, allow_slow_non_contiguous=True)
            ck('rw1')
            XSv = dv(XS[:], bk)
            tt(XSv, UAv[:, :, :, 0:L], UAv[:, :, :, 1:W], ALU.subtract)
            tt(XSv, XSv, bc2(pv(l, "mu", 7), [128, 7, nseg, L]))
            tt(XSv, XSv, UAv[:, :, :, 1:W], ALU.add)
            ck('rw2')
            X6 = XS[:, 6, :]
            T6 = g.t()
            act(T6[:, 0:128], X6, AF.Tanh)
            act(T6[:, 128:256], X6, AF.Sigmoid)
            p = pb()
            mm(p[:, 0:256], T6[:, 0:128], smat(l, 0))
            LAM = g.t()
            tt(LAM[:], p[:, 0:256], row(l, "w0", 256), ALU.add)
            act(LAM[:], LAM[:], AF.Sigmoid)
            AT, GTt = g.t(), g.t()
            for pr in range(2):
                p = pb()
                mm(p[:, 0:128], smat(l, 1)[:, pr * 128:(pr + 1) * 128], X6)
                act(AT[:, pr * 128:(pr + 1) * 128], p[:, 0:128], AF.Sigmoid, bias=pv(l, "a0", 1, pr))
                mm(p[:, 128:256], smat(l, 2)[:, pr * 128:(pr + 1) * 128], T6[:, 128:256])
                cp(GTt[:, pr * 128:(pr + 1) * 128], p[:, 128:256])
            ck('rw3')
            yield
            KK, KP, E, E2, EQT, KC, K2C2, VT, KC2, RT, X, YS, Mt, Bt = [g.t() for _ in range(14)]
            AE = [g.t(), g.t()]
            AC = [g.t(), g.t()]
            Nn = [g.t(), g.t()]
            gsave = g.i
            MASK2 = CONST[:, CI["TRS" + sfx]:CI["TRS" + sfx] + 256]
            TRI3 = CONST[:, CI["TRS" + sfx]:CI["TRS" + sfx] + 384]
            Vp, Up = PADS[0], PADS[1]
            for pr in range(2):
                g.i = gsave
                rT, kT, vT = XS[:, pr, :], XS[:, 2 + pr, :], XS[:, 4 + pr, :]
                aT = AT[:, pr * 128:(pr + 1) * 128]
                ts(KK[:, 0:128], kT, pv(l, "k_k", 1, pr), ALU.mult)
                act(KK[:, 128:256], KK[:, 0:128], AF.Square)
                p = pb()
                mm(p[:, 0:128], BLK, KK[:, 128:256])
                act(KK[:, 128:256], p[:, 0:128], AF.Sqrt, bias=EPSc)
                I("dve", "reciprocal", out=KK[:, 128:256], in_=KK[:, 128:256])
                tt(KK[:, 0:128], KK[:, 0:128], KK[:, 128:256])
                ts(KP[:, 0:128], aT, -1.0, ALU.add, pv(l, "k_a", 1, pr), ALU.mult)
                stt(KP[:, 0:128], KP[:, 0:128], 1.0, kT, ALU.add, ALU.mult)
                tt(KP[:, 128:256], KK[:, 0:128], aT)
                stt(RT[:, 128:256], rT, pv(l, "r_k", 1, pr), KP[:, 0:128], ALU.mult, ALU.mult)
                p3 = pb()
                mm(p3[:, 0:128], BLK, RT[:, 128:256])
                tt(Bt[:, 128:256], p3[:, 0:128], vT)
                p = pb()
                mm(p[:, 0:384], LAM[:, pr * 128:(pr + 1) * 128], TRI3)
                act(E[:, 0:256], p[:, 0:256], AF.Exp, scale=-C_DEC)
                act(E2[:, 0:128], p[:, 256:384], AF.Exp, scale=-C_DEC)
                act(E2[:, 128:256], p[:, 128:256], AF.Exp, scale=C_DEC)
                tt(EQT[:, 0:128], KK[:, 0:128], E[:, 0:128])
                tt(EQT[:, 128:256], rT, E[:, 128:256])
                tt(KC[:, 0:128], KP[:, 0:128], E2[:, 128:256])
                tt(KC[:, 128:256], KP[:, 128:256], E2[:, 128:256])
                tt(K2C2[:, 0:128], KP[:, 0:128], E2[:, 0:128])
                stt(K2C2[:, 128:256], KP[:, 128:256], -1.0, E2[:, 0:128], ALU.mult, ALU.mult)
                ck('rw4')
                yield
                p = pb()
                tr(p[:, 0:128], vT)
                tr(p[:, 128:256], K2C2[:, 0:128])
                tr(p[:, 256:384], K2C2[:, 128:256])
                cp(VT[:, 0:128], p[:, 0:128])
                ck('rw4a1')
                cp(Vp[:, 0, 0:64], p[:, 0:64])
                cp(Vp[:, 1, 64:128], p[:, 64:128])
                ck('rw4a2')
                cp(KC2[:, 0:256], p[:, 128:384])
                ck('rw4b')
                yield
                for hh in range(2):
                    b = 64 * hh
                    if hh == 1:
                        ck('rw4c')
                    p = pb()
                    mm(p[:, 0:256], KC[b:b + 64, 0:128], EQT[b:b + 64, 0:256])
                    mm(p[:, 256:512], KC[b:b + 64, 128:256], EQT[b:b + 64, 0:256])
                    tt(AE[hh][:, 0:256], p[:, 0:256], MASK2)
                    stt(AC[hh][:, 0:256], p[:, 256:512], -1.0, MASK2, ALU.mult, ALU.mult)
                    p2 = pb()
                    tr(p2[:, 0:128], AC[hh][:, 0:128])
                    cp(Nn[hh][:, 0:128], p2[:, 0:128], "act")
                    yield
                ck('rw5')
                if nseg == 1:
                    p = pb()
                    mm(p[:, 0:128], EQT[:, 0:128], PST[(l, "wkv")][:, pr, :], True, False)
                    for hh in range(2):
                        mm(p[:, 0:128], AE[hh][:, 0:128], Vp[:, hh, :], False, hh == 1)
                    cp(X[:, 0:128], p[:, 0:128])
                    yield
                else:
                    p = pb()
                    inter(bk, l, "wkv", pr, sst, p, EQT[:, 0:128], False)
                    for hh in range(2):
                        mm(p[:, 0:128], Vp[:, hh, :], AE[hh][:, 0:128], False, hh == 1)
                    cp(RT[:, 0:128], p[:, 0:128])
                    yield
                    p = pb()
                    tr(p[:, 0:128], RT[:, 0:128])
                    cp(X[:, 0:128], p[:, 0:128])
                    yield
                ck('rw6')
                yield from tri_solve(bk, [AC[0], AC[1]], [Nn[0], Nn[1]], X, g)
                ck('rw7')
                cp(Up[:, 0, 0:64], X[:, 0:64])
                cp(Up[:, 1, 64:128], X[:, 64:128])
                p = pb()
                inter(bk, l, "wkv", pr, sst, p, EQT[:, 128:256], False)
                for hh in range(2):
                    mm(p[:, 0:128], Vp[:, hh, :], AE[hh][:, 128:256], False, False)
                    mm(p[:, 0:128], Up[:, hh, :], AC[hh][:, 128:256], False, hh == 1)
                ck('rw8')
                cp(YS[:, 0:128], p[:, 0:128])
                act(YS[:, 128:256], p[:, 0:128], AF.Square)
                yield
                p2 = pb()
                mm(p2[:, 0:256], BLK, YS[:, 0:256])
                ts(Mt[:, 0:256], p2[:, 0:256], 1.0 / 64.0, ALU.mult)
                stt(Bt[:, 0:128], Mt[:, 0:128], -1.0, Mt[:, 0:128], ALU.mult, ALU.mult)
                tt(Mt[:, 128:256], Mt[:, 128:256], Bt[:, 0:128], ALU.add)
                act(Mt[:, 128:256], Mt[:, 128:256], AF.Sqrt, bias=GNEPSc)
                I("dve", "reciprocal", out=Mt[:, 128:256], in_=Mt[:, 128:256])
                tt(YS[:, 0:128], YS[:, 0:128], Mt[:, 0:128], ALU.subtract)
                tt(YS[:, 0:128], YS[:, 0:128], Mt[:, 128:256])
                ts(YS[:, 0:128], YS[:, 0:128], pv(l, "ln_w", 1, pr), ALU.mult, pv(l, "ln_b", 1, pr), ALU.add)
                tt(YS[:, 0:128], YS[:, 0:128], Bt[:, 128:256], ALU.add)
                tt(YT[:, pr, :], YS[:, 0:128], GTt[:, pr * 128:(pr + 1) * 128])
                yield
                ck('rw9')
                state_update(bk, l, "wkv", pr, [KC2[:, 0:128], KC2[:, 128:256]], [VT[:, 0:128], X[:, 0:128]],
                             E[:, 128:256], sst, (pod if nseg == 1 else sod)["wkv"], last, g, transpose_out=True)

        def gla(bk, l, last, wget, sst, pool=None):
            g = GP(pool)
            w2 = wget(2)
            QK = [g.t(), g.t()]
            for c in range(4):
                p = pb()
                proj_fm(p[:, 0:128], w2, c * 128, 128)
                cp(QK[c % 2][:, (c // 2) * 128:(c // 2) * 128 + 128], p[:, 0:128], "act" if c % 2 else "dve")
            KTM = g.t()
            p = pb()
            proj_tm(p[:, 0:256], w2, 256, 256)
            cp(KTM[:], p[:, 0:256], "act")
            w3 = wget(3)
            VTM = g.t()
            p = pb()
            proj_tm(p[:, 0:256], w3, 0, 256)
            cp(VTM[:], p[:, 0:256])
            Vp = [PADS[0], PADS[2]]
            for pr in range(2):
                cp(Vp[pr][:, 0, 0:64], p[:, pr * 128:pr * 128 + 64])
                cp(Vp[pr][:, 1, 64:128], p[:, pr * 128 + 64:pr * 128 + 128])
            GT = g.t()
            for pr in range(2):
                p = pb()
                proj_fm(p[:, 0:128], w3, 256 + pr * 128, 128)
                act(GT[:, pr * 128:(pr + 1) * 128], p[:, 0:128], AF.Silu)
            p = pb()
            mm(p[:, 0:256], GLT[0:16, :], smat(l, 3)[0:16, :])
            LA = g.t()
            tt(LA[:], p[:, 0:256], row(l, "gkb", 256), ALU.add)
            act(LA[:], LA[:], AF.Sigmoid)
            act(LA[:], LA[:], AF.Ln)
            p = pb()
            mm(p[:, 0:256], bk.SU, LA[:])
            K2 = g.t()
            act(K2[:], p[:, 0:256], AF.Exp, scale=1.0 / 16.0)
            tt(K2[:], K2[:], KTM[:])
            yield
            E, QKh, YS = g.t(), g.t(), g.t()
            A = [g.t(), g.t()]
            gsave = g.i
            for pr in range(2):
                g.i = gsave
                p = pb()
                mm(p[:, 0:128], LA[:, pr * 128:(pr + 1) * 128], bk.TRI)
                act(E[:, 0:128], p[:, 0:128], AF.Exp, scale=1.0 / 16.0)
                act(E[:, 128:256], p[:, 0:128], AF.Exp, scale=-1.0 / 16.0)
                stt(QKh[:, 0:128], QK[pr][:, 0:128], 0.125, E[:, 0:128], ALU.mult, ALU.mult)
                tt(QKh[:, 128:256], QK[pr][:, 128:256], E[:, 128:256])
                for hh in range(2):
                    b = 64 * hh
                    p = pb()
                    mm(p[:, 0:128], QKh[b:b + 64, 128:256], QKh[b:b + 64, 0:128])
                    tt(A[hh][:, 0:128], p[:, 0:128], bk.TRI)
                    yield
                p = pb()
                inter(bk, l, "gla", pr, sst, p, QKh[:, 0:128], False)
                for hh in range(2):
                    mm(p[:, 0:128], Vp[pr][:, hh, :], A[hh][:, 0:128], False, hh == 1)
                cp(YS[:, 0:128], p[:, 0:128])
                yield
                post_rms(YS[:, 0:128], BLK, 64.0, pv(l, "gla_nw", 1, pr), GT[:, pr * 128:(pr + 1) * 128], YT[:, 2 + pr, :], g)
                yield
                state_update(bk, l, "gla", pr, [K2[:, pr * 128:(pr + 1) * 128]], [VTM[:, pr * 128:(pr + 1) * 128]],
                             E[:, 0:128], sst, (pod if bk.nseg == 1 else sod)["gla"], last, g)
                yield

        def conv_in(bk, l, ckey, skey):
            nseg, L = bk.nseg, bk.L
            W = L + 3
            XBv = XB[:, :, 0:nseg * W].f(lambda a: a.rearrange("p c (s t) -> p c s t", t=W))
            if nseg == 1:
                cp(XBv[:, :, 0, 0:3], CARRY[(l, ckey)][:])
            else:
                S.dma("pool", CST[0:48, 0:768], std[skey][l].rearrange("s i c -> (s i) c"))
                for c in range(6):
                    p = pb()
                    tr(p[:, 0:48], CST[0:48, c * 128:(c + 1) * 128], 48)
                    cp(XBv[:, c, :, 0:3], p[:, 0:48].f(lambda a: a.rearrange("p (s i) -> p s i", i=3)))
            return XBv

        def conv_run(bk, l, ckey, skey, cwname, XBv, last):
            nseg, L = bk.nseg, bk.L
            W = L + 3
            if nseg == 1:
                cp(CARRY[(l, ckey)][:], XBv[:, :, 0, L:L + 3])
                if last:
                    for c in range(6):
                        S.dma("pool", pod[skey][l][:, c * 128:(c + 1) * 128].rearrange("i p -> p i"), CARRY[(l, ckey)][:, c, :],
                              is_out=True, allow_slow_non_contiguous=True)
            else:
                for c in range(6):
                    cp(CS3[:].f(lambda a: a.rearrange("p (s i) -> p s i", i=3)), XBv[:, c, :, L:L + 3])
                    p = pb()
                    tr(p[0:48, 0:128], CS3[:])
                    cp(CSO[0:48, c * 128:(c + 1) * 128], p[0:48, 0:128], "act")
                S.dma("pool", sod[skey][l].rearrange("s i c -> (s i) c"), CSO[:], is_out=True)
            CVv = dv(CV[:], bk)
            TMv = dv(HTF[:, 0:6, :], bk)
            for i in range(4):
                wv = bc2(pv(l, cwname, 6, i * 6), [128, 6, nseg, L])
                if i == 0:
                    tt(CVv, XBv[:, :, :, 0:L], wv)
                else:
                    tt(TMv, XBv[:, :, :, i:i + L], wv)
                    tt(CVv, CVv, TMv, ALU.add)

        def softplus_cols(dst, src, biasrow):
            tt(dst, src, biasrow, ALU.add)
            act(dst, dst, AF.Exp)
            act(dst, dst, AF.Ln, bias=ONEc)

        def dnet(bk, l, last, wget, sst, pool=None):
            g = GP(pool)
            nseg, L = bk.nseg, bk.L
            W = L + 3
            XBv = conv_in(bk, l, "dn", "dnc")
            w4 = wget(4)
            for c in range(4):
                p = pb()
                proj_fm(p[:, 0:128], w4, c * 128, 128)
                cp(XBv[:, c, :, 3:W], sv(p[:, 0:128], bk), "act" if c % 2 else "dve")
            w5 = wget(5)
            for c in range(2):
                p = pb()
                proj_fm(p[:, 0:128], w5, c * 128, 128)
                cp(XBv[:, 4 + c, :, 3:W], sv(p[:, 0:128], bk), "act" if c % 2 else "dve")
            for c in range(2):
                p = pb()
                proj_fm(p[:, 0:128], w5, 256 + c * 128, 128)
                act(ZT[:, c * 128:(c + 1) * 128], p[:, 0:128], AF.Silu)
            w6 = wget(6)
            p = pb()
            proj_tm(p[:, 0:12], w6, 0, 12)
            cp(SMT[:, 0:12], p[:, 0:12])
            for c in range(2):
                p = pb()
                proj_fm(p[:, 0:128], w6, 12 + c * 128, 128)
                act(ZT[:, 256 + c * 128:256 + (c + 1) * 128], p[:, 0:128], AF.Silu)
            conv_run(bk, l, "dn", "dnc", "dn_cw", XBv, last)
            act(CV[:], CV[:], AF.Silu)
            act(SMT[:, 16:20], SMT[:, 4:8], AF.Sigmoid)
            softplus_cols(SMT[:, 20:24], SMT[:, 0:4], row(l, "dndt", 4))
            act(SMT[:, 24:28], row(l, "dnA", 4), AF.Exp)
            stt(SMT[:, 28:32], SMT[:, 20:24], -1.0, SMT[:, 24:28], ALU.mult, ALU.mult)
            p = pb()
            mm(p[:, 0:4], bk.SU, SMT[:, 28:32])
            act(SMT[:, 32:36], p[:, 0:4], AF.Exp)
            yield
            SQ, LB, ELb, R, X, QE, YS, K2, SQ2 = [g.t() for _ in range(9)]
            KQ = [g.t(), g.t()]
            ET = [g.t(), g.t()]
            A = [g.t(), g.t()]
            T0 = [g.t(), g.t()]
            Nn = [g.t(), g.t()]
            NT = [g.t(), g.t()]
            gsave = g.i
            Wp = PADS[2]
            for pr in range(2):
                g.i = gsave
                for which, (src, dst, scl) in enumerate(((CV[:, 2 + pr, :], KQ[pr][:, 0:128], 1.0),
                                                         (CV[:, pr, :], KQ[pr][:, 128:256], 0.125))):
                    sqt = SQ2 if which else SQ
                    act(sqt[:, 0:128], src, AF.Square)
                    p = pb()
                    mm(p[:, 0:128], BLK, sqt[:, 0:128])
                    act(sqt[:, 128:256], p[:, 0:128], AF.Sqrt, bias=EPSc)
                    I("dve", "reciprocal", out=sqt[:, 128:256], in_=sqt[:, 128:256])
                    stt(dst, src, scl, sqt[:, 128:256], ALU.mult, ALU.mult)
                tt(LB[:, 0:128].f(lambda a: a.rearrange("p (h d) -> p h d", d=64)),
                   ONES.f(lambda a: a.rearrange("p (h d) -> p h d", d=64)),
                   bc(SMT[:, 28 + 2 * pr:30 + 2 * pr], [128, 2, 64], 2))
                p = pb()
                mm(p[:, 0:128], LB[:, 0:128], bk.TRI)
                act(ELb[:, 0:128], p[:, 0:128], AF.Exp)
                yield
                p = pb()
                inter(bk, l, "dn", pr, sst, p, KQ[pr][:, 0:128], True)
                tt(R[:, 0:128], p[:, 0:128], ELb[:, 0:128])
                tt(R[:, 0:128], CV[:, 4 + pr, :], R[:, 0:128], ALU.subtract)
                yield
                p = pb()
                tr(p[:, 0:128], R[:, 0:128])
                for hh in range(2):
                    h = 2 * pr + hh
                    ts(X[:, hh * 64:hh * 64 + 64], p[:, hh * 64:hh * 64 + 64], SMT[:, 16 + h:17 + h], ALU.mult)
                yield
                tt(QE[:, 0:128], KQ[pr][:, 128:256], ELb[:, 0:128])
                p = pb()
                tr(p[:, 0:128], KQ[pr][:, 0:128])
                for hh in range(2):
                    h = 2 * pr + hh
                    ts(K2[:, hh * 64:hh * 64 + 64], p[:, hh * 64:hh * 64 + 64], SMT[:, 32 + h:33 + h], ALU.mult)
                yield
                for hh in range(2):
                    h = 2 * pr + hh
                    b = 64 * hh
                    decay_mask(bk, SMT[:, 28 + h:29 + h], ET[hh][:, 0:128], g)
                    tt(ET[hh][:, 128:256], ET[hh][:, 0:128], bk.TRS)
                    p = pb()
                    mm(p[:, 0:256], KQ[pr][b:b + 64, 0:128], KQ[pr][b:b + 64, 0:256])
                    tt(A[hh][:, 0:128], p[:, 128:256], ET[hh][:, 0:128])
                    tt(T0[hh][:, 0:128], p[:, 0:128], ET[hh][:, 128:256])
                    p2 = pb()
                    tr(p2[:, 0:128], T0[hh][:, 0:128])
                    ts(Nn[hh][:, 0:128], p2[:, 0:128], SMT[:, 16 + h:17 + h], ALU.mult, -1.0, ALU.mult)
                    p3 = pb()
                    tr(p3[:, 0:128], Nn[hh][:, 0:128])
                    cp(NT[hh][:, 0:128], p3[:, 0:128], "act")
                    g.i -= 1
                    yield
                yield from tri_solve(bk, [NT[0], NT[1]], [Nn[0], Nn[1]], X, g)
                cp(Wp[:, 0, 0:64], X[:, 0:64])
                cp(Wp[:, 1, 64:128], X[:, 64:128])
                p = pb()
                inter(bk, l, "dn", pr, sst, p, QE[:, 0:128], False)
                for hh in range(2):
                    mm(p[:, 0:128], Wp[:, hh, :], A[hh][:, 0:128], False, hh == 1)
                cp(YS[:, 0:128], p[:, 0:128])
                yield
                post_rms(YS[:, 0:128], BLK, 64.0, pv(l, "dn_nw", 1, pr), ZT[:, pr * 128:(pr + 1) * 128], YT[:, 4 + pr, :], g)
                yield
                state_update(bk, l, "dn", pr, [K2[:, 0:128]], [X[:, 0:128]], ELb[:, 0:128], sst,
                             (pod if nseg == 1 else sod)["dn"], last, g)
                yield

        def ssd(bk, l, last, wget, sst, pool=None):
            g = GP(pool)
            nseg, L = bk.nseg, bk.L
            W = L + 3
            XBv = conv_in(bk, l, "ss", "ssc")
            w7 = wget(7)
            for c in range(4):
                p = pb()
                proj_fm(p[:, 0:128], w7, c * 128, 128)
                cp(XBv[:, c, :, 3:W], sv(p[:, 0:128], bk), "act" if c % 2 else "dve")
            w8 = wget(8)
            for c in range(2):
                p = pb()
                proj_fm(p[:, 0:128], w8, c * 128, 128)
                cp(XBv[:, 4 + c, :, 3:W], sv(p[:, 0:128], bk), "act" if c % 2 else "dve")
            conv_run(bk, l, "ss", "ssc", "ss_cw", XBv, last)
            for c in range(6):
                act(CV[:, c, :], CV[:, c, :], AF.Silu, bias=pv(l, "ss_cb", 1, c))
            softplus_cols(SMT[:, 40:44], SMT[:, 8:12], row(l, "ssdt", 4))
            act(SMT[:, 44:48], row(l, "ssA", 4), AF.Exp)
            stt(SMT[:, 48:52], SMT[:, 40:44], -1.0, SMT[:, 44:48], ALU.mult, ALU.mult)
            p = pb()
            mm(p[:, 0:4], bk.SU, SMT[:, 48:52])
            act(SMT[:, 52:56], p[:, 0:4], AF.Exp)
            yield
            BTM, X2, YS, LB = [g.t() for _ in range(4)]
            ET = [g.t(), g.t()]
            A = [g.t(), g.t()]
            EL = [g.t(), g.t()]
            CH = [g.t(), g.t()]
            gsave = g.i
            Xp = PADS[1]
            outd = (pod if nseg == 1 else sod)["ssm"]
            for pr in range(2):
                g.i = gsave
                p = pb()
                tr(p[:, 0:128], CV[:, 2 + pr, :])
                tr(p[:, 128:256], CV[:, pr, :])
                cp(BTM[:, 0:128], p[:, 0:128])
                for hh in range(2):
                    h = 2 * pr + hh
                    ts(Xp[:, hh, hh * 64:hh * 64 + 64], p[:, 128 + hh * 64:128 + hh * 64 + 64], SMT[:, 40 + h:41 + h], ALU.mult)
                    ts(X2[:, hh * 64:hh * 64 + 64], Xp[:, hh, hh * 64:hh * 64 + 64], SMT[:, 52 + h:53 + h], ALU.mult)
                pG = pb()
                mm(pG[:, 0:128], CV[:, 2 + pr, :], CV[:, 4 + pr, :])
                for hh in range(2):
                    h = 2 * pr + hh
                    decay_mask(bk, SMT[:, 48 + h:49 + h], ET[hh][:, 0:128], g)
                    g.i -= 1
                    tt(A[hh][:, 0:128], pG[:, 0:128], ET[hh][:, 0:128])
                    ts(LB[:, 0:128], ONES, SMT[:, 48 + h:49 + h], ALU.mult)
                    p = pb()
                    mm(p[:, 0:128], LB[:, 0:128], bk.TRI)
                    act(EL[hh][:, 0:128], p[:, 0:128], AF.Exp)
                    tt(CH[hh][:, 0:128], CV[:, 4 + pr, :], EL[hh][:, 0:128])
                yield
                pY = pb()
                for hh in range(2):
                    reg = pY[:, hh * 128:(hh + 1) * 128]
                    for s in range(nseg):
                        hg = PST[(l, "ssm")][:, pr, :] if nseg == 1 else sst[:, s, pr, :]
                        mm(reg[:, s * L:(s + 1) * L], hg, CH[hh][:, s * L:(s + 1) * L], s == 0, False)
                    mm(reg, Xp[:, hh, :], A[hh][:, 0:128], False, True)
                cp(YS[0:64, 0:128], pY[0:64, 0:128])
                cp(YS[64:128, 0:128], pY[64:128, 128:256])
                yield
                stt(YS[:, 0:128], CV[:, pr, :], pv(l, "ss_D", 1, pr), YS[:, 0:128], ALU.mult, ALU.add)
                tt(YS[:, 0:128], YS[:, 0:128], ZT[:, 256 + pr * 128:256 + (pr + 1) * 128])
                post_rms(YS[:, 0:128], ONES, 128.0, pv(l, "ss_nw", 1, pr), None, YT[:, 6 + pr, :], g)
                yield
                for s in range(nseg):
                    lv = BTM[:, 0:128]
                    if nseg > 1:
                        m = g_rot()
                        I("act", "mul", out=m[:, 0:128], in_=lv, mul=CONST[:, C_SEG + s:C_SEG + s + 1])
                        lv = m[:, 0:128]
                    p = pb()
                    mm(p[:, 0:128], lv, X2[:, 0:128])
                    if nseg == 1:
                        hg = PST[(l, "ssm")][:, pr, :]
                        dest = hg
                    else:
                        hg = sst[:, s, pr, :]
                        dest = ost()[:]
                    for hh in range(2):
                        stt(dest[:, hh * 64:hh * 64 + 64], hg[:, hh * 64:hh * 64 + 64], lastcol(EL[hh][:, 0:128], bk, s),
                            p[:, hh * 64:hh * 64 + 64], ALU.mult, ALU.add)
                    if nseg > 1 or last:
                        for hh in range(2):
                            p2 = pb()
                            tr(p2[0:64, 0:128], dest[:, hh * 64:hh * 64 + 64])
                            o = ost()
                            cp(o[0:64, :], p2[0:64, 0:128], "act")
                            dd = outd[l][2 * pr + hh] if nseg == 1 else outd[l, s, 2 * pr + hh]
                            S.dma("pool", dd, o[0:64, :], is_out=True)

        import os
        class _Stop(Exception):
            pass
        kstop = os.environ.get('KSTOP', '')
        def ck(name):
            if name == kstop:
                raise _Stop()
        blks = [int(x) for x in os.environ.get('KBLKS', ','.join(str(i) for i in range(NBLK))).split(',')]
        try:
          ck('setup')
          def with_cx(cx, gen):
              while True:
                  CUR[0] = cx
                  try:
                      next(gen)
                  except StopIteration:
                      return
                  yield

          def rr_gen(gens):
              gens = list(gens)
              while gens:
                  for g_ in list(gens):
                      try:
                          next(g_)
                      except StopIteration:
                          gens.remove(g_)
                      yield

          def run_seq(gen):
              for _ in gen:
                  pass

          def mix_gen(blk, bk, l, last, samp):
              def wget(idx, l=l):
                  return wload16(l, idx)
              if not samp:
                  yield from rr_gen([rwkv(bk, l, last, wget, None, G), dnet(bk, l, last, wget, None, G2)])
                  yield from rr_gen([gla(bk, l, last, wget, None, G), ssd(bk, l, last, wget, None, G2)])
              else:
                  sst = SST[0]
                  I("dve", "memset", sst[:], 0.0)
                  load_bd(sst, std["wkv"][l])
                  for j in range(8):
                      p = pb()
                      for q in range(4):
                          tr(p[:, q * 128:(q + 1) * 128], sst[:, 2 * j + q // 2, q % 2, :])
                      cp(sst[:, 2 * j:2 * j + 2, :, :].f(lambda a: a.rearrange("p s r v -> p (s r v)")), p[:, 0:512])
                  yield
                  yield from rwkv(bk, l, last, wget, sst)
                  sst = SST[1]
                  I("dve", "memset", sst[:], 0.0)
                  load_bd(sst, std["gla"][l])
                  yield from gla(bk, l, last, wget, sst)
                  sst = SST[0]
                  I("dve", "memset", sst[:], 0.0)
                  load_bd(sst, std["dn"][l])
                  yield from dnet(bk, l, last, wget, sst)
                  sst = SST[1]
                  for s in range(16):
                      sl = SSL[0]
                      S.dma("pool", sl[:], std["ssm"][l, s].rearrange("h p n -> p h n"))
                      p = pb()
                      for h in range(4):
                          tr(p[:, h * 64:(h + 1) * 64], sl[0:64, h, :], 64)
                      cp(sst[:, s, :, :].f(lambda a: a.rearrange("p r v -> p (r v)")), p[:, 0:256])
                  yield
                  yield from ssd(bk, l, last, wget, sst)

          def dense_gen(blk, bk, l, samp):
              def wget(idx, l=l):
                  return wload16(l, idx)
              for half in range(2):
                  w = wget(9 + half)
                  p = pb()
                  for j in range(4):
                      for c8 in range(8):
                          mm(p[:, j * 128:(j + 1) * 128], w[:, c8, j * 128:(j + 1) * 128], YT[:, c8, :], c8 == 0, c8 == 7)
                  pv4 = p[:, 0:512].f(lambda a: a.rearrange("p (c s t) -> p c s t", c=4, t=bk.L))
                  tt(dv(TMP4[:], bk), pv4, modv(MODT[:, l, 16 + 4 * half:20 + 4 * half, :], bk, 4))
                  tt(XT[:, 4 * half:4 * half + 4, :], XT[:, 4 * half:4 * half + 4, :], TMP4[:], ALU.add)
                  yield
              if samp:
                  dump(6 * l + 2, XT)
              norm_mod(bk, modv(SC[:, l, 1], bk), modv(MODT[:, l, 24:32, :], bk))
              yield
              for s8 in range(8):
                  w = wget(11 + s8)
                  p = pb()
                  for j in range(4):
                      for kc in range(8):
                          mm(p[:, j * 128:(j + 1) * 128], w[:, kc, j * 128:(j + 1) * 128], HT[:, kc, :], kc == 0, kc == 7)
                  hv = HID[:, 4 * s8:4 * s8 + 4, :]
                  tmpr = (TMP4 if s8 % 2 else TMP5)
                  act(tmpr[:], p[:, 0:512].f(lambda a: a.rearrange("p (c t) -> p c t", t=128)), AF.Relu)
                  tt(hv, tmpr[:], tmpr[:], ALU.mult)
                  yield
              for half in range(2):
                  p = pb()
                  for fcg in range(4):
                      w = wget(19 + half * 4 + fcg)
                      for j in range(4):
                          for f8 in range(8):
                              mm(p[:, j * 128:(j + 1) * 128], w[:, f8, j * 128:(j + 1) * 128], HID[:, fcg * 8 + f8, :],
                                 fcg == 0 and f8 == 0 and j == 0, fcg == 3 and f8 == 7)
                  pv4 = p[:, 0:512].f(lambda a: a.rearrange("p (c s t) -> p c s t", c=4, t=bk.L))
                  tt(dv(TMP4[:], bk), pv4, modv(MODT[:, l, 40 + 4 * half:44 + 4 * half, :], bk, 4))
                  tt(XT[:, 4 * half:4 * half + 4, :], XT[:, 4 * half:4 * half + 4, :], TMP4[:], ALU.add)
                  yield
              if samp:
                  dump(6 * l + 4, XT)
              if l == NL - 1:
                  norm_mod(bk, bc2(pv(0, "fnw", 8), [128, 8, bk.nseg, bk.L]), None, HTF)
                  for b2 in range(2):
                      p = pb()
                      for j in range(4):
                          tr(p[:, j * 128:(j + 1) * 128], HTF[:, 4 * b2 + j, :])
                      cp(YTM[:, b2 * 512:(b2 + 1) * 512], p[:, 0:512])
                  S.dma("pool", yout[blk * 128:(blk + 1) * 128, :], YTM[:], is_out=True)

          pending = None
          for bi, blk in enumerate(blks):
              cx = CXS[bi % 2]
              CUR[0] = cx
              bk = Pk if blk < 16 else Sk
              last = blk == max(b for b in blks if b < 16) if blk < 16 else False
              samp = blk == 16
              S.dma("pool", XTM[:], xin[blk * 128:(blk + 1) * 128, :])
              for b2 in range(2):
                  p = pb()
                  for j in range(4):
                      tr(p[:, j * 128:(j + 1) * 128], XTM[:, (4 * b2 + j) * 128:(4 * b2 + j + 1) * 128])
                  cp(XT[:, 4 * b2:4 * b2 + 4, :], p[:, 0:512].f(lambda a: a.rearrange("p (c t) -> p c t", t=128)))
              for l in range(NL):
                  CUR[0] = cx
                  norm_mod(bk, modv(SC[:, l, 0], bk), modv(MODT[:, l, 0:8, :], bk))
                  M = with_cx(cx, mix_gen(blk, bk, l, last, samp))
                  if l == 0 and pending is not None:
                      run_seq(rr_gen([M, pending]))
                      pending = None
                  else:
                      run_seq(M)
                  D = with_cx(cx, dense_gen(blk, bk, l, samp))
                  if l == NL - 1 and not KDBG:
                      pending = D
                  else:
                      run_seq(D)
          if pending is not None:
              run_seq(pending)
        except _Stop:
            pass
        print('opcounts', {e: len(v) for e, v in S.ops.items()}, flush=True)
        S.emit()
    return nc


W_IN_COLS = None


def _win_colmap():
    def rng(a, b):
        return list(range(a, b))
    slots = []
    slots.append(rng(0, 512))
    slots.append(rng(512, 896) + rng(1920, 1936))
    slots.append(rng(896, 1408))
    slots.append(rng(1408, 1920))
    slots.append(rng(1936, 2448))
    slots.append(rng(2448, 2960))
    slots.append(rng(2960, 2968) + rng(3992, 3996) + rng(2968, 3224))
    slots.append(rng(3224, 3736))
    slots.append(rng(3736, 3992))
    return slots


def _tile_rows(w, cols):
    out = np.zeros((128, 8, 512), np.float32)
    sub = w[:, cols]
    out[:, :, :len(cols)] = sub.reshape(8, 128, len(cols)).transpose(1, 0, 2)
    return out


_NC_CACHE = {}


def kernel(**inp):
    f = lambda k: np.ascontiguousarray(np.asarray(inp[k], dtype=np.float32))
    wall = np.zeros((2, 27, 128, 8, 512), np.float32)
    adaw = np.zeros((2, 12, 128, 8, 512), np.float32)
    cm = _win_colmap()
    w_in, w_out, w_up, w_down, ada_w = f("w_in"), f("w_out"), f("w_up"), f("w_down"), f("ada_w")
    for l in range(2):
        for s in range(9):
            wall[l, s] = _tile_rows(w_in[l], cm[s])
        for s in range(2):
            wall[l, 9 + s] = _tile_rows(w_out[l], list(range(s * 512, (s + 1) * 512)))
        for s in range(8):
            wall[l, 11 + s] = _tile_rows(w_up[l], list(range(s * 512, (s + 1) * 512)))
        for half in range(2):
            for fcg in range(4):
                blk = w_down[l][fcg * 1024:(fcg + 1) * 1024, half * 512:(half + 1) * 512]
                wall[l, 19 + half * 4 + fcg] = blk.reshape(8, 128, 512).transpose(1, 0, 2)
        for s in range(12):
            adaw[l, s] = _tile_rows(ada_w[l], list(range(s * 512, (s + 1) * 512)))
    pvv = np.zeros((128, 2 * NPV), np.float32)
    rows = np.zeros((128, 2 * NR), np.float32)
    sm = np.zeros((128, 2 * 1024), np.float32)

    def putv(l, name, vec, off=0):
        n = vec.shape[0] // 128
        pvv[:, l * NPV + PVO[name] + off:l * NPV + PVO[name] + off + n] = vec.reshape(n, 128).T
    for l in range(2):
        putv(l, "ada_b", f("ada_b")[l])
        putv(l, "n1w", f("norm1_w")[l])
        putv(l, "n2w", f("norm2_w")[l])
        putv(l, "mu", f("rwkv_mu")[l])
        putv(l, "a0", f("rwkv_a0")[l])
        putv(l, "k_k", f("rwkv_k_k")[l])
        putv(l, "k_a", f("rwkv_k_a")[l])
        putv(l, "r_k", f("rwkv_r_k")[l])
        putv(l, "ln_w", f("rwkv_ln_w")[l])
        putv(l, "ln_b", f("rwkv_ln_b")[l])
        putv(l, "gla_nw", f("gla_norm_w")[l])
        for i in range(4):
            putv(l, "dn_cw", f("dn_conv_w")[l, i], i * 6)
            putv(l, "ss_cw", f("ssm_conv_w")[l, i], i * 6)
        putv(l, "dn_nw", f("dn_norm_w")[l])
        putv(l, "ss_cb", f("ssm_conv_b")[l])
        putv(l, "ss_nw", f("ssm_norm_w")[l])
        putv(l, "ss_D", np.repeat(f("ssm_D")[l], 64))
        putv(l, "fnw", f("final_norm_w"))
        o = l * NR
        rows[:, o + RO["w0"]:o + RO["w0"] + 256] = f("rwkv_w0")[l][None, :]
        rows[:, o + RO["gkb"]:o + RO["gkb"] + 256] = f("gla_gk_b")[l][None, :]
        rows[:, o + RO["dnA"]:o + RO["dnA"] + 4] = f("dn_A_log")[l][None, :]
        rows[:, o + RO["dndt"]:o + RO["dndt"] + 4] = f("dn_dt_bias")[l][None, :]
        rows[:, o + RO["ssdt"]:o + RO["ssdt"] + 4] = f("ssm_dt_bias")[l][None, :]
        rows[:, o + RO["ssA"]:o + RO["ssA"] + 4] = f("ssm_A_log")[l][None, :]
        o = l * 1024
        sm[0:32, o:o + 256] = f("rwkv_w2")[l]
        sm[32:64, o + 256:o + 512] = f("rwkv_a2")[l]
        sm[64:128, o + 512:o + 768] = f("rwkv_g2")[l]
        sm[0:16, o + 768:o + 1024] = f("gla_gk_w2")[l]
    consts = make_consts()
    xp, xs = f("x_prompt"), f("x_sample")
    cpr, csm = f("c_prompt"), f("c_sample")
    stn = dict(shift="state_rwkv_shift", wkv="state_rwkv_wkv", gla="state_gla", dnc="state_dn_conv", dn="state_dn",
               ssc="state_ssm_conv", ssm="state_ssm")
    stf = {k: f(v) for k, v in stn.items()}
    in_maps = []
    for c in range(8):
        m = dict(wall=wall, ada=adaw, pv=pvv, rows=rows, sm=sm, consts=consts)
        m["xin"] = np.ascontiguousarray(np.concatenate([xp[c], xs[16 * c:16 * c + 16].reshape(128, 1024)], 0))
        m["cc"] = np.ascontiguousarray(np.concatenate([cpr[c:c + 1], csm[16 * c:16 * c + 16]], 0))
        for k in ST_SHAPES:
            m["st_" + k] = np.ascontiguousarray(stf[k][:, 16 * c:16 * c + 16])
        in_maps.append(m)
    if "nc" not in _NC_CACHE:
        _NC_CACHE["nc"] = build()
    res = run_bass_kernel_spmd(_NC_CACHE["nc"], in_maps, core_ids=list(range(8)))
    R = res.results
    global _LAST_R
    _LAST_R = R
    y_prompt = np.stack([R[c]["y"][:2048] for c in range(8)], 0)
    y_sample = np.concatenate([R[c]["y"][2048:].reshape(16, 8, 1024) for c in range(8)], 0)
    outs = [y_prompt, y_sample]
    for k in ("shift", "wkv", "gla", "dnc", "dn", "ssc", "ssm"):
        outs.append(np.stack([R[c]["p_" + k] for c in range(8)], 1))
    for k in ("shift", "wkv", "gla", "dnc", "dn", "ssc", "ssm"):
        outs.append(np.concatenate([R[c]["s_" + k] for c in range(8)], 1))
    return tuple(np.ascontiguousarray(o.astype(np.float32)) for o in outs)
```

```python
import numpy as np
import concourse.bass as bass
import concourse.mybir as mybir
from concourse.bass_utils import run_bass_kernel_spmd

F32 = mybir.dt.float32
BF16 = mybir.dt.bfloat16
AF = mybir.ActivationFunctionType
ALU = mybir.AluOpType
AX = mybir.AxisListType


class V:
    __slots__ = ("tile", "ap", "sub")

    def __init__(self, tile, ap, sub):
        self.tile, self.ap, self.sub = tile, ap, sub

    def __getitem__(self, idx):
        return V(self.tile, self.ap[idx], self.sub)

    def f(self, fn):
        return V(self.tile, fn(self.ap), self.sub)


class Tile:
    def __init__(self, h, name):
        self.h, self.name = h, name
        self.ww = {}
        self.wr = {}
        self.subs = {}

    def __getitem__(self, idx):
        return V(self, self.h[idx], None)

    def s(self, k, idx=None):
        ap = self.h[idx] if idx is not None else self.h[:]
        return V(self, ap, k)


def _mx(d, s, v):
    if d.get(s, 0) < v:
        d[s] = v


class Sched:
    ENG = ("pe", "act", "dve", "pool", "sp")
    NDMA = 12

    def __init__(self, nc, stack):
        self.nc = nc
        self.ops = {e: [] for e in self.ENG}
        self.cnt = {e: 0 for e in self.ENG}
        self.waited = {e: {} for e in self.ENG}
        self.sem = {}
        for e in ("pe", "act", "dve", "pool"):
            self.sem[e] = stack.enter_context(nc.semaphore("s_" + e))
        self.dq = {}
        for q in ("sp", "pool", "act"):
            n = self.NDMA if q == "sp" else 6
            sems = [stack.enter_context(nc.semaphore("d_%s%d" % (q, j))) for j in range(n)]
            for j, s_ in enumerate(sems):
                self.sem[("d", q, j)] = s_
            self.dq[q] = [n, 0, [0] * n]
        self.out_tokens = {}
        self.stack = stack
        self.ntile = 0

    def sb(self, shape, name=None, dt=F32):
        self.ntile += 1
        name = name or ("t%d" % self.ntile)
        h = self.stack.enter_context(self.nc.sbuf_tensor(name, list(shape), dt))
        return Tile(h, name)

    def ps(self, shape, name=None, dt=F32):
        self.ntile += 1
        name = name or ("p%d" % self.ntile)
        h = self.stack.enter_context(self.nc.psum_tensor(name, list(shape), dt))
        return Tile(h, name)

    def _deps(self, reads, writes):
        tok = {}
        for v in reads:
            t = v.tile
            for s, x in t.ww.items():
                _mx(tok, s, x)
            if v.sub is None:
                for k, (w, r) in t.subs.items():
                    for s, x in w.items():
                        _mx(tok, s, x)
            elif v.sub in t.subs:
                for s, x in t.subs[v.sub][0].items():
                    _mx(tok, s, x)
        for v in writes:
            t = v.tile
            for d in (t.ww, t.wr):
                for s, x in d.items():
                    _mx(tok, s, x)
            if v.sub is None:
                for k, (w, r) in t.subs.items():
                    for d in (w, r):
                        for s, x in d.items():
                            _mx(tok, s, x)
            elif v.sub in t.subs:
                for d in t.subs[v.sub]:
                    for s, x in d.items():
                        _mx(tok, s, x)
        return tok

    def _commit(self, reads, writes, s, x):
        for v in reads:
            t = v.tile
            if v.sub is None:
                _mx(t.wr, s, x)
            else:
                if v.sub not in t.subs:
                    t.subs[v.sub] = [{}, {}]
                _mx(t.subs[v.sub][1], s, x)
        for v in writes:
            t = v.tile
            if v.sub is None:
                t.ww = {s: x}
                t.wr = {}
                t.subs = {}
            else:
                t.subs[v.sub] = [{s: x}, {}]

    def _add(self, eng, fn, reads, writes, tokens_extra=None, dma_q=None, is_out=False):
        tok = self._deps(reads, writes)
        if tokens_extra:
            for s, x in tokens_extra.items():
                _mx(tok, s, x)
        if dma_q is not None:
            n, nxt, cnts = self.dq[dma_q]
            j = nxt
            self.dq[dma_q][1] = (nxt + 1) % n
            if cnts[j] > 0:
                _mx(tok, ("d", dma_q, j), 16 * cnts[j])
            cnts[j] += 1
            mysem, myval, inc = ("d", dma_q, j), 16 * cnts[j], 16
        else:
            self.cnt[eng] += 1
            mysem, myval, inc = eng, self.cnt[eng], 1
        waits = []
        wd = self.waited[eng]
        for s, x in tok.items():
            if eng == "pe" and s == "pe":
                continue
            if wd.get(s, 0) >= x:
                continue
            wd[s] = x
            waits.append((s, x))
        self.ops[eng].append((fn, waits, mysem, inc))
        self._commit(reads, writes, mysem, myval)
        if is_out:
            _mx(self.out_tokens, mysem, myval)

    def I(self, eng, meth, *args, **kw):
        reads, writes = [], []
        a2 = []
        for a in args:
            if isinstance(a, V):
                reads.append(a)
                a2.append(a.ap)
            else:
                a2.append(a)
        k2 = {}
        rw = kw.pop("_rw", False)
        for k, a in kw.items():
            if isinstance(a, V):
                if k in ("out", "accum_out"):
                    writes.append(a)
                    if rw:
                        reads.append(a)
                else:
                    reads.append(a)
                k2[k] = a.ap
            else:
                k2[k] = a
        fn = lambda e: getattr(e, meth)(*a2, **k2)
        self._add(eng, fn, reads, writes)

    def dma(self, q, out, in_, is_out=False, xr=(), xw=(), **kw):
        reads, writes = list(xr), list(xw)
        o = out.ap if isinstance(out, V) else out
        i = in_.ap if isinstance(in_, V) else in_
        if isinstance(out, V):
            writes.append(out)
        if isinstance(in_, V):
            reads.append(in_)
        fn = lambda e: e.dma_start(out=o, in_=i, **kw)
        self._add(q, fn, reads, writes, dma_q=q, is_out=is_out)

    def emit(self):
        nc = self.nc
        fin = [(s, x) for s, x in self.out_tokens.items()]
        eobj = {"pe": "tensor", "act": "scalar", "dve": "vector", "pool": "gpsimd", "sp": "sync"}
        with nc.Block() as block:
            for ename in self.ENG:
                ops = self.ops[ename]
                extra = fin if ename == "sp" else []

                def body(e, ops=ops, extra=extra):
                    for fn, waits, mysem, inc in ops:
                        for s, x in waits:
                            e.wait_ge(self.sem[s], x)
                        fn(e).then_inc(self.sem[mysem], inc)
                    for s, x in extra:
                        e.wait_ge(self.sem[s], x)
                getattr(block, eobj[ename])(body)

NL = 2
NTOK = 2176
NBLK = 17
C_DEC = 0.6065306597126334
CNAMES = ["IDENT", "ONES", "BLK", "TRS_P", "TRI_P", "SU_P", "NEGM_P", "TRS_S", "TRI_S", "SU_S", "NEGM_S"]
CI = {n: i * 128 for i, n in enumerate(CNAMES)}
C_SEG = 11 * 128
C_EPS = C_SEG + 16
C_GNEPS = C_EPS + 1
C_ONE = C_EPS + 2
NCONST = C_EPS + 4
NPV = 160
PVO = dict(ada_b=0, n1w=48, n2w=56, mu=64, a0=71, k_k=73, k_a=75, r_k=77, ln_w=79, ln_b=81, gla_nw=83,
           dn_cw=85, dn_nw=109, ss_cw=111, ss_cb=135, ss_nw=141, ss_D=143, fnw=145)
NR = 528
RO = dict(w0=0, gkb=256, dnA=512, dndt=516, ssdt=520, ssA=524)
ST_SHAPES = dict(shift=[2, 16, 896], wkv=[2, 16, 4, 64, 64], gla=[2, 16, 4, 64, 64], dnc=[2, 16, 3, 768],
                 dn=[2, 16, 4, 64, 64], ssc=[2, 16, 3, 768], ssm=[2, 16, 4, 64, 128])


def make_consts():
    c = np.zeros((128, NCONST), np.float32)
    s = np.arange(128)[:, None]
    i = np.arange(128)[None, :]
    same = (s // 8) == (i // 8)
    m = {}
    m["IDENT"] = (s == i)
    m["ONES"] = np.ones((128, 128))
    m["BLK"] = (s // 64) == (i // 64)
    m["TRI_P"] = s <= i
    m["TRS_P"] = s < i
    m["SU_P"] = s > i
    m["NEGM_P"] = (m["TRI_P"].astype(np.float32) - 1.0) * 30000.0
    m["TRI_S"] = (s <= i) & same
    m["TRS_S"] = (s < i) & same
    m["SU_S"] = (s > i) & same
    m["NEGM_S"] = (m["TRI_S"].astype(np.float32) - 1.0) * 30000.0
    for n in CNAMES:
        c[:, CI[n]:CI[n] + 128] = m[n].astype(np.float32)
    c[:, C_SEG:C_SEG + 16] = ((np.arange(128)[:, None] // 8) == np.arange(16)[None, :]).astype(np.float32)
    c[:, C_EPS] = 1e-6
    c[:, C_GNEPS] = 64e-5
    c[:, C_ONE] = 1.0
    return c


class BK:
    pass


def build():
    from contextlib import ExitStack
    nc = bass.Bass("TRN2", target_bir_lowering=False)

    def din(name, shape):
        return nc.dram_tensor(name, list(shape), F32, kind="ExternalInput").ap()

    def dout(name, shape):
        return nc.dram_tensor(name, list(shape), F32, kind="ExternalOutput").ap()

    xin = din("xin", [NTOK, 1024])
    ccd = din("cc", [17, 1024])
    std = {k: din("st_" + k, v) for k, v in ST_SHAPES.items()}
    wall = din("wall", [2, 27, 128, 8, 512])
    adad = din("ada", [2, 12, 128, 8, 512])
    pvd = din("pv", [128, 2 * NPV])
    rowsd = din("rows", [128, 2 * NR])
    smd = din("sm", [128, 2 * 4 * 256])
    constd = din("consts", [128, NCONST])
    yout = dout("y", [NTOK, 1024])
    nc.allow_low_precision("bf16 operands (fp32 PSUM accumulation) for the dense projections")
    wbf = nc.dram_tensor("wbf", [2, 27, 128, 8, 512], BF16).ap()
    pod = {k: dout("p_" + k, [v[0]] + v[2:]) for k, v in ST_SHAPES.items()}
    sod = {k: dout("s_" + k, v) for k, v in ST_SHAPES.items()}
    import os as _os
    KDBG = int(_os.environ.get("KDBG", "0"))
    dbgd = dout("dbg", [12, 128, 1024]) if KDBG else None

    with ExitStack() as es:
        S = Sched(nc, es)
        I = S.I
        CONST = S.sb([128, NCONST], "CONST")
        PV = S.sb([128, 2 * NPV], "PV")
        ROWS = S.sb([128, 2 * NR], "ROWS")
        SM = S.sb([128, 2 * 1024], "SM")
        S.dma("pool", CONST[:], constd)
        S.dma("pool", PV[:], pvd)
        S.dma("pool", ROWS[:], rowsd)
        S.dma("pool", SM[:], smd)

        def K(n):
            return CONST[:, CI[n]:CI[n] + 128]
        IDENT, ONES, BLK = K("IDENT"), K("ONES"), K("BLK")
        EPSc = CONST[:, C_EPS:C_EPS + 1]
        GNEPSc = CONST[:, C_GNEPS:C_GNEPS + 1]
        ONEc = CONST[:, C_ONE:C_ONE + 1]

        def pv(l, name, n=1, off=0):
            o = l * NPV + PVO[name] + off
            return PV[:, o:o + n]

        def row(l, name, n):
            o = l * NR + RO[name]
            return ROWS[:, o:o + n]

        def smat(l, j):
            o = l * 1024 + j * 256
            return SM[:, o:o + 256]

        PB = [S.ps([128, 512], "pb%d" % i) for i in range(8)]
        pbi = [0]

        def pb():
            t = PB[pbi[0] % 8]
            pbi[0] += 1
            return t

        WR = [S.sb([128, 4096], "wr%d" % i) for i in range(2)]
        wri = [0]
        WBF = Tile(None, "wbf_dep")

        def wr32(i):
            return V(WR[i], WR[i].h[:].rearrange("p (k c) -> p k c", c=512), None)

        def wr16(i, hf):
            return V(WR[i], WR[i].h[:].bitcast(BF16)[:, hf * 4096:(hf + 1) * 4096].rearrange("p (k c) -> p k c", c=512), hf)

        def wload(ap):
            i = wri[0] % 2
            wri[0] += 1
            t = wr32(i)
            S.dma("sp", t, ap)
            return t

        def wload16(l, sl):
            r = wri[0] % 4
            wri[0] += 1
            t = wr16(r // 2, r % 2)
            S.dma("sp", t, wbf[l, sl], xr=[V(WBF, None, (l, sl))])
            return t

        def mm(out, lhsT, rhs, start=True, stop=True):
            I("pe", "matmul", out=out, lhsT=lhsT, rhs=rhs, start=start, stop=stop, skip_group_check=True)

        def tr(out, in_, npart=128):
            I("pe", "transpose", out=out, in_=in_, identity=CONST[0:npart, 0:npart])

        def act(out, in_, func, bias=None, scale=1.0):
            if bias is None:
                I("act", "activation", out=out, in_=in_, func=func, scale=scale)
            else:
                I("act", "activation", out=out, in_=in_, func=func, bias=bias, scale=scale)

        def tt(out, a, b, op=ALU.mult, eng="dve"):
            I(eng, "tensor_tensor", out=out, in0=a, in1=b, op=op)

        def ts(out, a, s1, op0, s2=None, op1=None, eng="dve"):
            if op1 is None:
                I(eng, "tensor_scalar", out=out, in0=a, scalar1=s1, scalar2=None, op0=op0)
            else:
                I(eng, "tensor_scalar", out=out, in0=a, scalar1=s1, scalar2=s2, op0=op0, op1=op1)

        def stt(out, a, sc, b, op0, op1, eng="dve"):
            I(eng, "scalar_tensor_tensor", out=out, in0=a, scalar=sc, in1=b, op0=op0, op1=op1)

        def cp(out, in_, eng="dve"):
            if eng == "act":
                I("act", "activation", out=out, in_=in_, func=AF.Identity, scale=1.0)
            else:
                I("dve", "tensor_copy", out=out, in_=in_)

        G = [S.sb([128, 256], "g%d" % i) for i in range(28)]

        class GP:
            def __init__(self, pool=None):
                self.i = 0
                self.pool = G if pool is None else pool

            def t(self):
                t = self.pool[self.i]
                self.i += 1
                return t

        class Alias:
            def __init__(self, tile, base, sub):
                self.tile, self.base, self.sub = tile, base, sub

            def __getitem__(self, idx):
                return V(self.tile, self.base[idx], self.sub)

        MODT = S.sb([128, 2, 48, 17], "MODT")
        SC = S.sb([128, 2, 2, 8, 17], "SC")
        CT = S.sb([128, 8, 17], "CT")
        class Cx:
            pass
        CXS = []
        for i_ in range(2):
            c_ = Cx()
            c_.XTM = S.sb([128, 1024], "XTM%d" % i_)
            c_.XT = S.sb([128, 8, 128], "XT%d" % i_)
            c_.HT = S.sb([128, 8, 128], "HT%d" % i_, BF16)
            c_.YT = S.sb([128, 8, 128], "YT%d" % i_, BF16)
            CXS.append(c_)
        CUR = [CXS[0]]

        class Proxy:
            def __init__(self, name):
                self.name = name

            def __getitem__(self, idx):
                return getattr(CUR[0], self.name)[idx]
        XTM, XT, HT, YT = Proxy("XTM"), Proxy("XT"), Proxy("HT"), Proxy("YT")
        HTF = S.sb([128, 8, 128], "HTF")
        HID = S.sb([128, 32, 128], "HID", BF16)
        TMP5 = S.sb([128, 4, 128], "TMP5")
        UA = S.sb([128, 7, 144], "UA")
        XS = S.sb([128, 7, 128], "XS")
        XB = S.sb([128, 6, 176], "XB")
        CV = S.sb([128, 6, 128], "CV")
        PST = {}
        for l in range(NL):
            for mname in ("wkv", "gla", "dn", "ssm"):
                PST[(l, mname)] = S.sb([128, 2, 128], "pst_%s%d" % (mname, l))
                I("dve", "memset", PST[(l, mname)][:], 0.0)
        CARRY = {}
        for l in range(NL):
            CARRY[(l, "rw")] = S.sb([128, 7], "c_rw%d" % l)
            CARRY[(l, "dn")] = S.sb([128, 6, 3], "c_dn%d" % l)
            CARRY[(l, "ss")] = S.sb([128, 6, 3], "c_ss%d" % l)
            for k in ("rw", "dn", "ss"):
                I("dve", "memset", CARRY[(l, k)][:], 0.0)
        SST = [S.sb([128, 16, 2, 128], "sst%d" % i) for i in range(2)]
        for t in SST:
            I("dve", "memset", t[:], 0.0)
        G2 = [Alias(SST[j], SST[j].h[:, s_].rearrange("p r v -> p (r v)"), ("a", s_)) for j in range(2) for s_ in range(16)]
        PADS = [S.sb([128, 2, 128], "pad%d" % i) for i in range(3)]
        for t in PADS:
            I("dve", "memset", t[:], 0.0)

        Pk, Sk = BK(), BK()
        Pk.nseg, Pk.L, Pk.sfx, Pk.nsolve = 1, 128, "_P", 7
        Sk.nseg, Sk.L, Sk.sfx, Sk.nsolve = 16, 8, "_S", 3
        for bk in (Pk, Sk):
            bk.TRI, bk.TRS, bk.SU, bk.NEGM = K("TRI" + bk.sfx), K("TRS" + bk.sfx), K("SU" + bk.sfx), K("NEGM" + bk.sfx)

        def bc(v, shape, axis):
            return v.f(lambda a: a.unsqueeze(axis).to_broadcast(shape))

        def modv(v3, bk, nch=8):
            if bk.nseg == 1:
                return bc(v3[:, :, 0:1], [128, nch, 1, 128], 3)
            return bc(v3[:, :, 1:17], [128, nch, 16, 8], 3)

        def dv(v, bk):
            return v.f(lambda a: a.rearrange("p c (s t) -> p c s t", t=bk.L))

        ctm = XTM
        S.dma("pool", ctm[0:17, :], ccd)
        act(ctm[0:17, :], ctm[0:17, :], AF.Silu)
        for kc in range(8):
            p = pb()
            tr(p[:, 0:17], ctm[0:17, kc * 128:(kc + 1) * 128], 17)
            cp(CT[:, kc, :], p[:, 0:17])
        for l in range(NL):
            for s in range(12):
                w = wload(adad[l, s])
                p = pb()
                for j in range(4):
                    for kc in range(8):
                        mm(p[:, j * 17:(j + 1) * 17], w[:, kc, j * 128:(j + 1) * 128], CT[:, kc, :], kc == 0, kc == 7)
                tt(MODT[:, l, 4 * s:4 * s + 4, :], p[:, 0:68].f(lambda a: a.rearrange("p (j n) -> p j n", n=17)),
                   bc(pv(l, "ada_b", 4, 4 * s), [128, 4, 17], 2), ALU.add)
            stt(SC[:, l, 0], MODT[:, l, 8:16, :], 1.0, bc(pv(l, "n1w", 8), [128, 8, 17], 2), ALU.add, ALU.mult)
            stt(SC[:, l, 1], MODT[:, l, 32:40, :], 1.0, bc(pv(l, "n2w", 8), [128, 8, 17], 2), ALU.add, ALU.mult)

        STG = [wr32(0)] + [V(SST[j], SST[j].h[:].rearrange("p s r v -> p (s r v)").rearrange("p (k c) -> p k c", c=512), None)
                           for j in range(2)]
        for l in range(NL):
            for sl in range(27):
                i = (l * 27 + sl)
                src = STG[i % 3]
                S.dma("sp", src, wall[l, sl])
                dst = wr16(1, i % 2)
                I("dve", "tensor_copy", out=dst[:, 0:4, :], in_=src[:, 0:4, :])
                I("act", "activation", out=dst[:, 4:8, :], in_=src[:, 4:8, :], func=AF.Identity, scale=1.0)
                S.dma("sp", wbf[l, sl], dst, xw=[V(WBF, None, (l, sl))])

        RS = S.sb([128, 128], "RS")

        def norm_mod(bk, scale_bv, shift_bv, outT=None):
            outT = HT if outT is None else outT
            HIDF = V(HID, HID.h[:].rearrange("p c t -> p (c t)").bitcast(F32), None)
            p = pb()
            for kc in range(8):
                sq = HIDF[:, kc * 128:(kc + 1) * 128]
                act(sq, XT[:, kc, :], AF.Square)
                mm(p[:, 0:128], ONES, sq, kc == 0, kc == 7)
            act(RS[:], p[:, 0:128], AF.Ln, bias=EPSc, scale=1.0 / 1024.0)
            act(RS[:], RS[:], AF.Exp, scale=-0.5)
            tt(HTF[:], XT[:], bc(RS[:], [128, 8, 128], 1))
            if shift_bv is not None:
                tt(dv(HTF[:], bk), dv(HTF[:], bk), scale_bv)
                tt(dv(outT[:], bk), dv(HTF[:], bk), shift_bv, ALU.add)
            else:
                tt(dv(outT[:], bk), dv(HTF[:], bk), scale_bv)

        def proj_fm(out, w, c0, M):
            for kc in range(8):
                mm(out, w[:, kc, c0:c0 + M], HT[:, kc, :], kc == 0, kc == 7)

        def proj_tm(out, w, c0, N):
            for kc in range(8):
                mm(out, HT[:, kc, :], w[:, kc, c0:c0 + N], kc == 0, kc == 7)

        def sv(v, bk):
            return v.f(lambda a: a.rearrange("p (s t) -> p s t", t=bk.L))

        def lastcol(v, bk, s):
            c = s * bk.L + bk.L - 1
            return v[:, c:c + 1]

        def tri_solve(bk, NTs, Ns, X, g):
            tmpN = [[g.t(), g.t()] for _ in range(2)]
            for k in range(bk.nsolve):
                p = pb()
                for h in range(2):
                    mm(p[:, h * 64:h * 64 + 64], NTs[h][:, 0:128], X[:, h * 64:h * 64 + 64])
                tt(X[:, 0:128], X[:, 0:128], p[:, 0:128], ALU.add)
                yield
                if k == bk.nsolve - 1:
                    break
                for h in range(2):
                    p2 = pb()
                    mm(p2[:, 0:128], Ns[h][:, 0:128], NTs[h][:, 0:128])
                    if k < bk.nsolve - 2:
                        mm(p2[:, 128:256], NTs[h][:, 0:128], Ns[h][:, 0:128])
                    nn = tmpN[h][k % 2]
                    if k < bk.nsolve - 2:
                        cp(nn[:, 0:256], p2[:, 0:256], "act")
                    else:
                        cp(nn[:, 0:128], p2[:, 0:128], "act")
                    NTs[h] = nn
                    Ns[h] = _Shift(nn)
                    yield
            return

        class _Shift:
            def __init__(self, base):
                self.base = base

            def __getitem__(self, idx):
                assert idx == (slice(None), slice(0, 128))
                return self.base[:, 128:256]

        def post_rms(Y, ones, gsize, nw_col, gateT, out, g):
            sq = g.t()
            act(sq[:, 0:128], Y, AF.Square)
            p = pb()
            mm(p[:, 0:128], ones, sq[:, 0:128])
            r = g.t()
            act(r[:, 0:128], p[:, 0:128], AF.Ln, bias=EPSc, scale=1.0 / gsize)
            act(r[:, 0:128], r[:, 0:128], AF.Exp, scale=-0.5)
            stt(r[:, 128:256], Y, nw_col, r[:, 0:128], ALU.mult, ALU.mult)
            if gateT is not None:
                tt(out, r[:, 128:256], gateT)
            else:
                cp(out, r[:, 128:256])

        def load_bd(sst, src):
            for h in range(4):
                b = 64 * (h % 2)
                S.dma("pool", sst[b:b + 64, :, h // 2, b:b + 64], src[:, h].rearrange("s d v -> d s v"))

        def store_bd(dst, tile_bd, pr):
            for hh in range(2):
                b = 64 * hh
                S.dma("pool", dst[2 * pr + hh], tile_bd[b:b + 64, b:b + 64], is_out=True)

        OST = [S.sb([128, 128], "ost%d" % i) for i in range(4)]
        osti = [0]

        def ost():
            t = OST[osti[0] % 4]
            osti[0] += 1
            return t

        def state_update(bk, l, mname, pr, lhs, rhs, pcv, sst, outd, last, g, transpose_out=False):
            for s in range(bk.nseg):
                p = pb()
                for k in range(len(lhs)):
                    lv = lhs[k]
                    if bk.nseg > 1:
                        m = g_rot()
                        I("act", "mul", out=m[:, 0:128], in_=lv, mul=CONST[:, C_SEG + s:C_SEG + s + 1])
                        lv = m[:, 0:128]
                    mm(p[:, 0:128], lv, rhs[k], k == 0, k == len(lhs) - 1)
                tmp = g_rot()
                tt(tmp[:, 0:128], p[:, 0:128], BLK)
                if bk.nseg == 1:
                    hp = PST[(l, mname)][:, pr, :]
                    stt(hp, hp, lastcol(pcv, bk, 0), tmp[:, 0:128], ALU.mult, ALU.add)
                    if last:
                        if transpose_out:
                            p2 = pb()
                            tr(p2[:, 0:128], hp)
                            o = ost()
                            cp(o[:], p2[:, 0:128])
                            store_bd(outd[l], o, pr)
                        else:
                            store_bd(outd[l], PST[(l, mname)][:, pr, :], pr)
                else:
                    hp = sst[:, s, pr, :]
                    o = ost()
                    stt(o[:], hp, lastcol(pcv, bk, s), tmp[:, 0:128], ALU.mult, ALU.add)
                    if transpose_out:
                        p2 = pb()
                        tr(p2[:, 0:128], o[:])
                        o2 = ost()
                        cp(o2[:], p2[:, 0:128])
                        o = o2
                    store_bd(outd[l, s], o, pr)

        GR = [S.sb([128, 128], "gr%d" % i) for i in range(6)]
        gri = [0]

        def g_rot():
            t = GR[gri[0] % 6]
            gri[0] += 1
            return t

        def inter(bk, l, mname, pr, sst, out_ps, opT, stop):
            for s in range(bk.nseg):
                hp = PST[(l, mname)][:, pr, :] if bk.nseg == 1 else sst[:, s, pr, :]
                mm(out_ps[:, s * bk.L:(s + 1) * bk.L], hp, opT[:, s * bk.L:(s + 1) * bk.L], s == 0, stop)

        def decay_mask(bk, la_col, out, g):
            t = g.t()
            ts(t[:, 0:128], bk.SU, la_col, ALU.mult)
            p = pb()
            mm(p[:, 0:128], t[:, 0:128], bk.TRI, True, False)
            mm(p[:, 0:128], IDENT, bk.NEGM, False, True)
            act(out, p[:, 0:128], AF.Exp)

        def bc2(v, shape):
            return v.f(lambda a: a.unsqueeze(2).unsqueeze(3).to_broadcast(shape))

        GLT = S.sb([16, 128], "GLT")
        SHT = S.sb([48, 896], "SHT")
        CST = SHT
        CSO = S.sb([48, 768], "CSO")
        CS3 = S.sb([128, 48], "CS3")
        SMT = S.sb([128, 64], "SMT")
        ZT = S.sb([128, 512], "ZT")
        TMP4 = S.sb([128, 4, 128], "TMP4")
        YTM = XTM

        def dump(k, tile3):
            if not KDBG:
                return
            for b2 in range(2):
                p = pb()
                for j in range(4):
                    tr(p[:, j * 128:(j + 1) * 128], tile3[:, 4 * b2 + j, :])
                cp(TMP4[:].f(lambda a: a.rearrange("p c t -> p (c t)")), p[:, 0:512])
                S.dma("pool", dbgd[k, :, b2 * 512:(b2 + 1) * 512], TMP4[:].f(lambda a: a.rearrange("p c t -> p (c t)")), is_out=True)
        SSL = [S.sb([64, 4, 128], "ssl%d" % i) for i in range(1)]

        def rwkv(bk, l, last, wget, sst, pool=None):
            g = GP(pool)
            w0 = wget(0)
            w1 = wget(1)
            nseg, L = bk.nseg, bk.L
            W = L + 1
            sfx = bk.sfx
            UAv = UA[:, :, 0:nseg * W].f(lambda a: a.rearrange("p c (s t) -> p c s t", t=W))
            if nseg == 1:
                cp(UA[:, :, 0:1], CARRY[(l, "rw")][:].f(lambda a: a.unsqueeze(2)))
            else:
                S.dma("pool", SHT[0:16, :], std["shift"][l])
                for c in range(7):
                    p = pb()
                    tr(p[:, 0:16], SHT[0:16, c * 128:(c + 1) * 128], 16)
                    cp(UAv[:, c, :, 0], p[:, 0:16])
            for c in range(7):
                w = w0 if c < 4 else w1
                p = pb()
                proj_fm(p[:, 0:128], w, (c % 4) * 128, 128)
                cp(UAv[:, c, :, 1:W], sv(p[:, 0:128], bk), "act" if c % 2 else "dve")
            p = pb()
            proj_fm(p[0:16, 0:128], w1, 384, 16)
            cp(GLT[0:16, :], p[0:16, 0:128])
            if nseg == 1:
                cp(CARRY[(l, "rw")][:].f(lambda a: a.unsqueeze(2)), UA[:, :, 128:129])
                if last:
                    S.dma("pool", pod["shift"][l].rearrange("(c p) -> p c", p=128), CARRY[(l, "rw")][:], is_out=True,
                          allow_slow_non_contiguous=True)
            else:
                for c in range(7):
                    S.dma("pool", sod["shift"][l][:, c * 128:(c + 1) * 128].rearrange("s p -> p s"), UAv[:, c, :, L],
                          is_out=True, allow_slow_non_contiguous=True)
            ck('rw1')
            XSv = dv(XS[:], bk)
            tt(XSv, UAv[:, :, :, 0:L], UAv[:, :, :, 1:W], ALU.subtract)
            tt(XSv, XSv, bc2(pv(l, "mu", 7), [128, 7, nseg, L]))
            tt(XSv, XSv, UAv[:, :, :, 1:W], ALU.add)
            ck('rw2')
            X6 = XS[:, 6, :]
            T6 = g.t()
            act(T6[:, 0:128], X6, AF.Tanh)
            act(T6[:, 128:256], X6, AF.Sigmoid)
            p = pb()
            mm(p[:, 0:256], T6[:, 0:128], smat(l, 0))
            LAM = g.t()
            tt(LAM[:], p[:, 0:256], row(l, "w0", 256), ALU.add)
            act(LAM[:], LAM[:], AF.Sigmoid)
            AT, GTt = g.t(), g.t()
            for pr in range(2):
                p = pb()
                mm(p[:, 0:128], smat(l, 1)[:, pr * 128:(pr + 1) * 128], X6)
                act(AT[:, pr * 128:(pr + 1) * 128], p[:, 0:128], AF.Sigmoid, bias=pv(l, "a0", 1, pr))
                mm(p[:, 128:256], smat(l, 2)[:, pr * 128:(pr + 1) * 128], T6[:, 128:256])
                cp(GTt[:, pr * 128:(pr + 1) * 128], p[:, 128:256])
            ck('rw3')
            yield
            KK, KP, E, E2, EQT, KC, K2C2, VT, KC2, RT, X, YS, Mt, Bt = [g.t() for _ in range(14)]
            AE = [g.t(), g.t()]
            AC = [g.t(), g.t()]
            Nn = [g.t(), g.t()]
            gsave = g.i
            MASK2 = CONST[:, CI["TRS" + sfx]:CI["TRS" + sfx] + 256]
            TRI3 = CONST[:, CI["TRS" + sfx]:CI["TRS" + sfx] + 384]
            Vp, Up = PADS[0], PADS[1]
            for pr in range(2):
                g.i = gsave
                rT, kT, vT = XS[:, pr, :], XS[:, 2 + pr, :], XS[:, 4 + pr, :]
                aT = AT[:, pr * 128:(pr + 1) * 128]
                ts(KK[:, 0:128], kT, pv(l, "k_k", 1, pr), ALU.mult)
                act(KK[:, 128:256], KK[:, 0:128], AF.Square)
                p = pb()
                mm(p[:, 0:128], BLK, KK[:, 128:256])
                act(KK[:, 128:256], p[:, 0:128], AF.Ln, bias=EPSc)
                act(KK[:, 128:256], KK[:, 128:256], AF.Exp, scale=-0.5)
                tt(KK[:, 0:128], KK[:, 0:128], KK[:, 128:256])
                ts(KP[:, 0:128], aT, -1.0, ALU.add, pv(l, "k_a", 1, pr), ALU.mult)
                stt(KP[:, 0:128], KP[:, 0:128], 1.0, kT, ALU.add, ALU.mult)
                tt(KP[:, 128:256], KK[:, 0:128], aT)
                stt(RT[:, 128:256], rT, pv(l, "r_k", 1, pr), KP[:, 0:128], ALU.mult, ALU.mult)
                p3 = pb()
                mm(p3[:, 0:128], BLK, RT[:, 128:256])
                tt(Bt[:, 128:256], p3[:, 0:128], vT)
                p = pb()
                mm(p[:, 0:384], LAM[:, pr * 128:(pr + 1) * 128], TRI3)
                act(E[:, 0:256], p[:, 0:256], AF.Exp, scale=-C_DEC)
                act(E2[:, 0:128], p[:, 256:384], AF.Exp, scale=-C_DEC)
                act(E2[:, 128:256], p[:, 128:256], AF.Exp, scale=C_DEC)
                tt(EQT[:, 0:128], KK[:, 0:128], E[:, 0:128])
                tt(EQT[:, 128:256], rT, E[:, 128:256])
                tt(KC[:, 0:128], KP[:, 0:128], E2[:, 128:256])
                tt(KC[:, 128:256], KP[:, 128:256], E2[:, 128:256])
                tt(K2C2[:, 0:128], KP[:, 0:128], E2[:, 0:128])
                stt(K2C2[:, 128:256], KP[:, 128:256], -1.0, E2[:, 0:128], ALU.mult, ALU.mult)
                ck('rw4')
                yield
                p = pb()
                tr(p[:, 0:128], vT)
                tr(p[:, 128:256], K2C2[:, 0:128])
                tr(p[:, 256:384], K2C2[:, 128:256])
                cp(VT[:, 0:128], p[:, 0:128])
                ck('rw4a1')
                cp(Vp[:, 0, 0:64], p[:, 0:64])
                cp(Vp[:, 1, 64:128], p[:, 64:128])
                ck('rw4a2')
                cp(KC2[:, 0:256], p[:, 128:384])
                ck('rw4b')
                yield
                for hh in range(2):
                    b = 64 * hh
                    if hh == 1:
                        ck('rw4c')
                    p = pb()
                    mm(p[:, 0:256], KC[b:b + 64, 0:128], EQT[b:b + 64, 0:256])
                    mm(p[:, 256:512], KC[b:b + 64, 128:256], EQT[b:b + 64, 0:256])
                    tt(AE[hh][:, 0:256], p[:, 0:256], MASK2)
                    stt(AC[hh][:, 0:256], p[:, 256:512], -1.0, MASK2, ALU.mult, ALU.mult)
                    p2 = pb()
                    tr(p2[:, 0:128], AC[hh][:, 0:128])
                    cp(Nn[hh][:, 0:128], p2[:, 0:128], "act")
                    yield
                ck('rw5')
                if nseg == 1:
                    p = pb()
                    mm(p[:, 0:128], EQT[:, 0:128], PST[(l, "wkv")][:, pr, :], True, False)
                    for hh in range(2):
                        mm(p[:, 0:128], AE[hh][:, 0:128], Vp[:, hh, :], False, hh == 1)
                    cp(X[:, 0:128], p[:, 0:128])
                    yield
                else:
                    p = pb()
                    inter(bk, l, "wkv", pr, sst, p, EQT[:, 0:128], False)
                    for hh in range(2):
                        mm(p[:, 0:128], Vp[:, hh, :], AE[hh][:, 0:128], False, hh == 1)
                    cp(RT[:, 0:128], p[:, 0:128])
                    yield
                    p = pb()
                    tr(p[:, 0:128], RT[:, 0:128])
                    cp(X[:, 0:128], p[:, 0:128])
                    yield
                ck('rw6')
                yield from tri_solve(bk, [AC[0], AC[1]], [Nn[0], Nn[1]], X, g)
                ck('rw7')
                cp(Up[:, 0, 0:64], X[:, 0:64])
                cp(Up[:, 1, 64:128], X[:, 64:128])
                p = pb()
                inter(bk, l, "wkv", pr, sst, p, EQT[:, 128:256], False)
                for hh in range(2):
                    mm(p[:, 0:128], Vp[:, hh, :], AE[hh][:, 128:256], False, False)
                    mm(p[:, 0:128], Up[:, hh, :], AC[hh][:, 128:256], False, hh == 1)
                ck('rw8')
                cp(YS[:, 0:128], p[:, 0:128])
                act(YS[:, 128:256], p[:, 0:128], AF.Square)
                yield
                p2 = pb()
                mm(p2[:, 0:256], BLK, YS[:, 0:256])
                ts(Mt[:, 0:256], p2[:, 0:256], 1.0 / 64.0, ALU.mult)
                stt(Bt[:, 0:128], Mt[:, 0:128], -1.0, Mt[:, 0:128], ALU.mult, ALU.mult)
                tt(Mt[:, 128:256], Mt[:, 128:256], Bt[:, 0:128], ALU.add)
                act(Mt[:, 128:256], Mt[:, 128:256], AF.Ln, bias=GNEPSc)
                act(Mt[:, 128:256], Mt[:, 128:256], AF.Exp, scale=-0.5)
                tt(YS[:, 0:128], YS[:, 0:128], Mt[:, 0:128], ALU.subtract)
                tt(YS[:, 0:128], YS[:, 0:128], Mt[:, 128:256])
                ts(YS[:, 0:128], YS[:, 0:128], pv(l, "ln_w", 1, pr), ALU.mult, pv(l, "ln_b", 1, pr), ALU.add)
                tt(YS[:, 0:128], YS[:, 0:128], Bt[:, 128:256], ALU.add)
                tt(YT[:, pr, :], YS[:, 0:128], GTt[:, pr * 128:(pr + 1) * 128])
                yield
                ck('rw9')
                state_update(bk, l, "wkv", pr, [KC2[:, 0:128], KC2[:, 128:256]], [VT[:, 0:128], X[:, 0:128]],
                             E[:, 128:256], sst, (pod if nseg == 1 else sod)["wkv"], last, g, transpose_out=True)

        def gla(bk, l, last, wget, sst, pool=None):
            g = GP(pool)
            w2 = wget(2)
            QK = [g.t(), g.t()]
            for c in range(4):
                p = pb()
                proj_fm(p[:, 0:128], w2, c * 128, 128)
                cp(QK[c % 2][:, (c // 2) * 128:(c // 2) * 128 + 128], p[:, 0:128], "act" if c % 2 else "dve")
            KTM = g.t()
            p = pb()
            proj_tm(p[:, 0:256], w2, 256, 256)
            cp(KTM[:], p[:, 0:256], "act")
            w3 = wget(3)
            VTM = g.t()
            p = pb()
            proj_tm(p[:, 0:256], w3, 0, 256)
            cp(VTM[:], p[:, 0:256])
            Vp = [PADS[0], PADS[2]]
            for pr in range(2):
                cp(Vp[pr][:, 0, 0:64], p[:, pr * 128:pr * 128 + 64])
                cp(Vp[pr][:, 1, 64:128], p[:, pr * 128 + 64:pr * 128 + 128])
            GT = g.t()
            for pr in range(2):
                p = pb()
                proj_fm(p[:, 0:128], w3, 256 + pr * 128, 128)
                act(GT[:, pr * 128:(pr + 1) * 128], p[:, 0:128], AF.Silu)
            p = pb()
            mm(p[:, 0:256], GLT[0:16, :], smat(l, 3)[0:16, :])
            LA = g.t()
            tt(LA[:], p[:, 0:256], row(l, "gkb", 256), ALU.add)
            act(LA[:], LA[:], AF.Exp, scale=-1.0)
            act(LA[:], LA[:], AF.Ln, bias=ONEc)
            p = pb()
            mm(p[:, 0:256], bk.SU, LA[:])
            K2 = g.t()
            act(K2[:], p[:, 0:256], AF.Exp, scale=-1.0 / 16.0)
            tt(K2[:], K2[:], KTM[:])
            yield
            E, QKh, YS = g.t(), g.t(), g.t()
            A = [g.t(), g.t()]
            gsave = g.i
            for pr in range(2):
                g.i = gsave
                p = pb()
                mm(p[:, 0:128], LA[:, pr * 128:(pr + 1) * 128], bk.TRI)
                act(E[:, 0:128], p[:, 0:128], AF.Exp, scale=-1.0 / 16.0)
                act(E[:, 128:256], p[:, 0:128], AF.Exp, scale=1.0 / 16.0)
                stt(QKh[:, 0:128], QK[pr][:, 0:128], 0.125, E[:, 0:128], ALU.mult, ALU.mult)
                tt(QKh[:, 128:256], QK[pr][:, 128:256], E[:, 128:256])
                for hh in range(2):
                    b = 64 * hh
                    p = pb()
                    mm(p[:, 0:128], QKh[b:b + 64, 128:256], QKh[b:b + 64, 0:128])
                    tt(A[hh][:, 0:128], p[:, 0:128], bk.TRI)
                    yield
                p = pb()
                inter(bk, l, "gla", pr, sst, p, QKh[:, 0:128], False)
                for hh in range(2):
                    mm(p[:, 0:128], Vp[pr][:, hh, :], A[hh][:, 0:128], False, hh == 1)
                cp(YS[:, 0:128], p[:, 0:128])
                yield
                post_rms(YS[:, 0:128], BLK, 64.0, pv(l, "gla_nw", 1, pr), GT[:, pr * 128:(pr + 1) * 128], YT[:, 2 + pr, :], g)
                yield
                state_update(bk, l, "gla", pr, [K2[:, pr * 128:(pr + 1) * 128]], [VTM[:, pr * 128:(pr + 1) * 128]],
                             E[:, 0:128], sst, (pod if bk.nseg == 1 else sod)["gla"], last, g)
                yield

        def conv_in(bk, l, ckey, skey):
            nseg, L = bk.nseg, bk.L
            W = L + 3
            XBv = XB[:, :, 0:nseg * W].f(lambda a: a.rearrange("p c (s t) -> p c s t", t=W))
            if nseg == 1:
                cp(XBv[:, :, 0, 0:3], CARRY[(l, ckey)][:])
            else:
                S.dma("pool", CST[0:48, 0:768], std[skey][l].rearrange("s i c -> (s i) c"))
                for c in range(6):
                    p = pb()
                    tr(p[:, 0:48], CST[0:48, c * 128:(c + 1) * 128], 48)
                    cp(XBv[:, c, :, 0:3], p[:, 0:48].f(lambda a: a.rearrange("p (s i) -> p s i", i=3)))
            return XBv

        def conv_run(bk, l, ckey, skey, cwname, XBv, last):
            nseg, L = bk.nseg, bk.L
            W = L + 3
            if nseg == 1:
                cp(CARRY[(l, ckey)][:], XBv[:, :, 0, L:L + 3])
                if last:
                    for c in range(6):
                        S.dma("pool", pod[skey][l][:, c * 128:(c + 1) * 128].rearrange("i p -> p i"), CARRY[(l, ckey)][:, c, :],
                              is_out=True, allow_slow_non_contiguous=True)
            else:
                for c in range(6):
                    cp(CS3[:].f(lambda a: a.rearrange("p (s i) -> p s i", i=3)), XBv[:, c, :, L:L + 3])
                    p = pb()
                    tr(p[0:48, 0:128], CS3[:])
                    cp(CSO[0:48, c * 128:(c + 1) * 128], p[0:48, 0:128], "act")
                S.dma("pool", sod[skey][l].rearrange("s i c -> (s i) c"), CSO[:], is_out=True)
            CVv = dv(CV[:], bk)
            TMv = dv(HTF[:, 0:6, :], bk)
            for i in range(4):
                wv = bc2(pv(l, cwname, 6, i * 6), [128, 6, nseg, L])
                if i == 0:
                    tt(CVv, XBv[:, :, :, 0:L], wv)
                else:
                    tt(TMv, XBv[:, :, :, i:i + L], wv)
                    tt(CVv, CVv, TMv, ALU.add)

        def softplus_cols(dst, src, biasrow):
            tt(dst, src, biasrow, ALU.add)
            act(dst, dst, AF.Exp)
            act(dst, dst, AF.Ln, bias=ONEc)

        def dnet(bk, l, last, wget, sst, pool=None):
            g = GP(pool)
            nseg, L = bk.nseg, bk.L
            W = L + 3
            XBv = conv_in(bk, l, "dn", "dnc")
            w4 = wget(4)
            for c in range(4):
                p = pb()
                proj_fm(p[:, 0:128], w4, c * 128, 128)
                cp(XBv[:, c, :, 3:W], sv(p[:, 0:128], bk), "act" if c % 2 else "dve")
            w5 = wget(5)
            for c in range(2):
                p = pb()
                proj_fm(p[:, 0:128], w5, c * 128, 128)
                cp(XBv[:, 4 + c, :, 3:W], sv(p[:, 0:128], bk), "act" if c % 2 else "dve")
            for c in range(2):
                p = pb()
                proj_fm(p[:, 0:128], w5, 256 + c * 128, 128)
                act(ZT[:, c * 128:(c + 1) * 128], p[:, 0:128], AF.Silu)
            w6 = wget(6)
            p = pb()
            proj_tm(p[:, 0:12], w6, 0, 12)
            cp(SMT[:, 0:12], p[:, 0:12])
            for c in range(2):
                p = pb()
                proj_fm(p[:, 0:128], w6, 12 + c * 128, 128)
                act(ZT[:, 256 + c * 128:256 + (c + 1) * 128], p[:, 0:128], AF.Silu)
            conv_run(bk, l, "dn", "dnc", "dn_cw", XBv, last)
            act(CV[:], CV[:], AF.Silu)
            act(SMT[:, 16:20], SMT[:, 4:8], AF.Sigmoid)
            softplus_cols(SMT[:, 20:24], SMT[:, 0:4], row(l, "dndt", 4))
            act(SMT[:, 24:28], row(l, "dnA", 4), AF.Exp)
            stt(SMT[:, 28:32], SMT[:, 20:24], -1.0, SMT[:, 24:28], ALU.mult, ALU.mult)
            p = pb()
            mm(p[:, 0:4], bk.SU, SMT[:, 28:32])
            act(SMT[:, 32:36], p[:, 0:4], AF.Exp)
            yield
            SQ, LB, ELb, R, X, QE, YS, K2, SQ2 = [g.t() for _ in range(9)]
            KQ = [g.t(), g.t()]
            ET = [g.t(), g.t()]
            A = [g.t(), g.t()]
            T0 = [g.t(), g.t()]
            Nn = [g.t(), g.t()]
            NT = [g.t(), g.t()]
            gsave = g.i
            Wp = PADS[2]
            for pr in range(2):
                g.i = gsave
                for which, (src, dst, scl) in enumerate(((CV[:, 2 + pr, :], KQ[pr][:, 0:128], 1.0),
                                                         (CV[:, pr, :], KQ[pr][:, 128:256], 0.125))):
                    sqt = SQ2 if which else SQ
                    act(sqt[:, 0:128], src, AF.Square)
                    p = pb()
                    mm(p[:, 0:128], BLK, sqt[:, 0:128])
                    act(sqt[:, 128:256], p[:, 0:128], AF.Ln, bias=EPSc)
                    act(sqt[:, 128:256], sqt[:, 128:256], AF.Exp, scale=-0.5)
                    stt(dst, src, scl, sqt[:, 128:256], ALU.mult, ALU.mult)
                tt(LB[:, 0:128].f(lambda a: a.rearrange("p (h d) -> p h d", d=64)),
                   ONES.f(lambda a: a.rearrange("p (h d) -> p h d", d=64)),
                   bc(SMT[:, 28 + 2 * pr:30 + 2 * pr], [128, 2, 64], 2))
                p = pb()
                mm(p[:, 0:128], LB[:, 0:128], bk.TRI)
                act(ELb[:, 0:128], p[:, 0:128], AF.Exp)
                yield
                p = pb()
                inter(bk, l, "dn", pr, sst, p, KQ[pr][:, 0:128], True)
                tt(R[:, 0:128], p[:, 0:128], ELb[:, 0:128])
                tt(R[:, 0:128], CV[:, 4 + pr, :], R[:, 0:128], ALU.subtract)
                yield
                p = pb()
                tr(p[:, 0:128], R[:, 0:128])
                for hh in range(2):
                    h = 2 * pr + hh
                    ts(X[:, hh * 64:hh * 64 + 64], p[:, hh * 64:hh * 64 + 64], SMT[:, 16 + h:17 + h], ALU.mult)
                yield
                tt(QE[:, 0:128], KQ[pr][:, 128:256], ELb[:, 0:128])
                p = pb()
                tr(p[:, 0:128], KQ[pr][:, 0:128])
                for hh in range(2):
                    h = 2 * pr + hh
                    ts(K2[:, hh * 64:hh * 64 + 64], p[:, hh * 64:hh * 64 + 64], SMT[:, 32 + h:33 + h], ALU.mult)
                yield
                for hh in range(2):
                    h = 2 * pr + hh
                    b = 64 * hh
                    decay_mask(bk, SMT[:, 28 + h:29 + h], ET[hh][:, 0:128], g)
                    tt(ET[hh][:, 128:256], ET[hh][:, 0:128], bk.TRS)
                    p = pb()
                    mm(p[:, 0:256], KQ[pr][b:b + 64, 0:128], KQ[pr][b:b + 64, 0:256])
                    tt(A[hh][:, 0:128], p[:, 128:256], ET[hh][:, 0:128])
                    tt(T0[hh][:, 0:128], p[:, 0:128], ET[hh][:, 128:256])
                    p2 = pb()
                    tr(p2[:, 0:128], T0[hh][:, 0:128])
                    ts(Nn[hh][:, 0:128], p2[:, 0:128], SMT[:, 16 + h:17 + h], ALU.mult, -1.0, ALU.mult)
                    p3 = pb()
                    tr(p3[:, 0:128], Nn[hh][:, 0:128])
                    cp(NT[hh][:, 0:128], p3[:, 0:128], "act")
                    g.i -= 1
                    yield
                yield from tri_solve(bk, [NT[0], NT[1]], [Nn[0], Nn[1]], X, g)
                cp(Wp[:, 0, 0:64], X[:, 0:64])
                cp(Wp[:, 1, 64:128], X[:, 64:128])
                p = pb()
                inter(bk, l, "dn", pr, sst, p, QE[:, 0:128], False)
                for hh in range(2):
                    mm(p[:, 0:128], Wp[:, hh, :], A[hh][:, 0:128], False, hh == 1)
                cp(YS[:, 0:128], p[:, 0:128])
                yield
                post_rms(YS[:, 0:128], BLK, 64.0, pv(l, "dn_nw", 1, pr), ZT[:, pr * 128:(pr + 1) * 128], YT[:, 4 + pr, :], g)
                yield
                state_update(bk, l, "dn", pr, [K2[:, 0:128]], [X[:, 0:128]], ELb[:, 0:128], sst,
                             (pod if nseg == 1 else sod)["dn"], last, g)
                yield

        def ssd(bk, l, last, wget, sst, pool=None):
            g = GP(pool)
            nseg, L = bk.nseg, bk.L
            W = L + 3
            XBv = conv_in(bk, l, "ss", "ssc")
            w7 = wget(7)
            for c in range(4):
                p = pb()
                proj_fm(p[:, 0:128], w7, c * 128, 128)
                cp(XBv[:, c, :, 3:W], sv(p[:, 0:128], bk), "act" if c % 2 else "dve")
            w8 = wget(8)
            for c in range(2):
                p = pb()
                proj_fm(p[:, 0:128], w8, c * 128, 128)
                cp(XBv[:, 4 + c, :, 3:W], sv(p[:, 0:128], bk), "act" if c % 2 else "dve")
            conv_run(bk, l, "ss", "ssc", "ss_cw", XBv, last)
            for c in range(6):
                act(CV[:, c, :], CV[:, c, :], AF.Silu, bias=pv(l, "ss_cb", 1, c))
            softplus_cols(SMT[:, 40:44], SMT[:, 8:12], row(l, "ssdt", 4))
            act(SMT[:, 44:48], row(l, "ssA", 4), AF.Exp)
            stt(SMT[:, 48:52], SMT[:, 40:44], -1.0, SMT[:, 44:48], ALU.mult, ALU.mult)
            p = pb()
            mm(p[:, 0:4], bk.SU, SMT[:, 48:52])
            act(SMT[:, 52:56], p[:, 0:4], AF.Exp)
            yield
            BTM, X2, YS, LB = [g.t() for _ in range(4)]
            ET = [g.t(), g.t()]
            A = [g.t(), g.t()]
            EL = [g.t(), g.t()]
            CH = [g.t(), g.t()]
            gsave = g.i
            Xp = PADS[1]
            outd = (pod if nseg == 1 else sod)["ssm"]
            for pr in range(2):
                g.i = gsave
                p = pb()
                tr(p[:, 0:128], CV[:, 2 + pr, :])
                tr(p[:, 128:256], CV[:, pr, :])
                cp(BTM[:, 0:128], p[:, 0:128])
                for hh in range(2):
                    h = 2 * pr + hh
                    ts(Xp[:, hh, hh * 64:hh * 64 + 64], p[:, 128 + hh * 64:128 + hh * 64 + 64], SMT[:, 40 + h:41 + h], ALU.mult)
                    ts(X2[:, hh * 64:hh * 64 + 64], Xp[:, hh, hh * 64:hh * 64 + 64], SMT[:, 52 + h:53 + h], ALU.mult)
                pG = pb()
                mm(pG[:, 0:128], CV[:, 2 + pr, :], CV[:, 4 + pr, :])
                for hh in range(2):
                    h = 2 * pr + hh
                    decay_mask(bk, SMT[:, 48 + h:49 + h], ET[hh][:, 0:128], g)
                    g.i -= 1
                    tt(A[hh][:, 0:128], pG[:, 0:128], ET[hh][:, 0:128])
                    ts(LB[:, 0:128], ONES, SMT[:, 48 + h:49 + h], ALU.mult)
                    p = pb()
                    mm(p[:, 0:128], LB[:, 0:128], bk.TRI)
                    act(EL[hh][:, 0:128], p[:, 0:128], AF.Exp)
                    tt(CH[hh][:, 0:128], CV[:, 4 + pr, :], EL[hh][:, 0:128])
                yield
                pY = pb()
                for hh in range(2):
                    reg = pY[:, hh * 128:(hh + 1) * 128]
                    for s in range(nseg):
                        hg = PST[(l, "ssm")][:, pr, :] if nseg == 1 else sst[:, s, pr, :]
                        mm(reg[:, s * L:(s + 1) * L], hg, CH[hh][:, s * L:(s + 1) * L], s == 0, False)
                    mm(reg, Xp[:, hh, :], A[hh][:, 0:128], False, True)
                cp(YS[0:64, 0:128], pY[0:64, 0:128])
                cp(YS[64:128, 0:128], pY[64:128, 128:256])
                yield
                stt(YS[:, 0:128], CV[:, pr, :], pv(l, "ss_D", 1, pr), YS[:, 0:128], ALU.mult, ALU.add)
                tt(YS[:, 0:128], YS[:, 0:128], ZT[:, 256 + pr * 128:256 + (pr + 1) * 128])
                post_rms(YS[:, 0:128], ONES, 128.0, pv(l, "ss_nw", 1, pr), None, YT[:, 6 + pr, :], g)
                yield
                for s in range(nseg):
                    lv = BTM[:, 0:128]
                    if nseg > 1:
                        m = g_rot()
                        I("act", "mul", out=m[:, 0:128], in_=lv, mul=CONST[:, C_SEG + s:C_SEG + s + 1])
                        lv = m[:, 0:128]
                    p = pb()
                    mm(p[:, 0:128], lv, X2[:, 0:128])
                    if nseg == 1:
                        hg = PST[(l, "ssm")][:, pr, :]
                        dest = hg
                    else:
                        hg = sst[:, s, pr, :]
                        dest = ost()[:]
                    for hh in range(2):
                        stt(dest[:, hh * 64:hh * 64 + 64], hg[:, hh * 64:hh * 64 + 64], lastcol(EL[hh][:, 0:128], bk, s),
                            p[:, hh * 64:hh * 64 + 64], ALU.mult, ALU.add)
                    if nseg > 1 or last:
                        for hh in range(2):
                            p2 = pb()
                            tr(p2[0:64, 0:128], dest[:, hh * 64:hh * 64 + 64])
                            o = ost()
                            cp(o[0:64, :], p2[0:64, 0:128], "act")
                            dd = outd[l][2 * pr + hh] if nseg == 1 else outd[l, s, 2 * pr + hh]
                            S.dma("pool", dd, o[0:64, :], is_out=True)

        import os
        class _Stop(Exception):
            pass
        kstop = os.environ.get('KSTOP', '')
        def ck(name):
            if name == kstop:
                raise _Stop()
        blks = [int(x) for x in os.environ.get('KBLKS', ','.join(str(i) for i in range(NBLK))).split(',')]
        try:
          ck('setup')
          def with_cx(cx, gen):
              while True:
                  CUR[0] = cx
                  try:
                      next(gen)
                  except StopIteration:
                      return
                  yield

          def rr_gen(gens):
              gens = list(gens)
              while gens:
                  for g_ in list(gens):
                      try:
                          next(g_)
                      except StopIteration:
                          gens.remove(g_)
                      yield

          def run_seq(gen):
              for _ in gen:
                  pass

          def mix_gen(blk, bk, l, last, samp):
              def wget(idx, l=l):
                  return wload16(l, idx)
              if not samp:
                  yield from rr_gen([rwkv(bk, l, last, wget, None, G), dnet(bk, l, last, wget, None, G2)])
                  yield from rr_gen([gla(bk, l, last, wget, None, G), ssd(bk, l, last, wget, None, G2)])
              else:
                  sst = SST[0]
                  I("dve", "memset", sst[:], 0.0)
                  load_bd(sst, std["wkv"][l])
                  for j in range(8):
                      p = pb()
                      for q in range(4):
                          tr(p[:, q * 128:(q + 1) * 128], sst[:, 2 * j + q // 2, q % 2, :])
                      cp(sst[:, 2 * j:2 * j + 2, :, :].f(lambda a: a.rearrange("p s r v -> p (s r v)")), p[:, 0:512])
                  yield
                  yield from rwkv(bk, l, last, wget, sst)
                  sst = SST[1]
                  I("dve", "memset", sst[:], 0.0)
                  load_bd(sst, std["gla"][l])
                  yield from gla(bk, l, last, wget, sst)
                  sst = SST[0]
                  I("dve", "memset", sst[:], 0.0)
                  load_bd(sst, std["dn"][l])
                  yield from dnet(bk, l, last, wget, sst)
                  sst = SST[1]
                  for s in range(16):
                      sl = SSL[0]
                      S.dma("pool", sl[:], std["ssm"][l, s].rearrange("h p n -> p h n"))
                      p = pb()
                      for h in range(4):
                          tr(p[:, h * 64:(h + 1) * 64], sl[0:64, h, :], 64)
                      cp(sst[:, s, :, :].f(lambda a: a.rearrange("p r v -> p (r v)")), p[:, 0:256])
                  yield
                  yield from ssd(bk, l, last, wget, sst)

          def dense_gen(blk, bk, l, samp):
              def wget(idx, l=l):
                  return wload16(l, idx)
              for half in range(2):
                  w = wget(9 + half)
                  p = pb()
                  for j in range(4):
                      for c8 in range(8):
                          mm(p[:, j * 128:(j + 1) * 128], w[:, c8, j * 128:(j + 1) * 128], YT[:, c8, :], c8 == 0, c8 == 7)
                  pv4 = p[:, 0:512].f(lambda a: a.rearrange("p (c s t) -> p c s t", c=4, t=bk.L))
                  tt(dv(TMP4[:], bk), pv4, modv(MODT[:, l, 16 + 4 * half:20 + 4 * half, :], bk, 4))
                  tt(XT[:, 4 * half:4 * half + 4, :], XT[:, 4 * half:4 * half + 4, :], TMP4[:], ALU.add)
                  yield
              if samp:
                  dump(6 * l + 2, XT)
              norm_mod(bk, modv(SC[:, l, 1], bk), modv(MODT[:, l, 24:32, :], bk))
              yield
              for s8 in range(8):
                  w = wget(11 + s8)
                  p = pb()
                  for j in range(4):
                      for kc in range(8):
                          mm(p[:, j * 128:(j + 1) * 128], w[:, kc, j * 128:(j + 1) * 128], HT[:, kc, :], kc == 0, kc == 7)
                  hv = HID[:, 4 * s8:4 * s8 + 4, :]
                  tmpr = (TMP4 if s8 % 2 else TMP5)
                  act(tmpr[:], p[:, 0:512].f(lambda a: a.rearrange("p (c t) -> p c t", t=128)), AF.Relu)
                  tt(hv, tmpr[:], tmpr[:], ALU.mult)
                  yield
              for half in range(2):
                  p = pb()
                  for fcg in range(4):
                      w = wget(19 + half * 4 + fcg)
                      for j in range(4):
                          for f8 in range(8):
                              mm(p[:, j * 128:(j + 1) * 128], w[:, f8, j * 128:(j + 1) * 128], HID[:, fcg * 8 + f8, :],
                                 fcg == 0 and f8 == 0 and j == 0, fcg == 3 and f8 == 7)
                  pv4 = p[:, 0:512].f(lambda a: a.rearrange("p (c s t) -> p c s t", c=4, t=bk.L))
                  tt(dv(TMP4[:], bk), pv4, modv(MODT[:, l, 40 + 4 * half:44 + 4 * half, :], bk, 4))
                  tt(XT[:, 4 * half:4 * half + 4, :], XT[:, 4 * half:4 * half + 4, :], TMP4[:], ALU.add)
                  yield
              if samp:
                  dump(6 * l + 4, XT)
              if l == NL - 1:
                  norm_mod(bk, bc2(pv(0, "fnw", 8), [128, 8, bk.nseg, bk.L]), None, HTF)
                  for b2 in range(2):
                      p = pb()
                      for j in range(4):
                          tr(p[:, j * 128:(j + 1) * 128], HTF[:, 4 * b2 + j, :])
                      cp(YTM[:, b2 * 512:(b2 + 1) * 512], p[:, 0:512])
                  S.dma("pool", yout[blk * 128:(blk + 1) * 128, :], YTM[:], is_out=True)

          pending = None
          for bi, blk in enumerate(blks):
              cx = CXS[bi % 2]
              CUR[0] = cx
              bk = Pk if blk < 16 else Sk
              last = blk == max(b for b in blks if b < 16) if blk < 16 else False
              samp = blk == 16
              S.dma("pool", XTM[:], xin[blk * 128:(blk + 1) * 128, :])
              for b2 in range(2):
                  p = pb()
                  for j in range(4):
                      tr(p[:, j * 128:(j + 1) * 128], XTM[:, (4 * b2 + j) * 128:(4 * b2 + j + 1) * 128])
                  cp(XT[:, 4 * b2:4 * b2 + 4, :], p[:, 0:512].f(lambda a: a.rearrange("p (c t) -> p c t", t=128)))
              for l in range(NL):
                  CUR[0] = cx
                  norm_mod(bk, modv(SC[:, l, 0], bk), modv(MODT[:, l, 0:8, :], bk))
                  M = with_cx(cx, mix_gen(blk, bk, l, last, samp))
                  if l == 0 and pending is not None:
                      run_seq(rr_gen([M, pending]))
                      pending = None
                  else:
                      run_seq(M)
                  D = with_cx(cx, dense_gen(blk, bk, l, samp))
                  if l == NL - 1 and not KDBG:
                      pending = D
                  else:
                      run_seq(D)
          if pending is not None:
              run_seq(pending)
        except _Stop:
            pass
        print('opcounts', {e: len(v) for e, v in S.ops.items()}, flush=True)
        S.emit()
    return nc


W_IN_COLS = None


def _win_colmap():
    def rng(a, b):
        return list(range(a, b))
    slots = []
    slots.append(rng(0, 512))
    slots.append(rng(512, 896) + rng(1920, 1936))
    slots.append(rng(896, 1408))
    slots.append(rng(1408, 1920))
    slots.append(rng(1936, 2448))
    slots.append(rng(2448, 2960))
    slots.append(rng(2960, 2968) + rng(3992, 3996) + rng(2968, 3224))
    slots.append(rng(3224, 3736))
    slots.append(rng(3736, 3992))
    return slots


def _tile_rows(w, cols):
    out = np.zeros((128, 8, 512), np.float32)
    sub = w[:, cols]
    out[:, :, :len(cols)] = sub.reshape(8, 128, len(cols)).transpose(1, 0, 2)
    return out


_NC_CACHE = {}


def kernel(**inp):
    f = lambda k: np.ascontiguousarray(np.asarray(inp[k], dtype=np.float32))
    wall = np.zeros((2, 27, 128, 8, 512), np.float32)
    adaw = np.zeros((2, 12, 128, 8, 512), np.float32)
    cm = _win_colmap()
    w_in, w_out, w_up, w_down, ada_w = f("w_in"), f("w_out"), f("w_up"), f("w_down"), f("ada_w")
    for l in range(2):
        for s in range(9):
            wall[l, s] = _tile_rows(w_in[l], cm[s])
        for s in range(2):
            wall[l, 9 + s] = _tile_rows(w_out[l], list(range(s * 512, (s + 1) * 512)))
        for s in range(8):
            wall[l, 11 + s] = _tile_rows(w_up[l], list(range(s * 512, (s + 1) * 512)))
        for half in range(2):
            for fcg in range(4):
                blk = w_down[l][fcg * 1024:(fcg + 1) * 1024, half * 512:(half + 1) * 512]
                wall[l, 19 + half * 4 + fcg] = blk.reshape(8, 128, 512).transpose(1, 0, 2)
        for s in range(12):
            adaw[l, s] = _tile_rows(ada_w[l], list(range(s * 512, (s + 1) * 512)))
    pvv = np.zeros((128, 2 * NPV), np.float32)
    rows = np.zeros((128, 2 * NR), np.float32)
    sm = np.zeros((128, 2 * 1024), np.float32)

    def putv(l, name, vec, off=0):
        n = vec.shape[0] // 128
        pvv[:, l * NPV + PVO[name] + off:l * NPV + PVO[name] + off + n] = vec.reshape(n, 128).T
    for l in range(2):
        putv(l, "ada_b", f("ada_b")[l])
        putv(l, "n1w", f("norm1_w")[l])
        putv(l, "n2w", f("norm2_w")[l])
        putv(l, "mu", f("rwkv_mu")[l])
        putv(l, "a0", f("rwkv_a0")[l])
        putv(l, "k_k", f("rwkv_k_k")[l])
        putv(l, "k_a", f("rwkv_k_a")[l])
        putv(l, "r_k", f("rwkv_r_k")[l])
        putv(l, "ln_w", f("rwkv_ln_w")[l])
        putv(l, "ln_b", f("rwkv_ln_b")[l])
        putv(l, "gla_nw", f("gla_norm_w")[l])
        for i in range(4):
            putv(l, "dn_cw", f("dn_conv_w")[l, i], i * 6)
            putv(l, "ss_cw", f("ssm_conv_w")[l, i], i * 6)
        putv(l, "dn_nw", f("dn_norm_w")[l])
        putv(l, "ss_cb", f("ssm_conv_b")[l])
        putv(l, "ss_nw", f("ssm_norm_w")[l])
        putv(l, "ss_D", np.repeat(f("ssm_D")[l], 64))
        putv(l, "fnw", f("final_norm_w"))
        o = l * NR
        rows[:, o + RO["w0"]:o + RO["w0"] + 256] = f("rwkv_w0")[l][None, :]
        rows[:, o + RO["gkb"]:o + RO["gkb"] + 256] = f("gla_gk_b")[l][None, :]
        rows[:, o + RO["dnA"]:o + RO["dnA"] + 4] = f("dn_A_log")[l][None, :]
        rows[:, o + RO["dndt"]:o + RO["dndt"] + 4] = f("dn_dt_bias")[l][None, :]
        rows[:, o + RO["ssdt"]:o + RO["ssdt"] + 4] = f("ssm_dt_bias")[l][None, :]
        rows[:, o + RO["ssA"]:o + RO["ssA"] + 4] = f("ssm_A_log")[l][None, :]
        o = l * 1024
        sm[0:32, o:o + 256] = f("rwkv_w2")[l]
        sm[32:64, o + 256:o + 512] = f("rwkv_a2")[l]
        sm[64:128, o + 512:o + 768] = f("rwkv_g2")[l]
        sm[0:16, o + 768:o + 1024] = f("gla_gk_w2")[l]
    consts = make_consts()
    xp, xs = f("x_prompt"), f("x_sample")
    cpr, csm = f("c_prompt"), f("c_sample")
    stn = dict(shift="state_rwkv_shift", wkv="state_rwkv_wkv", gla="state_gla", dnc="state_dn_conv", dn="state_dn",
               ssc="state_ssm_conv", ssm="state_ssm")
    stf = {k: f(v) for k, v in stn.items()}
    in_maps = []
    for c in range(8):
        m = dict(wall=wall, ada=adaw, pv=pvv, rows=rows, sm=sm, consts=consts)
        m["xin"] = np.ascontiguousarray(np.concatenate([xp[c], xs[16 * c:16 * c + 16].reshape(128, 1024)], 0))
        m["cc"] = np.ascontiguousarray(np.concatenate([cpr[c:c + 1], csm[16 * c:16 * c + 16]], 0))
        for k in ST_SHAPES:
            m["st_" + k] = np.ascontiguousarray(stf[k][:, 16 * c:16 * c + 16])
        in_maps.append(m)
    if "nc" not in _NC_CACHE:
        _NC_CACHE["nc"] = build()
    res = run_bass_kernel_spmd(_NC_CACHE["nc"], in_maps, core_ids=list(range(8)))
    R = res.results
    global _LAST_R
    _LAST_R = R
    y_prompt = np.stack([R[c]["y"][:2048] for c in range(8)], 0)
    y_sample = np.concatenate([R[c]["y"][2048:].reshape(16, 8, 1024) for c in range(8)], 0)
    outs = [y_prompt, y_sample]
    for k in ("shift", "wkv", "gla", "dnc", "dn", "ssc", "ssm"):
        outs.append(np.stack([R[c]["p_" + k] for c in range(8)], 1))
    for k in ("shift", "wkv", "gla", "dnc", "dn", "ssc", "ssm"):
        outs.append(np.concatenate([R[c]["s_" + k] for c in range(8)], 1))
    return tuple(np.ascontiguousarray(o.astype(np.float32)) for o in outs)
```

```python
import numpy as np
import concourse.bass as bass
import concourse.mybir as mybir
from concourse.bass_utils import run_bass_kernel_spmd

F32 = mybir.dt.float32
BF16 = mybir.dt.bfloat16
AF = mybir.ActivationFunctionType
ALU = mybir.AluOpType
AX = mybir.AxisListType


class V:
    __slots__ = ("tile", "ap", "sub")

    def __init__(self, tile, ap, sub):
        self.tile, self.ap, self.sub = tile, ap, sub

    def __getitem__(self, idx):
        return V(self.tile, self.ap[idx], self.sub)

    def f(self, fn):
        return V(self.tile, fn(self.ap), self.sub)


class Tile:
    def __init__(self, h, name):
        self.h, self.name = h, name
        self.ww = {}
        self.wr = {}
        self.subs = {}

    def __getitem__(self, idx):
        return V(self, self.h[idx], None)

    def s(self, k, idx=None):
        ap = self.h[idx] if idx is not None else self.h[:]
        return V(self, ap, k)


def _mx(d, s, v):
    if d.get(s, 0) < v:
        d[s] = v


class Sched:
    ENG = ("pe", "act", "dve", "pool", "sp")
    NDMA = 12

    def __init__(self, nc, stack):
        self.nc = nc
        self.ops = {e: [] for e in self.ENG}
        self.cnt = {e: 0 for e in self.ENG}
        self.waited = {e: {} for e in self.ENG}
        self.sem = {}
        for e in ("pe", "act", "dve", "pool"):
            self.sem[e] = stack.enter_context(nc.semaphore("s_" + e))
        self.dq = {}
        for q in ("sp", "pool", "act"):
            n = self.NDMA if q == "sp" else 6
            sems = [stack.enter_context(nc.semaphore("d_%s%d" % (q, j))) for j in range(n)]
            for j, s_ in enumerate(sems):
                self.sem[("d", q, j)] = s_
            self.dq[q] = [n, 0, [0] * n]
        self.out_tokens = {}
        self.stack = stack
        self.ntile = 0

    def sb(self, shape, name=None, dt=F32):
        self.ntile += 1
        name = name or ("t%d" % self.ntile)
        h = self.stack.enter_context(self.nc.sbuf_tensor(name, list(shape), dt))
        return Tile(h, name)

    def ps(self, shape, name=None, dt=F32):
        self.ntile += 1
        name = name or ("p%d" % self.ntile)
        h = self.stack.enter_context(self.nc.psum_tensor(name, list(shape), dt))
        return Tile(h, name)

    def _deps(self, reads, writes):
        tok = {}
        for v in reads:
            t = v.tile
            for s, x in t.ww.items():
                _mx(tok, s, x)
            if v.sub is None:
                for k, (w, r) in t.subs.items():
                    for s, x in w.items():
                        _mx(tok, s, x)
            elif v.sub in t.subs:
                for s, x in t.subs[v.sub][0].items():
                    _mx(tok, s, x)
        for v in writes:
            t = v.tile
            for d in (t.ww, t.wr):
                for s, x in d.items():
                    _mx(tok, s, x)
            if v.sub is None:
                for k, (w, r) in t.subs.items():
                    for d in (w, r):
                        for s, x in d.items():
                            _mx(tok, s, x)
            elif v.sub in t.subs:
                for d in t.subs[v.sub]:
                    for s, x in d.items():
                        _mx(tok, s, x)
        return tok

    def _commit(self, reads, writes, s, x):
        for v in reads:
            t = v.tile
            if v.sub is None:
                _mx(t.wr, s, x)
            else:
                if v.sub not in t.subs:
                    t.subs[v.sub] = [{}, {}]
                _mx(t.subs[v.sub][1], s, x)
        for v in writes:
            t = v.tile
            if v.sub is None:
                t.ww = {s: x}
                t.wr = {}
                t.subs = {}
            else:
                t.subs[v.sub] = [{s: x}, {}]

    def _add(self, eng, fn, reads, writes, tokens_extra=None, dma_q=None, is_out=False):
        tok = self._deps(reads, writes)
        if tokens_extra:
            for s, x in tokens_extra.items():
                _mx(tok, s, x)
        if dma_q is not None:
            n, nxt, cnts = self.dq[dma_q]
            j = nxt
            self.dq[dma_q][1] = (nxt + 1) % n
            if cnts[j] > 0:
                _mx(tok, ("d", dma_q, j), 16 * cnts[j])
            cnts[j] += 1
            mysem, myval, inc = ("d", dma_q, j), 16 * cnts[j], 16
        else:
            self.cnt[eng] += 1
            mysem, myval, inc = eng, self.cnt[eng], 1
        waits = []
        wd = self.waited[eng]
        for s, x in tok.items():
            if eng == "pe" and s == "pe":
                continue
            if wd.get(s, 0) >= x:
                continue
            wd[s] = x
            waits.append((s, x))
        self.ops[eng].append((fn, waits, mysem, inc))
        self._commit(reads, writes, mysem, myval)
        if is_out:
            _mx(self.out_tokens, mysem, myval)

    def I(self, eng, meth, *args, **kw):
        reads, writes = [], []
        a2 = []
        for a in args:
            if isinstance(a, V):
                reads.append(a)
                a2.append(a.ap)
            else:
                a2.append(a)
        k2 = {}
        rw = kw.pop("_rw", False)
        for k, a in kw.items():
            if isinstance(a, V):
                if k in ("out", "accum_out"):
                    writes.append(a)
                    if rw:
                        reads.append(a)
                else:
                    reads.append(a)
                k2[k] = a.ap
            else:
                k2[k] = a
        fn = lambda e: getattr(e, meth)(*a2, **k2)
        self._add(eng, fn, reads, writes)

    def dma(self, q, out, in_, is_out=False, xr=(), xw=(), **kw):
        reads, writes = list(xr), list(xw)
        o = out.ap if isinstance(out, V) else out
        i = in_.ap if isinstance(in_, V) else in_
        if isinstance(out, V):
            writes.append(out)
        if isinstance(in_, V):
            reads.append(in_)
        fn = lambda e: e.dma_start(out=o, in_=i, **kw)
        self._add(q, fn, reads, writes, dma_q=q, is_out=is_out)

    def emit(self):
        nc = self.nc
        fin = [(s, x) for s, x in self.out_tokens.items()]
        eobj = {"pe": "tensor", "act": "scalar", "dve": "vector", "pool": "gpsimd", "sp": "sync"}
        with nc.Block() as block:
            for ename in self.ENG:
                ops = self.ops[ename]
                extra = fin if ename == "sp" else []

                def body(e, ops=ops, extra=extra):
                    for fn, waits, mysem, inc in ops:
                        for s, x in waits:
                            e.wait_ge(self.sem[s], x)
                        fn(e).then_inc(self.sem[mysem], inc)
                    for s, x in extra:
                        e.wait_ge(self.sem[s], x)
                getattr(block, eobj[ename])(body)

NL = 2
NTOK = 2176
NBLK = 17
C_DEC = 0.6065306597126334
CNAMES = ["IDENT", "ONES", "BLK", "TRS_P", "TRI_P", "SU_P", "NEGM_P", "TRS_S", "TRI_S", "SU_S", "NEGM_S"]
CI = {n: i * 128 for i, n in enumerate(CNAMES)}
C_SEG = 11 * 128
C_EPS = C_SEG + 16
C_GNEPS = C_EPS + 1
C_ONE = C_EPS + 2
NCONST = C_EPS + 4
NPV = 160
PVO = dict(ada_b=0, n1w=48, n2w=56, mu=64, a0=71, k_k=73, k_a=75, r_k=77, ln_w=79, ln_b=81, gla_nw=83,
           dn_cw=85, dn_nw=109, ss_cw=111, ss_cb=135, ss_nw=141, ss_D=143, fnw=145)
NR = 528
RO = dict(w0=0, gkb=256, dnA=512, dndt=516, ssdt=520, ssA=524)
ST_SHAPES = dict(shift=[2, 16, 896], wkv=[2, 16, 4, 64, 64], gla=[2, 16, 4, 64, 64], dnc=[2, 16, 3, 768],
                 dn=[2, 16, 4, 64, 64], ssc=[2, 16, 3, 768], ssm=[2, 16, 4, 64, 128])


def make_consts():
    c = np.zeros((128, NCONST), np.float32)
    s = np.arange(128)[:, None]
    i = np.arange(128)[None, :]
    same = (s // 8) == (i // 8)
    m = {}
    m["IDENT"] = (s == i)
    m["ONES"] = np.ones((128, 128))
    m["BLK"] = (s // 64) == (i // 64)
    m["TRI_P"] = s <= i
    m["TRS_P"] = s < i
    m["SU_P"] = s > i
    m["NEGM_P"] = (m["TRI_P"].astype(np.float32) - 1.0) * 30000.0
    m["TRI_S"] = (s <= i) & same
    m["TRS_S"] = (s < i) & same
    m["SU_S"] = (s > i) & same
    m["NEGM_S"] = (m["TRI_S"].astype(np.float32) - 1.0) * 30000.0
    for n in CNAMES:
        c[:, CI[n]:CI[n] + 128] = m[n].astype(np.float32)
    c[:, C_SEG:C_SEG + 16] = ((np.arange(128)[:, None] // 8) == np.arange(16)[None, :]).astype(np.float32)
    c[:, C_EPS] = 1e-6
    c[:, C_GNEPS] = 64e-5
    c[:, C_ONE] = 1.0
    return c


class BK:
    pass


def build():
    from contextlib import ExitStack
    nc = bass.Bass("TRN2", target_bir_lowering=False)

    def din(name, shape):
        return nc.dram_tensor(name, list(shape), F32, kind="ExternalInput").ap()

    def dout(name, shape):
        return nc.dram_tensor(name, list(shape), F32, kind="ExternalOutput").ap()

    xin = din("xin", [NTOK, 1024])
    ccd = din("cc", [17, 1024])
    std = {k: din("st_" + k, v) for k, v in ST_SHAPES.items()}
    wall = din("wall", [2, 27, 128, 8, 512])
    adad = din("ada", [2, 12, 128, 8, 512])
    pvd = din("pv", [128, 2 * NPV])
    rowsd = din("rows", [128, 2 * NR])
    smd = din("sm", [128, 2 * 4 * 256])
    constd = din("consts", [128, NCONST])
    yout = dout("y", [NTOK, 1024])
    nc.allow_low_precision("bf16 operands (fp32 PSUM accumulation) for the dense projections")
    wbf = nc.dram_tensor("wbf", [2, 27, 128, 8, 512], BF16).ap()
    pod = {k: dout("p_" + k, [v[0]] + v[2:]) for k, v in ST_SHAPES.items()}
    sod = {k: dout("s_" + k, v) for k, v in ST_SHAPES.items()}
    import os as _os
    KDBG = int(_os.environ.get("KDBG", "0"))
    dbgd = dout("dbg", [12, 128, 1024]) if KDBG else None

    with ExitStack() as es:
        S = Sched(nc, es)
        I = S.I
        CONST = S.sb([128, NCONST], "CONST")
        PV = S.sb([128, 2 * NPV], "PV")
        ROWS = S.sb([128, 2 * NR], "ROWS")
        SM = S.sb([128, 2 * 1024], "SM")
        S.dma("pool", CONST[:], constd)
        S.dma("pool", PV[:], pvd)
        S.dma("pool", ROWS[:], rowsd)
        S.dma("pool", SM[:], smd)

        def K(n):
            return CONST[:, CI[n]:CI[n] + 128]
        IDENT, ONES, BLK = K("IDENT"), K("ONES"), K("BLK")
        EPSc = CONST[:, C_EPS:C_EPS + 1]
        GNEPSc = CONST[:, C_GNEPS:C_GNEPS + 1]
        ONEc = CONST[:, C_ONE:C_ONE + 1]

        def pv(l, name, n=1, off=0):
            o = l * NPV + PVO[name] + off
            return PV[:, o:o + n]

        def row(l, name, n):
            o = l * NR + RO[name]
            return ROWS[:, o:o + n]

        def smat(l, j):
            o = l * 1024 + j * 256
            return SM[:, o:o + 256]

        PB = [S.ps([128, 512], "pb%d" % i) for i in range(8)]
        pbi = [0]

        def pb():
            t = PB[pbi[0] % 8]
            pbi[0] += 1
            return t

        WR = [S.sb([128, 4096], "wr%d" % i) for i in range(2)]
        wri = [0]
        WBF = Tile(None, "wbf_dep")

        def wr32(i):
            return V(WR[i], WR[i].h[:].rearrange("p (k c) -> p k c", c=512), None)

        def wr16(i, hf):
            return V(WR[i], WR[i].h[:].bitcast(BF16)[:, hf * 4096:(hf + 1) * 4096].rearrange("p (k c) -> p k c", c=512), hf)

        def wload(ap):
            i = wri[0] % 2
            wri[0] += 1
            t = wr32(i)
            S.dma("sp", t, ap)
            return t

        def wload16(l, sl):
            r = wri[0] % 4
            wri[0] += 1
            t = wr16(r // 2, r % 2)
            S.dma("sp", t, wbf[l, sl], xr=[V(WBF, None, (l, sl))])
            return t

        def mm(out, lhsT, rhs, start=True, stop=True):
            I("pe", "matmul", out=out, lhsT=lhsT, rhs=rhs, start=start, stop=stop, skip_group_check=True)

        def tr(out, in_, npart=128):
            I("pe", "transpose", out=out, in_=in_, identity=CONST[0:npart, 0:npart])

        def act(out, in_, func, bias=None, scale=1.0):
            if bias is None:
                I("act", "activation", out=out, in_=in_, func=func, scale=scale)
            else:
                I("act", "activation", out=out, in_=in_, func=func, bias=bias, scale=scale)

        def tt(out, a, b, op=ALU.mult, eng="dve"):
            I(eng, "tensor_tensor", out=out, in0=a, in1=b, op=op)

        def ts(out, a, s1, op0, s2=None, op1=None, eng="dve"):
            if op1 is None:
                I(eng, "tensor_scalar", out=out, in0=a, scalar1=s1, scalar2=None, op0=op0)
            else:
                I(eng, "tensor_scalar", out=out, in0=a, scalar1=s1, scalar2=s2, op0=op0, op1=op1)

        def stt(out, a, sc, b, op0, op1, eng="dve"):
            I(eng, "scalar_tensor_tensor", out=out, in0=a, scalar=sc, in1=b, op0=op0, op1=op1)

        def cp(out, in_, eng="dve"):
            if eng == "act":
                I("act", "activation", out=out, in_=in_, func=AF.Identity, scale=1.0)
            else:
                I("dve", "tensor_copy", out=out, in_=in_)

        G = [S.sb([128, 256], "g%d" % i) for i in range(28)]

        class GP:
            def __init__(self, pool=None):
                self.i = 0
                self.pool = G if pool is None else pool

            def t(self):
                t = self.pool[self.i]
                self.i += 1
                return t

        class Alias:
            def __init__(self, tile, base, sub):
                self.tile, self.base, self.sub = tile, base, sub

            def __getitem__(self, idx):
                return V(self.tile, self.base[idx], self.sub)

        MODT = S.sb([128, 2, 48, 17], "MODT")
        SC = S.sb([128, 2, 2, 8, 17], "SC")
        CT = S.sb([128, 8, 17], "CT")
        class Cx:
            pass
        CXS = []
        for i_ in range(2):
            c_ = Cx()
            c_.XTM = S.sb([128, 1024], "XTM%d" % i_)
            c_.XT = S.sb([128, 8, 128], "XT%d" % i_)
            c_.HT = S.sb([128, 8, 128], "HT%d" % i_, BF16)
            c_.YT = S.sb([128, 8, 128], "YT%d" % i_, BF16)
            CXS.append(c_)
        CUR = [CXS[0]]

        class Proxy:
            def __init__(self, name):
                self.name = name

            def __getitem__(self, idx):
                return getattr(CUR[0], self.name)[idx]
        XTM, XT, HT, YT = Proxy("XTM"), Proxy("XT"), Proxy("HT"), Proxy("YT")
        HTF = S.sb([128, 8, 128], "HTF")
        HID = S.sb([128, 32, 128], "HID", BF16)
        TMP5 = S.sb([128, 4, 128], "TMP5")
        UA = S.sb([128, 7, 144], "UA")
        XS = S.sb([128, 7, 128], "XS")
        XB = S.sb([128, 6, 176], "XB")
        CV = S.sb([128, 6, 128], "CV")
        PST = {}
        for l in range(NL):
            for mname in ("wkv", "gla", "dn", "ssm"):
                PST[(l, mname)] = S.sb([128, 2, 128], "pst_%s%d" % (mname, l))
                I("dve", "memset", PST[(l, mname)][:], 0.0)
        CARRY = {}
        for l in range(NL):
            CARRY[(l, "rw")] = S.sb([128, 7], "c_rw%d" % l)
            CARRY[(l, "dn")] = S.sb([128, 6, 3], "c_dn%d" % l)
            CARRY[(l, "ss")] = S.sb([128, 6, 3], "c_ss%d" % l)
            for k in ("rw", "dn", "ss"):
                I("dve", "memset", CARRY[(l, k)][:], 0.0)
        SST = [S.sb([128, 16, 2, 128], "sst%d" % i) for i in range(2)]
        for t in SST:
            I("dve", "memset", t[:], 0.0)
        G2 = [Alias(SST[j], SST[j].h[:, s_].rearrange("p r v -> p (r v)"), ("a", s_)) for j in range(2) for s_ in range(16)]
        PADS = [S.sb([128, 2, 128], "pad%d" % i) for i in range(3)]
        for t in PADS:
            I("dve", "memset", t[:], 0.0)

        Pk, Sk = BK(), BK()
        Pk.nseg, Pk.L, Pk.sfx, Pk.nsolve = 1, 128, "_P", 7
        Sk.nseg, Sk.L, Sk.sfx, Sk.nsolve = 16, 8, "_S", 3
        for bk in (Pk, Sk):
            bk.TRI, bk.TRS, bk.SU, bk.NEGM = K("TRI" + bk.sfx), K("TRS" + bk.sfx), K("SU" + bk.sfx), K("NEGM" + bk.sfx)

        def bc(v, shape, axis):
            return v.f(lambda a: a.unsqueeze(axis).to_broadcast(shape))

        def modv(v3, bk, nch=8):
            if bk.nseg == 1:
                return bc(v3[:, :, 0:1], [128, nch, 1, 128], 3)
            return bc(v3[:, :, 1:17], [128, nch, 16, 8], 3)

        def dv(v, bk):
            return v.f(lambda a: a.rearrange("p c (s t) -> p c s t", t=bk.L))

        ctm = XTM
        S.dma("pool", ctm[0:17, :], ccd)
        act(ctm[0:17, :], ctm[0:17, :], AF.Silu)
        for kc in range(8):
            p = pb()
            tr(p[:, 0:17], ctm[0:17, kc * 128:(kc + 1) * 128], 17)
            cp(CT[:, kc, :], p[:, 0:17])
        for l in range(NL):
            for s in range(12):
                w = wload(adad[l, s])
                p = pb()
                for j in range(4):
                    for kc in range(8):
                        mm(p[:, j * 17:(j + 1) * 17], w[:, kc, j * 128:(j + 1) * 128], CT[:, kc, :], kc == 0, kc == 7)
                tt(MODT[:, l, 4 * s:4 * s + 4, :], p[:, 0:68].f(lambda a: a.rearrange("p (j n) -> p j n", n=17)),
                   bc(pv(l, "ada_b", 4, 4 * s), [128, 4, 17], 2), ALU.add)
            stt(SC[:, l, 0], MODT[:, l, 8:16, :], 1.0, bc(pv(l, "n1w", 8), [128, 8, 17], 2), ALU.add, ALU.mult)
            stt(SC[:, l, 1], MODT[:, l, 32:40, :], 1.0, bc(pv(l, "n2w", 8), [128, 8, 17], 2), ALU.add, ALU.mult)

        STG = [wr32(0)] + [V(SST[j], SST[j].h[:].rearrange("p s r v -> p (s r v)").rearrange("p (k c) -> p k c", c=512), None)
                           for j in range(2)]
        for l in range(NL):
            for sl in range(27):
                i = (l * 27 + sl)
                src = STG[i % 3]
                S.dma("sp", src, wall[l, sl])
                dst = wr16(1, i % 2)
                I("dve", "tensor_copy", out=dst[:, 0:4, :], in_=src[:, 0:4, :])
                I("act", "activation", out=dst[:, 4:8, :], in_=src[:, 4:8, :], func=AF.Identity, scale=1.0)
                S.dma("sp", wbf[l, sl], dst, xw=[V(WBF, None, (l, sl))])

        RS = S.sb([128, 128], "RS")

        def norm_mod(bk, scale_bv, shift_bv, outT=None):
            outT = HT if outT is None else outT
            HIDF = V(HID, HID.h[:].rearrange("p c t -> p (c t)").bitcast(F32), None)
            p = pb()
            for kc in range(8):
                sq = HIDF[:, kc * 128:(kc + 1) * 128]
                act(sq, XT[:, kc, :], AF.Square)
                mm(p[:, 0:128], ONES, sq, kc == 0, kc == 7)
            act(RS[:], p[:, 0:128], AF.Ln, bias=EPSc, scale=1.0 / 1024.0)
            act(RS[:], RS[:], AF.Exp, scale=-0.5)
            tt(HTF[:], XT[:], bc(RS[:], [128, 8, 128], 1))
            if shift_bv is not None:
                tt(dv(HTF[:], bk), dv(HTF[:], bk), scale_bv)
                tt(dv(outT[:], bk), dv(HTF[:], bk), shift_bv, ALU.add)
            else:
                tt(dv(outT[:], bk), dv(HTF[:], bk), scale_bv)

        def proj_fm(out, w, c0, M):
            for kc in range(8):
                mm(out, w[:, kc, c0:c0 + M], HT[:, kc, :], kc == 0, kc == 7)

        def proj_tm(out, w, c0, N):
            for kc in range(8):
                mm(out, HT[:, kc, :], w[:, kc, c0:c0 + N], kc == 0, kc == 7)

        def sv(v, bk):
            return v.f(lambda a: a.rearrange("p (s t) -> p s t", t=bk.L))

        def lastcol(v, bk, s):
            c = s * bk.L + bk.L - 1
            return v[:, c:c + 1]

        def tri_solve(bk, NTs, Ns, X, g):
            tmpN = [[g.t(), g.t()] for _ in range(2)]
            for k in range(bk.nsolve):
                p = pb()
                for h in range(2):
                    mm(p[:, h * 64:h * 64 + 64], NTs[h][:, 0:128], X[:, h * 64:h * 64 + 64])
                tt(X[:, 0:128], X[:, 0:128], p[:, 0:128], ALU.add)
                yield
                if k == bk.nsolve - 1:
                    break
                for h in range(2):
                    p2 = pb()
                    mm(p2[:, 0:128], Ns[h][:, 0:128], NTs[h][:, 0:128])
                    if k < bk.nsolve - 2:
                        mm(p2[:, 128:256], NTs[h][:, 0:128], Ns[h][:, 0:128])
                    nn = tmpN[h][k % 2]
                    if k < bk.nsolve - 2:
                        cp(nn[:, 0:256], p2[:, 0:256], "act")
                    else:
                        cp(nn[:, 0:128], p2[:, 0:128], "act")
                    NTs[h] = nn
                    Ns[h] = _Shift(nn)
                    yield
            return

        class _Shift:
            def __init__(self, base):
                self.base = base

            def __getitem__(self, idx):
                assert idx == (slice(None), slice(0, 128))
                return self.base[:, 128:256]

        def post_rms(Y, ones, gsize, nw_col, gateT, out, g):
            sq = g.t()
            act(sq[:, 0:128], Y, AF.Square)
            p = pb()
            mm(p[:, 0:128], ones, sq[:, 0:128])
            r = g.t()
            act(r[:, 0:128], p[:, 0:128], AF.Ln, bias=EPSc, scale=1.0 / gsize)
            act(r[:, 0:128], r[:, 0:128], AF.Exp, scale=-0.5)
            stt(r[:, 128:256], Y, nw_col, r[:, 0:128], ALU.mult, ALU.mult)
            if gateT is not None:
                tt(out, r[:, 128:256], gateT)
            else:
                cp(out, r[:, 128:256])

        def load_bd(sst, src):
            for h in range(4):
                b = 64 * (h % 2)
                S.dma("pool", sst[b:b + 64, :, h // 2, b:b + 64], src[:, h].rearrange("s d v -> d s v"))

        def store_bd(dst, tile_bd, pr):
            for hh in range(2):
                b = 64 * hh
                S.dma("pool", dst[2 * pr + hh], tile_bd[b:b + 64, b:b + 64], is_out=True)

        OST = [S.sb([128, 128], "ost%d" % i) for i in range(4)]
        osti = [0]

        def ost():
            t = OST[osti[0] % 4]
            osti[0] += 1
            return t

        def state_update(bk, l, mname, pr, lhs, rhs, pcv, sst, outd, last, g, transpose_out=False):
            for s in range(bk.nseg):
                p = pb()
                for k in range(len(lhs)):
                    lv = lhs[k]
                    if bk.nseg > 1:
                        m = g_rot()
                        I("act", "mul", out=m[:, 0:128], in_=lv, mul=CONST[:, C_SEG + s:C_SEG + s + 1])
                        lv = m[:, 0:128]
                    mm(p[:, 0:128], lv, rhs[k], k == 0, k == len(lhs) - 1)
                tmp = g_rot()
                tt(tmp[:, 0:128], p[:, 0:128], BLK)
                if bk.nseg == 1:
                    hp = PST[(l, mname)][:, pr, :]
                    stt(hp, hp, lastcol(pcv, bk, 0), tmp[:, 0:128], ALU.mult, ALU.add)
                    if last:
                        if transpose_out:
                            p2 = pb()
                            tr(p2[:, 0:128], hp)
                            o = ost()
                            cp(o[:], p2[:, 0:128])
                            store_bd(outd[l], o, pr)
                        else:
                            store_bd(outd[l], PST[(l, mname)][:, pr, :], pr)
                else:
                    hp = sst[:, s, pr, :]
                    o = ost()
                    stt(o[:], hp, lastcol(pcv, bk, s), tmp[:, 0:128], ALU.mult, ALU.add)
                    if transpose_out:
                        p2 = pb()
                        tr(p2[:, 0:128], o[:])
                        o2 = ost()
                        cp(o2[:], p2[:, 0:128])
                        o = o2
                    store_bd(outd[l, s], o, pr)

        GR = [S.sb([128, 128], "gr%d" % i) for i in range(6)]
        gri = [0]

        def g_rot():
            t = GR[gri[0] % 6]
            gri[0] += 1
            return t

        def inter(bk, l, mname, pr, sst, out_ps, opT, stop):
            for s in range(bk.nseg):
                hp = PST[(l, mname)][:, pr, :] if bk.nseg == 1 else sst[:, s, pr, :]
                mm(out_ps[:, s * bk.L:(s + 1) * bk.L], hp, opT[:, s * bk.L:(s + 1) * bk.L], s == 0, stop)

        def decay_mask(bk, la_col, out, g):
            t = g.t()
            ts(t[:, 0:128], bk.SU, la_col, ALU.mult)
            p = pb()
            mm(p[:, 0:128], t[:, 0:128], bk.TRI, True, False)
            mm(p[:, 0:128], IDENT, bk.NEGM, False, True)
            act(out, p[:, 0:128], AF.Exp)

        def bc2(v, shape):
            return v.f(lambda a: a.unsqueeze(2).unsqueeze(3).to_broadcast(shape))

        GLT = S.sb([16, 128], "GLT")
        SHT = S.sb([48, 896], "SHT")
        CST = SHT
        CSO = S.sb([48, 768], "CSO")
        CS3 = S.sb([128, 48], "CS3")
        SMT = S.sb([128, 64], "SMT")
        ZT = S.sb([128, 512], "ZT")
        TMP4 = S.sb([128, 4, 128], "TMP4")
        YTM = XTM

        def dump(k, tile3):
            if not KDBG:
                return
            for b2 in range(2):
                p = pb()
                for j in range(4):
                    tr(p[:, j * 128:(j + 1) * 128], tile3[:, 4 * b2 + j, :])
                cp(TMP4[:].f(lambda a: a.rearrange("p c t -> p (c t)")), p[:, 0:512])
                S.dma("pool", dbgd[k, :, b2 * 512:(b2 + 1) * 512], TMP4[:].f(lambda a: a.rearrange("p c t -> p (c t)")), is_out=True)
        SSL = [S.sb([64, 4, 128], "ssl%d" % i) for i in range(1)]

        def rwkv(bk, l, last, wget, sst, pool=None):
            g = GP(pool)
            w0 = wget(0)
            w1 = wget(1)
            nseg, L = bk.nseg, bk.L
            W = L + 1
            sfx = bk.sfx
            UAv = UA[:, :, 0:nseg * W].f(lambda a: a.rearrange("p c (s t) -> p c s t", t=W))
            if nseg == 1:
                cp(UA[:, :, 0:1], CARRY[(l, "rw")][:].f(lambda a: a.unsqueeze(2)))
            else:
                S.dma("pool", SHT[0:16, :], std["shift"][l])
                for c in range(7):
                    p = pb()
                    tr(p[:, 0:16], SHT[0:16, c * 128:(c + 1) * 128], 16)
                    cp(UAv[:, c, :, 0], p[:, 0:16])
            for c in range(7):
                w = w0 if c < 4 else w1
                p = pb()
                proj_fm(p[:, 0:128], w, (c % 4) * 128, 128)
                cp(UAv[:, c, :, 1:W], sv(p[:, 0:128], bk), "act" if c % 2 else "dve")
            p = pb()
            proj_fm(p[0:16, 0:128], w1, 384, 16)
            cp(GLT[0:16, :], p[0:16, 0:128])
            if nseg == 1:
                cp(CARRY[(l, "rw")][:].f(lambda a: a.unsqueeze(2)), UA[:, :, 128:129])
                if last:
                    S.dma("pool", pod["shift"][l].rearrange("(c p) -> p c", p=128), CARRY[(l, "rw")][:], is_out=True,
                          allow_slow_non_contiguous=True)
            else:
                for c in range(7):
                    S.dma("pool", sod["shift"][l][:, c * 128:(c + 1) * 128].rearrange("s p -> p s"), UAv[:, c, :, L],
                          is_out=True, allow_slow_non_contiguous=True)
            ck('rw1')
            XSv = dv(XS[:], bk)
            tt(XSv, UAv[:, :, :, 0:L], UAv[:, :, :, 1:W], ALU.subtract)
            tt(XSv, XSv, bc2(pv(l, "mu", 7), [128, 7, nseg, L]))
            tt(XSv, XSv, UAv[:, :, :, 1:W], ALU.add)
            ck('rw2')
            X6 = XS[:, 6, :]
            T6 = g.t()
            act(T6[:, 0:128], X6, AF.Tanh)
            act(T6[:, 128:256], X6, AF.Sigmoid)
            p = pb()
            mm(p[:, 0:256], T6[:, 0:128], smat(l, 0))
            LAM = g.t()
            tt(LAM[:], p[:, 0:256], row(l, "w0", 256), ALU.add)
            act(LAM[:], LAM[:], AF.Sigmoid)
            AT, GTt = g.t(), g.t()
            for pr in range(2):
                p = pb()
                mm(p[:, 0:128], smat(l, 1)[:, pr * 128:(pr + 1) * 128], X6)
                act(AT[:, pr * 128:(pr + 1) * 128], p[:, 0:128], AF.Sigmoid, bias=pv(l, "a0", 1, pr))
                mm(p[:, 128:256], smat(l, 2)[:, pr * 128:(pr + 1) * 128], T6[:, 128:256])
                cp(GTt[:, pr * 128:(pr + 1) * 128], p[:, 128:256])
            ck('rw3')
            yield
            KK, KP, E, E2, EQT, KC, K2C2, VT, KC2, RT, X, YS, Mt, Bt = [g.t() for _ in range(14)]
            AE = [g.t(), g.t()]
            AC = [g.t(), g.t()]
            Nn = [g.t(), g.t()]
            gsave = g.i
            MASK2 = CONST[:, CI["TRS" + sfx]:CI["TRS" + sfx] + 256]
            TRI3 = CONST[:, CI["TRS" + sfx]:CI["TRS" + sfx] + 384]
            Vp, Up = PADS[0], PADS[1]
            for pr in range(2):
                g.i = gsave
                rT, kT, vT = XS[:, pr, :], XS[:, 2 + pr, :], XS[:, 4 + pr, :]
                aT = AT[:, pr * 128:(pr + 1) * 128]
                ts(KK[:, 0:128], kT, pv(l, "k_k", 1, pr), ALU.mult)
                act(KK[:, 128:256], KK[:, 0:128], AF.Square)
                p = pb()
                mm(p[:, 0:128], BLK, KK[:, 128:256])
                act(KK[:, 128:256], p[:, 0:128], AF.Ln, bias=EPSc)
                act(KK[:, 128:256], KK[:, 128:256], AF.Exp, scale=-0.5)
                tt(KK[:, 0:128], KK[:, 0:128], KK[:, 128:256])
                ts(KP[:, 0:128], aT, -1.0, ALU.add, pv(l, "k_a", 1, pr), ALU.mult)
                stt(KP[:, 0:128], KP[:, 0:128], 1.0, kT, ALU.add, ALU.mult)
                tt(KP[:, 128:256], KK[:, 0:128], aT)
                stt(RT[:, 128:256], rT, pv(l, "r_k", 1, pr), KP[:, 0:128], ALU.mult, ALU.mult)
                p3 = pb()
                mm(p3[:, 0:128], BLK, RT[:, 128:256])
                tt(Bt[:, 128:256], p3[:, 0:128], vT)
                p = pb()
                mm(p[:, 0:384], LAM[:, pr * 128:(pr + 1) * 128], TRI3)
                act(E[:, 0:256], p[:, 0:256], AF.Exp, scale=-C_DEC)
                act(E2[:, 0:128], p[:, 256:384], AF.Exp, scale=-C_DEC)
                act(E2[:, 128:256], p[:, 128:256], AF.Exp, scale=C_DEC)
                tt(EQT[:, 0:128], KK[:, 0:128], E[:, 0:128])
                tt(EQT[:, 128:256], rT, E[:, 128:256])
                tt(KC[:, 0:128], KP[:, 0:128], E2[:, 128:256])
                tt(KC[:, 128:256], KP[:, 128:256], E2[:, 128:256])
                tt(K2C2[:, 0:128], KP[:, 0:128], E2[:, 0:128])
                stt(K2C2[:, 128:256], KP[:, 128:256], -1.0, E2[:, 0:128], ALU.mult, ALU.mult)
                ck('rw4')
                yield
                p = pb()
                tr(p[:, 0:128], vT)
                tr(p[:, 128:256], K2C2[:, 0:128])
                tr(p[:, 256:384], K2C2[:, 128:256])
                cp(VT[:, 0:128], p[:, 0:128])
                ck('rw4a1')
                cp(Vp[:, 0, 0:64], p[:, 0:64])
                cp(Vp[:, 1, 64:128], p[:, 64:128])
                ck('rw4a2')
                cp(KC2[:, 0:256], p[:, 128:384])
                ck('rw4b')
                yield
                for hh in range(2):
                    b = 64 * hh
                    if hh == 1:
                        ck('rw4c')
                    p = pb()
                    mm(p[:, 0:256], KC[b:b + 64, 0:128], EQT[b:b + 64, 0:256])
                    mm(p[:, 256:512], KC[b:b + 64, 128:256], EQT[b:b + 64, 0:256])
                    tt(AE[hh][:, 0:256], p[:, 0:256], MASK2)
                    stt(AC[hh][:, 0:256], p[:, 256:512], -1.0, MASK2, ALU.mult, ALU.mult)
                    p2 = pb()
                    tr(p2[:, 0:128], AC[hh][:, 0:128])
                    cp(Nn[hh][:, 0:128], p2[:, 0:128], "act")
                    yield
                ck('rw5')
                if nseg == 1:
                    p = pb()
                    mm(p[:, 0:128], EQT[:, 0:128], PST[(l, "wkv")][:, pr, :], True, False)
                    for hh in range(2):
                        mm(p[:, 0:128], AE[hh][:, 0:128], Vp[:, hh, :], False, hh == 1)
                    cp(X[:, 0:128], p[:, 0:128])
                    yield
                else:
                    p = pb()
                    inter(bk, l, "wkv", pr, sst, p, EQT[:, 0:128], False)
                    for hh in range(2):
                        mm(p[:, 0:128], Vp[:, hh, :], AE[hh][:, 0:128], False, hh == 1)
                    cp(RT[:, 0:128], p[:, 0:128])
                    yield
                    p = pb()
                    tr(p[:, 0:128], RT[:, 0:128])
                    cp(X[:, 0:128], p[:, 0:128])
                    yield
                ck('rw6')
                yield from tri_solve(bk, [AC[0], AC[1]], [Nn[0], Nn[1]], X, g)
                ck('rw7')
                cp(Up[:, 0, 0:64], X[:, 0:64])
                cp(Up[:, 1, 64:128], X[:, 64:128])
                p = pb()
                inter(bk, l, "wkv", pr, sst, p, EQT[:, 128:256], False)
                for hh in range(2):
                    mm(p[:, 0:128], Vp[:, hh, :], AE[hh][:, 128:256], False, False)
                    mm(p[:, 0:128], Up[:, hh, :], AC[hh][:, 128:256], False, hh == 1)
                ck('rw8')
                cp(YS[:, 0:128], p[:, 0:128])
                act(YS[:, 128:256], p[:, 0:128], AF.Square)
                yield
                p2 = pb()
                mm(p2[:, 0:256], BLK, YS[:, 0:256])
                ts(Mt[:, 0:256], p2[:, 0:256], 1.0 / 64.0, ALU.mult)
                stt(Bt[:, 0:128], Mt[:, 0:128], -1.0, Mt[:, 0:128], ALU.mult, ALU.mult)
                tt(Mt[:, 128:256], Mt[:, 128:256], Bt[:, 0:128], ALU.add)
                act(Mt[:, 128:256], Mt[:, 128:256], AF.Ln, bias=GNEPSc)
                act(Mt[:, 128:256], Mt[:, 128:256], AF.Exp, scale=-0.5)
                tt(YS[:, 0:128], YS[:, 0:128], Mt[:, 0:128], ALU.subtract)
                tt(YS[:, 0:128], YS[:, 0:128], Mt[:, 128:256])
                ts(YS[:, 0:128], YS[:, 0:128], pv(l, "ln_w", 1, pr), ALU.mult, pv(l, "ln_b", 1, pr), ALU.add)
                tt(YS[:, 0:128], YS[:, 0:128], Bt[:, 128:256], ALU.add)
                tt(YT[:, pr, :], YS[:, 0:128], GTt[:, pr * 128:(pr + 1) * 128])
                yield
                ck('rw9')
                state_update(bk, l, "wkv", pr, [KC2[:, 0:128], KC2[:, 128:256]], [VT[:, 0:128], X[:, 0:128]],
                             E[:, 128:256], sst, (pod if nseg == 1 else sod)["wkv"], last, g, transpose_out=True)

        def gla(bk, l, last, wget, sst, pool=None):
            g = GP(pool)
            w2 = wget(2)
            QK = [g.t(), g.t()]
            for c in range(4):
                p = pb()
                proj_fm(p[:, 0:128], w2, c * 128, 128)
                cp(QK[c % 2][:, (c // 2) * 128:(c // 2) * 128 + 128], p[:, 0:128], "act" if c % 2 else "dve")
            KTM = g.t()
            p = pb()
            proj_tm(p[:, 0:256], w2, 256, 256)
            cp(KTM[:], p[:, 0:256], "act")
            w3 = wget(3)
            VTM = g.t()
            p = pb()
            proj_tm(p[:, 0:256], w3, 0, 256)
            cp(VTM[:], p[:, 0:256])
            Vp = [PADS[0], PADS[2]]
            for pr in range(2):
                cp(Vp[pr][:, 0, 0:64], p[:, pr * 128:pr * 128 + 64])
                cp(Vp[pr][:, 1, 64:128], p[:, pr * 128 + 64:pr * 128 + 128])
            GT = g.t()
            for pr in range(2):
                p = pb()
                proj_fm(p[:, 0:128], w3, 256 + pr * 128, 128)
                act(GT[:, pr * 128:(pr + 1) * 128], p[:, 0:128], AF.Silu)
            p = pb()
            mm(p[:, 0:256], GLT[0:16, :], smat(l, 3)[0:16, :])
            LA = g.t()
            tt(LA[:], p[:, 0:256], row(l, "gkb", 256), ALU.add)
            act(LA[:], LA[:], AF.Exp, scale=-1.0)
            act(LA[:], LA[:], AF.Ln, bias=ONEc)
            p = pb()
            mm(p[:, 0:256], bk.SU, LA[:])
            K2 = g.t()
            act(K2[:], p[:, 0:256], AF.Exp, scale=-1.0 / 16.0)
            tt(K2[:], K2[:], KTM[:])
            yield
            E, QKh, YS = g.t(), g.t(), g.t()
            A = [g.t(), g.t()]
            gsave = g.i
            for pr in range(2):
                g.i = gsave
                p = pb()
                mm(p[:, 0:128], LA[:, pr * 128:(pr + 1) * 128], bk.TRI)
                act(E[:, 0:128], p[:, 0:128], AF.Exp, scale=-1.0 / 16.0)
                act(E[:, 128:256], p[:, 0:128], AF.Exp, scale=1.0 / 16.0)
                stt(QKh[:, 0:128], QK[pr][:, 0:128], 0.125, E[:, 0:128], ALU.mult, ALU.mult)
                tt(QKh[:, 128:256], QK[pr][:, 128:256], E[:, 128:256])
                for hh in range(2):
                    b = 64 * hh
                    p = pb()
                    mm(p[:, 0:128], QKh[b:b + 64, 128:256], QKh[b:b + 64, 0:128])
                    tt(A[hh][:, 0:128], p[:, 0:128], bk.TRI)
                    yield
                p = pb()
                inter(bk, l, "gla", pr, sst, p, QKh[:, 0:128], False)
                for hh in range(2):
                    mm(p[:, 0:128], Vp[pr][:, hh, :], A[hh][:, 0:128], False, hh == 1)
                cp(YS[:, 0:128], p[:, 0:128])
                yield
                post_rms(YS[:, 0:128], BLK, 64.0, pv(l, "gla_nw", 1, pr), GT[:, pr * 128:(pr + 1) * 128], YT[:, 2 + pr, :], g)
                yield
                state_update(bk, l, "gla", pr, [K2[:, pr * 128:(pr + 1) * 128]], [VTM[:, pr * 128:(pr + 1) * 128]],
                             E[:, 0:128], sst, (pod if bk.nseg == 1 else sod)["gla"], last, g)
                yield

        def conv_in(bk, l, ckey, skey):
            nseg, L = bk.nseg, bk.L
            W = L + 3
            XBv = XB[:, :, 0:nseg * W].f(lambda a: a.rearrange("p c (s t) -> p c s t", t=W))
            if nseg == 1:
                cp(XBv[:, :, 0, 0:3], CARRY[(l, ckey)][:])
            else:
                S.dma("pool", CST[0:48, 0:768], std[skey][l].rearrange("s i c -> (s i) c"))
                for c in range(6):
                    p = pb()
                    tr(p[:, 0:48], CST[0:48, c * 128:(c + 1) * 128], 48)
                    cp(XBv[:, c, :, 0:3], p[:, 0:48].f(lambda a: a.rearrange("p (s i) -> p s i", i=3)))
            return XBv

        def conv_run(bk, l, ckey, skey, cwname, XBv, last):
            nseg, L = bk.nseg, bk.L
            W = L + 3
            if nseg == 1:
                cp(CARRY[(l, ckey)][:], XBv[:, :, 0, L:L + 3])
                if last:
                    for c in range(6):
                        S.dma("pool", pod[skey][l][:, c * 128:(c + 1) * 128].rearrange("i p -> p i"), CARRY[(l, ckey)][:, c, :],
                              is_out=True, allow_slow_non_contiguous=True)
            else:
                for c in range(6):
                    cp(CS3[:].f(lambda a: a.rearrange("p (s i) -> p s i", i=3)), XBv[:, c, :, L:L + 3])
                    p = pb()
                    tr(p[0:48, 0:128], CS3[:])
                    cp(CSO[0:48, c * 128:(c + 1) * 128], p[0:48, 0:128], "act")
                S.dma("pool", sod[skey][l].rearrange("s i c -> (s i) c"), CSO[:], is_out=True)
            CVv = dv(CV[:], bk)
            TMv = dv(HTF[:, 0:6, :], bk)
            for i in range(4):
                wv = bc2(pv(l, cwname, 6, i * 6), [128, 6, nseg, L])
                if i == 0:
                    tt(CVv, XBv[:, :, :, 0:L], wv)
                else:
                    tt(TMv, XBv[:, :, :, i:i + L], wv)
                    tt(CVv, CVv, TMv, ALU.add)

        def softplus_cols(dst, src, biasrow):
            tt(dst, src, biasrow, ALU.add)
            act(dst, dst, AF.Exp)
            act(dst, dst, AF.Ln, bias=ONEc)

        def dnet(bk, l, last, wget, sst, pool=None):
            g = GP(pool)
            nseg, L = bk.nseg, bk.L
            W = L + 3
            XBv = conv_in(bk, l, "dn", "dnc")
            w4 = wget(4)
            for c in range(4):
                p = pb()
                proj_fm(p[:, 0:128], w4, c * 128, 128)
                cp(XBv[:, c, :, 3:W], sv(p[:, 0:128], bk), "act" if c % 2 else "dve")
            w5 = wget(5)
            for c in range(2):
                p = pb()
                proj_fm(p[:, 0:128], w5, c * 128, 128)
                cp(XBv[:, 4 + c, :, 3:W], sv(p[:, 0:128], bk), "act" if c % 2 else "dve")
            for c in range(2):
                p = pb()
                proj_fm(p[:, 0:128], w5, 256 + c * 128, 128)
                act(ZT[:, c * 128:(c + 1) * 128], p[:, 0:128], AF.Silu)
            w6 = wget(6)
            p = pb()
            proj_tm(p[:, 0:12], w6, 0, 12)
            cp(SMT[:, 0:12], p[:, 0:12])
            for c in range(2):
                p = pb()
                proj_fm(p[:, 0:128], w6, 12 + c * 128, 128)
                act(ZT[:, 256 + c * 128:256 + (c + 1) * 128], p[:, 0:128], AF.Silu)
            conv_run(bk, l, "dn", "dnc", "dn_cw", XBv, last)
            act(CV[:], CV[:], AF.Silu)
            act(SMT[:, 16:20], SMT[:, 4:8], AF.Sigmoid)
            softplus_cols(SMT[:, 20:24], SMT[:, 0:4], row(l, "dndt", 4))
            act(SMT[:, 24:28], row(l, "dnA", 4), AF.Exp)
            stt(SMT[:, 28:32], SMT[:, 20:24], -1.0, SMT[:, 24:28], ALU.mult, ALU.mult)
            p = pb()
            mm(p[:, 0:4], bk.SU, SMT[:, 28:32])
            act(SMT[:, 32:36], p[:, 0:4], AF.Exp)
            yield
            SQ, LB, ELb, R, X, QE, YS, K2, SQ2 = [g.t() for _ in range(9)]
            KQ = [g.t(), g.t()]
            ET = [g.t(), g.t()]
            A = [g.t(), g.t()]
            T0 = [g.t(), g.t()]
            Nn = [g.t(), g.t()]
            NT = [g.t(), g.t()]
            gsave = g.i
            Wp = PADS[2]
            for pr in range(2):
                g.i = gsave
                for which, (src, dst, scl) in enumerate(((CV[:, 2 + pr, :], KQ[pr][:, 0:128], 1.0),
                                                         (CV[:, pr, :], KQ[pr][:, 128:256], 0.125))):
                    sqt = SQ2 if which else SQ
                    act(sqt[:, 0:128], src, AF.Square)
                    p = pb()
                    mm(p[:, 0:128], BLK, sqt[:, 0:128])
                    act(sqt[:, 128:256], p[:, 0:128], AF.Ln, bias=EPSc)
                    act(sqt[:, 128:256], sqt[:, 128:256], AF.Exp, scale=-0.5)
                    stt(dst, src, scl, sqt[:, 128:256], ALU.mult, ALU.mult)
                tt(LB[:, 0:128].f(lambda a: a.rearrange("p (h d) -> p h d", d=64)),
                   ONES.f(lambda a: a.rearrange("p (h d) -> p h d", d=64)),
                   bc(SMT[:, 28 + 2 * pr:30 + 2 * pr], [128, 2, 64], 2))
                p = pb()
                mm(p[:, 0:128], LB[:, 0:128], bk.TRI)
                act(ELb[:, 0:128], p[:, 0:128], AF.Exp)
                yield
                p = pb()
                inter(bk, l, "dn", pr, sst, p, KQ[pr][:, 0:128], True)
                tt(R[:, 0:128], p[:, 0:128], ELb[:, 0:128])
                tt(R[:, 0:128], CV[:, 4 + pr, :], R[:, 0:128], ALU.subtract)
                yield
                p = pb()
                tr(p[:, 0:128], R[:, 0:128])
                for hh in range(2):
                    h = 2 * pr + hh
                    ts(X[:, hh * 64:hh * 64 + 64], p[:, hh * 64:hh * 64 + 64], SMT[:, 16 + h:17 + h], ALU.mult)
                yield
                tt(QE[:, 0:128], KQ[pr][:, 128:256], ELb[:, 0:128])
                p = pb()
                tr(p[:, 0:128], KQ[pr][:, 0:128])
                for hh in range(2):
                    h = 2 * pr + hh
                    ts(K2[:, hh * 64:hh * 64 + 64], p[:, hh * 64:hh * 64 + 64], SMT[:, 32 + h:33 + h], ALU.mult)
                yield
                for hh in range(2):
                    h = 2 * pr + hh
                    b = 64 * hh
                    decay_mask(bk, SMT[:, 28 + h:29 + h], ET[hh][:, 0:128], g)
                    tt(ET[hh][:, 128:256], ET[hh][:, 0:128], bk.TRS)
                    p = pb()
                    mm(p[:, 0:256], KQ[pr][b:b + 64, 0:128], KQ[pr][b:b + 64, 0:256])
                    tt(A[hh][:, 0:128], p[:, 128:256], ET[hh][:, 0:128])
                    tt(T0[hh][:, 0:128], p[:, 0:128], ET[hh][:, 128:256])
                    p2 = pb()
                    tr(p2[:, 0:128], T0[hh][:, 0:128])
                    ts(Nn[hh][:, 0:128], p2[:, 0:128], SMT[:, 16 + h:17 + h], ALU.mult, -1.0, ALU.mult)
                    p3 = pb()
                    tr(p3[:, 0:128], Nn[hh][:, 0:128])
                    cp(NT[hh][:, 0:128], p3[:, 0:128], "act")
                    g.i -= 1
                    yield
                yield from tri_solve(bk, [NT[0], NT[1]], [Nn[0], Nn[1]], X, g)
                cp(Wp[:, 0, 0:64], X[:, 0:64])
                cp(Wp[:, 1, 64:128], X[:, 64:128])
                p = pb()
                inter(bk, l, "dn", pr, sst, p, QE[:, 0:128], False)
                for hh in range(2):
                    mm(p[:, 0:128], Wp[:, hh, :], A[hh][:, 0:128], False, hh == 1)
                cp(YS[:, 0:128], p[:, 0:128])
                yield
                post_rms(YS[:, 0:128], BLK, 64.0, pv(l, "dn_nw", 1, pr), ZT[:, pr * 128:(pr + 1) * 128], YT[:, 4 + pr, :], g)
                yield
                state_update(bk, l, "dn", pr, [K2[:, 0:128]], [X[:, 0:128]], ELb[:, 0:128], sst,
                             (pod if nseg == 1 else sod)["dn"], last, g)
                yield

        def ssd(bk, l, last, wget, sst, pool=None):
            g = GP(pool)
            nseg, L = bk.nseg, bk.L
            W = L + 3
            XBv = conv_in(bk, l, "ss", "ssc")
            w7 = wget(7)
            for c in range(4):
                p = pb()
                proj_fm(p[:, 0:128], w7, c * 128, 128)
                cp(XBv[:, c, :, 3:W], sv(p[:, 0:128], bk), "act" if c % 2 else "dve")
            w8 = wget(8)
            for c in range(2):
                p = pb()
                proj_fm(p[:, 0:128], w8, c * 128, 128)
                cp(XBv[:, 4 + c, :, 3:W], sv(p[:, 0:128], bk), "act" if c % 2 else "dve")
            conv_run(bk, l, "ss", "ssc", "ss_cw", XBv, last)
            for c in range(6):
                act(CV[:, c, :], CV[:, c, :], AF.Silu, bias=pv(l, "ss_cb", 1, c))
            softplus_cols(SMT[:, 40:44], SMT[:, 8:12], row(l, "ssdt", 4))
            act(SMT[:, 44:48], row(l, "ssA", 4), AF.Exp)
            stt(SMT[:, 48:52], SMT[:, 40:44], -1.0, SMT[:, 44:48], ALU.mult, ALU.mult)
            p = pb()
            mm(p[:, 0:4], bk.SU, SMT[:, 48:52])
            act(SMT[:, 52:56], p[:, 0:4], AF.Exp)
            yield
            BTM, X2, YS, LB = [g.t() for _ in range(4)]
            ET = [g.t(), g.t()]
            A = [g.t(), g.t()]
            EL = [g.t(), g.t()]
            CH = [g.t(), g.t()]
            gsave = g.i
            Xp = PADS[1]
            outd = (pod if nseg == 1 else sod)["ssm"]
            for pr in range(2):
                g.i = gsave
                p = pb()
                tr(p[:, 0:128], CV[:, 2 + pr, :])
                tr(p[:, 128:256], CV[:, pr, :])
                cp(BTM[:, 0:128], p[:, 0:128])
                for hh in range(2):
                    h = 2 * pr + hh
                    ts(Xp[:, hh, hh * 64:hh * 64 + 64], p[:, 128 + hh * 64:128 + hh * 64 + 64], SMT[:, 40 + h:41 + h], ALU.mult)
                    ts(X2[:, hh * 64:hh * 64 + 64], Xp[:, hh, hh * 64:hh * 64 + 64], SMT[:, 52 + h:53 + h], ALU.mult)
                pG = pb()
                mm(pG[:, 0:128], CV[:, 2 + pr, :], CV[:, 4 + pr, :])
                for hh in range(2):
                    h = 2 * pr + hh
                    decay_mask(bk, SMT[:, 48 + h:49 + h], ET[hh][:, 0:128], g)
                    g.i -= 1
                    tt(A[hh][:, 0:128], pG[:, 0:128], ET[hh][:, 0:128])
                    ts(LB[:, 0:128], ONES, SMT[:, 48 + h:49 + h], ALU.mult)
                    p = pb()
                    mm(p[:, 0:128], LB[:, 0:128], bk.TRI)
                    act(EL[hh][:, 0:128], p[:, 0:128], AF.Exp)
                    tt(CH[hh][:, 0:128], CV[:, 4 + pr, :], EL[hh][:, 0:128])
                yield
                pY = pb()
                for hh in range(2):
                    reg = pY[:, hh * 128:(hh + 1) * 128]
                    for s in range(nseg):
                        hg = PST[(l, "ssm")][:, pr, :] if nseg == 1 else sst[:, s, pr, :]
                        mm(reg[:, s * L:(s + 1) * L], hg, CH[hh][:, s * L:(s + 1) * L], s == 0, False)
                    mm(reg, Xp[:, hh, :], A[hh][:, 0:128], False, True)
                cp(YS[0:64, 0:128], pY[0:64, 0:128])
                cp(YS[64:128, 0:128], pY[64:128, 128:256])
                yield
                stt(YS[:, 0:128], CV[:, pr, :], pv(l, "ss_D", 1, pr), YS[:, 0:128], ALU.mult, ALU.add)
                tt(YS[:, 0:128], YS[:, 0:128], ZT[:, 256 + pr * 128:256 + (pr + 1) * 128])
                post_rms(YS[:, 0:128], ONES, 128.0, pv(l, "ss_nw", 1, pr), None, YT[:, 6 + pr, :], g)
                yield
                for s in range(nseg):
                    lv = BTM[:, 0:128]
                    if nseg > 1:
                        m = g_rot()
                        I("act", "mul", out=m[:, 0:128], in_=lv, mul=CONST[:, C_SEG + s:C_SEG + s + 1])
                        lv = m[:, 0:128]
                    p = pb()
                    mm(p[:, 0:128], lv, X2[:, 0:128])
                    if nseg == 1:
                        hg = PST[(l, "ssm")][:, pr, :]
                        dest = hg
                    else:
                        hg = sst[:, s, pr, :]
                        dest = ost()[:]
                    for hh in range(2):
                        stt(dest[:, hh * 64:hh * 64 + 64], hg[:, hh * 64:hh * 64 + 64], lastcol(EL[hh][:, 0:128], bk, s),
                            p[:, hh * 64:hh * 64 + 64], ALU.mult, ALU.add)
                    if nseg > 1 or last:
                        for hh in range(2):
                            p2 = pb()
                            tr(p2[0:64, 0:128], dest[:, hh * 64:hh * 64 + 64])
                            o = ost()
                            cp(o[0:64, :], p2[0:64, 0:128], "act")
                            dd = outd[l][2 * pr + hh] if nseg == 1 else outd[l, s, 2 * pr + hh]
                            S.dma("pool", dd, o[0:64, :], is_out=True)

        import os
        class _Stop(Exception):
            pass
        kstop = os.environ.get('KSTOP', '')
        def ck(name):
            if name == kstop:
                raise _Stop()
        blks = [int(x) for x in os.environ.get('KBLKS', ','.join(str(i) for i in range(NBLK))).split(',')]
        try:
          ck('setup')
          def with_cx(cx, gen):
              while True:
                  CUR[0] = cx
                  try:
                      next(gen)
                  except StopIteration:
                      return
                  yield

          def rr_gen(gens):
              gens = list(gens)
              while gens:
                  for g_ in list(gens):
                      try:
                          next(g_)
                      except StopIteration:
                          gens.remove(g_)
                      yield

          def run_seq(gen):
              for _ in gen:
                  pass

          def mix_gen(blk, bk, l, last, samp):
              def wget(idx, l=l):
                  return wload16(l, idx)
              if not samp:
                  yield from rr_gen([dnet(bk, l, last, wget, None, G2), rwkv(bk, l, last, wget, None, G)])
                  yield from rr_gen([ssd(bk, l, last, wget, None, G2), gla(bk, l, last, wget, None, G)])
              else:
                  sst = SST[0]
                  I("dve", "memset", sst[:], 0.0)
                  load_bd(sst, std["wkv"][l])
                  for j in range(8):
                      p = pb()
                      for q in range(4):
                          tr(p[:, q * 128:(q + 1) * 128], sst[:, 2 * j + q // 2, q % 2, :])
                      cp(sst[:, 2 * j:2 * j + 2, :, :].f(lambda a: a.rearrange("p s r v -> p (s r v)")), p[:, 0:512])
                  yield
                  yield from rwkv(bk, l, last, wget, sst)
                  sst = SST[1]
                  I("dve", "memset", sst[:], 0.0)
                  load_bd(sst, std["gla"][l])
                  yield from gla(bk, l, last, wget, sst)
                  sst = SST[0]
                  I("dve", "memset", sst[:], 0.0)
                  load_bd(sst, std["dn"][l])
                  yield from dnet(bk, l, last, wget, sst)
                  sst = SST[1]
                  for s in range(16):
                      sl = SSL[0]
                      S.dma("pool", sl[:], std["ssm"][l, s].rearrange("h p n -> p h n"))
                      p = pb()
                      for h in range(4):
                          tr(p[:, h * 64:(h + 1) * 64], sl[0:64, h, :], 64)
                      cp(sst[:, s, :, :].f(lambda a: a.rearrange("p r v -> p (r v)")), p[:, 0:256])
                  yield
                  yield from ssd(bk, l, last, wget, sst)

          def dense_gen(blk, bk, l, samp):
              def wget(idx, l=l):
                  return wload16(l, idx)
              for half in range(2):
                  w = wget(9 + half)
                  p = pb()
                  for j in range(4):
                      for c8 in range(8):
                          mm(p[:, j * 128:(j + 1) * 128], w[:, c8, j * 128:(j + 1) * 128], YT[:, c8, :], c8 == 0, c8 == 7)
                  pv4 = p[:, 0:512].f(lambda a: a.rearrange("p (c s t) -> p c s t", c=4, t=bk.L))
                  tt(dv(TMP4[:], bk), pv4, modv(MODT[:, l, 16 + 4 * half:20 + 4 * half, :], bk, 4))
                  tt(XT[:, 4 * half:4 * half + 4, :], XT[:, 4 * half:4 * half + 4, :], TMP4[:], ALU.add)
                  yield
              if samp:
                  dump(6 * l + 2, XT)
              norm_mod(bk, modv(SC[:, l, 1], bk), modv(MODT[:, l, 24:32, :], bk))
              yield
              for s8 in range(8):
                  w = wget(11 + s8)
                  p = pb()
                  for j in range(4):
                      for kc in range(8):
                          mm(p[:, j * 128:(j + 1) * 128], w[:, kc, j * 128:(j + 1) * 128], HT[:, kc, :], kc == 0, kc == 7)
                  hv = HID[:, 4 * s8:4 * s8 + 4, :]
                  tmpr = (TMP4 if s8 % 2 else TMP5)
                  act(tmpr[:], p[:, 0:512].f(lambda a: a.rearrange("p (c t) -> p c t", t=128)), AF.Relu)
                  tt(hv, tmpr[:], tmpr[:], ALU.mult)
                  yield
              for half in range(2):
                  p = pb()
                  for fcg in range(4):
                      w = wget(19 + half * 4 + fcg)
                      for j in range(4):
                          for f8 in range(8):
                              mm(p[:, j * 128:(j + 1) * 128], w[:, f8, j * 128:(j + 1) * 128], HID[:, fcg * 8 + f8, :],
                                 fcg == 0 and f8 == 0 and j == 0, fcg == 3 and f8 == 7)
                  pv4 = p[:, 0:512].f(lambda a: a.rearrange("p (c s t) -> p c s t", c=4, t=bk.L))
                  tt(dv(TMP4[:], bk), pv4, modv(MODT[:, l, 40 + 4 * half:44 + 4 * half, :], bk, 4))
                  tt(XT[:, 4 * half:4 * half + 4, :], XT[:, 4 * half:4 * half + 4, :], TMP4[:], ALU.add)
                  yield
              if samp:
                  dump(6 * l + 4, XT)
              if l == NL - 1:
                  norm_mod(bk, bc2(pv(0, "fnw", 8), [128, 8, bk.nseg, bk.L]), None, HTF)
                  for b2 in range(2):
                      p = pb()
                      for j in range(4):
                          tr(p[:, j * 128:(j + 1) * 128], HTF[:, 4 * b2 + j, :])
                      cp(YTM[:, b2 * 512:(b2 + 1) * 512], p[:, 0:512])
                  S.dma("pool", yout[blk * 128:(blk + 1) * 128, :], YTM[:], is_out=True)

          pending = None
          for bi, blk in enumerate(blks):
              cx = CXS[bi % 2]
              CUR[0] = cx
              bk = Pk if blk < 16 else Sk
              last = blk == max(b for b in blks if b < 16) if blk < 16 else False
              samp = blk == 16
              S.dma("pool", XTM[:], xin[blk * 128:(blk + 1) * 128, :])
              for b2 in range(2):
                  p = pb()
                  for j in range(4):
                      tr(p[:, j * 128:(j + 1) * 128], XTM[:, (4 * b2 + j) * 128:(4 * b2 + j + 1) * 128])
                  cp(XT[:, 4 * b2:4 * b2 + 4, :], p[:, 0:512].f(lambda a: a.rearrange("p (c t) -> p c t", t=128)))
              for l in range(NL):
                  CUR[0] = cx
                  norm_mod(bk, modv(SC[:, l, 0], bk), modv(MODT[:, l, 0:8, :], bk))
                  M = with_cx(cx, mix_gen(blk, bk, l, last, samp))
                  if l == 0 and pending is not None:
                      run_seq(rr_gen([M, pending]))
                      pending = None
                  else:
                      run_seq(M)
                  D = with_cx(cx, dense_gen(blk, bk, l, samp))
                  if l == NL - 1 and not KDBG:
                      pending = D
                  else:
                      run_seq(D)
          if pending is not None:
              run_seq(pending)
        except _Stop:
            pass
        print('opcounts', {e: len(v) for e, v in S.ops.items()}, flush=True)
        S.emit()
    return nc


W_IN_COLS = None


def _win_colmap():
    def rng(a, b):
        return list(range(a, b))
    slots = []
    slots.append(rng(0, 512))
    slots.append(rng(512, 896) + rng(1920, 1936))
    slots.append(rng(896, 1408))
    slots.append(rng(1408, 1920))
    slots.append(rng(1936, 2448))
    slots.append(rng(2448, 2960))
    slots.append(rng(2960, 2968) + rng(3992, 3996) + rng(2968, 3224))
    slots.append(rng(3224, 3736))
    slots.append(rng(3736, 3992))
    return slots


def _tile_rows(w, cols):
    out = np.zeros((128, 8, 512), np.float32)
    sub = w[:, cols]
    out[:, :, :len(cols)] = sub.reshape(8, 128, len(cols)).transpose(1, 0, 2)
    return out


_NC_CACHE = {}


def kernel(**inp):
    f = lambda k: np.ascontiguousarray(np.asarray(inp[k], dtype=np.float32))
    wall = np.zeros((2, 27, 128, 8, 512), np.float32)
    adaw = np.zeros((2, 12, 128, 8, 512), np.float32)
    cm = _win_colmap()
    w_in, w_out, w_up, w_down, ada_w = f("w_in"), f("w_out"), f("w_up"), f("w_down"), f("ada_w")
    for l in range(2):
        for s in range(9):
            wall[l, s] = _tile_rows(w_in[l], cm[s])
        for s in range(2):
            wall[l, 9 + s] = _tile_rows(w_out[l], list(range(s * 512, (s + 1) * 512)))
        for s in range(8):
            wall[l, 11 + s] = _tile_rows(w_up[l], list(range(s * 512, (s + 1) * 512)))
        for half in range(2):
            for fcg in range(4):
                blk = w_down[l][fcg * 1024:(fcg + 1) * 1024, half * 512:(half + 1) * 512]
                wall[l, 19 + half * 4 + fcg] = blk.reshape(8, 128, 512).transpose(1, 0, 2)
        for s in range(12):
            adaw[l, s] = _tile_rows(ada_w[l], list(range(s * 512, (s + 1) * 512)))
    pvv = np.zeros((128, 2 * NPV), np.float32)
    rows = np.zeros((128, 2 * NR), np.float32)
    sm = np.zeros((128, 2 * 1024), np.float32)

    def putv(l, name, vec, off=0):
        n = vec.shape[0] // 128
        pvv[:, l * NPV + PVO[name] + off:l * NPV + PVO[name] + off + n] = vec.reshape(n, 128).T
    for l in range(2):
        putv(l, "ada_b", f("ada_b")[l])
        putv(l, "n1w", f("norm1_w")[l])
        putv(l, "n2w", f("norm2_w")[l])
        putv(l, "mu", f("rwkv_mu")[l])
        putv(l, "a0", f("rwkv_a0")[l])
        putv(l, "k_k", f("rwkv_k_k")[l])
        putv(l, "k_a", f("rwkv_k_a")[l])
        putv(l, "r_k", f("rwkv_r_k")[l])
        putv(l, "ln_w", f("rwkv_ln_w")[l])
        putv(l, "ln_b", f("rwkv_ln_b")[l])
        putv(l, "gla_nw", f("gla_norm_w")[l])
        for i in range(4):
            putv(l, "dn_cw", f("dn_conv_w")[l, i], i * 6)
            putv(l, "ss_cw", f("ssm_conv_w")[l, i], i * 6)
        putv(l, "dn_nw", f("dn_norm_w")[l])
        putv(l, "ss_cb", f("ssm_conv_b")[l])
        putv(l, "ss_nw", f("ssm_norm_w")[l])
        putv(l, "ss_D", np.repeat(f("ssm_D")[l], 64))
        putv(l, "fnw", f("final_norm_w"))
        o = l * NR
        rows[:, o + RO["w0"]:o + RO["w0"] + 256] = f("rwkv_w0")[l][None, :]
        rows[:, o + RO["gkb"]:o + RO["gkb"] + 256] = f("gla_gk_b")[l][None, :]
        rows[:, o + RO["dnA"]:o + RO["dnA"] + 4] = f("dn_A_log")[l][None, :]
        rows[:, o + RO["dndt"]:o + RO["dndt"] + 4] = f("dn_dt_bias")[l][None, :]
        rows[:, o + RO["ssdt"]:o + RO["ssdt"] + 4] = f("ssm_dt_bias")[l][None, :]
        rows[:, o + RO["ssA"]:o + RO["ssA"] + 4] = f("ssm_A_log")[l][None, :]
        o = l * 1024
        sm[0:32, o:o + 256] = f("rwkv_w2")[l]
        sm[32:64, o + 256:o + 512] = f("rwkv_a2")[l]
        sm[64:128, o + 512:o + 768] = f("rwkv_g2")[l]
        sm[0:16, o + 768:o + 1024] = f("gla_gk_w2")[l]
    consts = make_consts()
    xp, xs = f("x_prompt"), f("x_sample")
    cpr, csm = f("c_prompt"), f("c_sample")
    stn = dict(shift="state_rwkv_shift", wkv="state_rwkv_wkv", gla="state_gla", dnc="state_dn_conv", dn="state_dn",
               ssc="state_ssm_conv", ssm="state_ssm")
    stf = {k: f(v) for k, v in stn.items()}
    in_maps = []
    for c in range(8):
        m = dict(wall=wall, ada=adaw, pv=pvv, rows=rows, sm=sm, consts=consts)
        m["xin"] = np.ascontiguousarray(np.concatenate([xp[c], xs[16 * c:16 * c + 16].reshape(128, 1024)], 0))
        m["cc"] = np.ascontiguousarray(np.concatenate([cpr[c:c + 1], csm[16 * c:16 * c + 16]], 0))
        for k in ST_SHAPES:
            m["st_" + k] = np.ascontiguousarray(stf[k][:, 16 * c:16 * c + 16])
        in_maps.append(m)
    if "nc" not in _NC_CACHE:
        _NC_CACHE["nc"] = build()
    res = run_bass_kernel_spmd(_NC_CACHE["nc"], in_maps, core_ids=list(range(8)))
    R = res.results
    global _LAST_R
    _LAST_R = R
    y_prompt = np.stack([R[c]["y"][:2048] for c in range(8)], 0)
    y_sample = np.concatenate([R[c]["y"][2048:].reshape(16, 8, 1024) for c in range(8)], 0)
    outs = [y_prompt, y_sample]
    for k in ("shift", "wkv", "gla", "dnc", "dn", "ssc", "ssm"):
        outs.append(np.stack([R[c]["p_" + k] for c in range(8)], 1))
    for k in ("shift", "wkv", "gla", "dnc", "dn", "ssc", "ssm"):
        outs.append(np.concatenate([R[c]["s_" + k] for c in range(8)], 1))
    return tuple(np.ascontiguousarray(o.astype(np.float32)) for o in outs)
```

```python
import numpy as np
import concourse.bass as bass
import concourse.mybir as mybir
from concourse.bass_utils import run_bass_kernel_spmd

F32 = mybir.dt.float32
BF16 = mybir.dt.bfloat16
AF = mybir.ActivationFunctionType
ALU = mybir.AluOpType
AX = mybir.AxisListType


class V:
    __slots__ = ("tile", "ap", "sub")

    def __init__(self, tile, ap, sub):
        self.tile, self.ap, self.sub = tile, ap, sub

    def __getitem__(self, idx):
        return V(self.tile, self.ap[idx], self.sub)

    def f(self, fn):
        return V(self.tile, fn(self.ap), self.sub)


class Tile:
    def __init__(self, h, name):
        self.h, self.name = h, name
        self.ww = {}
        self.wr = {}
        self.subs = {}

    def __getitem__(self, idx):
        return V(self, self.h[idx], None)

    def s(self, k, idx=None):
        ap = self.h[idx] if idx is not None else self.h[:]
        return V(self, ap, k)


def _mx(d, s, v):
    if d.get(s, 0) < v:
        d[s] = v


class Sched:
    ENG = ("pe", "act", "dve", "pool", "sp")
    NDMA = 12

    def __init__(self, nc, stack):
        self.nc = nc
        self.ops = {e: [] for e in self.ENG}
        self.cnt = {e: 0 for e in self.ENG}
        self.waited = {e: {} for e in self.ENG}
        self.sem = {}
        for e in ("pe", "act", "dve", "pool"):
            self.sem[e] = stack.enter_context(nc.semaphore("s_" + e))
        self.dq = {}
        for q in ("sp", "pool", "act"):
            n = self.NDMA if q == "sp" else 6
            sems = [stack.enter_context(nc.semaphore("d_%s%d" % (q, j))) for j in range(n)]
            for j, s_ in enumerate(sems):
                self.sem[("d", q, j)] = s_
            self.dq[q] = [n, 0, [0] * n]
        self.out_tokens = {}
        self.stack = stack
        self.ntile = 0

    def sb(self, shape, name=None, dt=F32):
        self.ntile += 1
        name = name or ("t%d" % self.ntile)
        h = self.stack.enter_context(self.nc.sbuf_tensor(name, list(shape), dt))
        return Tile(h, name)

    def ps(self, shape, name=None, dt=F32):
        self.ntile += 1
        name = name or ("p%d" % self.ntile)
        h = self.stack.enter_context(self.nc.psum_tensor(name, list(shape), dt))
        return Tile(h, name)

    def _deps(self, reads, writes):
        tok = {}
        for v in reads:
            t = v.tile
            for s, x in t.ww.items():
                _mx(tok, s, x)
            if v.sub is None:
                for k, (w, r) in t.subs.items():
                    for s, x in w.items():
                        _mx(tok, s, x)
            elif v.sub in t.subs:
                for s, x in t.subs[v.sub][0].items():
                    _mx(tok, s, x)
        for v in writes:
            t = v.tile
            for d in (t.ww, t.wr):
                for s, x in d.items():
                    _mx(tok, s, x)
            if v.sub is None:
                for k, (w, r) in t.subs.items():
                    for d in (w, r):
                        for s, x in d.items():
                            _mx(tok, s, x)
            elif v.sub in t.subs:
                for d in t.subs[v.sub]:
                    for s, x in d.items():
                        _mx(tok, s, x)
        return tok

    def _commit(self, reads, writes, s, x):
        for v in reads:
            t = v.tile
            if v.sub is None:
                _mx(t.wr, s, x)
            else:
                if v.sub not in t.subs:
                    t.subs[v.sub] = [{}, {}]
                _mx(t.subs[v.sub][1], s, x)
        for v in writes:
            t = v.tile
            if v.sub is None:
                t.ww = {s: x}
                t.wr = {}
                t.subs = {}
            else:
                t.subs[v.sub] = [{s: x}, {}]

    def _add(self, eng, fn, reads, writes, tokens_extra=None, dma_q=None, is_out=False):
        tok = self._deps(reads, writes)
        if tokens_extra:
            for s, x in tokens_extra.items():
                _mx(tok, s, x)
        if dma_q is not None:
            n, nxt, cnts = self.dq[dma_q]
            j = nxt
            self.dq[dma_q][1] = (nxt + 1) % n
            if cnts[j] > 0:
                _mx(tok, ("d", dma_q, j), 16 * cnts[j])
            cnts[j] += 1
            mysem, myval, inc = ("d", dma_q, j), 16 * cnts[j], 16
        else:
            self.cnt[eng] += 1
            mysem, myval, inc = eng, self.cnt[eng], 1
        waits = []
        wd = self.waited[eng]
        for s, x in tok.items():
            if eng == "pe" and s == "pe":
                continue
            if wd.get(s, 0) >= x:
                continue
            wd[s] = x
            waits.append((s, x))
        self.ops[eng].append((fn, waits, mysem, inc))
        self._commit(reads, writes, mysem, myval)
        if is_out:
            _mx(self.out_tokens, mysem, myval)

    def I(self, eng, meth, *args, **kw):
        reads, writes = [], []
        a2 = []
        for a in args:
            if isinstance(a, V):
                reads.append(a)
                a2.append(a.ap)
            else:
                a2.append(a)
        k2 = {}
        rw = kw.pop("_rw", False)
        for k, a in kw.items():
            if isinstance(a, V):
                if k in ("out", "accum_out"):
                    writes.append(a)
                    if rw:
                        reads.append(a)
                else:
                    reads.append(a)
                k2[k] = a.ap
            else:
                k2[k] = a
        fn = lambda e: getattr(e, meth)(*a2, **k2)
        self._add(eng, fn, reads, writes)

    def dma(self, q, out, in_, is_out=False, xr=(), xw=(), **kw):
        reads, writes = list(xr), list(xw)
        o = out.ap if isinstance(out, V) else out
        i = in_.ap if isinstance(in_, V) else in_
        if isinstance(out, V):
            writes.append(out)
        if isinstance(in_, V):
            reads.append(in_)
        fn = lambda e: e.dma_start(out=o, in_=i, **kw)
        self._add(q, fn, reads, writes, dma_q=q, is_out=is_out)

    def emit(self):
        nc = self.nc
        fin = [(s, x) for s, x in self.out_tokens.items()]
        eobj = {"pe": "tensor", "act": "scalar", "dve": "vector", "pool": "gpsimd", "sp": "sync"}
        with nc.Block() as block:
            for ename in self.ENG:
                ops = self.ops[ename]
                extra = fin if ename == "sp" else []

                def body(e, ops=ops, extra=extra):
                    for fn, waits, mysem, inc in ops:
                        for s, x in waits:
                            e.wait_ge(self.sem[s], x)
                        fn(e).then_inc(self.sem[mysem], inc)
                    for s, x in extra:
                        e.wait_ge(self.sem[s], x)
                getattr(block, eobj[ename])(body)

NL = 2
NTOK = 2176
NBLK = 17
C_DEC = 0.6065306597126334
CNAMES = ["IDENT", "ONES", "BLK", "TRS_P", "TRI_P", "SU_P", "NEGM_P", "TRS_S", "TRI_S", "SU_S", "NEGM_S"]
CI = {n: i * 128 for i, n in enumerate(CNAMES)}
C_SEG = 11 * 128
C_EPS = C_SEG + 16
C_GNEPS = C_EPS + 1
C_ONE = C_EPS + 2
NCONST = C_EPS + 4
NPV = 160
PVO = dict(ada_b=0, n1w=48, n2w=56, mu=64, a0=71, k_k=73, k_a=75, r_k=77, ln_w=79, ln_b=81, gla_nw=83,
           dn_cw=85, dn_nw=109, ss_cw=111, ss_cb=135, ss_nw=141, ss_D=143, fnw=145)
NR = 528
RO = dict(w0=0, gkb=256, dnA=512, dndt=516, ssdt=520, ssA=524)
ST_SHAPES = dict(shift=[2, 16, 896], wkv=[2, 16, 4, 64, 64], gla=[2, 16, 4, 64, 64], dnc=[2, 16, 3, 768],
                 dn=[2, 16, 4, 64, 64], ssc=[2, 16, 3, 768], ssm=[2, 16, 4, 64, 128])


def make_consts():
    c = np.zeros((128, NCONST), np.float32)
    s = np.arange(128)[:, None]
    i = np.arange(128)[None, :]
    same = (s // 8) == (i // 8)
    m = {}
    m["IDENT"] = (s == i)
    m["ONES"] = np.ones((128, 128))
    m["BLK"] = (s // 64) == (i // 64)
    m["TRI_P"] = s <= i
    m["TRS_P"] = s < i
    m["SU_P"] = s > i
    m["NEGM_P"] = (m["TRI_P"].astype(np.float32) - 1.0) * 30000.0
    m["TRI_S"] = (s <= i) & same
    m["TRS_S"] = (s < i) & same
    m["SU_S"] = (s > i) & same
    m["NEGM_S"] = (m["TRI_S"].astype(np.float32) - 1.0) * 30000.0
    for n in CNAMES:
        c[:, CI[n]:CI[n] + 128] = m[n].astype(np.float32)
    c[:, C_SEG:C_SEG + 16] = ((np.arange(128)[:, None] // 8) == np.arange(16)[None, :]).astype(np.float32)
    c[:, C_EPS] = 1e-6
    c[:, C_GNEPS] = 64e-5
    c[:, C_ONE] = 1.0
    return c


class BK:
    pass


def build():
    from contextlib import ExitStack
    nc = bass.Bass("TRN2", target_bir_lowering=False)

    def din(name, shape):
        return nc.dram_tensor(name, list(shape), F32, kind="ExternalInput").ap()

    def dout(name, shape):
        return nc.dram_tensor(name, list(shape), F32, kind="ExternalOutput").ap()

    xin = din("xin", [NTOK, 1024])
    ccd = din("cc", [17, 1024])
    std = {k: din("st_" + k, v) for k, v in ST_SHAPES.items()}
    wall = din("wall", [2, 27, 128, 8, 512])
    adad = din("ada", [2, 12, 128, 8, 512])
    pvd = din("pv", [128, 2 * NPV])
    rowsd = din("rows", [128, 2 * NR])
    smd = din("sm", [128, 2 * 4 * 256])
    constd = din("consts", [128, NCONST])
    yout = dout("y", [NTOK, 1024])
    nc.allow_low_precision("bf16 operands (fp32 PSUM accumulation) for the dense projections")
    wbf = nc.dram_tensor("wbf", [2, 27, 128, 8, 512], BF16).ap()
    pod = {k: dout("p_" + k, [v[0]] + v[2:]) for k, v in ST_SHAPES.items()}
    sod = {k: dout("s_" + k, v) for k, v in ST_SHAPES.items()}
    import os as _os
    KDBG = int(_os.environ.get("KDBG", "0"))
    dbgd = dout("dbg", [12, 128, 1024]) if KDBG else None

    with ExitStack() as es:
        S = Sched(nc, es)
        I = S.I
        CONST = S.sb([128, NCONST], "CONST")
        PV = S.sb([128, 2 * NPV], "PV")
        ROWS = S.sb([128, 2 * NR], "ROWS")
        SM = S.sb([128, 2 * 1024], "SM")
        S.dma("pool", CONST[:], constd)
        S.dma("pool", PV[:], pvd)
        S.dma("pool", ROWS[:], rowsd)
        S.dma("pool", SM[:], smd)

        def K(n):
            return CONST[:, CI[n]:CI[n] + 128]
        IDENT, ONES, BLK = K("IDENT"), K("ONES"), K("BLK")
        EPSc = CONST[:, C_EPS:C_EPS + 1]
        GNEPSc = CONST[:, C_GNEPS:C_GNEPS + 1]
        ONEc = CONST[:, C_ONE:C_ONE + 1]

        def pv(l, name, n=1, off=0):
            o = l * NPV + PVO[name] + off
            return PV[:, o:o + n]

        def row(l, name, n):
            o = l * NR + RO[name]
            return ROWS[:, o:o + n]

        def smat(l, j):
            o = l * 1024 + j * 256
            return SM[:, o:o + 256]

        PB = [S.ps([128, 512], "pb%d" % i) for i in range(8)]
        pbi = [0]

        def pb():
            t = PB[pbi[0] % 8]
            pbi[0] += 1
            return t

        WR = [S.sb([128, 4096], "wr%d" % i) for i in range(2)]
        wri = [0]
        WBF = Tile(None, "wbf_dep")

        def wr32(i):
            return V(WR[i], WR[i].h[:].rearrange("p (k c) -> p k c", c=512), None)

        def wr16(i, hf):
            return V(WR[i], WR[i].h[:].bitcast(BF16)[:, hf * 4096:(hf + 1) * 4096].rearrange("p (k c) -> p k c", c=512), hf)

        def wload(ap):
            i = wri[0] % 2
            wri[0] += 1
            t = wr32(i)
            S.dma("sp", t, ap)
            return t

        def wload16(l, sl):
            r = wri[0] % 4
            wri[0] += 1
            t = wr16(r // 2, r % 2)
            S.dma("sp", t, wbf[l, sl], xr=[V(WBF, None, (l, sl))])
            return t

        def mm(out, lhsT, rhs, start=True, stop=True):
            I("pe", "matmul", out=out, lhsT=lhsT, rhs=rhs, start=start, stop=stop, skip_group_check=True)

        def tr(out, in_, npart=128):
            I("pe", "transpose", out=out, in_=in_, identity=CONST[0:npart, 0:npart])

        def act(out, in_, func, bias=None, scale=1.0):
            if bias is None:
                I("act", "activation", out=out, in_=in_, func=func, scale=scale)
            else:
                I("act", "activation", out=out, in_=in_, func=func, bias=bias, scale=scale)

        def tt(out, a, b, op=ALU.mult, eng="dve"):
            I(eng, "tensor_tensor", out=out, in0=a, in1=b, op=op)

        def ts(out, a, s1, op0, s2=None, op1=None, eng="dve"):
            if op1 is None:
                I(eng, "tensor_scalar", out=out, in0=a, scalar1=s1, scalar2=None, op0=op0)
            else:
                I(eng, "tensor_scalar", out=out, in0=a, scalar1=s1, scalar2=s2, op0=op0, op1=op1)

        def stt(out, a, sc, b, op0, op1, eng="dve"):
            I(eng, "scalar_tensor_tensor", out=out, in0=a, scalar=sc, in1=b, op0=op0, op1=op1)

        def cp(out, in_, eng="dve"):
            if eng == "act":
                I("act", "activation", out=out, in_=in_, func=AF.Identity, scale=1.0)
            else:
                I("dve", "tensor_copy", out=out, in_=in_)

        G = [S.sb([128, 256], "g%d" % i) for i in range(28)]

        class GP:
            def __init__(self, pool=None):
                self.i = 0
                self.pool = G if pool is None else pool

            def t(self):
                t = self.pool[self.i]
                self.i += 1
                return t

        class Alias:
            def __init__(self, tile, base, sub):
                self.tile, self.base, self.sub = tile, base, sub

            def __getitem__(self, idx):
                return V(self.tile, self.base[idx], self.sub)

        MODT = S.sb([128, 2, 48, 17], "MODT")
        SC = S.sb([128, 2, 2, 8, 17], "SC")
        CT = S.sb([128, 8, 17], "CT")
        class Cx:
            pass
        CXS = []
        for i_ in range(2):
            c_ = Cx()
            c_.XTM = S.sb([128, 1024], "XTM%d" % i_)
            c_.XT = S.sb([128, 8, 128], "XT%d" % i_)
            c_.HT = S.sb([128, 8, 128], "HT%d" % i_, BF16)
            c_.YT = S.sb([128, 8, 128], "YT%d" % i_, BF16)
            CXS.append(c_)
        CUR = [CXS[0]]

        class Proxy:
            def __init__(self, name):
                self.name = name

            def __getitem__(self, idx):
                return getattr(CUR[0], self.name)[idx]

            def s(self, k, idx):
                return getattr(CUR[0], self.name).s(k, idx)
        XTM, XT, HT, YT = Proxy("XTM"), Proxy("XT"), Proxy("HT"), Proxy("YT")
        HTF = S.sb([128, 8, 128], "HTF")
        HID = S.sb([128, 32, 128], "HID", BF16)
        TMP5 = S.sb([128, 4, 128], "TMP5")
        UA = S.sb([128, 7, 144], "UA")
        XS = S.sb([128, 7, 128], "XS")
        XB = S.sb([128, 6, 176], "XB")
        CV = S.sb([128, 6, 128], "CV")
        PST = {}
        for l in range(NL):
            for mname in ("wkv", "gla", "dn", "ssm"):
                PST[(l, mname)] = S.sb([128, 2, 128], "pst_%s%d" % (mname, l))
                I("dve", "memset", PST[(l, mname)][:], 0.0)
        CARRY = {}
        for l in range(NL):
            CARRY[(l, "rw")] = S.sb([128, 7], "c_rw%d" % l)
            CARRY[(l, "dn")] = S.sb([128, 6, 3], "c_dn%d" % l)
            CARRY[(l, "ss")] = S.sb([128, 6, 3], "c_ss%d" % l)
            for k in ("rw", "dn", "ss"):
                I("dve", "memset", CARRY[(l, k)][:], 0.0)
        SST = [S.sb([128, 16, 2, 128], "sst%d" % i) for i in range(2)]
        for t in SST:
            I("dve", "memset", t[:], 0.0)
        G2 = [Alias(SST[j], SST[j].h[:, s_].rearrange("p r v -> p (r v)"), ("a", s_)) for j in range(2) for s_ in range(16)]
        PADS = [S.sb([128, 2, 128], "pad%d" % i) for i in range(3)]
        for t in PADS:
            I("dve", "memset", t[:], 0.0)

        Pk, Sk = BK(), BK()
        Pk.nseg, Pk.L, Pk.sfx, Pk.nsolve = 1, 128, "_P", 7
        Sk.nseg, Sk.L, Sk.sfx, Sk.nsolve = 16, 8, "_S", 3
        for bk in (Pk, Sk):
            bk.TRI, bk.TRS, bk.SU, bk.NEGM = K("TRI" + bk.sfx), K("TRS" + bk.sfx), K("SU" + bk.sfx), K("NEGM" + bk.sfx)

        def bc(v, shape, axis):
            return v.f(lambda a: a.unsqueeze(axis).to_broadcast(shape))

        def modv(v3, bk, nch=8):
            if bk.nseg == 1:
                return bc(v3[:, :, 0:1], [128, nch, 1, 128], 3)
            return bc(v3[:, :, 1:17], [128, nch, 16, 8], 3)

        def dv(v, bk):
            return v.f(lambda a: a.rearrange("p c (s t) -> p c s t", t=bk.L))

        ctm = XTM
        S.dma("pool", ctm[0:17, :], ccd)
        act(ctm[0:17, :], ctm[0:17, :], AF.Silu)
        for kc in range(8):
            p = pb()
            tr(p[:, 0:17], ctm[0:17, kc * 128:(kc + 1) * 128], 17)
            cp(CT[:, kc, :], p[:, 0:17])
        for l in range(NL):
            for s in range(12):
                w = wload(adad[l, s])
                p = pb()
                for j in range(4):
                    for kc in range(8):
                        mm(p[:, j * 17:(j + 1) * 17], w[:, kc, j * 128:(j + 1) * 128], CT[:, kc, :], kc == 0, kc == 7)
                tt(MODT[:, l, 4 * s:4 * s + 4, :], p[:, 0:68].f(lambda a: a.rearrange("p (j n) -> p j n", n=17)),
                   bc(pv(l, "ada_b", 4, 4 * s), [128, 4, 17], 2), ALU.add)
            stt(SC[:, l, 0], MODT[:, l, 8:16, :], 1.0, bc(pv(l, "n1w", 8), [128, 8, 17], 2), ALU.add, ALU.mult)
            stt(SC[:, l, 1], MODT[:, l, 32:40, :], 1.0, bc(pv(l, "n2w", 8), [128, 8, 17], 2), ALU.add, ALU.mult)

        STG = [wr32(0)] + [V(SST[j], SST[j].h[:].rearrange("p s r v -> p (s r v)").rearrange("p (k c) -> p k c", c=512), None)
                           for j in range(2)]
        for l in range(NL):
            for sl in range(27):
                i = (l * 27 + sl)
                src = STG[i % 3]
                S.dma("sp", src, wall[l, sl])
                dst = wr16(1, i % 2)
                I("dve", "tensor_copy", out=dst[:, 0:4, :], in_=src[:, 0:4, :])
                I("act", "activation", out=dst[:, 4:8, :], in_=src[:, 4:8, :], func=AF.Identity, scale=1.0)
                S.dma("sp", wbf[l, sl], dst, xw=[V(WBF, None, (l, sl))])

        RS = S.sb([128, 128], "RS")

        def norm_mod(bk, scale_bv, shift_bv, outT=None, cols=None):
            outT = HT if outT is None else outT
            HIDF = V(HID, HID.h[:].rearrange("p c t -> p (c t)").bitcast(F32), None)
            p = pb()
            for kc in range(8):
                sq = HIDF[:, kc * 128:(kc + 1) * 128]
                act(sq, XT[:, kc, :], AF.Square)
                mm(p[:, 0:128], ONES, sq, kc == 0, kc == 7)
            act(RS[:], p[:, 0:128], AF.Ln, bias=EPSc, scale=1.0 / 1024.0)
            act(RS[:], RS[:], AF.Exp, scale=-0.5)
            if bk.nseg == 1 and cols is not None:
                sl3 = lambda kc: (slice(None), kc, slice(None))
                for kc in range(8):
                    if cols[1] is not None:
                        stt(HTF.s(kc, sl3(kc)), XT[:, kc, :], cols[0](kc), RS[:], ALU.mult, ALU.mult)
                        ts(outT.s(kc, sl3(kc)), HTF.s(kc, sl3(kc)), cols[1](kc), ALU.add)
                    else:
                        stt(outT.s(kc, sl3(kc)), XT[:, kc, :], cols[0](kc), RS[:], ALU.mult, ALU.mult)
                return
            tt(HTF[:], XT[:], bc(RS[:], [128, 8, 128], 1))
            if shift_bv is not None:
                tt(dv(HTF[:], bk), dv(HTF[:], bk), scale_bv)
                tt(dv(outT[:], bk), dv(HTF[:], bk), shift_bv, ALU.add)
            else:
                tt(dv(outT[:], bk), dv(HTF[:], bk), scale_bv)

        def proj_fm(out, w, c0, M):
            for kc in range(8):
                mm(out, w[:, kc, c0:c0 + M], HT.s(kc, (slice(None), kc, slice(None))), kc == 0, kc == 7)

        def proj_tm(out, w, c0, N):
            for kc in range(8):
                mm(out, HT.s(kc, (slice(None), kc, slice(None))), w[:, kc, c0:c0 + N], kc == 0, kc == 7)

        def sv(v, bk):
            return v.f(lambda a: a.rearrange("p (s t) -> p s t", t=bk.L))

        def lastcol(v, bk, s):
            c = s * bk.L + bk.L - 1
            return v[:, c:c + 1]

        def tri_solve(bk, NTs, Ns, X, g):
            tmpN = [[g.t(), g.t()] for _ in range(2)]
            for k in range(bk.nsolve):
                p = pb()
                for h in range(2):
                    mm(p[:, h * 64:h * 64 + 64], NTs[h][:, 0:128], X[:, h * 64:h * 64 + 64])
                tt(X[:, 0:128], X[:, 0:128], p[:, 0:128], ALU.add)
                yield
                if k == bk.nsolve - 1:
                    break
                for h in range(2):
                    p2 = pb()
                    mm(p2[:, 0:128], Ns[h][:, 0:128], NTs[h][:, 0:128])
                    if k < bk.nsolve - 2:
                        mm(p2[:, 128:256], NTs[h][:, 0:128], Ns[h][:, 0:128])
                    nn = tmpN[h][k % 2]
                    if k < bk.nsolve - 2:
                        cp(nn[:, 0:256], p2[:, 0:256], "act")
                    else:
                        cp(nn[:, 0:128], p2[:, 0:128], "act")
                    NTs[h] = nn
                    Ns[h] = _Shift(nn)
                    yield
            return

        class _Shift:
            def __init__(self, base):
                self.base = base

            def __getitem__(self, idx):
                assert idx == (slice(None), slice(0, 128))
                return self.base[:, 128:256]

        def post_rms(Y, ones, gsize, nw_col, gateT, out, g):
            sq = g.t()
            act(sq[:, 0:128], Y, AF.Square)
            p = pb()
            mm(p[:, 0:128], ones, sq[:, 0:128])
            r = g.t()
            act(r[:, 0:128], p[:, 0:128], AF.Ln, bias=EPSc, scale=1.0 / gsize)
            act(r[:, 0:128], r[:, 0:128], AF.Exp, scale=-0.5)
            stt(r[:, 128:256], Y, nw_col, r[:, 0:128], ALU.mult, ALU.mult)
            if gateT is not None:
                tt(out, r[:, 128:256], gateT)
            else:
                cp(out, r[:, 128:256])

        def load_bd(sst, src):
            for h in range(4):
                b = 64 * (h % 2)
                S.dma("pool", sst[b:b + 64, :, h // 2, b:b + 64], src[:, h].rearrange("s d v -> d s v"))

        def store_bd(dst, tile_bd, pr):
            for hh in range(2):
                b = 64 * hh
                S.dma("pool", dst[2 * pr + hh], tile_bd[b:b + 64, b:b + 64], is_out=True)

        OST = [S.sb([128, 128], "ost%d" % i) for i in range(4)]
        osti = [0]

        def ost():
            t = OST[osti[0] % 4]
            osti[0] += 1
            return t

        def state_update(bk, l, mname, pr, lhs, rhs, pcv, sst, outd, last, g, transpose_out=False):
            for s in range(bk.nseg):
                p = pb()
                for k in range(len(lhs)):
                    lv = lhs[k]
                    if bk.nseg > 1:
                        m = g_rot()
                        I("act", "mul", out=m[:, 0:128], in_=lv, mul=CONST[:, C_SEG + s:C_SEG + s + 1])
                        lv = m[:, 0:128]
                    mm(p[:, 0:128], lv, rhs[k], k == 0, k == len(lhs) - 1)
                tmp = g_rot()
                tt(tmp[:, 0:128], p[:, 0:128], BLK)
                if bk.nseg == 1:
                    hp = PST[(l, mname)][:, pr, :]
                    stt(hp, hp, lastcol(pcv, bk, 0), tmp[:, 0:128], ALU.mult, ALU.add)
                    if last:
                        if transpose_out:
                            p2 = pb()
                            tr(p2[:, 0:128], hp)
                            o = ost()
                            cp(o[:], p2[:, 0:128])
                            store_bd(outd[l], o, pr)
                        else:
                            store_bd(outd[l], PST[(l, mname)][:, pr, :], pr)
                else:
                    hp = sst[:, s, pr, :]
                    o = ost()
                    stt(o[:], hp, lastcol(pcv, bk, s), tmp[:, 0:128], ALU.mult, ALU.add)
                    if transpose_out:
                        p2 = pb()
                        tr(p2[:, 0:128], o[:])
                        o2 = ost()
                        cp(o2[:], p2[:, 0:128])
                        o = o2
                    store_bd(outd[l, s], o, pr)

        GR = [S.sb([128, 128], "gr%d" % i) for i in range(6)]
        gri = [0]

        def g_rot():
            t = GR[gri[0] % 6]
            gri[0] += 1
            return t

        def inter(bk, l, mname, pr, sst, out_ps, opT, stop):
            for s in range(bk.nseg):
                hp = PST[(l, mname)][:, pr, :] if bk.nseg == 1 else sst[:, s, pr, :]
                mm(out_ps[:, s * bk.L:(s + 1) * bk.L], hp, opT[:, s * bk.L:(s + 1) * bk.L], s == 0, stop)

        def decay_mask(bk, la_col, out, g):
            t = g.t()
            ts(t[:, 0:128], bk.SU, la_col, ALU.mult)
            p = pb()
            mm(p[:, 0:128], t[:, 0:128], bk.TRI, True, False)
            mm(p[:, 0:128], IDENT, bk.NEGM, False, True)
            act(out, p[:, 0:128], AF.Exp)

        def bc2(v, shape):
            return v.f(lambda a: a.unsqueeze(2).unsqueeze(3).to_broadcast(shape))

        GLT = S.sb([16, 128], "GLT")
        SHT = S.sb([48, 896], "SHT")
        CST = SHT
        CSO = S.sb([48, 768], "CSO")
        CS3 = S.sb([128, 48], "CS3")
        SMT = S.sb([128, 64], "SMT")
        ZT = S.sb([128, 512], "ZT")
        TMP4 = S.sb([128, 4, 128], "TMP4")
        YTM = XTM

        def dump(k, tile3):
            if not KDBG:
                return
            for b2 in range(2):
                p = pb()
                for j in range(4):
                    tr(p[:, j * 128:(j + 1) * 128], tile3[:, 4 * b2 + j, :])
                cp(TMP4[:].f(lambda a: a.rearrange("p c t -> p (c t)")), p[:, 0:512])
                S.dma("pool", dbgd[k, :, b2 * 512:(b2 + 1) * 512], TMP4[:].f(lambda a: a.rearrange("p c t -> p (c t)")), is_out=True)
        SSL = [S.sb([64, 4, 128], "ssl%d" % i) for i in range(1)]

        def rwkv(bk, l, last, wget, sst, pool=None):
            g = GP(pool)
            w0 = wget(0)
            w1 = wget(1)
            nseg, L = bk.nseg, bk.L
            W = L + 1
            sfx = bk.sfx
            UAv = UA[:, :, 0:nseg * W].f(lambda a: a.rearrange("p c (s t) -> p c s t", t=W))
            if nseg == 1:
                cp(UA[:, :, 0:1], CARRY[(l, "rw")][:].f(lambda a: a.unsqueeze(2)))
            else:
                S.dma("pool", SHT[0:16, :], std["shift"][l])
                for c in range(7):
                    p = pb()
                    tr(p[:, 0:16], SHT[0:16, c * 128:(c + 1) * 128], 16)
                    cp(UAv[:, c, :, 0], p[:, 0:16])
            for c in range(7):
                w = w0 if c < 4 else w1
                p = pb()
                proj_fm(p[:, 0:128], w, (c % 4) * 128, 128)
                cp(UAv[:, c, :, 1:W], sv(p[:, 0:128], bk), "act" if c % 2 else "dve")
            p = pb()
            proj_fm(p[0:16, 0:128], w1, 384, 16)
            cp(GLT[0:16, :], p[0:16, 0:128])
            if nseg == 1:
                cp(CARRY[(l, "rw")][:].f(lambda a: a.unsqueeze(2)), UA[:, :, 128:129])
                if last:
                    S.dma("pool", pod["shift"][l].rearrange("(c p) -> p c", p=128), CARRY[(l, "rw")][:], is_out=True,
                          allow_slow_non_contiguous=True)
            else:
                for c in range(7):
                    S.dma("pool", sod["shift"][l][:, c * 128:(c + 1) * 128].rearrange("s p -> p s"), UAv[:, c, :, L],
                          is_out=True, allow_slow_non_contiguous=True)
            ck('rw1')
            XSv = dv(XS[:], bk)
            tt(XSv, UAv[:, :, :, 0:L], UAv[:, :, :, 1:W], ALU.subtract)
            tt(XSv, XSv, bc2(pv(l, "mu", 7), [128, 7, nseg, L]))
            tt(XSv, XSv, UAv[:, :, :, 1:W], ALU.add)
            ck('rw2')
            X6 = XS[:, 6, :]
            T6 = g.t()
            act(T6[:, 0:128], X6, AF.Tanh)
            act(T6[:, 128:256], X6, AF.Sigmoid)
            p = pb()
            mm(p[:, 0:256], T6[:, 0:128], smat(l, 0))
            LAM = g.t()
            tt(LAM[:], p[:, 0:256], row(l, "w0", 256), ALU.add)
            act(LAM[:], LAM[:], AF.Sigmoid)
            AT, GTt = g.t(), g.t()
            for pr in range(2):
                p = pb()
                mm(p[:, 0:128], smat(l, 1)[:, pr * 128:(pr + 1) * 128], X6)
                act(AT[:, pr * 128:(pr + 1) * 128], p[:, 0:128], AF.Sigmoid, bias=pv(l, "a0", 1, pr))
                mm(p[:, 128:256], smat(l, 2)[:, pr * 128:(pr + 1) * 128], T6[:, 128:256])
                cp(GTt[:, pr * 128:(pr + 1) * 128], p[:, 128:256])
            ck('rw3')
            yield
            KK, KP, E, E2, EQT, KC, K2C2, VT, KC2, RT, X, YS, Mt, Bt = [g.t() for _ in range(14)]
            AE = [g.t(), g.t()]
            AC = [g.t(), g.t()]
            Nn = [g.t(), g.t()]
            gsave = g.i
            MASK2 = CONST[:, CI["TRS" + sfx]:CI["TRS" + sfx] + 256]
            TRI3 = CONST[:, CI["TRS" + sfx]:CI["TRS" + sfx] + 384]
            Vp, Up = PADS[0], PADS[1]
            for pr in range(2):
                g.i = gsave
                rT, kT, vT = XS[:, pr, :], XS[:, 2 + pr, :], XS[:, 4 + pr, :]
                aT = AT[:, pr * 128:(pr + 1) * 128]
                ts(KK[:, 0:128], kT, pv(l, "k_k", 1, pr), ALU.mult)
                act(KK[:, 128:256], KK[:, 0:128], AF.Square)
                p = pb()
                mm(p[:, 0:128], BLK, KK[:, 128:256])
                act(KK[:, 128:256], p[:, 0:128], AF.Ln, bias=EPSc)
                act(KK[:, 128:256], KK[:, 128:256], AF.Exp, scale=-0.5)
                tt(KK[:, 0:128], KK[:, 0:128], KK[:, 128:256])
                ts(KP[:, 0:128], aT, -1.0, ALU.add, pv(l, "k_a", 1, pr), ALU.mult)
                stt(KP[:, 0:128], KP[:, 0:128], 1.0, kT, ALU.add, ALU.mult)
                tt(KP[:, 128:256], KK[:, 0:128], aT)
                stt(RT[:, 128:256], rT, pv(l, "r_k", 1, pr), KP[:, 0:128], ALU.mult, ALU.mult)
                p3 = pb()
                mm(p3[:, 0:128], BLK, RT[:, 128:256])
                tt(Bt[:, 128:256], p3[:, 0:128], vT)
                p = pb()
                mm(p[:, 0:384], LAM[:, pr * 128:(pr + 1) * 128], TRI3)
                act(E[:, 0:256], p[:, 0:256], AF.Exp, scale=-C_DEC)
                act(E2[:, 0:128], p[:, 256:384], AF.Exp, scale=-C_DEC)
                act(E2[:, 128:256], p[:, 128:256], AF.Exp, scale=C_DEC)
                tt(EQT[:, 0:128], KK[:, 0:128], E[:, 0:128])
                tt(EQT[:, 128:256], rT, E[:, 128:256])
                tt(KC[:, 0:128], KP[:, 0:128], E2[:, 128:256])
                tt(KC[:, 128:256], KP[:, 128:256], E2[:, 128:256])
                tt(K2C2[:, 0:128], KP[:, 0:128], E2[:, 0:128])
                stt(K2C2[:, 128:256], KP[:, 128:256], -1.0, E2[:, 0:128], ALU.mult, ALU.mult)
                ck('rw4')
                yield
                p = pb()
                tr(p[:, 0:128], vT)
                tr(p[:, 128:256], K2C2[:, 0:128])
                tr(p[:, 256:384], K2C2[:, 128:256])
                cp(VT[:, 0:128], p[:, 0:128])
                ck('rw4a1')
                cp(Vp[:, 0, 0:64], p[:, 0:64])
                cp(Vp[:, 1, 64:128], p[:, 64:128])
                ck('rw4a2')
                cp(KC2[:, 0:256], p[:, 128:384])
                ck('rw4b')
                yield
                for hh in range(2):
                    b = 64 * hh
                    if hh == 1:
                        ck('rw4c')
                    p = pb()
                    mm(p[:, 0:256], KC[b:b + 64, 0:128], EQT[b:b + 64, 0:256])
                    mm(p[:, 256:512], KC[b:b + 64, 128:256], EQT[b:b + 64, 0:256])
                    tt(AE[hh][:, 0:256], p[:, 0:256], MASK2)
                    stt(AC[hh][:, 0:256], p[:, 256:512], -1.0, MASK2, ALU.mult, ALU.mult)
                    p2 = pb()
                    tr(p2[:, 0:128], AC[hh][:, 0:128])
                    cp(Nn[hh][:, 0:128], p2[:, 0:128], "act")
                    yield
                ck('rw5')
                if nseg == 1:
                    p = pb()
                    mm(p[:, 0:128], EQT[:, 0:128], PST[(l, "wkv")][:, pr, :], True, False)
                    for hh in range(2):
                        mm(p[:, 0:128], AE[hh][:, 0:128], Vp[:, hh, :], False, hh == 1)
                    cp(X[:, 0:128], p[:, 0:128])
                    yield
                else:
                    p = pb()
                    inter(bk, l, "wkv", pr, sst, p, EQT[:, 0:128], False)
                    for hh in range(2):
                        mm(p[:, 0:128], Vp[:, hh, :], AE[hh][:, 0:128], False, hh == 1)
                    cp(RT[:, 0:128], p[:, 0:128])
                    yield
                    p = pb()
                    tr(p[:, 0:128], RT[:, 0:128])
                    cp(X[:, 0:128], p[:, 0:128])
                    yield
                ck('rw6')
                yield from tri_solve(bk, [AC[0], AC[1]], [Nn[0], Nn[1]], X, g)
                ck('rw7')
                cp(Up[:, 0, 0:64], X[:, 0:64])
                cp(Up[:, 1, 64:128], X[:, 64:128])
                p = pb()
                inter(bk, l, "wkv", pr, sst, p, EQT[:, 128:256], False)
                for hh in range(2):
                    mm(p[:, 0:128], Vp[:, hh, :], AE[hh][:, 128:256], False, False)
                    mm(p[:, 0:128], Up[:, hh, :], AC[hh][:, 128:256], False, hh == 1)
                ck('rw8')
                cp(YS[:, 0:128], p[:, 0:128])
                act(YS[:, 128:256], p[:, 0:128], AF.Square)
                yield
                p2 = pb()
                mm(p2[:, 0:256], BLK, YS[:, 0:256])
                ts(Mt[:, 0:256], p2[:, 0:256], 1.0 / 64.0, ALU.mult)
                stt(Bt[:, 0:128], Mt[:, 0:128], -1.0, Mt[:, 0:128], ALU.mult, ALU.mult)
                tt(Mt[:, 128:256], Mt[:, 128:256], Bt[:, 0:128], ALU.add)
                act(Mt[:, 128:256], Mt[:, 128:256], AF.Ln, bias=GNEPSc)
                act(Mt[:, 128:256], Mt[:, 128:256], AF.Exp, scale=-0.5)
                tt(YS[:, 0:128], YS[:, 0:128], Mt[:, 0:128], ALU.subtract)
                tt(YS[:, 0:128], YS[:, 0:128], Mt[:, 128:256])
                ts(YS[:, 0:128], YS[:, 0:128], pv(l, "ln_w", 1, pr), ALU.mult, pv(l, "ln_b", 1, pr), ALU.add)
                tt(YS[:, 0:128], YS[:, 0:128], Bt[:, 128:256], ALU.add)
                tt(YT[:, pr, :], YS[:, 0:128], GTt[:, pr * 128:(pr + 1) * 128])
                yield
                ck('rw9')
                state_update(bk, l, "wkv", pr, [KC2[:, 0:128], KC2[:, 128:256]], [VT[:, 0:128], X[:, 0:128]],
                             E[:, 128:256], sst, (pod if nseg == 1 else sod)["wkv"], last, g, transpose_out=True)

        def gla(bk, l, last, wget, sst, pool=None):
            g = GP(pool)
            w2 = wget(2)
            QK = [g.t(), g.t()]
            for c in range(4):
                p = pb()
                proj_fm(p[:, 0:128], w2, c * 128, 128)
                cp(QK[c % 2][:, (c // 2) * 128:(c // 2) * 128 + 128], p[:, 0:128], "act" if c % 2 else "dve")
            KTM = g.t()
            p = pb()
            proj_tm(p[:, 0:256], w2, 256, 256)
            cp(KTM[:], p[:, 0:256], "act")
            w3 = wget(3)
            VTM = g.t()
            p = pb()
            proj_tm(p[:, 0:256], w3, 0, 256)
            cp(VTM[:], p[:, 0:256])
            Vp = [PADS[0], PADS[2]]
            for pr in range(2):
                cp(Vp[pr][:, 0, 0:64], p[:, pr * 128:pr * 128 + 64])
                cp(Vp[pr][:, 1, 64:128], p[:, pr * 128 + 64:pr * 128 + 128])
            GT = g.t()
            for pr in range(2):
                p = pb()
                proj_fm(p[:, 0:128], w3, 256 + pr * 128, 128)
                act(GT[:, pr * 128:(pr + 1) * 128], p[:, 0:128], AF.Silu)
            p = pb()
            mm(p[:, 0:256], GLT[0:16, :], smat(l, 3)[0:16, :])
            LA = g.t()
            tt(LA[:], p[:, 0:256], row(l, "gkb", 256), ALU.add)
            act(LA[:], LA[:], AF.Exp, scale=-1.0)
            act(LA[:], LA[:], AF.Ln, bias=ONEc)
            p = pb()
            mm(p[:, 0:256], bk.SU, LA[:])
            K2 = g.t()
            act(K2[:], p[:, 0:256], AF.Exp, scale=-1.0 / 16.0)
            tt(K2[:], K2[:], KTM[:])
            yield
            E, QKh, YS = g.t(), g.t(), g.t()
            A = [g.t(), g.t()]
            gsave = g.i
            for pr in range(2):
                g.i = gsave
                p = pb()
                mm(p[:, 0:128], LA[:, pr * 128:(pr + 1) * 128], bk.TRI)
                act(E[:, 0:128], p[:, 0:128], AF.Exp, scale=-1.0 / 16.0)
                act(E[:, 128:256], p[:, 0:128], AF.Exp, scale=1.0 / 16.0)
                stt(QKh[:, 0:128], QK[pr][:, 0:128], 0.125, E[:, 0:128], ALU.mult, ALU.mult)
                tt(QKh[:, 128:256], QK[pr][:, 128:256], E[:, 128:256])
                for hh in range(2):
                    b = 64 * hh
                    p = pb()
                    mm(p[:, 0:128], QKh[b:b + 64, 128:256], QKh[b:b + 64, 0:128])
                    tt(A[hh][:, 0:128], p[:, 0:128], bk.TRI)
                    yield
                p = pb()
                inter(bk, l, "gla", pr, sst, p, QKh[:, 0:128], False)
                for hh in range(2):
                    mm(p[:, 0:128], Vp[pr][:, hh, :], A[hh][:, 0:128], False, hh == 1)
                cp(YS[:, 0:128], p[:, 0:128])
                yield
                post_rms(YS[:, 0:128], BLK, 64.0, pv(l, "gla_nw", 1, pr), GT[:, pr * 128:(pr + 1) * 128], YT[:, 2 + pr, :], g)
                yield
                state_update(bk, l, "gla", pr, [K2[:, pr * 128:(pr + 1) * 128]], [VTM[:, pr * 128:(pr + 1) * 128]],
                             E[:, 0:128], sst, (pod if bk.nseg == 1 else sod)["gla"], last, g)
                yield

        def conv_in(bk, l, ckey, skey):
            nseg, L = bk.nseg, bk.L
            W = L + 3
            XBv = XB[:, :, 0:nseg * W].f(lambda a: a.rearrange("p c (s t) -> p c s t", t=W))
            if nseg == 1:
                cp(XBv[:, :, 0, 0:3], CARRY[(l, ckey)][:])
            else:
                S.dma("pool", CST[0:48, 0:768], std[skey][l].rearrange("s i c -> (s i) c"))
                for c in range(6):
                    p = pb()
                    tr(p[:, 0:48], CST[0:48, c * 128:(c + 1) * 128], 48)
                    cp(XBv[:, c, :, 0:3], p[:, 0:48].f(lambda a: a.rearrange("p (s i) -> p s i", i=3)))
            return XBv

        def conv_run(bk, l, ckey, skey, cwname, XBv, last):
            nseg, L = bk.nseg, bk.L
            W = L + 3
            if nseg == 1:
                cp(CARRY[(l, ckey)][:], XBv[:, :, 0, L:L + 3])
                if last:
                    for c in range(6):
                        S.dma("pool", pod[skey][l][:, c * 128:(c + 1) * 128].rearrange("i p -> p i"), CARRY[(l, ckey)][:, c, :],
                              is_out=True, allow_slow_non_contiguous=True)
            else:
                for c in range(6):
                    cp(CS3[:].f(lambda a: a.rearrange("p (s i) -> p s i", i=3)), XBv[:, c, :, L:L + 3])
                    p = pb()
                    tr(p[0:48, 0:128], CS3[:])
                    cp(CSO[0:48, c * 128:(c + 1) * 128], p[0:48, 0:128], "act")
                S.dma("pool", sod[skey][l].rearrange("s i c -> (s i) c"), CSO[:], is_out=True)
            CVv = dv(CV[:], bk)
            TMv = dv(HTF[:, 0:6, :], bk)
            for i in range(4):
                wv = bc2(pv(l, cwname, 6, i * 6), [128, 6, nseg, L])
                if i == 0:
                    tt(CVv, XBv[:, :, :, 0:L], wv)
                else:
                    tt(TMv, XBv[:, :, :, i:i + L], wv)
                    tt(CVv, CVv, TMv, ALU.add)

        def softplus_cols(dst, src, biasrow):
            tt(dst, src, biasrow, ALU.add)
            act(dst, dst, AF.Exp)
            act(dst, dst, AF.Ln, bias=ONEc)

        def dnet(bk, l, last, wget, sst, pool=None):
            g = GP(pool)
            nseg, L = bk.nseg, bk.L
            W = L + 3
            XBv = conv_in(bk, l, "dn", "dnc")
            w4 = wget(4)
            for c in range(4):
                p = pb()
                proj_fm(p[:, 0:128], w4, c * 128, 128)
                cp(XBv[:, c, :, 3:W], sv(p[:, 0:128], bk), "act" if c % 2 else "dve")
            w5 = wget(5)
            for c in range(2):
                p = pb()
                proj_fm(p[:, 0:128], w5, c * 128, 128)
                cp(XBv[:, 4 + c, :, 3:W], sv(p[:, 0:128], bk), "act" if c % 2 else "dve")
            for c in range(2):
                p = pb()
                proj_fm(p[:, 0:128], w5, 256 + c * 128, 128)
                act(ZT[:, c * 128:(c + 1) * 128], p[:, 0:128], AF.Silu)
            w6 = wget(6)
            p = pb()
            proj_tm(p[:, 0:12], w6, 0, 12)
            cp(SMT[:, 0:12], p[:, 0:12])
            for c in range(2):
                p = pb()
                proj_fm(p[:, 0:128], w6, 12 + c * 128, 128)
                act(ZT[:, 256 + c * 128:256 + (c + 1) * 128], p[:, 0:128], AF.Silu)
            conv_run(bk, l, "dn", "dnc", "dn_cw", XBv, last)
            act(CV[:], CV[:], AF.Silu)
            act(SMT[:, 16:20], SMT[:, 4:8], AF.Sigmoid)
            softplus_cols(SMT[:, 20:24], SMT[:, 0:4], row(l, "dndt", 4))
            act(SMT[:, 24:28], row(l, "dnA", 4), AF.Exp)
            stt(SMT[:, 28:32], SMT[:, 20:24], -1.0, SMT[:, 24:28], ALU.mult, ALU.mult)
            p = pb()
            mm(p[:, 0:4], bk.SU, SMT[:, 28:32])
            act(SMT[:, 32:36], p[:, 0:4], AF.Exp)
            yield
            SQ, LB, ELb, R, X, QE, YS, K2, SQ2 = [g.t() for _ in range(9)]
            KQ = [g.t(), g.t()]
            ET = [g.t(), g.t()]
            A = [g.t(), g.t()]
            T0 = [g.t(), g.t()]
            Nn = [g.t(), g.t()]
            NT = [g.t(), g.t()]
            gsave = g.i
            Wp = PADS[2]
            for pr in range(2):
                g.i = gsave
                for which, (src, dst, scl) in enumerate(((CV[:, 2 + pr, :], KQ[pr][:, 0:128], 1.0),
                                                         (CV[:, pr, :], KQ[pr][:, 128:256], 0.125))):
                    sqt = SQ2 if which else SQ
                    act(sqt[:, 0:128], src, AF.Square)
                    p = pb()
                    mm(p[:, 0:128], BLK, sqt[:, 0:128])
                    act(sqt[:, 128:256], p[:, 0:128], AF.Ln, bias=EPSc)
                    act(sqt[:, 128:256], sqt[:, 128:256], AF.Exp, scale=-0.5)
                    stt(dst, src, scl, sqt[:, 128:256], ALU.mult, ALU.mult)
                tt(LB[:, 0:128].f(lambda a: a.rearrange("p (h d) -> p h d", d=64)),
                   ONES.f(lambda a: a.rearrange("p (h d) -> p h d", d=64)),
                   bc(SMT[:, 28 + 2 * pr:30 + 2 * pr], [128, 2, 64], 2))
                p = pb()
                mm(p[:, 0:128], LB[:, 0:128], bk.TRI)
                act(ELb[:, 0:128], p[:, 0:128], AF.Exp)
                yield
                p = pb()
                inter(bk, l, "dn", pr, sst, p, KQ[pr][:, 0:128], True)
                tt(R[:, 0:128], p[:, 0:128], ELb[:, 0:128])
                tt(R[:, 0:128], CV[:, 4 + pr, :], R[:, 0:128], ALU.subtract)
                yield
                p = pb()
                tr(p[:, 0:128], R[:, 0:128])
                for hh in range(2):
                    h = 2 * pr + hh
                    ts(X[:, hh * 64:hh * 64 + 64], p[:, hh * 64:hh * 64 + 64], SMT[:, 16 + h:17 + h], ALU.mult)
                yield
                tt(QE[:, 0:128], KQ[pr][:, 128:256], ELb[:, 0:128])
                p = pb()
                tr(p[:, 0:128], KQ[pr][:, 0:128])
                for hh in range(2):
                    h = 2 * pr + hh
                    ts(K2[:, hh * 64:hh * 64 + 64], p[:, hh * 64:hh * 64 + 64], SMT[:, 32 + h:33 + h], ALU.mult)
                yield
                for hh in range(2):
                    h = 2 * pr + hh
                    b = 64 * hh
                    decay_mask(bk, SMT[:, 28 + h:29 + h], ET[hh][:, 0:128], g)
                    tt(ET[hh][:, 128:256], ET[hh][:, 0:128], bk.TRS)
                    p = pb()
                    mm(p[:, 0:256], KQ[pr][b:b + 64, 0:128], KQ[pr][b:b + 64, 0:256])
                    tt(A[hh][:, 0:128], p[:, 128:256], ET[hh][:, 0:128])
                    tt(T0[hh][:, 0:128], p[:, 0:128], ET[hh][:, 128:256])
                    p2 = pb()
                    tr(p2[:, 0:128], T0[hh][:, 0:128])
                    ts(Nn[hh][:, 0:128], p2[:, 0:128], SMT[:, 16 + h:17 + h], ALU.mult, -1.0, ALU.mult)
                    p3 = pb()
                    tr(p3[:, 0:128], Nn[hh][:, 0:128])
                    cp(NT[hh][:, 0:128], p3[:, 0:128], "act")
                    g.i -= 1
                    yield
                yield from tri_solve(bk, [NT[0], NT[1]], [Nn[0], Nn[1]], X, g)
                cp(Wp[:, 0, 0:64], X[:, 0:64])
                cp(Wp[:, 1, 64:128], X[:, 64:128])
                p = pb()
                inter(bk, l, "dn", pr, sst, p, QE[:, 0:128], False)
                for hh in range(2):
                    mm(p[:, 0:128], Wp[:, hh, :], A[hh][:, 0:128], False, hh == 1)
                cp(YS[:, 0:128], p[:, 0:128])
                yield
                post_rms(YS[:, 0:128], BLK, 64.0, pv(l, "dn_nw", 1, pr), ZT[:, pr * 128:(pr + 1) * 128], YT[:, 4 + pr, :], g)
                yield
                state_update(bk, l, "dn", pr, [K2[:, 0:128]], [X[:, 0:128]], ELb[:, 0:128], sst,
                             (pod if nseg == 1 else sod)["dn"], last, g)
                yield

        def ssd(bk, l, last, wget, sst, pool=None):
            g = GP(pool)
            nseg, L = bk.nseg, bk.L
            W = L + 3
            XBv = conv_in(bk, l, "ss", "ssc")
            w7 = wget(7)
            for c in range(4):
                p = pb()
                proj_fm(p[:, 0:128], w7, c * 128, 128)
                cp(XBv[:, c, :, 3:W], sv(p[:, 0:128], bk), "act" if c % 2 else "dve")
            w8 = wget(8)
            for c in range(2):
                p = pb()
                proj_fm(p[:, 0:128], w8, c * 128, 128)
                cp(XBv[:, 4 + c, :, 3:W], sv(p[:, 0:128], bk), "act" if c % 2 else "dve")
            conv_run(bk, l, "ss", "ssc", "ss_cw", XBv, last)
            for c in range(6):
                act(CV[:, c, :], CV[:, c, :], AF.Silu, bias=pv(l, "ss_cb", 1, c))
            softplus_cols(SMT[:, 40:44], SMT[:, 8:12], row(l, "ssdt", 4))
            act(SMT[:, 44:48], row(l, "ssA", 4), AF.Exp)
            stt(SMT[:, 48:52], SMT[:, 40:44], -1.0, SMT[:, 44:48], ALU.mult, ALU.mult)
            p = pb()
            mm(p[:, 0:4], bk.SU, SMT[:, 48:52])
            act(SMT[:, 52:56], p[:, 0:4], AF.Exp)
            yield
            BTM, X2, YS, LB = [g.t() for _ in range(4)]
            ET = [g.t(), g.t()]
            A = [g.t(), g.t()]
            EL = [g.t(), g.t()]
            CH = [g.t(), g.t()]
            gsave = g.i
            Xp = PADS[1]
            outd = (pod if nseg == 1 else sod)["ssm"]
            for pr in range(2):
                g.i = gsave
                p = pb()
                tr(p[:, 0:128], CV[:, 2 + pr, :])
                tr(p[:, 128:256], CV[:, pr, :])
                cp(BTM[:, 0:128], p[:, 0:128])
                for hh in range(2):
                    h = 2 * pr + hh
                    ts(Xp[:, hh, hh * 64:hh * 64 + 64], p[:, 128 + hh * 64:128 + hh * 64 + 64], SMT[:, 40 + h:41 + h], ALU.mult)
                    ts(X2[:, hh * 64:hh * 64 + 64], Xp[:, hh, hh * 64:hh * 64 + 64], SMT[:, 52 + h:53 + h], ALU.mult)
                pG = pb()
                mm(pG[:, 0:128], CV[:, 2 + pr, :], CV[:, 4 + pr, :])
                for hh in range(2):
                    h = 2 * pr + hh
                    decay_mask(bk, SMT[:, 48 + h:49 + h], ET[hh][:, 0:128], g)
                    g.i -= 1
                    tt(A[hh][:, 0:128], pG[:, 0:128], ET[hh][:, 0:128])
                    ts(LB[:, 0:128], ONES, SMT[:, 48 + h:49 + h], ALU.mult)
                    p = pb()
                    mm(p[:, 0:128], LB[:, 0:128], bk.TRI)
                    act(EL[hh][:, 0:128], p[:, 0:128], AF.Exp)
                    tt(CH[hh][:, 0:128], CV[:, 4 + pr, :], EL[hh][:, 0:128])
                yield
                pY = pb()
                for hh in range(2):
                    reg = pY[:, hh * 128:(hh + 1) * 128]
                    for s in range(nseg):
                        hg = PST[(l, "ssm")][:, pr, :] if nseg == 1 else sst[:, s, pr, :]
                        mm(reg[:, s * L:(s + 1) * L], hg, CH[hh][:, s * L:(s + 1) * L], s == 0, False)
                    mm(reg, Xp[:, hh, :], A[hh][:, 0:128], False, True)
                cp(YS[0:64, 0:128], pY[0:64, 0:128])
                cp(YS[64:128, 0:128], pY[64:128, 128:256])
                yield
                stt(YS[:, 0:128], CV[:, pr, :], pv(l, "ss_D", 1, pr), YS[:, 0:128], ALU.mult, ALU.add)
                tt(YS[:, 0:128], YS[:, 0:128], ZT[:, 256 + pr * 128:256 + (pr + 1) * 128])
                post_rms(YS[:, 0:128], ONES, 128.0, pv(l, "ss_nw", 1, pr), None, YT[:, 6 + pr, :], g)
                yield
                for s in range(nseg):
                    lv = BTM[:, 0:128]
                    if nseg > 1:
                        m = g_rot()
                        I("act", "mul", out=m[:, 0:128], in_=lv, mul=CONST[:, C_SEG + s:C_SEG + s + 1])
                        lv = m[:, 0:128]
                    p = pb()
                    mm(p[:, 0:128], lv, X2[:, 0:128])
                    if nseg == 1:
                        hg = PST[(l, "ssm")][:, pr, :]
                        dest = hg
                    else:
                        hg = sst[:, s, pr, :]
                        dest = ost()[:]
                    for hh in range(2):
                        stt(dest[:, hh * 64:hh * 64 + 64], hg[:, hh * 64:hh * 64 + 64], lastcol(EL[hh][:, 0:128], bk, s),
                            p[:, hh * 64:hh * 64 + 64], ALU.mult, ALU.add)
                    if nseg > 1 or last:
                        for hh in range(2):
                            p2 = pb()
                            tr(p2[0:64, 0:128], dest[:, hh * 64:hh * 64 + 64])
                            o = ost()
                            cp(o[0:64, :], p2[0:64, 0:128], "act")
                            dd = outd[l][2 * pr + hh] if nseg == 1 else outd[l, s, 2 * pr + hh]
                            S.dma("pool", dd, o[0:64, :], is_out=True)

        import os
        class _Stop(Exception):
            pass
        kstop = os.environ.get('KSTOP', '')
        def ck(name):
            if name == kstop:
                raise _Stop()
        blks = [int(x) for x in os.environ.get('KBLKS', ','.join(str(i) for i in range(NBLK))).split(',')]
        try:
          ck('setup')
          def with_cx(cx, gen):
              while True:
                  CUR[0] = cx
                  try:
                      next(gen)
                  except StopIteration:
                      return
                  yield

          def rr_gen(gens):
              gens = list(gens)
              while gens:
                  for g_ in list(gens):
                      try:
                          next(g_)
                      except StopIteration:
                          gens.remove(g_)
                      yield

          def run_seq(gen):
              for _ in gen:
                  pass

          def mix_gen(blk, bk, l, last, samp):
              def wget(idx, l=l):
                  return wload16(l, idx)
              if not samp:
                  yield from rr_gen([dnet(bk, l, last, wget, None, G2), rwkv(bk, l, last, wget, None, G)])
                  yield from rr_gen([ssd(bk, l, last, wget, None, G2), gla(bk, l, last, wget, None, G)])
              else:
                  sst = SST[0]
                  I("dve", "memset", sst[:], 0.0)
                  load_bd(sst, std["wkv"][l])
                  for j in range(8):
                      p = pb()
                      for q in range(4):
                          tr(p[:, q * 128:(q + 1) * 128], sst[:, 2 * j + q // 2, q % 2, :])
                      cp(sst[:, 2 * j:2 * j + 2, :, :].f(lambda a: a.rearrange("p s r v -> p (s r v)")), p[:, 0:512])
                  yield
                  yield from rwkv(bk, l, last, wget, sst)
                  sst = SST[1]
                  I("dve", "memset", sst[:], 0.0)
                  load_bd(sst, std["gla"][l])
                  yield from gla(bk, l, last, wget, sst)
                  sst = SST[0]
                  I("dve", "memset", sst[:], 0.0)
                  load_bd(sst, std["dn"][l])
                  yield from dnet(bk, l, last, wget, sst)
                  sst = SST[1]
                  for s in range(16):
                      sl = SSL[0]
                      S.dma("pool", sl[:], std["ssm"][l, s].rearrange("h p n -> p h n"))
                      p = pb()
                      for h in range(4):
                          tr(p[:, h * 64:(h + 1) * 64], sl[0:64, h, :], 64)
                      cp(sst[:, s, :, :].f(lambda a: a.rearrange("p r v -> p (r v)")), p[:, 0:256])
                  yield
                  yield from ssd(bk, l, last, wget, sst)

          def dense_gen(blk, bk, l, samp):
              def wget(idx, l=l):
                  return wload16(l, idx)
              for half in range(2):
                  w = wget(9 + half)
                  p = pb()
                  for j in range(4):
                      for c8 in range(8):
                          mm(p[:, j * 128:(j + 1) * 128], w[:, c8, j * 128:(j + 1) * 128], YT[:, c8, :], c8 == 0, c8 == 7)
                  pv4 = p[:, 0:512].f(lambda a: a.rearrange("p (c s t) -> p c s t", c=4, t=bk.L))
                  tt(dv(TMP4[:], bk), pv4, modv(MODT[:, l, 16 + 4 * half:20 + 4 * half, :], bk, 4))
                  tt(XT[:, 4 * half:4 * half + 4, :], XT[:, 4 * half:4 * half + 4, :], TMP4[:], ALU.add)
                  yield
              if samp:
                  dump(6 * l + 2, XT)
              norm_mod(bk, modv(SC[:, l, 1], bk), modv(MODT[:, l, 24:32, :], bk),
                       cols=(lambda kc, l=l: SC[:, l, 1, kc, 0:1], lambda kc, l=l: MODT[:, l, 24 + kc, 0:1]))
              yield
              for s8 in range(8):
                  w = wget(11 + s8)
                  p = pb()
                  for j in range(4):
                      for kc in range(8):
                          mm(p[:, j * 128:(j + 1) * 128], w[:, kc, j * 128:(j + 1) * 128], HT.s(kc, (slice(None), kc, slice(None))), kc == 0, kc == 7)
                  hv = HID[:, 4 * s8:4 * s8 + 4, :]
                  tmpr = (TMP4 if s8 % 2 else TMP5)
                  act(tmpr[:], p[:, 0:512].f(lambda a: a.rearrange("p (c t) -> p c t", t=128)), AF.Relu)
                  tt(hv, tmpr[:], tmpr[:], ALU.mult)
                  yield
              for half in range(2):
                  p = pb()
                  for fcg in range(4):
                      w = wget(19 + half * 4 + fcg)
                      for j in range(4):
                          for f8 in range(8):
                              mm(p[:, j * 128:(j + 1) * 128], w[:, f8, j * 128:(j + 1) * 128], HID[:, fcg * 8 + f8, :],
                                 fcg == 0 and f8 == 0 and j == 0, fcg == 3 and f8 == 7)
                  pv4 = p[:, 0:512].f(lambda a: a.rearrange("p (c s t) -> p c s t", c=4, t=bk.L))
                  tt(dv(TMP4[:], bk), pv4, modv(MODT[:, l, 40 + 4 * half:44 + 4 * half, :], bk, 4))
                  tt(XT[:, 4 * half:4 * half + 4, :], XT[:, 4 * half:4 * half + 4, :], TMP4[:], ALU.add)
                  yield
              if samp:
                  dump(6 * l + 4, XT)
              if l == NL - 1:
                  norm_mod(bk, bc2(pv(0, "fnw", 8), [128, 8, bk.nseg, bk.L]), None, HTF,
                           cols=(lambda kc: pv(0, "fnw", 1, kc), None))
                  for b2 in range(2):
                      p = pb()
                      for j in range(4):
                          tr(p[:, j * 128:(j + 1) * 128], HTF.s(4 * b2 + j, (slice(None), 4 * b2 + j, slice(None))))
                      cp(YTM[:, b2 * 512:(b2 + 1) * 512], p[:, 0:512])
                  S.dma("pool", yout[blk * 128:(blk + 1) * 128, :], YTM[:], is_out=True)

          pending = None
          for bi, blk in enumerate(blks):
              cx = CXS[bi % 2]
              CUR[0] = cx
              bk = Pk if blk < 16 else Sk
              last = blk == max(b for b in blks if b < 16) if blk < 16 else False
              samp = blk == 16
              S.dma("pool", XTM[:], xin[blk * 128:(blk + 1) * 128, :])
              for b2 in range(2):
                  p = pb()
                  for j in range(4):
                      tr(p[:, j * 128:(j + 1) * 128], XTM[:, (4 * b2 + j) * 128:(4 * b2 + j + 1) * 128])
                  cp(XT[:, 4 * b2:4 * b2 + 4, :], p[:, 0:512].f(lambda a: a.rearrange("p (c t) -> p c t", t=128)))
              for l in range(NL):
                  CUR[0] = cx
                  norm_mod(bk, modv(SC[:, l, 0], bk), modv(MODT[:, l, 0:8, :], bk),
                           cols=(lambda kc, l=l: SC[:, l, 0, kc, 0:1], lambda kc, l=l: MODT[:, l, kc, 0:1]))
                  M = with_cx(cx, mix_gen(blk, bk, l, last, samp))
                  if l == 0 and pending is not None:
                      run_seq(rr_gen([M, pending]))
                      pending = None
                  else:
                      run_seq(M)
                  D = with_cx(cx, dense_gen(blk, bk, l, samp))
                  if l == NL - 1 and not KDBG:
                      pending = D
                  else:
                      run_seq(D)
          if pending is not None:
              run_seq(pending)
        except _Stop:
            pass
        print('opcounts', {e: len(v) for e, v in S.ops.items()}, flush=True)
        S.emit()
    return nc


W_IN_COLS = None


def _win_colmap():
    def rng(a, b):
        return list(range(a, b))
    slots = []
    slots.append(rng(0, 512))
    slots.append(rng(512, 896) + rng(1920, 1936))
    slots.append(rng(896, 1408))
    slots.append(rng(1408, 1920))
    slots.append(rng(1936, 2448))
    slots.append(rng(2448, 2960))
    slots.append(rng(2960, 2968) + rng(3992, 3996) + rng(2968, 3224))
    slots.append(rng(3224, 3736))
    slots.append(rng(3736, 3992))
    return slots


def _tile_rows(w, cols):
    out = np.zeros((128, 8, 512), np.float32)
    sub = w[:, cols]
    out[:, :, :len(cols)] = sub.reshape(8, 128, len(cols)).transpose(1, 0, 2)
    return out


_NC_CACHE = {}


def kernel(**inp):
    f = lambda k: np.ascontiguousarray(np.asarray(inp[k], dtype=np.float32))
    wall = np.zeros((2, 27, 128, 8, 512), np.float32)
    adaw = np.zeros((2, 12, 128, 8, 512), np.float32)
    cm = _win_colmap()
    w_in, w_out, w_up, w_down, ada_w = f("w_in"), f("w_out"), f("w_up"), f("w_down"), f("ada_w")
    for l in range(2):
        for s in range(9):
            wall[l, s] = _tile_rows(w_in[l], cm[s])
        for s in range(2):
            wall[l, 9 + s] = _tile_rows(w_out[l], list(range(s * 512, (s + 1) * 512)))
        for s in range(8):
            wall[l, 11 + s] = _tile_rows(w_up[l], list(range(s * 512, (s + 1) * 512)))
        for half in range(2):
            for fcg in range(4):
                blk = w_down[l][fcg * 1024:(fcg + 1) * 1024, half * 512:(half + 1) * 512]
                wall[l, 19 + half * 4 + fcg] = blk.reshape(8, 128, 512).transpose(1, 0, 2)
        for s in range(12):
            adaw[l, s] = _tile_rows(ada_w[l], list(range(s * 512, (s + 1) * 512)))
    pvv = np.zeros((128, 2 * NPV), np.float32)
    rows = np.zeros((128, 2 * NR), np.float32)
    sm = np.zeros((128, 2 * 1024), np.float32)

    def putv(l, name, vec, off=0):
        n = vec.shape[0] // 128
        pvv[:, l * NPV + PVO[name] + off:l * NPV + PVO[name] + off + n] = vec.reshape(n, 128).T
    for l in range(2):
        putv(l, "ada_b", f("ada_b")[l])
        putv(l, "n1w", f("norm1_w")[l])
        putv(l, "n2w", f("norm2_w")[l])
        putv(l, "mu", f("rwkv_mu")[l])
        putv(l, "a0", f("rwkv_a0")[l])
        putv(l, "k_k", f("rwkv_k_k")[l])
        putv(l, "k_a", f("rwkv_k_a")[l])
        putv(l, "r_k", f("rwkv_r_k")[l])
        putv(l, "ln_w", f("rwkv_ln_w")[l])
        putv(l, "ln_b", f("rwkv_ln_b")[l])
        putv(l, "gla_nw", f("gla_norm_w")[l])
        for i in range(4):
            putv(l, "dn_cw", f("dn_conv_w")[l, i], i * 6)
            putv(l, "ss_cw", f("ssm_conv_w")[l, i], i * 6)
        putv(l, "dn_nw", f("dn_norm_w")[l])
        putv(l, "ss_cb", f("ssm_conv_b")[l])
        putv(l, "ss_nw", f("ssm_norm_w")[l])
        putv(l, "ss_D", np.repeat(f("ssm_D")[l], 64))
        putv(l, "fnw", f("final_norm_w"))
        o = l * NR
        rows[:, o + RO["w0"]:o + RO["w0"] + 256] = f("rwkv_w0")[l][None, :]
        rows[:, o + RO["gkb"]:o + RO["gkb"] + 256] = f("gla_gk_b")[l][None, :]
        rows[:, o + RO["dnA"]:o + RO["dnA"] + 4] = f("dn_A_log")[l][None, :]
        rows[:, o + RO["dndt"]:o + RO["dndt"] + 4] = f("dn_dt_bias")[l][None, :]
        rows[:, o + RO["ssdt"]:o + RO["ssdt"] + 4] = f("ssm_dt_bias")[l][None, :]
        rows[:, o + RO["ssA"]:o + RO["ssA"] + 4] = f("ssm_A_log")[l][None, :]
        o = l * 1024
        sm[0:32, o:o + 256] = f("rwkv_w2")[l]
        sm[32:64, o + 256:o + 512] = f("rwkv_a2")[l]
        sm[64:128, o + 512:o + 768] = f("rwkv_g2")[l]
        sm[0:16, o + 768:o + 1024] = f("gla_gk_w2")[l]
    consts = make_consts()
    xp, xs = f("x_prompt"), f("x_sample")
    cpr, csm = f("c_prompt"), f("c_sample")
    stn = dict(shift="state_rwkv_shift", wkv="state_rwkv_wkv", gla="state_gla", dnc="state_dn_conv", dn="state_dn",
               ssc="state_ssm_conv", ssm="state_ssm")
    stf = {k: f(v) for k, v in stn.items()}
    in_maps = []
    for c in range(8):
        m = dict(wall=wall, ada=adaw, pv=pvv, rows=rows, sm=sm, consts=consts)
        m["xin"] = np.ascontiguousarray(np.concatenate([xp[c], xs[16 * c:16 * c + 16].reshape(128, 1024)], 0))
        m["cc"] = np.ascontiguousarray(np.concatenate([cpr[c:c + 1], csm[16 * c:16 * c + 16]], 0))
        for k in ST_SHAPES:
            m["st_" + k] = np.ascontiguousarray(stf[k][:, 16 * c:16 * c + 16])
        in_maps.append(m)
    if "nc" not in _NC_CACHE:
        _NC_CACHE["nc"] = build()
    res = run_bass_kernel_spmd(_NC_CACHE["nc"], in_maps, core_ids=list(range(8)))
    R = res.results
    global _LAST_R
    _LAST_R = R
    y_prompt = np.stack([R[c]["y"][:2048] for c in range(8)], 0)
    y_sample = np.concatenate([R[c]["y"][2048:].reshape(16, 8, 1024) for c in range(8)], 0)
    outs = [y_prompt, y_sample]
    for k in ("shift", "wkv", "gla", "dnc", "dn", "ssc", "ssm"):
        outs.append(np.stack([R[c]["p_" + k] for c in range(8)], 1))
    for k in ("shift", "wkv", "gla", "dnc", "dn", "ssc", "ssm"):
        outs.append(np.concatenate([R[c]["s_" + k] for c in range(8)], 1))
    return tuple(np.ascontiguousarray(o.astype(np.float32)) for o in outs)
```

```python
import numpy as np
import concourse.bass as bass
import concourse.mybir as mybir
from concourse.bass_utils import run_bass_kernel_spmd

F32 = mybir.dt.float32
BF16 = mybir.dt.bfloat16
AF = mybir.ActivationFunctionType
ALU = mybir.AluOpType
AX = mybir.AxisListType


class V:
    __slots__ = ("tile", "ap", "sub")

    def __init__(self, tile, ap, sub):
        self.tile, self.ap, self.sub = tile, ap, sub

    def __getitem__(self, idx):
        return V(self.tile, self.ap[idx], self.sub)

    def f(self, fn):
        return V(self.tile, fn(self.ap), self.sub)


class Tile:
    def __init__(self, h, name):
        self.h, self.name = h, name
        self.ww = {}
        self.wr = {}
        self.subs = {}

    def __getitem__(self, idx):
        return V(self, self.h[idx], None)

    def s(self, k, idx=None):
        ap = self.h[idx] if idx is not None else self.h[:]
        return V(self, ap, k)


def _mx(d, s, v):
    if d.get(s, 0) < v:
        d[s] = v


class Sched:
    ENG = ("pe", "act", "dve", "pool", "sp")
    NDMA = 12

    def __init__(self, nc, stack):
        self.nc = nc
        self.ops = {e: [] for e in self.ENG}
        self.cnt = {e: 0 for e in self.ENG}
        self.waited = {e: {} for e in self.ENG}
        self.sem = {}
        for e in ("pe", "act", "dve", "pool"):
            self.sem[e] = stack.enter_context(nc.semaphore("s_" + e))
        self.dq = {}
        for q in ("sp", "pool", "act"):
            n = self.NDMA if q == "sp" else 6
            sems = [stack.enter_context(nc.semaphore("d_%s%d" % (q, j))) for j in range(n)]
            for j, s_ in enumerate(sems):
                self.sem[("d", q, j)] = s_
            self.dq[q] = [n, 0, [0] * n]
        self.out_tokens = {}
        self.stack = stack
        self.ntile = 0

    def sb(self, shape, name=None, dt=F32):
        self.ntile += 1
        name = name or ("t%d" % self.ntile)
        h = self.stack.enter_context(self.nc.sbuf_tensor(name, list(shape), dt))
        return Tile(h, name)

    def ps(self, shape, name=None, dt=F32):
        self.ntile += 1
        name = name or ("p%d" % self.ntile)
        h = self.stack.enter_context(self.nc.psum_tensor(name, list(shape), dt))
        return Tile(h, name)

    def _deps(self, reads, writes):
        tok = {}
        for v in reads:
            t = v.tile
            for s, x in t.ww.items():
                _mx(tok, s, x)
            if v.sub is None:
                for k, (w, r) in t.subs.items():
                    for s, x in w.items():
                        _mx(tok, s, x)
            elif v.sub in t.subs:
                for s, x in t.subs[v.sub][0].items():
                    _mx(tok, s, x)
        for v in writes:
            t = v.tile
            for d in (t.ww, t.wr):
                for s, x in d.items():
                    _mx(tok, s, x)
            if v.sub is None:
                for k, (w, r) in t.subs.items():
                    for d in (w, r):
                        for s, x in d.items():
                            _mx(tok, s, x)
            elif v.sub in t.subs:
                for d in t.subs[v.sub]:
                    for s, x in d.items():
                        _mx(tok, s, x)
        return tok

    def _commit(self, reads, writes, s, x):
        for v in reads:
            t = v.tile
            if v.sub is None:
                _mx(t.wr, s, x)
            else:
                if v.sub not in t.subs:
                    t.subs[v.sub] = [{}, {}]
                _mx(t.subs[v.sub][1], s, x)
        for v in writes:
            t = v.tile
            if v.sub is None:
                t.ww = {s: x}
                t.wr = {}
                t.subs = {}
            else:
                t.subs[v.sub] = [{s: x}, {}]

    def _add(self, eng, fn, reads, writes, tokens_extra=None, dma_q=None, is_out=False):
        tok = self._deps(reads, writes)
        if tokens_extra:
            for s, x in tokens_extra.items():
                _mx(tok, s, x)
        if dma_q is not None:
            n, nxt, cnts = self.dq[dma_q]
            j = nxt
            self.dq[dma_q][1] = (nxt + 1) % n
            if cnts[j] > 0:
                _mx(tok, ("d", dma_q, j), 16 * cnts[j])
            cnts[j] += 1
            mysem, myval, inc = ("d", dma_q, j), 16 * cnts[j], 16
        else:
            self.cnt[eng] += 1
            mysem, myval, inc = eng, self.cnt[eng], 1
        waits = []
        wd = self.waited[eng]
        for s, x in tok.items():
            if eng == "pe" and s == "pe":
                continue
            if wd.get(s, 0) >= x:
                continue
            wd[s] = x
            waits.append((s, x))
        self.ops[eng].append((fn, waits, mysem, inc))
        self._commit(reads, writes, mysem, myval)
        if is_out:
            _mx(self.out_tokens, mysem, myval)

    def I(self, eng, meth, *args, **kw):
        reads, writes = [], []
        a2 = []
        for a in args:
            if isinstance(a, V):
                reads.append(a)
                a2.append(a.ap)
            else:
                a2.append(a)
        k2 = {}
        rw = kw.pop("_rw", False)
        for k, a in kw.items():
            if isinstance(a, V):
                if k in ("out", "accum_out"):
                    writes.append(a)
                    if rw:
                        reads.append(a)
                else:
                    reads.append(a)
                k2[k] = a.ap
            else:
                k2[k] = a
        fn = lambda e: getattr(e, meth)(*a2, **k2)
        self._add(eng, fn, reads, writes)

    def dma(self, q, out, in_, is_out=False, xr=(), xw=(), **kw):
        reads, writes = list(xr), list(xw)
        o = out.ap if isinstance(out, V) else out
        i = in_.ap if isinstance(in_, V) else in_
        if isinstance(out, V):
            writes.append(out)
        if isinstance(in_, V):
            reads.append(in_)
        fn = lambda e: e.dma_start(out=o, in_=i, **kw)
        self._add(q, fn, reads, writes, dma_q=q, is_out=is_out)

    def emit(self):
        nc = self.nc
        fin = [(s, x) for s, x in self.out_tokens.items()]
        eobj = {"pe": "tensor", "act": "scalar", "dve": "vector", "pool": "gpsimd", "sp": "sync"}
        with nc.Block() as block:
            for ename in self.ENG:
                ops = self.ops[ename]
                extra = fin if ename == "sp" else []

                def body(e, ops=ops, extra=extra):
                    for fn, waits, mysem, inc in ops:
                        for s, x in waits:
                            e.wait_ge(self.sem[s], x)
                        fn(e).then_inc(self.sem[mysem], inc)
                    for s, x in extra:
                        e.wait_ge(self.sem[s], x)
                getattr(block, eobj[ename])(body)

NL = 2
NTOK = 2176
NBLK = 17
C_DEC = 0.6065306597126334
CNAMES = ["IDENT", "ONES", "BLK", "TRS_P", "TRI_P", "SU_P", "NEGM_P", "TRS_S", "TRI_S", "SU_S", "NEGM_S"]
CI = {n: i * 128 for i, n in enumerate(CNAMES)}
C_SEG = 11 * 128
C_EPS = C_SEG + 16
C_GNEPS = C_EPS + 1
C_ONE = C_EPS + 2
NCONST = C_EPS + 4
NPV = 160
PVO = dict(ada_b=0, n1w=48, n2w=56, mu=64, a0=71, k_k=73, k_a=75, r_k=77, ln_w=79, ln_b=81, gla_nw=83,
           dn_cw=85, dn_nw=109, ss_cw=111, ss_cb=135, ss_nw=141, ss_D=143, fnw=145)
NR = 528
RO = dict(w0=0, gkb=256, dnA=512, dndt=516, ssdt=520, ssA=524)
ST_SHAPES = dict(shift=[2, 16, 896], wkv=[2, 16, 4, 64, 64], gla=[2, 16, 4, 64, 64], dnc=[2, 16, 3, 768],
                 dn=[2, 16, 4, 64, 64], ssc=[2, 16, 3, 768], ssm=[2, 16, 4, 64, 128])


def make_consts():
    c = np.zeros((128, NCONST), np.float32)
    s = np.arange(128)[:, None]
    i = np.arange(128)[None, :]
    same = (s // 8) == (i // 8)
    m = {}
    m["IDENT"] = (s == i)
    m["ONES"] = np.ones((128, 128))
    m["BLK"] = (s // 64) == (i // 64)
    m["TRI_P"] = s <= i
    m["TRS_P"] = s < i
    m["SU_P"] = s > i
    m["NEGM_P"] = (m["TRI_P"].astype(np.float32) - 1.0) * 30000.0
    m["TRI_S"] = (s <= i) & same
    m["TRS_S"] = (s < i) & same
    m["SU_S"] = (s > i) & same
    m["NEGM_S"] = (m["TRI_S"].astype(np.float32) - 1.0) * 30000.0
    for n in CNAMES:
        c[:, CI[n]:CI[n] + 128] = m[n].astype(np.float32)
    c[:, C_SEG:C_SEG + 16] = ((np.arange(128)[:, None] // 8) == np.arange(16)[None, :]).astype(np.float32)
    c[:, C_EPS] = 1e-6
    c[:, C_GNEPS] = 64e-5
    c[:, C_ONE] = 1.0
    return c


class BK:
    pass


def build():
    from contextlib import ExitStack
    nc = bass.Bass("TRN2", target_bir_lowering=False)

    def din(name, shape):
        return nc.dram_tensor(name, list(shape), F32, kind="ExternalInput").ap()

    def dout(name, shape):
        return nc.dram_tensor(name, list(shape), F32, kind="ExternalOutput").ap()

    xin = din("xin", [NTOK, 1024])
    ccd = din("cc", [17, 1024])
    std = {k: din("st_" + k, v) for k, v in ST_SHAPES.items()}
    wall = din("wall", [2, 27, 128, 8, 512])
    adad = din("ada", [2, 12, 128, 8, 512])
    pvd = din("pv", [128, 2 * NPV])
    rowsd = din("rows", [128, 2 * NR])
    smd = din("sm", [128, 2 * 4 * 256])
    constd = din("consts", [128, NCONST])
    yout = dout("y", [NTOK, 1024])
    nc.allow_low_precision("bf16 operands (fp32 PSUM accumulation) for the dense projections")
    wbf = nc.dram_tensor("wbf", [2, 27, 128, 8, 512], BF16).ap()
    pod = {k: dout("p_" + k, [v[0]] + v[2:]) for k, v in ST_SHAPES.items()}
    sod = {k: dout("s_" + k, v) for k, v in ST_SHAPES.items()}
    import os as _os
    KDBG = int(_os.environ.get("KDBG", "0"))
    dbgd = dout("dbg", [12, 128, 1024]) if KDBG else None

    with ExitStack() as es:
        S = Sched(nc, es)
        I = S.I
        CONST = S.sb([128, NCONST], "CONST")
        PV = S.sb([128, 2 * NPV], "PV")
        ROWS = S.sb([128, 2 * NR], "ROWS")
        SM = S.sb([128, 2 * 1024], "SM")
        S.dma("pool", CONST[:], constd)
        S.dma("pool", PV[:], pvd)
        S.dma("pool", ROWS[:], rowsd)
        S.dma("pool", SM[:], smd)

        def K(n):
            return CONST[:, CI[n]:CI[n] + 128]
        IDENT, ONES, BLK = K("IDENT"), K("ONES"), K("BLK")
        EPSc = CONST[:, C_EPS:C_EPS + 1]
        GNEPSc = CONST[:, C_GNEPS:C_GNEPS + 1]
        ONEc = CONST[:, C_ONE:C_ONE + 1]

        def pv(l, name, n=1, off=0):
            o = l * NPV + PVO[name] + off
            return PV[:, o:o + n]

        def row(l, name, n):
            o = l * NR + RO[name]
            return ROWS[:, o:o + n]

        def smat(l, j):
            o = l * 1024 + j * 256
            return SM[:, o:o + 256]

        PB = [S.ps([128, 512], "pb%d" % i) for i in range(8)]
        pbi = [0]

        def pb():
            t = PB[pbi[0] % 8]
            pbi[0] += 1
            return t

        WR = [S.sb([128, 4096], "wr%d" % i) for i in range(2)]
        wri = [0]
        WBF = Tile(None, "wbf_dep")

        def wr32(i):
            return V(WR[i], WR[i].h[:].rearrange("p (k c) -> p k c", c=512), None)

        def wr16(i, hf):
            return V(WR[i], WR[i].h[:].bitcast(BF16)[:, hf * 4096:(hf + 1) * 4096].rearrange("p (k c) -> p k c", c=512), hf)

        def wload(ap):
            i = wri[0] % 2
            wri[0] += 1
            t = wr32(i)
            S.dma("sp", t, ap)
            return t

        def wload16(l, sl):
            r = wri[0] % 4
            wri[0] += 1
            t = wr16(r // 2, r % 2)
            S.dma("sp", t, wbf[l, sl], xr=[V(WBF, None, (l, sl))])
            return t

        def mm(out, lhsT, rhs, start=True, stop=True):
            I("pe", "matmul", out=out, lhsT=lhsT, rhs=rhs, start=start, stop=stop, skip_group_check=True)

        def tr(out, in_, npart=128):
            I("pe", "transpose", out=out, in_=in_, identity=CONST[0:npart, 0:npart])

        def act(out, in_, func, bias=None, scale=1.0):
            if bias is None:
                I("act", "activation", out=out, in_=in_, func=func, scale=scale)
            else:
                I("act", "activation", out=out, in_=in_, func=func, bias=bias, scale=scale)

        def tt(out, a, b, op=ALU.mult, eng="dve"):
            I(eng, "tensor_tensor", out=out, in0=a, in1=b, op=op)

        def ts(out, a, s1, op0, s2=None, op1=None, eng="dve"):
            if op1 is None:
                I(eng, "tensor_scalar", out=out, in0=a, scalar1=s1, scalar2=None, op0=op0)
            else:
                I(eng, "tensor_scalar", out=out, in0=a, scalar1=s1, scalar2=s2, op0=op0, op1=op1)

        def stt(out, a, sc, b, op0, op1, eng="dve"):
            I(eng, "scalar_tensor_tensor", out=out, in0=a, scalar=sc, in1=b, op0=op0, op1=op1)

        def cp(out, in_, eng="dve"):
            if eng == "act":
                I("act", "activation", out=out, in_=in_, func=AF.Identity, scale=1.0)
            else:
                I("dve", "tensor_copy", out=out, in_=in_)

        G = [S.sb([128, 256], "g%d" % i) for i in range(28)]

        class GP:
            def __init__(self, pool=None):
                self.i = 0
                self.pool = G if pool is None else pool

            def t(self):
                t = self.pool[self.i]
                self.i += 1
                return t

        class Alias:
            def __init__(self, tile, base, sub):
                self.tile, self.base, self.sub = tile, base, sub

            def __getitem__(self, idx):
                return V(self.tile, self.base[idx], self.sub)

        MODT = S.sb([128, 2, 48, 17], "MODT")
        SC = S.sb([128, 2, 2, 8, 17], "SC")
        CT = S.sb([128, 8, 17], "CT")
        class Cx:
            pass
        CXS = []
        for i_ in range(2):
            c_ = Cx()
            c_.XTM = S.sb([128, 1024], "XTM%d" % i_)
            c_.XT = S.sb([128, 8, 128], "XT%d" % i_)
            c_.HT = S.sb([128, 8, 128], "HT%d" % i_, BF16)
            c_.YT = S.sb([128, 8, 128], "YT%d" % i_, BF16)
            CXS.append(c_)
        CUR = [CXS[0]]

        class Proxy:
            def __init__(self, name):
                self.name = name

            def __getitem__(self, idx):
                return getattr(CUR[0], self.name)[idx]
        XTM, XT, HT, YT = Proxy("XTM"), Proxy("XT"), Proxy("HT"), Proxy("YT")
        HTF = S.sb([128, 8, 128], "HTF")
        HID = S.sb([128, 32, 128], "HID", BF16)
        TMP5 = S.sb([128, 4, 128], "TMP5")
        UA = S.sb([128, 7, 144], "UA")
        XS = S.sb([128, 7, 128], "XS")
        XB = S.sb([128, 6, 176], "XB")
        CV = S.sb([128, 6, 128], "CV")
        PST = {}
        for l in range(NL):
            for mname in ("wkv", "gla", "dn", "ssm"):
                PST[(l, mname)] = S.sb([128, 2, 128], "pst_%s%d" % (mname, l))
                I("dve", "memset", PST[(l, mname)][:], 0.0)
        CARRY = {}
        for l in range(NL):
            CARRY[(l, "rw")] = S.sb([128, 7], "c_rw%d" % l)
            CARRY[(l, "dn")] = S.sb([128, 6, 3], "c_dn%d" % l)
            CARRY[(l, "ss")] = S.sb([128, 6, 3], "c_ss%d" % l)
            for k in ("rw", "dn", "ss"):
                I("dve", "memset", CARRY[(l, k)][:], 0.0)
        SST = [S.sb([128, 16, 2, 128], "sst%d" % i) for i in range(2)]
        for t in SST:
            I("dve", "memset", t[:], 0.0)
        G2 = [Alias(SST[j], SST[j].h[:, s_].rearrange("p r v -> p (r v)"), ("a", s_)) for j in range(2) for s_ in range(16)]
        PADS = [S.sb([128, 2, 128], "pad%d" % i) for i in range(3)]
        for t in PADS:
            I("dve", "memset", t[:], 0.0)

        Pk, Sk = BK(), BK()
        Pk.nseg, Pk.L, Pk.sfx, Pk.nsolve = 1, 128, "_P", 7
        Sk.nseg, Sk.L, Sk.sfx, Sk.nsolve = 16, 8, "_S", 3
        for bk in (Pk, Sk):
            bk.TRI, bk.TRS, bk.SU, bk.NEGM = K("TRI" + bk.sfx), K("TRS" + bk.sfx), K("SU" + bk.sfx), K("NEGM" + bk.sfx)

        def bc(v, shape, axis):
            return v.f(lambda a: a.unsqueeze(axis).to_broadcast(shape))

        def modv(v3, bk, nch=8):
            if bk.nseg == 1:
                return bc(v3[:, :, 0:1], [128, nch, 1, 128], 3)
            return bc(v3[:, :, 1:17], [128, nch, 16, 8], 3)

        def dv(v, bk):
            return v.f(lambda a: a.rearrange("p c (s t) -> p c s t", t=bk.L))

        ctm = XTM
        S.dma("pool", ctm[0:17, :], ccd)
        act(ctm[0:17, :], ctm[0:17, :], AF.Silu)
        for kc in range(8):
            p = pb()
            tr(p[:, 0:17], ctm[0:17, kc * 128:(kc + 1) * 128], 17)
            cp(CT[:, kc, :], p[:, 0:17])
        for l in range(NL):
            for s in range(12):
                w = wload(adad[l, s])
                p = pb()
                for j in range(4):
                    for kc in range(8):
                        mm(p[:, j * 17:(j + 1) * 17], w[:, kc, j * 128:(j + 1) * 128], CT[:, kc, :], kc == 0, kc == 7)
                tt(MODT[:, l, 4 * s:4 * s + 4, :], p[:, 0:68].f(lambda a: a.rearrange("p (j n) -> p j n", n=17)),
                   bc(pv(l, "ada_b", 4, 4 * s), [128, 4, 17], 2), ALU.add)
            stt(SC[:, l, 0], MODT[:, l, 8:16, :], 1.0, bc(pv(l, "n1w", 8), [128, 8, 17], 2), ALU.add, ALU.mult)
            stt(SC[:, l, 1], MODT[:, l, 32:40, :], 1.0, bc(pv(l, "n2w", 8), [128, 8, 17], 2), ALU.add, ALU.mult)

        STG = [wr32(0)] + [V(SST[j], SST[j].h[:].rearrange("p s r v -> p (s r v)").rearrange("p (k c) -> p k c", c=512), None)
                           for j in range(2)]
        for l in range(NL):
            for sl in range(27):
                i = (l * 27 + sl)
                src = STG[i % 3]
                S.dma("sp", src, wall[l, sl])
                dst = wr16(1, i % 2)
                I("dve", "tensor_copy", out=dst[:, 0:4, :], in_=src[:, 0:4, :])
                I("act", "activation", out=dst[:, 4:8, :], in_=src[:, 4:8, :], func=AF.Identity, scale=1.0)
                S.dma("sp", wbf[l, sl], dst, xw=[V(WBF, None, (l, sl))])

        RS = S.sb([128, 128], "RS")

        def norm_mod(bk, scale_bv, shift_bv, outT=None):
            outT = HT if outT is None else outT
            HIDF = V(HID, HID.h[:].rearrange("p c t -> p (c t)").bitcast(F32), None)
            p = pb()
            for kc in range(8):
                sq = HIDF[:, kc * 128:(kc + 1) * 128]
                act(sq, XT[:, kc, :], AF.Square)
                mm(p[:, 0:128], ONES, sq, kc == 0, kc == 7)
            act(RS[:], p[:, 0:128], AF.Ln, bias=EPSc, scale=1.0 / 1024.0)
            act(RS[:], RS[:], AF.Exp, scale=-0.5)
            tt(HTF[:], XT[:], bc(RS[:], [128, 8, 128], 1))
            if shift_bv is not None:
                tt(dv(HTF[:], bk), dv(HTF[:], bk), scale_bv)
                tt(dv(outT[:], bk), dv(HTF[:], bk), shift_bv, ALU.add)
            else:
                tt(dv(outT[:], bk), dv(HTF[:], bk), scale_bv)

        def proj_fm(out, w, c0, M):
            for kc in range(8):
                mm(out, w[:, kc, c0:c0 + M], HT[:, kc, :], kc == 0, kc == 7)

        def proj_tm(out, w, c0, N):
            for kc in range(8):
                mm(out, HT[:, kc, :], w[:, kc, c0:c0 + N], kc == 0, kc == 7)

        def sv(v, bk):
            return v.f(lambda a: a.rearrange("p (s t) -> p s t", t=bk.L))

        def lastcol(v, bk, s):
            c = s * bk.L + bk.L - 1
            return v[:, c:c + 1]

        def tri_solve(bk, NTs, Ns, X, g):
            tmpN = [[g.t(), g.t()] for _ in range(2)]
            for k in range(bk.nsolve):
                p = pb()
                for h in range(2):
                    mm(p[:, h * 64:h * 64 + 64], NTs[h][:, 0:128], X[:, h * 64:h * 64 + 64])
                tt(X[:, 0:128], X[:, 0:128], p[:, 0:128], ALU.add)
                yield
                if k == bk.nsolve - 1:
                    break
                for h in range(2):
                    p2 = pb()
                    mm(p2[:, 0:128], Ns[h][:, 0:128], NTs[h][:, 0:128])
                    if k < bk.nsolve - 2:
                        mm(p2[:, 128:256], NTs[h][:, 0:128], Ns[h][:, 0:128])
                    nn = tmpN[h][k % 2]
                    if k < bk.nsolve - 2:
                        cp(nn[:, 0:256], p2[:, 0:256], "act")
                    else:
                        cp(nn[:, 0:128], p2[:, 0:128], "act")
                    NTs[h] = nn
                    Ns[h] = _Shift(nn)
                    yield
            return

        class _Shift:
            def __init__(self, base):
                self.base = base

            def __getitem__(self, idx):
                assert idx == (slice(None), slice(0, 128))
                return self.base[:, 128:256]

        def post_rms(Y, ones, gsize, nw_col, gateT, out, g):
            sq = g.t()
            act(sq[:, 0:128], Y, AF.Square)
            p = pb()
            mm(p[:, 0:128], ones, sq[:, 0:128])
            r = g.t()
            act(r[:, 0:128], p[:, 0:128], AF.Ln, bias=EPSc, scale=1.0 / gsize)
            act(r[:, 0:128], r[:, 0:128], AF.Exp, scale=-0.5)
            stt(r[:, 128:256], Y, nw_col, r[:, 0:128], ALU.mult, ALU.mult)
            if gateT is not None:
                tt(out, r[:, 128:256], gateT)
            else:
                cp(out, r[:, 128:256])

        def load_bd(sst, src):
            for h in range(4):
                b = 64 * (h % 2)
                S.dma("pool", sst[b:b + 64, :, h // 2, b:b + 64], src[:, h].rearrange("s d v -> d s v"))

        def store_bd(dst, tile_bd, pr):
            for hh in range(2):
                b = 64 * hh
                S.dma("pool", dst[2 * pr + hh], tile_bd[b:b + 64, b:b + 64], is_out=True)

        OST = [S.sb([128, 128], "ost%d" % i) for i in range(4)]
        osti = [0]

        def ost():
            t = OST[osti[0] % 4]
            osti[0] += 1
            return t

        def state_update(bk, l, mname, pr, lhs, rhs, pcv, sst, outd, last, g, transpose_out=False):
            for s in range(bk.nseg):
                p = pb()
                for k in range(len(lhs)):
                    lv = lhs[k]
                    if bk.nseg > 1:
                        m = g_rot()
                        I("act", "mul", out=m[:, 0:128], in_=lv, mul=CONST[:, C_SEG + s:C_SEG + s + 1])
                        lv = m[:, 0:128]
                    mm(p[:, 0:128], lv, rhs[k], k == 0, k == len(lhs) - 1)
                tmp = g_rot()
                tt(tmp[:, 0:128], p[:, 0:128], BLK)
                if bk.nseg == 1:
                    hp = PST[(l, mname)][:, pr, :]
                    stt(hp, hp, lastcol(pcv, bk, 0), tmp[:, 0:128], ALU.mult, ALU.add)
                    if last:
                        if transpose_out:
                            p2 = pb()
                            tr(p2[:, 0:128], hp)
                            o = ost()
                            cp(o[:], p2[:, 0:128])
                            store_bd(outd[l], o, pr)
                        else:
                            store_bd(outd[l], PST[(l, mname)][:, pr, :], pr)
                else:
                    hp = sst[:, s, pr, :]
                    if transpose_out:
                        o = ost()
                        stt(o[:], hp, lastcol(pcv, bk, s), tmp[:, 0:128], ALU.mult, ALU.add)
                        p2 = pb()
                        tr(p2[:, 0:128], o[:])
                        cp(hp, p2[:, 0:128])
                    else:
                        stt(hp, hp, lastcol(pcv, bk, s), tmp[:, 0:128], ALU.mult, ALU.add)
                    if s == bk.nseg - 1:
                        for hh in range(2):
                            b = 64 * hh
                            S.dma("pool", outd[l][:, 2 * pr + hh].rearrange("s d v -> d s v"), sst[b:b + 64, :, pr, b:b + 64],
                                  is_out=True)

        GR = [S.sb([128, 128], "gr%d" % i) for i in range(6)]
        gri = [0]

        def g_rot():
            t = GR[gri[0] % 6]
            gri[0] += 1
            return t

        def inter(bk, l, mname, pr, sst, out_ps, opT, stop):
            for s in range(bk.nseg):
                hp = PST[(l, mname)][:, pr, :] if bk.nseg == 1 else sst[:, s, pr, :]
                mm(out_ps[:, s * bk.L:(s + 1) * bk.L], hp, opT[:, s * bk.L:(s + 1) * bk.L], s == 0, stop)

        def decay_mask(bk, la_col, out, g):
            t = g.t()
            ts(t[:, 0:128], bk.SU, la_col, ALU.mult)
            p = pb()
            mm(p[:, 0:128], t[:, 0:128], bk.TRI, True, False)
            mm(p[:, 0:128], IDENT, bk.NEGM, False, True)
            act(out, p[:, 0:128], AF.Exp)

        def bc2(v, shape):
            return v.f(lambda a: a.unsqueeze(2).unsqueeze(3).to_broadcast(shape))

        GLT = S.sb([16, 128], "GLT")
        SHT = S.sb([48, 896], "SHT")
        CST = SHT
        CSO = S.sb([48, 768], "CSO")
        CS3 = S.sb([128, 48], "CS3")
        SMT = S.sb([128, 64], "SMT")
        ZT = S.sb([128, 512], "ZT")
        TMP4 = S.sb([128, 4, 128], "TMP4")
        YTM = XTM

        def dump(k, tile3):
            if not KDBG:
                return
            for b2 in range(2):
                p = pb()
                for j in range(4):
                    tr(p[:, j * 128:(j + 1) * 128], tile3[:, 4 * b2 + j, :])
                cp(TMP4[:].f(lambda a: a.rearrange("p c t -> p (c t)")), p[:, 0:512])
                S.dma("pool", dbgd[k, :, b2 * 512:(b2 + 1) * 512], TMP4[:].f(lambda a: a.rearrange("p c t -> p (c t)")), is_out=True)
        SSL = [S.sb([64, 4, 128], "ssl%d" % i) for i in range(1)]

        def rwkv(bk, l, last, wget, sst, pool=None):
            g = GP(pool)
            w0 = wget(0)
            w1 = wget(1)
            nseg, L = bk.nseg, bk.L
            W = L + 1
            sfx = bk.sfx
            UAv = UA[:, :, 0:nseg * W].f(lambda a: a.rearrange("p c (s t) -> p c s t", t=W))
            if nseg == 1:
                cp(UA[:, :, 0:1], CARRY[(l, "rw")][:].f(lambda a: a.unsqueeze(2)))
            else:
                S.dma("pool", SHT[0:16, :], std["shift"][l])
                for c in range(7):
                    p = pb()
                    tr(p[:, 0:16], SHT[0:16, c * 128:(c + 1) * 128], 16)
                    cp(UAv[:, c, :, 0], p[:, 0:16])
            for c in range(7):
                w = w0 if c < 4 else w1
                p = pb()
                proj_fm(p[:, 0:128], w, (c % 4) * 128, 128)
                cp(UAv[:, c, :, 1:W], sv(p[:, 0:128], bk), "act" if c % 2 else "dve")
            p = pb()
            proj_fm(p[0:16, 0:128], w1, 384, 16)
            cp(GLT[0:16, :], p[0:16, 0:128])
            if nseg == 1:
                cp(CARRY[(l, "rw")][:].f(lambda a: a.unsqueeze(2)), UA[:, :, 128:129])
                if last:
                    S.dma("pool", pod["shift"][l].rearrange("(c p) -> p c", p=128), CARRY[(l, "rw")][:], is_out=True,
                          allow_slow_non_contiguous=True)
            else:
                for c in range(7):
                    S.dma("pool", sod["shift"][l][:, c * 128:(c + 1) * 128].rearrange("s p -> p s"), UAv[:, c, :, L],
                          is_out=True, allow_slow_non_contiguous=True)
            ck('rw1')
            XSv = dv(XS[:], bk)
            tt(XSv, UAv[:, :, :, 0:L], UAv[:, :, :, 1:W], ALU.subtract)
            tt(XSv, XSv, bc2(pv(l, "mu", 7), [128, 7, nseg, L]))
            tt(XSv, XSv, UAv[:, :, :, 1:W], ALU.add)
            ck('rw2')
            X6 = XS[:, 6, :]
            T6 = g.t()
            act(T6[:, 0:128], X6, AF.Tanh)
            act(T6[:, 128:256], X6, AF.Sigmoid)
            p = pb()
            mm(p[:, 0:256], T6[:, 0:128], smat(l, 0))
            LAM = g.t()
            tt(LAM[:], p[:, 0:256], row(l, "w0", 256), ALU.add)
            act(LAM[:], LAM[:], AF.Sigmoid)
            AT, GTt = g.t(), g.t()
            for pr in range(2):
                p = pb()
                mm(p[:, 0:128], smat(l, 1)[:, pr * 128:(pr + 1) * 128], X6)
                act(AT[:, pr * 128:(pr + 1) * 128], p[:, 0:128], AF.Sigmoid, bias=pv(l, "a0", 1, pr))
                mm(p[:, 128:256], smat(l, 2)[:, pr * 128:(pr + 1) * 128], T6[:, 128:256])
                cp(GTt[:, pr * 128:(pr + 1) * 128], p[:, 128:256])
            ck('rw3')
            yield
            KK, KP, E, E2, EQT, KC, K2C2, VT, KC2, RT, X, YS, Mt, Bt = [g.t() for _ in range(14)]
            AE = [g.t(), g.t()]
            AC = [g.t(), g.t()]
            Nn = [g.t(), g.t()]
            gsave = g.i
            MASK2 = CONST[:, CI["TRS" + sfx]:CI["TRS" + sfx] + 256]
            TRI3 = CONST[:, CI["TRS" + sfx]:CI["TRS" + sfx] + 384]
            Vp, Up = PADS[0], PADS[1]
            for pr in range(2):
                g.i = gsave
                rT, kT, vT = XS[:, pr, :], XS[:, 2 + pr, :], XS[:, 4 + pr, :]
                aT = AT[:, pr * 128:(pr + 1) * 128]
                ts(KK[:, 0:128], kT, pv(l, "k_k", 1, pr), ALU.mult)
                act(KK[:, 128:256], KK[:, 0:128], AF.Square)
                p = pb()
                mm(p[:, 0:128], BLK, KK[:, 128:256])
                act(KK[:, 128:256], p[:, 0:128], AF.Ln, bias=EPSc)
                act(KK[:, 128:256], KK[:, 128:256], AF.Exp, scale=-0.5)
                tt(KK[:, 0:128], KK[:, 0:128], KK[:, 128:256])
                ts(KP[:, 0:128], aT, -1.0, ALU.add, pv(l, "k_a", 1, pr), ALU.mult)
                stt(KP[:, 0:128], KP[:, 0:128], 1.0, kT, ALU.add, ALU.mult)
                tt(KP[:, 128:256], KK[:, 0:128], aT)
                stt(RT[:, 128:256], rT, pv(l, "r_k", 1, pr), KP[:, 0:128], ALU.mult, ALU.mult)
                p3 = pb()
                mm(p3[:, 0:128], BLK, RT[:, 128:256])
                tt(Bt[:, 128:256], p3[:, 0:128], vT)
                p = pb()
                mm(p[:, 0:384], LAM[:, pr * 128:(pr + 1) * 128], TRI3)
                act(E[:, 0:256], p[:, 0:256], AF.Exp, scale=-C_DEC)
                act(E2[:, 0:128], p[:, 256:384], AF.Exp, scale=-C_DEC)
                act(E2[:, 128:256], p[:, 128:256], AF.Exp, scale=C_DEC)
                tt(EQT[:, 0:128], KK[:, 0:128], E[:, 0:128])
                tt(EQT[:, 128:256], rT, E[:, 128:256])
                tt(KC[:, 0:128], KP[:, 0:128], E2[:, 128:256])
                tt(KC[:, 128:256], KP[:, 128:256], E2[:, 128:256])
                tt(K2C2[:, 0:128], KP[:, 0:128], E2[:, 0:128])
                stt(K2C2[:, 128:256], KP[:, 128:256], -1.0, E2[:, 0:128], ALU.mult, ALU.mult)
                ck('rw4')
                yield
                p = pb()
                tr(p[:, 0:128], vT)
                tr(p[:, 128:256], K2C2[:, 0:128])
                tr(p[:, 256:384], K2C2[:, 128:256])
                cp(VT[:, 0:128], p[:, 0:128])
                ck('rw4a1')
                cp(Vp[:, 0, 0:64], p[:, 0:64])
                cp(Vp[:, 1, 64:128], p[:, 64:128])
                ck('rw4a2')
                cp(KC2[:, 0:256], p[:, 128:384])
                ck('rw4b')
                yield
                for hh in range(2):
                    b = 64 * hh
                    if hh == 1:
                        ck('rw4c')
                    p = pb()
                    mm(p[:, 0:256], KC[b:b + 64, 0:128], EQT[b:b + 64, 0:256])
                    mm(p[:, 256:512], KC[b:b + 64, 128:256], EQT[b:b + 64, 0:256])
                    tt(AE[hh][:, 0:256], p[:, 0:256], MASK2)
                    stt(AC[hh][:, 0:256], p[:, 256:512], -1.0, MASK2, ALU.mult, ALU.mult)
                    p2 = pb()
                    tr(p2[:, 0:128], AC[hh][:, 0:128])
                    cp(Nn[hh][:, 0:128], p2[:, 0:128], "act")
                    yield
                ck('rw5')
                if nseg == 1:
                    p = pb()
                    mm(p[:, 0:128], EQT[:, 0:128], PST[(l, "wkv")][:, pr, :], True, False)
                    for hh in range(2):
                        mm(p[:, 0:128], AE[hh][:, 0:128], Vp[:, hh, :], False, hh == 1)
                    cp(X[:, 0:128], p[:, 0:128])
                    yield
                else:
                    p = pb()
                    inter(bk, l, "wkv", pr, sst, p, EQT[:, 0:128], False)
                    for hh in range(2):
                        mm(p[:, 0:128], Vp[:, hh, :], AE[hh][:, 0:128], False, hh == 1)
                    cp(RT[:, 0:128], p[:, 0:128])
                    yield
                    p = pb()
                    tr(p[:, 0:128], RT[:, 0:128])
                    cp(X[:, 0:128], p[:, 0:128])
                    yield
                ck('rw6')
                yield from tri_solve(bk, [AC[0], AC[1]], [Nn[0], Nn[1]], X, g)
                ck('rw7')
                cp(Up[:, 0, 0:64], X[:, 0:64])
                cp(Up[:, 1, 64:128], X[:, 64:128])
                p = pb()
                inter(bk, l, "wkv", pr, sst, p, EQT[:, 128:256], False)
                for hh in range(2):
                    mm(p[:, 0:128], Vp[:, hh, :], AE[hh][:, 128:256], False, False)
                    mm(p[:, 0:128], Up[:, hh, :], AC[hh][:, 128:256], False, hh == 1)
                ck('rw8')
                cp(YS[:, 0:128], p[:, 0:128])
                act(YS[:, 128:256], p[:, 0:128], AF.Square)
                yield
                p2 = pb()
                mm(p2[:, 0:256], BLK, YS[:, 0:256])
                ts(Mt[:, 0:256], p2[:, 0:256], 1.0 / 64.0, ALU.mult)
                stt(Bt[:, 0:128], Mt[:, 0:128], -1.0, Mt[:, 0:128], ALU.mult, ALU.mult)
                tt(Mt[:, 128:256], Mt[:, 128:256], Bt[:, 0:128], ALU.add)
                act(Mt[:, 128:256], Mt[:, 128:256], AF.Ln, bias=GNEPSc)
                act(Mt[:, 128:256], Mt[:, 128:256], AF.Exp, scale=-0.5)
                tt(YS[:, 0:128], YS[:, 0:128], Mt[:, 0:128], ALU.subtract)
                tt(YS[:, 0:128], YS[:, 0:128], Mt[:, 128:256])
                ts(YS[:, 0:128], YS[:, 0:128], pv(l, "ln_w", 1, pr), ALU.mult, pv(l, "ln_b", 1, pr), ALU.add)
                tt(YS[:, 0:128], YS[:, 0:128], Bt[:, 128:256], ALU.add)
                tt(YT[:, pr, :], YS[:, 0:128], GTt[:, pr * 128:(pr + 1) * 128])
                yield
                ck('rw9')
                state_update(bk, l, "wkv", pr, [KC2[:, 0:128], KC2[:, 128:256]], [VT[:, 0:128], X[:, 0:128]],
                             E[:, 128:256], sst, (pod if nseg == 1 else sod)["wkv"], last, g, transpose_out=True)

        def gla(bk, l, last, wget, sst, pool=None):
            g = GP(pool)
            w2 = wget(2)
            QK = [g.t(), g.t()]
            for c in range(4):
                p = pb()
                proj_fm(p[:, 0:128], w2, c * 128, 128)
                cp(QK[c % 2][:, (c // 2) * 128:(c // 2) * 128 + 128], p[:, 0:128], "act" if c % 2 else "dve")
            KTM = g.t()
            p = pb()
            proj_tm(p[:, 0:256], w2, 256, 256)
            cp(KTM[:], p[:, 0:256], "act")
            w3 = wget(3)
            VTM = g.t()
            p = pb()
            proj_tm(p[:, 0:256], w3, 0, 256)
            cp(VTM[:], p[:, 0:256])
            Vp = [PADS[0], PADS[2]]
            for pr in range(2):
                cp(Vp[pr][:, 0, 0:64], p[:, pr * 128:pr * 128 + 64])
                cp(Vp[pr][:, 1, 64:128], p[:, pr * 128 + 64:pr * 128 + 128])
            GT = g.t()
            for pr in range(2):
                p = pb()
                proj_fm(p[:, 0:128], w3, 256 + pr * 128, 128)
                act(GT[:, pr * 128:(pr + 1) * 128], p[:, 0:128], AF.Silu)
            p = pb()
            mm(p[:, 0:256], GLT[0:16, :], smat(l, 3)[0:16, :])
            LA = g.t()
            tt(LA[:], p[:, 0:256], row(l, "gkb", 256), ALU.add)
            act(LA[:], LA[:], AF.Exp, scale=-1.0)
            act(LA[:], LA[:], AF.Ln, bias=ONEc)
            p = pb()
            mm(p[:, 0:256], bk.SU, LA[:])
            K2 = g.t()
            act(K2[:], p[:, 0:256], AF.Exp, scale=-1.0 / 16.0)
            tt(K2[:], K2[:], KTM[:])
            yield
            E, QKh, YS = g.t(), g.t(), g.t()
            A = [g.t(), g.t()]
            gsave = g.i
            for pr in range(2):
                g.i = gsave
                p = pb()
                mm(p[:, 0:128], LA[:, pr * 128:(pr + 1) * 128], bk.TRI)
                act(E[:, 0:128], p[:, 0:128], AF.Exp, scale=-1.0 / 16.0)
                act(E[:, 128:256], p[:, 0:128], AF.Exp, scale=1.0 / 16.0)
                stt(QKh[:, 0:128], QK[pr][:, 0:128], 0.125, E[:, 0:128], ALU.mult, ALU.mult)
                tt(QKh[:, 128:256], QK[pr][:, 128:256], E[:, 128:256])
                for hh in range(2):
                    b = 64 * hh
                    p = pb()
                    mm(p[:, 0:128], QKh[b:b + 64, 128:256], QKh[b:b + 64, 0:128])
                    tt(A[hh][:, 0:128], p[:, 0:128], bk.TRI)
                    yield
                p = pb()
                inter(bk, l, "gla", pr, sst, p, QKh[:, 0:128], False)
                for hh in range(2):
                    mm(p[:, 0:128], Vp[pr][:, hh, :], A[hh][:, 0:128], False, hh == 1)
                cp(YS[:, 0:128], p[:, 0:128])
                yield
                post_rms(YS[:, 0:128], BLK, 64.0, pv(l, "gla_nw", 1, pr), GT[:, pr * 128:(pr + 1) * 128], YT[:, 2 + pr, :], g)
                yield
                state_update(bk, l, "gla", pr, [K2[:, pr * 128:(pr + 1) * 128]], [VTM[:, pr * 128:(pr + 1) * 128]],
                             E[:, 0:128], sst, (pod if bk.nseg == 1 else sod)["gla"], last, g)
                yield

        def conv_in(bk, l, ckey, skey):
            nseg, L = bk.nseg, bk.L
            W = L + 3
            XBv = XB[:, :, 0:nseg * W].f(lambda a: a.rearrange("p c (s t) -> p c s t", t=W))
            if nseg == 1:
                cp(XBv[:, :, 0, 0:3], CARRY[(l, ckey)][:])
            else:
                S.dma("pool", CST[0:48, 0:768], std[skey][l].rearrange("s i c -> (s i) c"))
                for c in range(6):
                    p = pb()
                    tr(p[:, 0:48], CST[0:48, c * 128:(c + 1) * 128], 48)
                    cp(XBv[:, c, :, 0:3], p[:, 0:48].f(lambda a: a.rearrange("p (s i) -> p s i", i=3)))
            return XBv

        def conv_run(bk, l, ckey, skey, cwname, XBv, last):
            nseg, L = bk.nseg, bk.L
            W = L + 3
            if nseg == 1:
                cp(CARRY[(l, ckey)][:], XBv[:, :, 0, L:L + 3])
                if last:
                    for c in range(6):
                        S.dma("pool", pod[skey][l][:, c * 128:(c + 1) * 128].rearrange("i p -> p i"), CARRY[(l, ckey)][:, c, :],
                              is_out=True, allow_slow_non_contiguous=True)
            else:
                for c in range(6):
                    cp(CS3[:].f(lambda a: a.rearrange("p (s i) -> p s i", i=3)), XBv[:, c, :, L:L + 3])
                    p = pb()
                    tr(p[0:48, 0:128], CS3[:])
                    cp(CSO[0:48, c * 128:(c + 1) * 128], p[0:48, 0:128], "act")
                S.dma("pool", sod[skey][l].rearrange("s i c -> (s i) c"), CSO[:], is_out=True)
            CVv = dv(CV[:], bk)
            TMv = dv(HTF[:, 0:6, :], bk)
            for i in range(4):
                wv = bc2(pv(l, cwname, 6, i * 6), [128, 6, nseg, L])
                if i == 0:
                    tt(CVv, XBv[:, :, :, 0:L], wv)
                else:
                    tt(TMv, XBv[:, :, :, i:i + L], wv)
                    tt(CVv, CVv, TMv, ALU.add)

        def softplus_cols(dst, src, biasrow):
            tt(dst, src, biasrow, ALU.add)
            act(dst, dst, AF.Exp)
            act(dst, dst, AF.Ln, bias=ONEc)

        def dnet(bk, l, last, wget, sst, pool=None):
            g = GP(pool)
            nseg, L = bk.nseg, bk.L
            W = L + 3
            XBv = conv_in(bk, l, "dn", "dnc")
            w4 = wget(4)
            for c in range(4):
                p = pb()
                proj_fm(p[:, 0:128], w4, c * 128, 128)
                cp(XBv[:, c, :, 3:W], sv(p[:, 0:128], bk), "act" if c % 2 else "dve")
            w5 = wget(5)
            for c in range(2):
                p = pb()
                proj_fm(p[:, 0:128], w5, c * 128, 128)
                cp(XBv[:, 4 + c, :, 3:W], sv(p[:, 0:128], bk), "act" if c % 2 else "dve")
            for c in range(2):
                p = pb()
                proj_fm(p[:, 0:128], w5, 256 + c * 128, 128)
                act(ZT[:, c * 128:(c + 1) * 128], p[:, 0:128], AF.Silu)
            w6 = wget(6)
            p = pb()
            proj_tm(p[:, 0:12], w6, 0, 12)
            cp(SMT[:, 0:12], p[:, 0:12])
            for c in range(2):
                p = pb()
                proj_fm(p[:, 0:128], w6, 12 + c * 128, 128)
                act(ZT[:, 256 + c * 128:256 + (c + 1) * 128], p[:, 0:128], AF.Silu)
            conv_run(bk, l, "dn", "dnc", "dn_cw", XBv, last)
            act(CV[:], CV[:], AF.Silu)
            act(SMT[:, 16:20], SMT[:, 4:8], AF.Sigmoid)
            softplus_cols(SMT[:, 20:24], SMT[:, 0:4], row(l, "dndt", 4))
            act(SMT[:, 24:28], row(l, "dnA", 4), AF.Exp)
            stt(SMT[:, 28:32], SMT[:, 20:24], -1.0, SMT[:, 24:28], ALU.mult, ALU.mult)
            p = pb()
            mm(p[:, 0:4], bk.SU, SMT[:, 28:32])
            act(SMT[:, 32:36], p[:, 0:4], AF.Exp)
            yield
            SQ, LB, ELb, R, X, QE, YS, K2, SQ2 = [g.t() for _ in range(9)]
            KQ = [g.t(), g.t()]
            ET = [g.t(), g.t()]
            A = [g.t(), g.t()]
            T0 = [g.t(), g.t()]
            Nn = [g.t(), g.t()]
            NT = [g.t(), g.t()]
            gsave = g.i
            Wp = PADS[2]
            for pr in range(2):
                g.i = gsave
                for which, (src, dst, scl) in enumerate(((CV[:, 2 + pr, :], KQ[pr][:, 0:128], 1.0),
                                                         (CV[:, pr, :], KQ[pr][:, 128:256], 0.125))):
                    sqt = SQ2 if which else SQ
                    act(sqt[:, 0:128], src, AF.Square)
                    p = pb()
                    mm(p[:, 0:128], BLK, sqt[:, 0:128])
                    act(sqt[:, 128:256], p[:, 0:128], AF.Ln, bias=EPSc)
                    act(sqt[:, 128:256], sqt[:, 128:256], AF.Exp, scale=-0.5)
                    stt(dst, src, scl, sqt[:, 128:256], ALU.mult, ALU.mult)
                tt(LB[:, 0:128].f(lambda a: a.rearrange("p (h d) -> p h d", d=64)),
                   ONES.f(lambda a: a.rearrange("p (h d) -> p h d", d=64)),
                   bc(SMT[:, 28 + 2 * pr:30 + 2 * pr], [128, 2, 64], 2))
                p = pb()
                mm(p[:, 0:128], LB[:, 0:128], bk.TRI)
                act(ELb[:, 0:128], p[:, 0:128], AF.Exp)
                yield
                p = pb()
                inter(bk, l, "dn", pr, sst, p, KQ[pr][:, 0:128], True)
                tt(R[:, 0:128], p[:, 0:128], ELb[:, 0:128])
                tt(R[:, 0:128], CV[:, 4 + pr, :], R[:, 0:128], ALU.subtract)
                yield
                p = pb()
                tr(p[:, 0:128], R[:, 0:128])
                for hh in range(2):
                    h = 2 * pr + hh
                    ts(X[:, hh * 64:hh * 64 + 64], p[:, hh * 64:hh * 64 + 64], SMT[:, 16 + h:17 + h], ALU.mult)
                yield
                tt(QE[:, 0:128], KQ[pr][:, 128:256], ELb[:, 0:128])
                p = pb()
                tr(p[:, 0:128], KQ[pr][:, 0:128])
                for hh in range(2):
                    h = 2 * pr + hh
                    ts(K2[:, hh * 64:hh * 64 + 64], p[:, hh * 64:hh * 64 + 64], SMT[:, 32 + h:33 + h], ALU.mult)
                yield
                for hh in range(2):
                    h = 2 * pr + hh
                    b = 64 * hh
                    decay_mask(bk, SMT[:, 28 + h:29 + h], ET[hh][:, 0:128], g)
                    tt(ET[hh][:, 128:256], ET[hh][:, 0:128], bk.TRS)
                    p = pb()
                    mm(p[:, 0:256], KQ[pr][b:b + 64, 0:128], KQ[pr][b:b + 64, 0:256])
                    tt(A[hh][:, 0:128], p[:, 128:256], ET[hh][:, 0:128])
                    tt(T0[hh][:, 0:128], p[:, 0:128], ET[hh][:, 128:256])
                    p2 = pb()
                    tr(p2[:, 0:128], T0[hh][:, 0:128])
                    ts(Nn[hh][:, 0:128], p2[:, 0:128], SMT[:, 16 + h:17 + h], ALU.mult, -1.0, ALU.mult)
                    p3 = pb()
                    tr(p3[:, 0:128], Nn[hh][:, 0:128])
                    cp(NT[hh][:, 0:128], p3[:, 0:128], "act")
                    g.i -= 1
                    yield
                yield from tri_solve(bk, [NT[0], NT[1]], [Nn[0], Nn[1]], X, g)
                cp(Wp[:, 0, 0:64], X[:, 0:64])
                cp(Wp[:, 1, 64:128], X[:, 64:128])
                p = pb()
                inter(bk, l, "dn", pr, sst, p, QE[:, 0:128], False)
                for hh in range(2):
                    mm(p[:, 0:128], Wp[:, hh, :], A[hh][:, 0:128], False, hh == 1)
                cp(YS[:, 0:128], p[:, 0:128])
                yield
                post_rms(YS[:, 0:128], BLK, 64.0, pv(l, "dn_nw", 1, pr), ZT[:, pr * 128:(pr + 1) * 128], YT[:, 4 + pr, :], g)
                yield
                state_update(bk, l, "dn", pr, [K2[:, 0:128]], [X[:, 0:128]], ELb[:, 0:128], sst,
                             (pod if nseg == 1 else sod)["dn"], last, g)
                yield

        def ssd(bk, l, last, wget, sst, pool=None):
            g = GP(pool)
            nseg, L = bk.nseg, bk.L
            W = L + 3
            XBv = conv_in(bk, l, "ss", "ssc")
            w7 = wget(7)
            for c in range(4):
                p = pb()
                proj_fm(p[:, 0:128], w7, c * 128, 128)
                cp(XBv[:, c, :, 3:W], sv(p[:, 0:128], bk), "act" if c % 2 else "dve")
            w8 = wget(8)
            for c in range(2):
                p = pb()
                proj_fm(p[:, 0:128], w8, c * 128, 128)
                cp(XBv[:, 4 + c, :, 3:W], sv(p[:, 0:128], bk), "act" if c % 2 else "dve")
            conv_run(bk, l, "ss", "ssc", "ss_cw", XBv, last)
            for c in range(6):
                act(CV[:, c, :], CV[:, c, :], AF.Silu, bias=pv(l, "ss_cb", 1, c))
            softplus_cols(SMT[:, 40:44], SMT[:, 8:12], row(l, "ssdt", 4))
            act(SMT[:, 44:48], row(l, "ssA", 4), AF.Exp)
            stt(SMT[:, 48:52], SMT[:, 40:44], -1.0, SMT[:, 44:48], ALU.mult, ALU.mult)
            p = pb()
            mm(p[:, 0:4], bk.SU, SMT[:, 48:52])
            act(SMT[:, 52:56], p[:, 0:4], AF.Exp)
            yield
            BTM, X2, YS, LB = [g.t() for _ in range(4)]
            ET = [g.t(), g.t()]
            A = [g.t(), g.t()]
            EL = [g.t(), g.t()]
            CH = [g.t(), g.t()]
            gsave = g.i
            Xp = PADS[1]
            outd = (pod if nseg == 1 else sod)["ssm"]
            for pr in range(2):
                g.i = gsave
                p = pb()
                tr(p[:, 0:128], CV[:, 2 + pr, :])
                tr(p[:, 128:256], CV[:, pr, :])
                cp(BTM[:, 0:128], p[:, 0:128])
                for hh in range(2):
                    h = 2 * pr + hh
                    ts(Xp[:, hh, hh * 64:hh * 64 + 64], p[:, 128 + hh * 64:128 + hh * 64 + 64], SMT[:, 40 + h:41 + h], ALU.mult)
                    ts(X2[:, hh * 64:hh * 64 + 64], Xp[:, hh, hh * 64:hh * 64 + 64], SMT[:, 52 + h:53 + h], ALU.mult)
                pG = pb()
                mm(pG[:, 0:128], CV[:, 2 + pr, :], CV[:, 4 + pr, :])
                for hh in range(2):
                    h = 2 * pr + hh
                    decay_mask(bk, SMT[:, 48 + h:49 + h], ET[hh][:, 0:128], g)
                    g.i -= 1
                    tt(A[hh][:, 0:128], pG[:, 0:128], ET[hh][:, 0:128])
                    ts(LB[:, 0:128], ONES, SMT[:, 48 + h:49 + h], ALU.mult)
                    p = pb()
                    mm(p[:, 0:128], LB[:, 0:128], bk.TRI)
                    act(EL[hh][:, 0:128], p[:, 0:128], AF.Exp)
                    tt(CH[hh][:, 0:128], CV[:, 4 + pr, :], EL[hh][:, 0:128])
                yield
                pY = pb()
                for hh in range(2):
                    reg = pY[:, hh * 128:(hh + 1) * 128]
                    for s in range(nseg):
                        hg = PST[(l, "ssm")][:, pr, :] if nseg == 1 else sst[:, s, pr, :]
                        mm(reg[:, s * L:(s + 1) * L], hg, CH[hh][:, s * L:(s + 1) * L], s == 0, False)
                    mm(reg, Xp[:, hh, :], A[hh][:, 0:128], False, True)
                cp(YS[0:64, 0:128], pY[0:64, 0:128])
                cp(YS[64:128, 0:128], pY[64:128, 128:256])
                yield
                stt(YS[:, 0:128], CV[:, pr, :], pv(l, "ss_D", 1, pr), YS[:, 0:128], ALU.mult, ALU.add)
                tt(YS[:, 0:128], YS[:, 0:128], ZT[:, 256 + pr * 128:256 + (pr + 1) * 128])
                post_rms(YS[:, 0:128], ONES, 128.0, pv(l, "ss_nw", 1, pr), None, YT[:, 6 + pr, :], g)
                yield
                for s in range(nseg):
                    lv = BTM[:, 0:128]
                    if nseg > 1:
                        m = g_rot()
                        I("act", "mul", out=m[:, 0:128], in_=lv, mul=CONST[:, C_SEG + s:C_SEG + s + 1])
                        lv = m[:, 0:128]
                    p = pb()
                    mm(p[:, 0:128], lv, X2[:, 0:128])
                    if nseg == 1:
                        hg = PST[(l, "ssm")][:, pr, :]
                        dest = hg
                    else:
                        hg = sst[:, s, pr, :]
                        dest = ost()[:]
                    for hh in range(2):
                        stt(dest[:, hh * 64:hh * 64 + 64], hg[:, hh * 64:hh * 64 + 64], lastcol(EL[hh][:, 0:128], bk, s),
                            p[:, hh * 64:hh * 64 + 64], ALU.mult, ALU.add)
                    if nseg > 1 or last:
                        for hh in range(2):
                            p2 = pb()
                            tr(p2[0:64, 0:128], dest[:, hh * 64:hh * 64 + 64])
                            o = ost()
                            cp(o[0:64, :], p2[0:64, 0:128], "act")
                            dd = outd[l][2 * pr + hh] if nseg == 1 else outd[l, s, 2 * pr + hh]
                            S.dma("pool", dd, o[0:64, :], is_out=True)

        import os
        class _Stop(Exception):
            pass
        kstop = os.environ.get('KSTOP', '')
        def ck(name):
            if name == kstop:
                raise _Stop()
        blks = [int(x) for x in os.environ.get('KBLKS', ','.join(str(i) for i in range(NBLK))).split(',')]
        try:
          ck('setup')
          def with_cx(cx, gen):
              while True:
                  CUR[0] = cx
                  try:
                      next(gen)
                  except StopIteration:
                      return
                  yield

          def rr_gen(gens):
              gens = list(gens)
              while gens:
                  for g_ in list(gens):
                      try:
                          next(g_)
                      except StopIteration:
                          gens.remove(g_)
                      yield

          def run_seq(gen):
              for _ in gen:
                  pass

          def mix_gen(blk, bk, l, last, samp):
              def wget(idx, l=l):
                  return wload16(l, idx)
              if not samp:
                  yield from rr_gen([dnet(bk, l, last, wget, None, G2), rwkv(bk, l, last, wget, None, G)])
                  yield from rr_gen([ssd(bk, l, last, wget, None, G2), gla(bk, l, last, wget, None, G)])
              else:
                  sst = SST[0]
                  I("dve", "memset", sst[:], 0.0)
                  load_bd(sst, std["wkv"][l])
                  for j in range(8):
                      p = pb()
                      for q in range(4):
                          tr(p[:, q * 128:(q + 1) * 128], sst[:, 2 * j + q // 2, q % 2, :])
                      cp(sst[:, 2 * j:2 * j + 2, :, :].f(lambda a: a.rearrange("p s r v -> p (s r v)")), p[:, 0:512])
                  yield
                  yield from rwkv(bk, l, last, wget, sst)
                  sst = SST[1]
                  I("dve", "memset", sst[:], 0.0)
                  load_bd(sst, std["gla"][l])
                  yield from gla(bk, l, last, wget, sst)
                  sst = SST[0]
                  I("dve", "memset", sst[:], 0.0)
                  load_bd(sst, std["dn"][l])
                  yield from dnet(bk, l, last, wget, sst)
                  sst = SST[1]
                  for s in range(16):
                      sl = SSL[0]
                      S.dma("pool", sl[:], std["ssm"][l, s].rearrange("h p n -> p h n"))
                      p = pb()
                      for h in range(4):
                          tr(p[:, h * 64:(h + 1) * 64], sl[0:64, h, :], 64)
                      cp(sst[:, s, :, :].f(lambda a: a.rearrange("p r v -> p (r v)")), p[:, 0:256])
                  yield
                  yield from ssd(bk, l, last, wget, sst)

          def dense_gen(blk, bk, l, samp):
              def wget(idx, l=l):
                  return wload16(l, idx)
              for half in range(2):
                  w = wget(9 + half)
                  p = pb()
                  for j in range(4):
                      for c8 in range(8):
                          mm(p[:, j * 128:(j + 1) * 128], w[:, c8, j * 128:(j + 1) * 128], YT[:, c8, :], c8 == 0, c8 == 7)
                  pv4 = p[:, 0:512].f(lambda a: a.rearrange("p (c s t) -> p c s t", c=4, t=bk.L))
                  tt(dv(TMP4[:], bk), pv4, modv(MODT[:, l, 16 + 4 * half:20 + 4 * half, :], bk, 4))
                  tt(XT[:, 4 * half:4 * half + 4, :], XT[:, 4 * half:4 * half + 4, :], TMP4[:], ALU.add)
                  yield
              if samp:
                  dump(6 * l + 2, XT)
              norm_mod(bk, modv(SC[:, l, 1], bk), modv(MODT[:, l, 24:32, :], bk))
              yield
              for s8 in range(8):
                  w = wget(11 + s8)
                  p = pb()
                  for j in range(4):
                      for kc in range(8):
                          mm(p[:, j * 128:(j + 1) * 128], w[:, kc, j * 128:(j + 1) * 128], HT[:, kc, :], kc == 0, kc == 7)
                  hv = HID[:, 4 * s8:4 * s8 + 4, :]
                  tmpr = (TMP4 if s8 % 2 else TMP5)
                  act(tmpr[:], p[:, 0:512].f(lambda a: a.rearrange("p (c t) -> p c t", t=128)), AF.Relu)
                  tt(hv, tmpr[:], tmpr[:], ALU.mult)
                  yield
              for half in range(2):
                  p = pb()
                  for fcg in range(4):
                      w = wget(19 + half * 4 + fcg)
                      for j in range(4):
                          for f8 in range(8):
                              mm(p[:, j * 128:(j + 1) * 128], w[:, f8, j * 128:(j + 1) * 128], HID[:, fcg * 8 + f8, :],
                                 fcg == 0 and f8 == 0 and j == 0, fcg == 3 and f8 == 7)
                  pv4 = p[:, 0:512].f(lambda a: a.rearrange("p (c s t) -> p c s t", c=4, t=bk.L))
                  tt(dv(TMP4[:], bk), pv4, modv(MODT[:, l, 40 + 4 * half:44 + 4 * half, :], bk, 4))
                  tt(XT[:, 4 * half:4 * half + 4, :], XT[:, 4 * half:4 * half + 4, :], TMP4[:], ALU.add)
                  yield
              if samp:
                  dump(6 * l + 4, XT)
              if l == NL - 1:
                  norm_mod(bk, bc2(pv(0, "fnw", 8), [128, 8, bk.nseg, bk.L]), None, HTF)
                  for b2 in range(2):
                      p = pb()
                      for j in range(4):
                          tr(p[:, j * 128:(j + 1) * 128], HTF[:, 4 * b2 + j, :])
                      cp(YTM[:, b2 * 512:(b2 + 1) * 512], p[:, 0:512])
                  S.dma("pool", yout[blk * 128:(blk + 1) * 128, :], YTM[:], is_out=True)

          pending = None
          for bi, blk in enumerate(blks):
              cx = CXS[bi % 2]
              CUR[0] = cx
              bk = Pk if blk < 16 else Sk
              last = blk == max(b for b in blks if b < 16) if blk < 16 else False
              samp = blk == 16
              if bi == 0:
                  S.dma("pool", XTM[:], xin[blk * 128:(blk + 1) * 128, :])
              for b2 in range(2):
                  p = pb()
                  for j in range(4):
                      tr(p[:, j * 128:(j + 1) * 128], XTM[:, (4 * b2 + j) * 128:(4 * b2 + j + 1) * 128])
                  cp(XT[:, 4 * b2:4 * b2 + 4, :], p[:, 0:512].f(lambda a: a.rearrange("p (c t) -> p c t", t=128)))
              for l in range(NL):
                  CUR[0] = cx
                  norm_mod(bk, modv(SC[:, l, 0], bk), modv(MODT[:, l, 0:8, :], bk))
                  M = with_cx(cx, mix_gen(blk, bk, l, last, samp))
                  if l == 0 and pending is not None:
                      run_seq(rr_gen([M, pending]))
                      pending = None
                  else:
                      run_seq(M)
                  if l == 0 and bi + 1 < len(blks):
                      nb = blks[bi + 1]
                      S.dma("pool", CXS[(bi + 1) % 2].XTM[:], xin[nb * 128:(nb + 1) * 128, :])
                  D = with_cx(cx, dense_gen(blk, bk, l, samp))
                  if l == NL - 1 and not KDBG:
                      pending = D
                  else:
                      run_seq(D)
          if pending is not None:
              run_seq(pending)
        except _Stop:
            pass
        print('opcounts', {e: len(v) for e, v in S.ops.items()}, flush=True)
        S.emit()
    return nc


W_IN_COLS = None


def _win_colmap():
    def rng(a, b):
        return list(range(a, b))
    slots = []
    slots.append(rng(0, 512))
    slots.append(rng(512, 896) + rng(1920, 1936))
    slots.append(rng(896, 1408))
    slots.append(rng(1408, 1920))
    slots.append(rng(1936, 2448))
    slots.append(rng(2448, 2960))
    slots.append(rng(2960, 2968) + rng(3992, 3996) + rng(2968, 3224))
    slots.append(rng(3224, 3736))
    slots.append(rng(3736, 3992))
    return slots


def _tile_rows(w, cols):
    out = np.zeros((128, 8, 512), np.float32)
    sub = w[:, cols]
    out[:, :, :len(cols)] = sub.reshape(8, 128, len(cols)).transpose(1, 0, 2)
    return out


_NC_CACHE = {}


def kernel(**inp):
    f = lambda k: np.ascontiguousarray(np.asarray(inp[k], dtype=np.float32))
    wall = np.zeros((2, 27, 128, 8, 512), np.float32)
    adaw = np.zeros((2, 12, 128, 8, 512), np.float32)
    cm = _win_colmap()
    w_in, w_out, w_up, w_down, ada_w = f("w_in"), f("w_out"), f("w_up"), f("w_down"), f("ada_w")
    for l in range(2):
        for s in range(9):
            wall[l, s] = _tile_rows(w_in[l], cm[s])
        for s in range(2):
            wall[l, 9 + s] = _tile_rows(w_out[l], list(range(s * 512, (s + 1) * 512)))
        for s in range(8):
            wall[l, 11 + s] = _tile_rows(w_up[l], list(range(s * 512, (s + 1) * 512)))
        for half in range(2):
            for fcg in range(4):
                blk = w_down[l][fcg * 1024:(fcg + 1) * 1024, half * 512:(half + 1) * 512]
                wall[l, 19 + half * 4 + fcg] = blk.reshape(8, 128, 512).transpose(1, 0, 2)
        for s in range(12):
            adaw[l, s] = _tile_rows(ada_w[l], list(range(s * 512, (s + 1) * 512)))
    pvv = np.zeros((128, 2 * NPV), np.float32)
    rows = np.zeros((128, 2 * NR), np.float32)
    sm = np.zeros((128, 2 * 1024), np.float32)

    def putv(l, name, vec, off=0):
        n = vec.shape[0] // 128
        pvv[:, l * NPV + PVO[name] + off:l * NPV + PVO[name] + off + n] = vec.reshape(n, 128).T
    for l in range(2):
        putv(l, "ada_b", f("ada_b")[l])
        putv(l, "n1w", f("norm1_w")[l])
        putv(l, "n2w", f("norm2_w")[l])
        putv(l, "mu", f("rwkv_mu")[l])
        putv(l, "a0", f("rwkv_a0")[l])
        putv(l, "k_k", f("rwkv_k_k")[l])
        putv(l, "k_a", f("rwkv_k_a")[l])
        putv(l, "r_k", f("rwkv_r_k")[l])
        putv(l, "ln_w", f("rwkv_ln_w")[l])
        putv(l, "ln_b", f("rwkv_ln_b")[l])
        putv(l, "gla_nw", f("gla_norm_w")[l])
        for i in range(4):
            putv(l, "dn_cw", f("dn_conv_w")[l, i], i * 6)
            putv(l, "ss_cw", f("ssm_conv_w")[l, i], i * 6)
        putv(l, "dn_nw", f("dn_norm_w")[l])
        putv(l, "ss_cb", f("ssm_conv_b")[l])
        putv(l, "ss_nw", f("ssm_norm_w")[l])
        putv(l, "ss_D", np.repeat(f("ssm_D")[l], 64))
        putv(l, "fnw", f("final_norm_w"))
        o = l * NR
        rows[:, o + RO["w0"]:o + RO["w0"] + 256] = f("rwkv_w0")[l][None, :]
        rows[:, o + RO["gkb"]:o + RO["gkb"] + 256] = f("gla_gk_b")[l][None, :]
        rows[:, o + RO["dnA"]:o + RO["dnA"] + 4] = f("dn_A_log")[l][None, :]
        rows[:, o + RO["dndt"]:o + RO["dndt"] + 4] = f("dn_dt_bias")[l][None, :]
        rows[:, o + RO["ssdt"]:o + RO["ssdt"] + 4] = f("ssm_dt_bias")[l][None, :]
        rows[:, o + RO["ssA"]:o + RO["ssA"] + 4] = f("ssm_A_log")[l][None, :]
        o = l * 1024
        sm[0:32, o:o + 256] = f("rwkv_w2")[l]
        sm[32:64, o + 256:o + 512] = f("rwkv_a2")[l]
        sm[64:128, o + 512:o + 768] = f("rwkv_g2")[l]
        sm[0:16, o + 768:o + 1024] = f("gla_gk_w2")[l]
    consts = make_consts()
    xp, xs = f("x_prompt"), f("x_sample")
    cpr, csm = f("c_prompt"), f("c_sample")
    stn = dict(shift="state_rwkv_shift", wkv="state_rwkv_wkv", gla="state_gla", dnc="state_dn_conv", dn="state_dn",
               ssc="state_ssm_conv", ssm="state_ssm")
    stf = {k: f(v) for k, v in stn.items()}
    in_maps = []
    for c in range(8):
        m = dict(wall=wall, ada=adaw, pv=pvv, rows=rows, sm=sm, consts=consts)
        m["xin"] = np.ascontiguousarray(np.concatenate([xp[c], xs[16 * c:16 * c + 16].reshape(128, 1024)], 0))
        m["cc"] = np.ascontiguousarray(np.concatenate([cpr[c:c + 1], csm[16 * c:16 * c + 16]], 0))
        for k in ST_SHAPES:
            m["st_" + k] = np.ascontiguousarray(stf[k][:, 16 * c:16 * c + 16])
        in_maps.append(m)
    if "nc" not in _NC_CACHE:
        _NC_CACHE["nc"] = build()
    res = run_bass_kernel_spmd(_NC_CACHE["nc"], in_maps, core_ids=list(range(8)))
    R = res.results
    global _LAST_R
    _LAST_R = R
    y_prompt = np.stack([R[c]["y"][:2048] for c in range(8)], 0)
    y_sample = np.concatenate([R[c]["y"][2048:].reshape(16, 8, 1024) for c in range(8)], 0)
    outs = [y_prompt, y_sample]
    for k in ("shift", "wkv", "gla", "dnc", "dn", "ssc", "ssm"):
        outs.append(np.stack([R[c]["p_" + k] for c in range(8)], 1))
    for k in ("shift", "wkv", "gla", "dnc", "dn", "ssc", "ssm"):
        outs.append(np.concatenate([R[c]["s_" + k] for c in range(8)], 1))
    return tuple(np.ascontiguousarray(o.astype(np.float32)) for o in outs)
```
